# Optimizing a Trainium2 kernel written in Bass

```python
import math
import jax, jax.numpy as jnp
from jax import lax
import numpy as np

D_MODEL = 1024
BATCH = 8
SEQ = 4096
DEPTH = 2

GRID_W = 64
CTX_LEN = 256
EPS = 1e-6

LRU_WIDTH = 512
LRU_BLOCKS = 8
LRU_BLOCK = LRU_WIDTH // LRU_BLOCKS
LRU_CONV = 4
LRU_C = 8.0

DA_HEADS = 4
DA_HEAD_DIM = 64
DA_V_DIM = 2 * DA_HEAD_DIM
DA_WIDTH = DA_HEADS * DA_V_DIM
ROPE_BASE = 10000.0
Q_BLOCK = 128

E_IN = 2 * LRU_WIDTH + 4 * DA_WIDTH
E_MIX = LRU_WIDTH + DA_WIDTH

SSD_INNER = 2 * D_MODEL
SSD_HEAD_DIM = 64
SSD_HEADS = SSD_INNER // SSD_HEAD_DIM
SSD_STATE = 128
SSD_GROUPS = 4
SSD_REP = SSD_HEADS // SSD_GROUPS
SSD_CONV = 4
SSD_CHUNK = 128
SSD_CONV_DIM = SSD_INNER + 2 * SSD_GROUPS * SSD_STATE
O_IN = SSD_INNER + SSD_CONV_DIM + 2 * SSD_HEADS

N_EVEN = (DEPTH + 1) // 2
N_ODD = DEPTH // 2

kernel_name = "hybrid_rglru_diffattn_ssd_prefix_dit"


def rms_norm(x, g):
    xf = x.astype(jnp.float32)
    y = xf * lax.rsqrt(jnp.mean(xf * xf, axis=-1, keepdims=True) + EPS)
    return (y * g.astype(jnp.float32)).astype(x.dtype)


def modulation(cvec, w, b):
    m = jax.nn.silu(cvec) @ w + b
    return jnp.split(m, 3, axis=-1)


def dwconv_centred(x, w, b):
    k = w.shape[0]
    y = lax.conv_general_dilated(
        x, w[:, None, :].astype(x.dtype), window_strides=(1,),
        padding=[(k // 2, k - 1 - k // 2)],
        dimension_numbers=("NWC", "WIO", "NWC"),
        feature_group_count=x.shape[-1])
    return y + b


def axial_rope_tables(n_tokens):
    rows = n_tokens // GRID_W
    row = jnp.repeat(jnp.arange(rows, dtype=jnp.float32), GRID_W)
    col = jnp.tile(jnp.arange(GRID_W, dtype=jnp.float32), rows)
    n_freq = DA_HEAD_DIM // 4
    inv = ROPE_BASE ** (-jnp.arange(n_freq, dtype=jnp.float32) / n_freq)
    ang = jnp.concatenate([row[:, None] * inv, col[:, None] * inv], axis=-1)
    return jnp.cos(ang), jnp.sin(ang)


def apply_axial_rope(x, cos, sin):
    n = cos.shape[-1] // 2

    def rot(u, cs, sn):
        u1, u2 = jnp.split(u, 2, axis=-1)
        cs = cs[None, :, None, None, :]
        sn = sn[None, :, None, None, :]
        return jnp.concatenate([u1 * cs - u2 * sn, u1 * sn + u2 * cs], axis=-1)

    xr, xc = jnp.split(x, 2, axis=-1)
    out = jnp.concatenate([rot(xr, cos[:, :n], sin[:, :n]), rot(xc, cos[:, n:], sin[:, n:])], axis=-1)
    return out.astype(x.dtype)


def diff_attention(q, k, v, lam):
    s = jnp.einsum("bqhcd,bkhcd->bhcqk", q, k).astype(jnp.float32) * (DA_HEAD_DIM ** -0.5)
    p = jax.nn.softmax(s, axis=-1)
    w = p[:, :, 0] - lam * p[:, :, 1]
    return jnp.einsum("bhqk,bkhe->bqhe", w.astype(v.dtype), v)


def _combine(e1, e2):
    a1, b1 = e1
    a2, b2 = e2
    return a1 * a2, a2 * b1 + b2


def linear_scan(a, b, h0, reverse):
    if reverse:
        a, b = jnp.flip(a, 1), jnp.flip(b, 1)
    b = b.at[:, 0].add(a[:, 0] * h0)
    _, h = lax.associative_scan(_combine, (a, b), axis=1)
    h_final = h[:, -1]
    if reverse:
        h = jnp.flip(h, 1)
    return h, h_final


def rglru_coeffs(xc, w_r, b_r, w_i, b_i, lam):
    bz, t, _ = xc.shape
    xb = xc.reshape(bz, t, LRU_BLOCKS, LRU_BLOCK)
    r = jax.nn.sigmoid(jnp.einsum("btnc,ncd->btnd", xb, w_r).reshape(bz, t, LRU_WIDTH) + b_r)
    i = jax.nn.sigmoid(jnp.einsum("btnc,ncd->btnd", xb, w_i).reshape(bz, t, LRU_WIDTH) + b_i)
    log_a = -LRU_C * r * jax.nn.softplus(-lam)
    a = jnp.exp(log_a)
    return a, jnp.sqrt(-jnp.expm1(2.0 * log_a)) * (i * xc)


def even_mixer(u_ctx, u_lat, cos, sin, lambda_init, ctx_out, w_in, w_out, conv_w, conv_b,
               w_r, b_r, w_i, b_i, lru_lambda, da_lambda, da_subln):
    bz, s, _ = u_lat.shape
    tc = u_ctx.shape[1]
    xr_c, gr_c, q_c, k_c, v_c, gd_c = jnp.split(u_ctx @ w_in, 6, axis=-1)
    xr_l, gr_l, q_l, k_l, v_l, gd_l = jnp.split(u_lat @ w_in, 6, axis=-1)

    xc_c = dwconv_centred(xr_c, conv_w, conv_b)
    xc_l = dwconv_centred(xr_l, conv_w, conv_b)
    hs_c, hs_l = [], []
    for d, rev in enumerate((False, True)):
        a, b = rglru_coeffs(xc_c, w_r[d], b_r[d], w_i[d], b_i[d], lru_lambda[d])
        h_c, h_fin = linear_scan(a, b, jnp.zeros((bz, LRU_WIDTH), b.dtype), rev)
        a, b = rglru_coeffs(xc_l, w_r[d], b_r[d], w_i[d], b_i[d], lru_lambda[d])
        h_l, _ = linear_scan(a, b, h_fin, rev)
        hs_c.append(h_c)
        hs_l.append(h_l)
    r_l = hs_l[0] + hs_l[1]

    lq1, lk1, lq2, lk2 = da_lambda.astype(jnp.float32)
    lam = jnp.exp(jnp.sum(lq1 * lk1)) - jnp.exp(jnp.sum(lq2 * lk2)) + lambda_init
    q_c = q_c.reshape(bz, tc, DA_HEADS, 2, DA_HEAD_DIM)
    k_c = k_c.reshape(bz, tc, DA_HEADS, 2, DA_HEAD_DIM)
    v_c = v_c.reshape(bz, tc, DA_HEADS, DA_V_DIM)
    q_l = apply_axial_rope(q_l.reshape(bz, s, DA_HEADS, 2, DA_HEAD_DIM), cos, sin)
    k_l = apply_axial_rope(k_l.reshape(bz, s, DA_HEADS, 2, DA_HEAD_DIM), cos, sin)
    v_l = v_l.reshape(bz, s, DA_HEADS, DA_V_DIM)
    k_all = jnp.concatenate([k_c, k_l], axis=1)
    v_all = jnp.concatenate([v_c, v_l], axis=1)
    n_blk = s // Q_BLOCK
    q_blocks = jnp.moveaxis(q_l.reshape(bz, n_blk, Q_BLOCK, DA_HEADS, 2, DA_HEAD_DIM), 1, 0)
    o_l = lax.map(lambda qb: diff_attention(qb, k_all, v_all, lam), q_blocks)
    o_l = jnp.moveaxis(o_l, 0, 1).reshape(bz, s, DA_HEADS, DA_V_DIM)

    def head_norm(o):
        return (rms_norm(o, da_subln) * (1.0 - lambda_init)).reshape(o.shape[0], o.shape[1], DA_WIDTH)

    y_l = jnp.concatenate([r_l * jax.nn.silu(gr_l), head_norm(o_l) * jax.nn.silu(gd_l)], axis=-1) @ w_out
    y_c = None
    if ctx_out:
        o_c = diff_attention(q_c, k_c, v_c, lam)
        r_c = hs_c[0] + hs_c[1]
        y_c = jnp.concatenate([r_c * jax.nn.silu(gr_c), head_norm(o_c) * jax.nn.silu(gd_c)], axis=-1) @ w_out
    return y_c, y_l


def ssd_chunked(X, Adt, Bm, Cm, h0):
    bz, t = X.shape[:2]
    nc, L = t // SSD_CHUNK, SSD_CHUNK
    Xc = X.reshape(bz, nc, L, SSD_GROUPS, SSD_REP, SSD_HEAD_DIM)
    Bc = Bm.reshape(bz, nc, L, SSD_GROUPS, SSD_STATE)
    Cc = Cm.reshape(bz, nc, L, SSD_GROUPS, SSD_STATE)
    A = jnp.transpose(Adt.astype(jnp.float32).reshape(bz, nc, L, SSD_GROUPS, SSD_REP), (0, 3, 4, 1, 2))
    A_cs = jnp.cumsum(A, axis=-1)
    seg = A_cs[..., :, None] - A_cs[..., None, :]
    tri = jnp.tril(jnp.ones((L, L), dtype=bool))
    Lmat = jnp.exp(jnp.where(tri, seg, -jnp.inf))
    CB = jnp.einsum("bclgn,bcsgn->bcgls", Cc, Bc)
    y_diag = jnp.einsum("bcgls,bgrcls,bcsgrp->bclgrp", CB, Lmat, Xc)
    decay_states = jnp.exp(A_cs[..., -1:] - A_cs)
    states = jnp.einsum("bclgn,bgrcl,bclgrp->bcgrpn", Bc, decay_states, Xc)
    states = jnp.concatenate([h0[:, None].astype(states.dtype), states], axis=1)
    chunk_tot = jnp.pad(A_cs[..., -1], ((0, 0), (0, 0), (0, 0), (1, 0)))
    cs = jnp.cumsum(chunk_tot, axis=-1)
    tri_c = jnp.tril(jnp.ones((nc + 1, nc + 1), dtype=bool))
    decay_chunk = jnp.exp(jnp.where(tri_c, cs[..., :, None] - cs[..., None, :], -jnp.inf))
    new_states = jnp.einsum("bgrzc,bcgrpn->bzgrpn", decay_chunk, states)
    states_in, final_state = new_states[:, :-1], new_states[:, -1]
    y_off = jnp.einsum("bclgn,bcgrpn,bgrcl->bclgrp", Cc, states_in, jnp.exp(A_cs))
    y = (y_diag + y_off).reshape(bz, t, SSD_GROUPS, SSD_REP, SSD_HEAD_DIM).astype(X.dtype)
    return y, final_state


def ssd_direction(xs, Bm, Cm, dt_d, A_d, h0, reverse):
    bz, t = xs.shape[:2]
    dt_g = dt_d.reshape(bz, t, SSD_GROUPS, SSD_REP)
    X = xs * dt_g[..., None]
    Adt = dt_g * A_d.reshape(SSD_GROUPS, SSD_REP)
    if reverse:
        X, Adt, Bm, Cm = jnp.flip(X, 1), jnp.flip(Adt, 1), jnp.flip(Bm, 1), jnp.flip(Cm, 1)
    y, h = ssd_chunked(X, Adt, Bm, Cm, h0)
    if reverse:
        y = jnp.flip(y, 1)
    return y, h


def odd_mixer(u_ctx, u_lat, ctx_out, w_in, w_out, conv_w, conv_b, a_log, dt_bias, d_skip, norm_w):
    def project(u):
        bz, t = u.shape[:2]
        z, xbc, dt = jnp.split(u @ w_in, [SSD_INNER, SSD_INNER + SSD_CONV_DIM], axis=-1)
        xbc = jax.nn.silu(dwconv_centred(xbc, conv_w, conv_b))
        xs, Bm, Cm = jnp.split(xbc, [SSD_INNER, SSD_INNER + SSD_GROUPS * SSD_STATE], axis=-1)
        xs = xs.reshape(bz, t, SSD_GROUPS, SSD_REP, SSD_HEAD_DIM)
        Bm = Bm.reshape(bz, t, SSD_GROUPS, SSD_STATE)
        Cm = Cm.reshape(bz, t, SSD_GROUPS, SSD_STATE)
        dt = jax.nn.softplus(dt.reshape(bz, t, 2, SSD_HEADS) + dt_bias)
        return z, xs, Bm, Cm, dt

    def finish(z, xs, ys):
        bz, t = xs.shape[:2]
        y = ys[0] + ys[1] + d_skip.reshape(SSD_GROUPS, SSD_REP)[:, :, None] * xs
        y = y.reshape(bz, t, SSD_INNER) * jax.nn.silu(z)
        y = rms_norm(y.reshape(bz, t, SSD_GROUPS, SSD_INNER // SSD_GROUPS),
                     norm_w.reshape(SSD_GROUPS, SSD_INNER // SSD_GROUPS)).reshape(bz, t, SSD_INNER)
        return y @ w_out

    A = -jnp.exp(a_log)
    z_c, xs_c, B_c, C_c, dt_c = project(u_ctx)
    z_l, xs_l, B_l, C_l, dt_l = project(u_lat)
    bz = u_lat.shape[0]
    ys_c, ys_l = [], []
    for d, rev in enumerate((False, True)):
        h0 = jnp.zeros((bz, SSD_GROUPS, SSD_REP, SSD_HEAD_DIM, SSD_STATE), jnp.float32)
        y_c, h_fin = ssd_direction(xs_c, B_c, C_c, dt_c[:, :, d], A[d], h0, rev)
        y_l, _ = ssd_direction(xs_l, B_l, C_l, dt_l[:, :, d], A[d], h_fin, rev)
        ys_c.append(y_c)
        ys_l.append(y_l)
    y_lat = finish(z_l, xs_l, ys_l)
    y_ctx = finish(z_c, xs_c, ys_c) if ctx_out else None
    return y_ctx, y_lat


def setup_inputs(seed: int = 0) -> dict:
    key = jax.random.key(seed)
    ks = jax.random.split(key, 27)
    f32 = jnp.float32
    nrm = lambda k, shape, s: jax.random.normal(k, shape, f32) * s
    u_lru = jax.random.uniform(ks[16], (N_EVEN, 2, LRU_WIDTH), f32, 0.9, 0.999)
    a_lru = u_lru ** (1.0 / LRU_C)
    dt0 = jnp.exp(jax.random.uniform(ks[24], (N_ODD, 2, SSD_HEADS), f32, math.log(1e-3), math.log(1e-1)))
    return {
        "x": nrm(ks[0], (BATCH, SEQ, D_MODEL), 1.0),
        "c": nrm(ks[1], (BATCH, D_MODEL), 1.0),
        "ctx": nrm(ks[2], (BATCH, CTX_LEN, D_MODEL), 1.0),
        "c_ctx": nrm(ks[3], (D_MODEL,), 1.0),
        "w_mod": nrm(ks[4], (DEPTH, D_MODEL, 3 * D_MODEL), 0.5 * D_MODEL ** -0.5),
        "b_mod": nrm(ks[5], (DEPTH, 3 * D_MODEL), 0.02),
        "g_pre": 1.0 + nrm(ks[6], (DEPTH, D_MODEL), 0.05),
        "g_post": 1.0 + nrm(ks[7], (DEPTH, D_MODEL), 0.05),
        "e_w_in": nrm(ks[8], (N_EVEN, D_MODEL, E_IN), D_MODEL ** -0.5),
        "e_w_out": nrm(ks[9], (N_EVEN, E_MIX, D_MODEL), E_MIX ** -0.5),
        "lru_conv_w": nrm(ks[10], (N_EVEN, LRU_CONV, LRU_WIDTH), LRU_CONV ** -0.5),
        "lru_conv_b": nrm(ks[11], (N_EVEN, LRU_WIDTH), 0.02),
        "lru_w_r": nrm(ks[12], (N_EVEN, 2, LRU_BLOCKS, LRU_BLOCK, LRU_BLOCK), LRU_BLOCK ** -0.5),
        "lru_b_r": nrm(ks[13], (N_EVEN, 2, LRU_WIDTH), 0.02),
        "lru_w_i": nrm(ks[14], (N_EVEN, 2, LRU_BLOCKS, LRU_BLOCK, LRU_BLOCK), LRU_BLOCK ** -0.5),
        "lru_b_i": nrm(ks[15], (N_EVEN, 2, LRU_WIDTH), 0.02),
        "lru_lambda": jnp.log(a_lru) - jnp.log1p(-a_lru),
        "da_lambda": nrm(ks[17], (N_EVEN, 4, DA_HEAD_DIM), 0.1),
        "da_subln": 1.0 + nrm(ks[18], (N_EVEN, DA_V_DIM), 0.05),
        "o_w_in": nrm(ks[19], (N_ODD, D_MODEL, O_IN), D_MODEL ** -0.5),
        "o_w_out": nrm(ks[20], (N_ODD, SSD_INNER, D_MODEL), SSD_INNER ** -0.5),
        "ssd_conv_w": nrm(ks[21], (N_ODD, SSD_CONV, SSD_CONV_DIM), SSD_CONV ** -0.5),
        "ssd_conv_b": nrm(ks[22], (N_ODD, SSD_CONV_DIM), 0.02),
        "ssd_a_log": jnp.log(jax.random.uniform(ks[23], (N_ODD, 2, SSD_HEADS), f32, 1.0, 16.0)),
        "ssd_dt_bias": dt0 + jnp.log(-jnp.expm1(-dt0)),
        "ssd_d": 1.0 + nrm(ks[25], (N_ODD, SSD_HEADS), 0.05),
        "ssd_norm": 1.0 + nrm(ks[26], (N_ODD, SSD_INNER), 0.05),
    }


def reference(x, c, ctx, c_ctx, w_mod, b_mod, g_pre, g_post, e_w_in, e_w_out, lru_conv_w, lru_conv_b,
              lru_w_r, lru_b_r, lru_w_i, lru_b_i, lru_lambda, da_lambda, da_subln, o_w_in, o_w_out,
              ssd_conv_w, ssd_conv_b, ssd_a_log, ssd_dt_bias, ssd_d, ssd_norm):
    cos, sin = axial_rope_tables(x.shape[1])
    h_lat, h_ctx = x, ctx
    for i in range(DEPTH):
        last = i == DEPTH - 1
        sh, sc, gt = modulation(c, w_mod[i], b_mod[i])
        sh_c, sc_c, gt_c = modulation(c_ctx, w_mod[i], b_mod[i])
        u_lat = rms_norm(h_lat, g_pre[i]) * (1.0 + sc[:, None]) + sh[:, None]
        u_ctx = rms_norm(h_ctx, g_pre[i]) * (1.0 + sc_c) + sh_c
        j = i // 2
        if i % 2 == 0:
            lambda_init = 0.8 - 0.6 * math.exp(-0.3 * i)
            y_ctx, y_lat = even_mixer(u_ctx, u_lat, cos, sin, lambda_init, not last,
                                      e_w_in[j], e_w_out[j], lru_conv_w[j], lru_conv_b[j],
                                      lru_w_r[j], lru_b_r[j], lru_w_i[j], lru_b_i[j], lru_lambda[j],
                                      da_lambda[j], da_subln[j])
        else:
            y_ctx, y_lat = odd_mixer(u_ctx, u_lat, not last, o_w_in[j], o_w_out[j], ssd_conv_w[j],
                                     ssd_conv_b[j], ssd_a_log[j], ssd_dt_bias[j], ssd_d[j], ssd_norm[j])
        h_lat = h_lat + gt[:, None] * rms_norm(y_lat, g_post[i])
        if not last:
            h_ctx = h_ctx + gt_c * rms_norm(y_ctx, g_post[i])
    return h_lat
```

```python
import os
from contextlib import ExitStack
import numpy as np
import concourse.bass as bass
import concourse.mybir as mybir
from concourse.bass_utils import run_bass_kernel_spmd

F32, BF16 = mybir.dt.float32, mybir.dt.bfloat16
AF = mybir.ActivationFunctionType
ALU = mybir.AluOpType
AX = mybir.AxisListType

D = 1024
SEQ = 4096
CTX = 256
T = SEQ + CTX
NT = T // 128
EPS = 1e-6


class Sched:
    ENG = ("pe", "dve", "act", "pool", "sp")

    def __init__(self, nc, es):
        self.nc, self.es = nc, es
        self.e = {"pe": nc.tensor, "dve": nc.vector, "act": nc.scalar, "pool": nc.gpsimd, "sp": nc.sync}
        self.sem, self.cnt = {}, {}
        for n in self.ENG:
            self.sem[n] = es.enter_context(nc.semaphore("s_" + n))
            self.cnt[n] = 0
        self.seen = {n: {} for n in self.ENG}
        self.W, self.Rd = {}, {}
        self.pend = {n: [] for n in self.ENG}
        self.nwait = 0
        self.nins = 0

    def _wait(self, eng, need):
        for s, v in need.items():
            if s == "pe" and eng == "pe":
                continue
            if self.seen[eng].get(s, 0) >= v:
                continue
            self.e[eng].wait_ge(self.sem[s], v)
            self.seen[eng][s] = v
            self.nwait += 1

    def _deps(self, eng, reads, writes, accw):
        need = {}

        def add(d):
            for s, v in d.items():
                if need.get(s, 0) < v:
                    need[s] = v
        for r in reads:
            add(self.W.get(r, {}))
        for w in writes:
            add(self.W.get(w, {}))
            add(self.Rd.get(w, {}))
        for w in accw:
            add(self.Rd.get(w, {}))
        self._wait(eng, need)

    def _register(self, ev, reads, writes, accw):
        s, v = ev
        for r in reads:
            d = self.Rd.setdefault(r, {})
            d[s] = max(d.get(s, 0), v)
        for w in writes:
            self.W[w] = {s: v}
            self.Rd[w] = {}
        for w in accw:
            d = self.W.setdefault(w, {})
            d[s] = max(d.get(s, 0), v)

    def op(self, eng, fn, reads=(), writes=(), accw=(), inc=True):
        self._deps(eng, reads, writes, accw)
        ins = fn(self.e[eng])
        self.nins += 1
        if inc:
            self.cnt[eng] += 1
            ins.then_inc(self.sem[eng], 1)
            ev = (eng, self.cnt[eng])
            for (r, w, a) in self.pend[eng]:
                self._register(ev, r, w, a)
            self.pend[eng] = []
            self._register(ev, reads, writes, accw)
        else:
            self.pend[eng].append((tuple(reads), tuple(writes), tuple(accw)))

    def dma(self, q, out, in_, semkey, reads=(), writes=(), accw=(), **kw):
        if semkey not in self.sem:
            self.sem[semkey] = self.es.enter_context(self.nc.semaphore("d_" + semkey.replace("#", "_")))
            self.cnt[semkey] = 0
        self._deps(q, reads, writes, accw)
        ins = self.e[q].dma_start(out=out, in_=in_, **kw)
        ins.then_inc(self.sem[semkey], 16)
        self.cnt[semkey] += 16
        self.nins += 1
        self._register((semkey, self.cnt[semkey]), reads, writes, accw)

    def tt(self, eng, out, a, b, op, reads, writes=(), accw=()):
        self.op(eng, lambda e: e.tensor_tensor(out, a, b, op), reads, writes, accw)

    def ts(self, eng, out, a, s1, s2, op0, op1=None, reads=(), writes=(), accw=()):
        if op1 is None:
            self.op(eng, lambda e: e.tensor_scalar(out, a, s1, None, op0), reads, writes, accw)
        else:
            self.op(eng, lambda e: e.tensor_scalar(out, a, s1, s2, op0, op1), reads, writes, accw)

    def stt(self, eng, out, a, sc, b, op0, op1, reads, writes=(), accw=()):
        self.op(eng, lambda e: e.scalar_tensor_tensor(out, a, sc, b, op0, op1), reads, writes, accw)

    def act(self, out, in_, func, reads, writes=(), accw=(), **kw):
        self.op("act", lambda e: e.activation(out=out, in_=in_, func=func, **kw), reads, writes, accw)

    def cp(self, eng, out, in_, reads, writes=(), accw=()):
        if eng == "act":
            self.op("act", lambda e: e.copy(out, in_), reads, writes, accw)
        else:
            self.op(eng, lambda e: e.tensor_copy(out, in_), reads, writes, accw)

    def mm(self, out, lhsT, rhs, start, stop, reads, writes, inc):
        self.op("pe", lambda e: e.matmul(out, lhsT, rhs, start=start, stop=stop), reads, writes, inc=inc)

    def barrier(self):
        for n in self.ENG:
            assert not self.pend[n]
        allev = {s: c for s, c in self.cnt.items() if c > 0}
        for n in self.ENG:
            self._wait(n, allev)
        self.W, self.Rd = {}, {}


class Buf:
    def __init__(self, t, key):
        self.t, self.key = t, key

    def __getitem__(self, k):
        return self.t[k]


class Ring:
    def __init__(self, alloc, name, n, shape, dtype):
        self.bufs = [Buf(alloc(f"{name}{i}", shape, dtype), f"{name}#{i}") for i in range(n)]
        self.i = 0

    def next(self):
        b = self.bufs[self.i % len(self.bufs)]
        self.i += 1
        return b


class Prog:
    def __init__(self, dbg=None):
        self.dbg = dbg or {}
        self.nc = nc = bass.Bass("TRN2", target_bir_lowering=False)
        self.es = ExitStack()
        self.K = Sched(nc, self.es)
        self.dram = {}
        self.uid = 0

    def din(self, name, shape, dt=F32):
        self.dram[name] = self.nc.dram_tensor(name, list(shape), dt, kind="ExternalInput").ap()
        return self.dram[name]

    def dout(self, name, shape, dt=F32):
        self.dram[name] = self.nc.dram_tensor(name, list(shape), dt, kind="ExternalOutput").ap()
        return self.dram[name]

    def dscr(self, name, shape, dt):
        if name in self.dbg.get("dump", ()):
            return self.dout(name, shape, dt)
        self.dram[name] = self.nc.dram_tensor(name, list(shape), dt).ap()
        return self.dram[name]


def build_program(dbg=None):
    P = Prog(dbg)
    nc, K = P.nc, P.K
    stop_after = P.dbg.get("stop_after", "all")

    src0 = P.din("src0", [T, D])
    cvec = P.din("cvec", [2, D])
    w_mod = P.din("w_mod", [2, D, 3 * D])
    b_mod = P.din("b_mod", [2, 3 * D])
    g_pre = P.din("g_pre", [2, D])
    g_post = P.din("g_post", [2, D])
    e_w_in = P.din("e_w_in_aug", [D, 4096])
    e_w_out = P.din("e_w_out", [D, D])
    ident = P.din("ident", [128, 128])
    ropetab = P.din("ropetab", [2, 128, T])
    out_h = P.dout("out", [SEQ, D])
    lru_wbd = P.din("lru_wbd", [16, 128, 128])
    lru_vec = P.din("lru_vec", [11, 512])
    da_lam = P.din("da_lam", [1, 256])
    da_sub = P.din("da_sub", [1, 128])
    mixT0 = P.dscr("mixT0", [D, T], BF16)
    o_w_in = P.din("o_w_in", [D, 5184])
    o_w_out = P.din("o_w_out", [2048, D])
    ssd_cv = P.din("ssd_cv", [5, 3072])
    ssd_vec = P.din("ssd_vec", [1, 160])
    ssd_norm = P.din("ssd_norm", [2048])
    maskT = P.din("maskT", [2, 128, 128])
    selc = P.din("selc", [64, 32 * 128])
    xbcT = P.dscr("xbcT", [3072, T], BF16)
    zs = P.dscr("zs", [T, 2048], BF16)
    dtraw = P.dscr("dtraw", [T, 64], F32)
    xsB = P.dscr("xsB", [T, 2560], BF16)
    bcT = P.dscr("bcT", [1024, T], BF16)
    yfw = P.dscr("yfw", [T, 2048], F32)
    mixT1 = P.dscr("mixT1", [2048, T], BF16)
    h1 = P.dscr("h1", [T, D], F32)

    xrT = P.dscr("xrT", [512, T], F32)
    grT = P.dscr("grT", [512, T], BF16)
    gdT = P.dscr("gdT", [512, T], BF16)
    qT = P.dscr("qT", [512, T], BF16)
    kT = P.dscr("kT", [512, T], BF16)
    vtok = P.dscr("vtok", [T, 512], BF16)

    pes = ExitStack()
    def palloc(name, shape, dt):
        return pes.enter_context(nc.sbuf_tensor(name, list(shape), dt))
    psum = [pes.enter_context(nc.psum_tensor(f"psb{i}", [128, 512], F32)) for i in range(8)]
    idf = palloc("idf", [128, 128], F32)
    idb = palloc("idb", [128, 128], BF16)
    epsb = palloc("epsb", [128, 1], F32)
    modA = palloc("modA", [128, 2, 8, 2], F32)
    modS = palloc("modS", [128, 2, 8, 2], F32)
    ggbc = palloc("ggbc", [128, 2, 2, D], F32)

    K.dma("sp", idf[:], ident[:, :], "Lidf", writes=["idf"])
    K.op("dve", lambda e: e.tensor_copy(idb[:], idf[:]), reads=["idf"], writes=["idb"])
    K.op("dve", lambda e: e.memset(epsb[:], EPS), writes=["epsb"])

    def phase_mod():
        with ExitStack() as es:
            def alloc(name, shape, dt):
                P.uid += 1
                return es.enter_context(nc.sbuf_tensor(f"{name}_{P.uid}", list(shape), dt))
            cT = alloc("cT", [128, 2, 8], F32)
            sig = alloc("sig", [128, 2, 8], F32)
            srep = alloc("srep", [128, 2, 8, 128], F32)
            bT = alloc("bT", [128, 2, 24], F32)
            gpT = alloc("gpT", [128, 2, 8], F32)
            bgbc = alloc("bgbc", [128, 2, D], F32)
            gpbc = alloc("gpbc", [128, 2, D], F32)
            wst = Ring(alloc, "wst", 2, [128, 8, 512], F32)
            tmp = alloc("mtmp", [128, 16, 2], F32)

            rows = alloc("rows", [80, 128], F32)
            K.dma("sp", rows[0:16, :], cvec.rearrange("t (j p) -> (t j) p", p=128), "Lrows", accw=["rows"])
            K.dma("sp", rows[16:64, :], b_mod.rearrange("l (f p) -> (l f) p", p=128), "Lrows", accw=["rows"])
            K.dma("sp", rows[64:80, :], g_pre.rearrange("l (j p) -> (l j) p", p=128), "Lrows", accw=["rows"])
            K.op("pe", lambda e: e.transpose(psum[4][:, 0:80], rows[:, :], idf[0:80, 0:80]),
                 reads=["rows", "idf"], writes=["ps#4"])
            K.op("dve", lambda e: e.tensor_copy(cT[:].rearrange("p t j -> p (t j)"), psum[4][:, 0:16]),
                 reads=["ps#4"], writes=["cT"])
            K.op("dve", lambda e: e.tensor_copy(bT[:].rearrange("p l f -> p (l f)"), psum[4][:, 16:64]),
                 reads=["ps#4"], writes=["bT"])
            K.op("dve", lambda e: e.tensor_copy(gpT[:].rearrange("p l j -> p (l j)"), psum[4][:, 64:80]),
                 reads=["ps#4"], writes=["gpT"])
            for l in range(2):
                K.dma("sp", bgbc[:, l, :], b_mod[l, 2 * D:3 * D].partition_broadcast(128), "Lbgbc", accw=["bgbc"])
                K.dma("sp", gpbc[:, l, :], g_post[l, :].partition_broadcast(128), "Lgpbc", accw=["gpbc"])
            K.op("act", lambda e: e.activation(out=sig[:], in_=cT[:], func=AF.Sigmoid), reads=["cT"], writes=["sig"])
            K.op("dve", lambda e: e.tensor_tensor(cT[:], cT[:], sig[:], ALU.mult), reads=["sig", "cT"], writes=["cT"])
            for t in range(2):
                K.op("dve", lambda e, t=t: e.tensor_copy(
                    srep[:, t, :, :], cT[:, t, :].unsqueeze(2).to_broadcast([128, 8, 128])),
                    reads=["cT"], accw=["srep"])
            for l in range(2):
                for pc in range(6):
                    wb = wst.next()
                    K.dma("sp", wb[:], w_mod[l, :, pc * 512:(pc + 1) * 512].rearrange("(j p) n -> p j n", p=128),
                          "L" + wb.key, writes=[wb.key])
                    if pc < 4:
                        pst = psum[pc % 2]
                        for f in range(4):
                            for j in range(8):
                                K.op("pe", lambda e, f=f, j=j, pst=pst, wb=wb: e.matmul(
                                    pst[:, 2 * f:2 * f + 2], wb[:, j, f * 128:(f + 1) * 128], cT[:, :, j],
                                    start=(j == 0), stop=(j == 7)),
                                    reads=[wb.key, "cT"], writes=[f"ps#{pc % 2}"], inc=(j == 7 and f == 3))
                        K.op("dve", lambda e, pst=pst, pc=pc: e.tensor_copy(
                            tmp[:, pc * 4:(pc + 1) * 4, :], pst[:, 0:8].rearrange("p (f t) -> p f t", t=2)),
                            reads=[f"ps#{pc % 2}"], accw=["mtmp"])
                    else:
                        for t in range(2):
                            pst = psum[2 + t]
                            for j in range(8):
                                K.op("pe", lambda e, j=j, t=t, pst=pst, wb=wb: e.matmul(
                                    pst[:, :], srep[:, t, j, :], wb[:, j, :], start=(j == 0), stop=(j == 7)),
                                    reads=[wb.key, "srep"], writes=[f"ps#{2 + t}"], inc=(j == 7))
                            c0 = (pc - 4) * 512
                            K.op("dve", lambda e, t=t, l=l, c0=c0, pst=pst: e.tensor_tensor(
                                ggbc[:, l, t, c0:c0 + 512], pst[:, :], bgbc[:, l, c0:c0 + 512], ALU.add),
                                reads=[f"ps#{2 + t}", "bgbc"], accw=["ggbc"])
                for t in range(2):
                    K.op("dve", lambda e, t=t, l=l: e.tensor_tensor(
                        modS[:, l, :, t], tmp[:, 0:8, t], bT[:, l, 0:8], ALU.add),
                        reads=["mtmp", "bT"], accw=["modS"])
                    K.op("dve", lambda e, t=t, l=l: e.scalar_tensor_tensor(
                        modA[:, l, :, t], tmp[:, 8:16, t], 1.0, bT[:, l, 8:16], ALU.add, ALU.add),
                        reads=["mtmp", "bT"], accw=["modA"])
                    K.op("dve", lambda e, t=t, l=l: e.tensor_tensor(
                        modA[:, l, :, t], modA[:, l, :, t], gpT[:, l, :], ALU.mult),
                        reads=["modA", "gpT"], writes=["modA"])
                    K.op("dve", lambda e, t=t, l=l: e.tensor_tensor(
                        ggbc[:, l, t, :], ggbc[:, l, t, :], gpbc[:, l, :], ALU.mult),
                        reads=["ggbc", "gpbc"], writes=["ggbc"])
            K.barrier()

    phase_mod()
    if "mod" in P.dbg.get("dump", ()):
        dA = P.dout("dbg_modA", [128, 32]); dS = P.dout("dbg_modS", [128, 32]); dG = P.dout("dbg_gg", [128, 4 * D])
        K.dma("sp", dA[:, :], modA[:].rearrange("p l j t -> p (l j t)"), "Sdbg", reads=["modA"])
        K.dma("sp", dS[:, :], modS[:].rearrange("p l j t -> p (l j t)"), "Sdbg", reads=["modS"])
        K.dma("sp", dG[:, :], ggbc[:].rearrange("p l t f -> p (l t f)"), "Sdbg", reads=["ggbc"])
    if stop_after == "mod":
        K.barrier(); pes.close(); P.es.close(); return P

    def phase_proj(layer, src, Wd, ncols, fspecs, tspecs, extra_alloc=None, per_group=None):
        with ExitStack() as es:
            def alloc(name, shape, dt):
                P.uid += 1
                return es.enter_context(nc.sbuf_tensor(f"{name}_{P.uid}", list(shape), dt))
            Wb = alloc("Wb", [128, 8, ncols], BF16)
            wst = Ring(alloc, "wst", 2, [128, 8, 256], F32)
            xr_ = Ring(alloc, "xin", 3, [128, D], F32)
            xn_ = Ring(alloc, "xn", 2, [128, D], BF16)
            uT_ = Ring(alloc, "uT", 2, [128, 8, 512], BF16)
            junk = alloc("junk", [128, D], BF16)
            stat = Ring(alloc, "stat", 4, [128, 4], F32)
            ctxo = extra_alloc(alloc) if extra_alloc else None
            ceng = ["dve", "pool", "act"]
            for pc in range(ncols // 256 + (1 if ncols % 256 else 0)):
                c0 = pc * 256
                cw = min(256, ncols - c0)
                wb = wst.next()
                K.dma("sp", wb[:, :, 0:cw], Wd[:, c0:c0 + cw].rearrange("(j p) n -> p j n", p=128),
                      "L" + wb.key, writes=[wb.key])
                en = ceng[pc % 3]
                if en == "act":
                    K.op("act", lambda e, wb=wb, c0=c0, cw=cw: e.copy(Wb[:, :, c0:c0 + cw], wb[:, :, 0:cw]),
                         reads=[wb.key], accw=["Wb"])
                else:
                    K.op(en, lambda e, wb=wb, c0=c0, cw=cw: e.tensor_copy(Wb[:, :, c0:c0 + cw], wb[:, :, 0:cw]),
                         reads=[wb.key], accw=["Wb"])
            groups = [(0, CTX, 1)] + [(CTX + 512 * g, 512, 0) for g in range(SEQ // 512)]
            pst_i = [0]
            pso_i = [0]

            def front(gi):
                tok0, ntok, tmod = groups[gi]
                uT = uT_.next()
                for ti in range(ntok // 128):
                    xt = xr_.next(); xn = xn_.next(); st = stat.next()
                    K.dma("sp", xt[:], src[tok0 + ti * 128: tok0 + (ti + 1) * 128, :], "L" + xt.key, writes=[xt.key])
                    K.op("act", lambda e, xt=xt, st=st: e.activation(out=junk[:], in_=xt[:], func=AF.Square,
                                                                     accum_out=st[:, 0:1]),
                         reads=[xt.key], writes=["junk", st.key])
                    fl = P.dbg.get("fl", 9)
                    if fl < 2: continue
                    K.op("act", lambda e, st=st: e.activation(out=st[:, 1:2], in_=st[:, 0:1], func=AF.Sqrt,
                                                              scale=1.0 / D, bias=epsb[:, 0:1]),
                         reads=[st.key, "epsb"], writes=[st.key])
                    K.op("dve", lambda e, st=st: e.reciprocal(st[:, 2:3], st[:, 1:2]), reads=[st.key], writes=[st.key])
                    if fl < 3: continue
                    K.op("pool", lambda e, xt=xt, xn=xn, st=st: e.tensor_scalar(xn[:], xt[:], st[:, 2:3], None, ALU.mult),
                         reads=[xt.key, st.key], writes=[xn.key])
                    if fl < 4: continue
                    pk = 6 + (pst_i[0] % 2); pst_i[0] += 1
                    pst = psum[pk][:].bitcast(BF16)
                    for j in range(8):
                        K.op("pe", lambda e, j=j, pst=pst, xn=xn: e.transpose(
                            pst[:, j * 128:(j + 1) * 128], xn[:, j * 128:(j + 1) * 128], idb[:]),
                            reads=[xn.key, "idb"], writes=[f"ps#{pk}"], inc=(j == 7))
                    if fl < 5: continue
                    for j in range(8):
                        if True:
                            K.op("dve", lambda e, j=j, pst=pst, uT=uT, ti=ti, tmod=tmod: e.tensor_scalar(
                                uT[:, j, ti * 128:(ti + 1) * 128], pst[:, j * 128:(j + 1) * 128],
                                modA[:, layer, j, tmod:tmod + 1], modS[:, layer, j, tmod:tmod + 1], ALU.mult, ALU.add),
                                reads=[f"ps#{pk}", "modA", "modS"], accw=[uT.key])
                        else:
                            K.op("act", lambda e, j=j, pst=pst, uT=uT, ti=ti, tmod=tmod: e.activation(
                                out=uT[:, j, ti * 128:(ti + 1) * 128], in_=pst[:, j * 128:(j + 1) * 128],
                                func=AF.Identity, scale=modA[:, layer, j, tmod:tmod + 1],
                                bias=modS[:, layer, j, tmod:tmod + 1]),
                                reads=[f"ps#{pk}", "modA", "modS"], accw=[uT.key])
                return uT

            def mm(gi, uT):
                tok0, ntok, tmod = groups[gi]
                if per_group:
                    per_group(ctxo, gi, tok0, ntok)
                for (name, bundles, epi) in fspecs:
                    for bi, cols in enumerate(bundles):
                        pks = []
                        for c0 in cols:
                            pk = pso_i[0] % 6; pso_i[0] += 1
                            pks.append(pk)
                            for j in range(8):
                                K.op("pe", lambda e, j=j, pk=pk, c0=c0, uT=uT, ntok=ntok: e.matmul(
                                    psum[pk][:, 0:ntok], Wb[:, j, c0:c0 + 128], uT[:, j, 0:ntok],
                                    start=(j == 0), stop=(j == 7)),
                                    reads=["Wb", uT.key], writes=[f"ps#{pk}"], inc=(j == 7))
                        epi(ctxo, pks, bi, tok0, ntok)
                for (c0, cw, epi) in tspecs:
                    for ti in range(ntok // 128):
                        pk = pso_i[0] % 6; pso_i[0] += 1
                        for j in range(8):
                            K.op("pe", lambda e, j=j, pk=pk, uT=uT, ti=ti: e.matmul(
                                psum[pk][:, 0:cw], uT[:, j, ti * 128:(ti + 1) * 128], Wb[:, j, c0:c0 + cw],
                                start=(j == 0), stop=(j == 7)),
                                reads=["Wb", uT.key], writes=[f"ps#{pk}"], inc=(j == 7))
                        epi(ctxo, pk, tok0 + ti * 128)

            ng = P.dbg.get("ngroups", len(groups))
            lvl = P.dbg.get("lvl", 9)
            if lvl == 1:
                K.barrier(); return
            if lvl == 2:
                front(0); K.barrier(); return
            uts = {0: front(0)}
            for gi in range(ng):
                if gi + 1 < ng:
                    uts[gi + 1] = front(gi + 1)
                mm(gi, uts.pop(gi))
            K.barrier()

    def l0_alloc(alloc):
        c = {}
        c["sf"] = Ring(alloc, "sf", 3, [128, 512], F32)
        c["sb"] = Ring(alloc, "sb", 4, [128, 512], BF16)
        c["t1"] = Ring(alloc, "t1", 2, [128, 512], F32)
        c["t2"] = Ring(alloc, "t2", 2, [128, 512], F32)
        c["tab"] = Ring(alloc, "tab", 2, [128, 2, 512], F32)
        return c

    def l0_group(c, gi, tok0, ntok):
        tb = c["tab"].next()
        c["curtab"] = tb
        K.dma("sp", tb[:, :, 0:ntok], ropetab[:, :, tok0:tok0 + ntok].rearrange("c p t -> p c t"),
              "L" + tb.key, writes=[tb.key])

    def epi_copy_f32(dst):
        def f(c, pks, bi, tok0, ntok):
            pk = pks[0]; b = c["sf"].next()
            K.op("act", lambda e: e.copy(b[:, 0:ntok], psum[pk][:, 0:ntok]), reads=[f"ps#{pk}"], writes=[b.key])
            K.dma("sp", dst[bi * 128:(bi + 1) * 128, tok0:tok0 + ntok], b[:, 0:ntok], "S" + b.key,
                  reads=[b.key], accw=[dst.name])
        return f

    def epi_silu_bf(dst):
        def f(c, pks, bi, tok0, ntok):
            pk = pks[0]; b = c["sb"].next()
            K.op("act", lambda e: e.activation(out=b[:, 0:ntok], in_=psum[pk][:, 0:ntok], func=AF.Silu),
                 reads=[f"ps#{pk}"], writes=[b.key])
            K.dma("sp", dst[bi * 128:(bi + 1) * 128, tok0:tok0 + ntok], b[:, 0:ntok], "S" + b.key,
                  reads=[b.key], accw=[dst.name])
        return f

    def epi_rope(dst):
        def f(c, pks, bi, tok0, ntok):
            pa, pb = pks; t1 = c["t1"].next(); t2 = c["t2"].next(); b = c["sb"].next(); tb = c["curtab"]
            K.op("dve", lambda e: e.tensor_tensor(t1[:, 0:ntok], psum[pa][:, 0:ntok], tb[:, 0, 0:ntok], ALU.mult),
                 reads=[f"ps#{pa}", tb.key], writes=[t1.key])
            K.op("dve", lambda e: e.tensor_tensor(t2[:, 0:ntok], psum[pb][:, 0:ntok], tb[:, 1, 0:ntok], ALU.mult),
                 reads=[f"ps#{pb}", tb.key], writes=[t2.key])
            K.op("pool", lambda e: e.tensor_tensor(b[:, 0:ntok], t1[:, 0:ntok], t2[:, 0:ntok], ALU.add),
                 reads=[t1.key, t2.key], writes=[b.key])
            K.dma("sp", dst[bi * 128:(bi + 1) * 128, tok0:tok0 + ntok], b[:, 0:ntok], "S" + b.key,
                  reads=[b.key], accw=[dst.name])
        return f

    def epi_v(c, pk, tok0):
        b = c["sb"].next()
        K.op("act", lambda e: e.copy(b[:, :], psum[pk][:, :]), reads=[f"ps#{pk}"], writes=[b.key])
        K.dma("sp", vtok[tok0:tok0 + 128, :], b[:, :], "S" + b.key, reads=[b.key], accw=["vtok"])

    l0_f = [
        ("xr", [[f * 128] for f in range(0, 4)], epi_copy_f32(xrT)),
        ("gr", [[f * 128] for f in range(4, 8)], epi_silu_bf(grT)),
        ("q", [[1024 + f * 128, 3072 + f * 128] for f in range(4)], epi_rope(qT)),
        ("k", [[1536 + f * 128, 3584 + f * 128] for f in range(4)], epi_rope(kT)),
        ("gd", [[f * 128] for f in range(20, 24)], epi_silu_bf(gdT)),
    ]
    l0_t = [(2048, 512, epi_v)]
    phase_proj(0, src0, e_w_in, 4096, l0_f, l0_t, l0_alloc, l0_group)
    if stop_after == "proj0":
        K.barrier(); pes.close(); P.es.close(); return P


    LAMBDA_INIT0 = 0.8 - 0.6 * 1.0

    def phase_lru():
        with ExitStack() as es:
            def alloc(name, shape, dt):
                P.uid += 1
                return es.enter_context(nc.sbuf_tensor(f"{name}_{P.uid}", list(shape), dt))
            rows = alloc("lrows", [44, 128], F32)
            pv = alloc("lpv", [128, 11, 4], F32)
            coef = alloc("lcoef", [128, 2, 4], F32)
            ones1 = alloc("ones1", [128, 1], F32)
            wbf = alloc("wbf", [128, 16, 128], F32)
            wbb = alloc("wbb", [128, 16, 128], BF16)
            big = Ring(alloc, "big", 7, [128, T], F32)
            xcb = alloc("xcb", [128, T], BF16)
            grs = alloc("grs", [128, T], BF16)
            mo = alloc("mixo", [128, T], BF16)
            K.op("dve", lambda e: e.memset(ones1[:], 1.0), writes=["ones1"])
            K.dma("sp", rows[:, :], lru_vec.rearrange("v (c p) -> (v c) p", p=128), "Lrows", writes=["lrows"])
            K.op("pe", lambda e: e.transpose(psum[0][:, 0:44], rows[:, :], idf[0:44, 0:44]),
                 reads=["lrows", "idf"], writes=["ps#0"])
            K.cp("dve", pv[:].rearrange("p v c -> p (v c)"), psum[0][:, 0:44], ["ps#0"], ["lpv"])
            K.act(coef[:].rearrange("p d c -> p (d c)"), pv[:, 9:11, :].rearrange("p d c -> p (d c)"), AF.Exp,
                  ["lpv"], ["lcoef"], scale=-1.0)
            K.act(coef[:].rearrange("p d c -> p (d c)"), coef[:].rearrange("p d c -> p (d c)"), AF.Ln,
                  ["lcoef", "ones1"], ["lcoef"], bias=ones1[:, 0:1])
            K.ts("dve", coef[:].rearrange("p d c -> p (d c)"), coef[:].rearrange("p d c -> p (d c)"), -8.0, None,
                 ALU.mult, None, ["lcoef"], ["lcoef"])
            K.dma("sp", wbf[:], lru_wbd.rearrange("n k m -> k n m"), "Lwbf", writes=["wbf"])
            K.cp("dve", wbb[:], wbf[:], ["wbf"], ["wbb"])
            segs = [(0, CTX), (CTX, T)]
            blocks = [(b0, min(512, T - b0)) for b0 in range(0, T, 512)]
            for cc in range(4):
                x = big.next(); xc = big.next()
                K.dma("sp", x[:, :], xrT[cc * 128:(cc + 1) * 128, :], "L" + x.key, writes=[x.key])
                K.dma("sp", grs[:, :], grT[cc * 128:(cc + 1) * 128, :], "Lgrs", writes=["grs"])
                K.ts("dve", xc[:, :], x[:, :], pv[:, 2, cc:cc + 1], pv[:, 4, cc:cc + 1], ALU.mult, ALU.add,
                     [x.key, "lpv"], [xc.key])
                for (a, b) in segs:
                    for tap, sh in ((0, -2), (1, -1), (3, 1)):
                        lo = max(a, a - sh); hi = min(b, b - sh)
                        K.stt("dve", xc[:, lo:hi], x[:, lo + sh:hi + sh], pv[:, tap, cc:cc + 1],
                              xc[:, lo:hi], ALU.mult, ALU.add, [x.key, xc.key, "lpv"], [xc.key])
                K.cp("pool", xcb[:, :], xc[:, :], [xc.key], ["xcb"])
                hs = []
                for d in range(2):
                    rb = big.next(); ib = big.next()
                    for (b0, bn) in blocks:
                        for g, dstb, brow in ((0, rb, 5 + d), (1, ib, 7 + d)):
                            pk = (2 * (b0 // 512) + g) % 6
                            K.mm(psum[pk][:, 0:bn], wbb[:, (d * 2 + g) * 4 + cc, :], xcb[:, b0:b0 + bn], True, True,
                                 ["wbb", "xcb"], [f"ps#{pk}"], True)
                            K.act(dstb[:, b0:b0 + bn], psum[pk][:, 0:bn], AF.Sigmoid, [f"ps#{pk}", "lpv"], accw=[dstb.key],
                                  bias=pv[:, brow, cc:cc + 1])
                    K.ts("dve", rb[:, :], rb[:, :], coef[:, d, cc:cc + 1], None, ALU.mult, None, [rb.key, "lcoef"], [rb.key])
                    K.act(rb[:, :], rb[:, :], AF.Exp, [rb.key], [rb.key])
                    sq = big.next()
                    K.tt("pool", sq[:, :], rb[:, :], rb[:, :], ALU.mult, [rb.key], [sq.key])
                    K.act(sq[:, :], sq[:, :], AF.Sqrt, [sq.key, "ones1"], [sq.key], scale=-1.0, bias=ones1[:, 0:1])
                    K.tt("pool", ib[:, :], ib[:, :], xc[:, :], ALU.mult, [ib.key, xc.key], [ib.key])
                    K.tt("dve", ib[:, :], ib[:, :], sq[:, :], ALU.mult, [ib.key, sq.key], [ib.key])
                    h = sq
                    if d == 0:
                        K.op("dve", lambda e, h=h, rb=rb, ib=ib: e.tensor_tensor_scan(
                            h[:, :], rb[:, :], ib[:, :], 0.0, ALU.mult, ALU.add), [rb.key, ib.key, h.key], [h.key])
                    else:
                        K.op("dve", lambda e, h=h, rb=rb, ib=ib: e.tensor_tensor_scan(
                            h[:, CTX - 1::-1] if False else h[:, 0:CTX][:, ::-1], rb[:, 0:CTX][:, ::-1], ib[:, 0:CTX][:, ::-1],
                            0.0, ALU.mult, ALU.add), [rb.key, ib.key, h.key], [h.key])
                        K.op("dve", lambda e, h=h, rb=rb, ib=ib: e.tensor_tensor_scan(
                            h[:, CTX:T][:, ::-1], rb[:, CTX:T][:, ::-1], ib[:, CTX:T][:, ::-1],
                            h[:, 0:1], ALU.mult, ALU.add), [rb.key, ib.key, h.key], [h.key])
                    hs.append(h)
                K.tt("pool", hs[0][:, :], hs[0][:, :], hs[1][:, :], ALU.add, [hs[0].key, hs[1].key], [hs[0].key])
                K.tt("dve", mo[:, :], hs[0][:, :], grs[:, :], ALU.mult, [hs[0].key, "grs"], ["mixo"])
                K.dma("sp", mixT0[cc * 128:(cc + 1) * 128, :], mo[:, :], "Smixo", reads=["mixo"], accw=["mixT0"])
            K.barrier()

    if "lru" not in P.dbg.get("skip", ()):
        phase_lru()
    if stop_after == "lru":
        K.barrier(); pes.close(); P.es.close(); return P

    def phase_attn():
        with ExitStack() as es:
            def alloc(name, shape, dt):
                P.uid += 1
                return es.enter_context(nc.sbuf_tensor(f"{name}_{P.uid}", list(shape), dt))
            kres = alloc("kres", [128, 4, T], BF16)
            vres = alloc("vres", [128, NT, 512], BF16)
            onesb = alloc("onesb", [128, 128], BF16)
            onesf = alloc("onesf", [128, 128], F32)
            lrow = alloc("lamrow", [1, 260], F32)
            lamc = alloc("lamc", [128, 2], F32)
            subc = alloc("subc", [128, 2], F32)
            subrow = alloc("subrow", [1, 128], F32)
            qb_ = Ring(alloc, "qblk", 2, [128, 4, 512], BF16)
            gd_ = Ring(alloc, "gdblk", 2, [128, 512], BF16)
            E_ = Ring(alloc, "Eb", 4, [128, 512], BF16)
            f_ = Ring(alloc, "af", 6, [128, 512], F32)
            ob_ = Ring(alloc, "aob", 2, [128, 512], BF16)
            K.op("dve", lambda e: e.memset(onesb[:], 1.0), writes=["onesb"])
            K.op("dve", lambda e: e.memset(onesf[:], 1.0), writes=["onesf"])
            K.dma("sp", lrow[:, 0:256], da_lam[:, :], "Llam", writes=["lamrow"])
            K.dma("sp", subrow[:, :], da_sub[:, :], "Lsub", writes=["subrow"])
            K.tt("dve", lrow[:, 0:64], lrow[:, 0:64], lrow[:, 64:128], ALU.mult, ["lamrow"], ["lamrow"])
            K.tt("dve", lrow[:, 128:192], lrow[:, 128:192], lrow[:, 192:256], ALU.mult, ["lamrow"], ["lamrow"])
            K.op("dve", lambda e: e.reduce_sum(lrow[:, 256:257], lrow[:, 0:64], AX.X), ["lamrow"], ["lamrow"])
            K.op("dve", lambda e: e.reduce_sum(lrow[:, 257:258], lrow[:, 128:192], AX.X), ["lamrow"], ["lamrow"])
            K.act(lrow[:, 256:258], lrow[:, 256:258], AF.Exp, ["lamrow"], ["lamrow"])
            K.tt("dve", lrow[:, 258:259], lrow[:, 256:257], lrow[:, 257:258], ALU.subtract, ["lamrow"], ["lamrow"])
            K.ts("dve", lrow[:, 258:259], lrow[:, 258:259], -1.0, -LAMBDA_INIT0, ALU.mult, ALU.add, ["lamrow"], ["lamrow"])
            K.mm(psum[0][:, 0:1], onesf[0:1, :], lrow[0:1, 258:259], True, True, ["onesf", "lamrow"], ["ps#0"], True)
            K.cp("dve", lamc[:, 0:1], psum[0][:, 0:1], ["ps#0"], ["lamc"])
            K.op("pe", lambda e: e.transpose(psum[1][:, 0:1], subrow[0:1, :], idf[0:1, 0:1]),
                 reads=["subrow", "idf"], writes=["ps#1"])
            K.ts("dve", subc[:, 0:1], psum[1][:, 0:1], 1.0 - LAMBDA_INIT0, None, ALU.mult, None, ["ps#1"], ["subc"])
            for h in range(4):
                K.dma("sp", kres[:, h, :], kT[h * 128:(h + 1) * 128, :], "Lkres", accw=["kres"])
            for n0 in range(0, NT, 2):
                K.dma("sp", vres[:, n0:n0 + 2, :], vtok[n0 * 128:(n0 + 2) * 128, :].rearrange("(n p) e -> p n e", p=128),
                      "Lvres", accw=["vres"])
            qblocks = [(0, CTX, 0, 2)] + [(CTX + 512 * g, 512, 0, NT) for g in range(SEQ // 512)]
            nqb = P.dbg.get("nqb", len(qblocks))
            sti = 0
            for (q0, nq, kt0, kt1) in qblocks[:nqb]:
                qb = qb_.next()
                for h in range(4):
                    K.dma("sp", qb[:, h, 0:nq], qT[h * 128:(h + 1) * 128, q0:q0 + nq], "L" + qb.key, accw=[qb.key])
                for h in range(4):
                    gd = gd_.next()
                    K.dma("sp", gd[:, 0:nq], gdT[h * 128:(h + 1) * 128, q0:q0 + nq], "L" + gd.key, writes=[gd.key])
                    for kt in range(kt0, kt1):
                        for c in range(2):
                            pk = sti % 3; sti += 1
                            E = E_.next()
                            K.mm(psum[pk][:, 0:nq], kres[c * 64:(c + 1) * 64, h, kt * 128:(kt + 1) * 128],
                                 qb[c * 64:(c + 1) * 64, h, 0:nq], True, True, ["kres", qb.key], [f"ps#{pk}"], True)
                            K.act(E[:, 0:nq], psum[pk][:, 0:nq], AF.Exp, [f"ps#{pk}"], [E.key], scale=0.125)
                            K.mm(psum[3 + c][:, 0:nq], vres[:, kt, h * 128:(h + 1) * 128], E[:, 0:nq],
                                 kt == kt0, kt == kt1 - 1, ["vres", E.key], [f"ps#{3 + c}"], kt == kt1 - 1)
                            K.mm(psum[5 + c][:, 0:nq], onesb[:, :], E[:, 0:nq],
                                 kt == kt0, kt == kt1 - 1, ["onesb", E.key], [f"ps#{5 + c}"], True)
                    r0 = f_.next(); r1 = f_.next(); t0 = f_.next(); t1 = f_.next()
                    K.op("dve", lambda e, r0=r0: e.reciprocal(r0[:, 0:nq], psum[5][:, 0:nq]), ["ps#5"], [r0.key])
                    K.op("dve", lambda e, r1=r1: e.reciprocal(r1[:, 0:nq], psum[6][:, 0:nq]), ["ps#6"], [r1.key])
                    K.tt("dve", t0[:, 0:nq], psum[3][:, 0:nq], r0[:, 0:nq], ALU.mult, ["ps#3", r0.key], [t0.key])
                    K.tt("dve", t1[:, 0:nq], psum[4][:, 0:nq], r1[:, 0:nq], ALU.mult, ["ps#4", r1.key], [t1.key])
                    K.stt("dve", t0[:, 0:nq], t1[:, 0:nq], lamc[:, 0:1], t0[:, 0:nq], ALU.mult, ALU.add,
                          [t0.key, t1.key, "lamc"], [t0.key])
                    osq = ob_.next()
                    K.tt("pool", osq[:, 0:nq], t0[:, 0:nq], t0[:, 0:nq], ALU.mult, [t0.key], [osq.key])
                    K.mm(psum[7][:, 0:nq], onesb[:, :], osq[:, 0:nq], True, True, ["onesb", osq.key], ["ps#7"], True)
                    K.act(r0[:, 0:nq], psum[7][:, 0:nq], AF.Sqrt, ["ps#7", "epsb"], [r0.key], scale=1.0 / 128, bias=epsb[:, 0:1])
                    K.op("dve", lambda e, r0=r0, r1=r1: e.reciprocal(r1[:, 0:nq], r0[:, 0:nq]), [r0.key], [r1.key])
                    K.tt("dve", t0[:, 0:nq], t0[:, 0:nq], r1[:, 0:nq], ALU.mult, [t0.key, r1.key], [t0.key])
                    mo = ob_.next()
                    K.stt("dve", mo[:, 0:nq], t0[:, 0:nq], subc[:, 0:1], gd[:, 0:nq], ALU.mult, ALU.mult,
                          [t0.key, "subc", gd.key], [mo.key])
                    K.dma("sp", mixT0[(4 + h) * 128:(5 + h) * 128, q0:q0 + nq], mo[:, 0:nq], "S" + mo.key,
                          reads=[mo.key], accw=["mixT0"])
            K.barrier()

    if "attn" not in P.dbg.get("skip", ()):
        phase_attn()
    if stop_after == "attn":
        K.barrier(); pes.close(); P.es.close(); return P

    def phase_out(layer, mixT, KC, Wd, res_src, dst, tiles):
        with ExitStack() as es:
            def alloc(name, shape, dt):
                P.uid += 1
                return es.enter_context(nc.sbuf_tensor(f"{name}_{P.uid}", list(shape), dt))
            Wb = alloc("Wo", [128, KC, D], BF16)
            wst = Ring(alloc, "wost", 2, [128, KC, 256], F32)
            mx_ = Ring(alloc, "mxin", 2, [128, KC, 512], BF16)
            rs_ = Ring(alloc, "resin", 3, [128, D], F32)
            tm_ = Ring(alloc, "otmp", 2, [128, D], F32)
            st_ = Ring(alloc, "ostat", 4, [128, 4], F32)
            junk = alloc("ojunk", [128, 512], BF16)
            for pc in range(4):
                wb = wst.next()
                K.dma("sp", wb[:], Wd[:, pc * 256:(pc + 1) * 256].rearrange("(j p) n -> p j n", p=128), "L" + wb.key,
                      writes=[wb.key])
                K.cp(["dve", "pool"][pc % 2], Wb[:, :, pc * 256:(pc + 1) * 256], wb[:], [wb.key], accw=["Wo"])
            gi = 0
            cur = None
            for (tok0, drow, tmod) in tiles:
                g0 = (tok0 // 512) * 512 if tok0 >= CTX else 0
                if tok0 >= CTX:
                    g0 = CTX + ((tok0 - CTX) // 512) * 512
                gn = CTX if tok0 < CTX else 512
                if cur is None or cur[0] != g0:
                    mx = mx_.next()
                    K.dma("sp", mx[:, :, 0:gn], mixT[:, g0:g0 + gn].rearrange("(j p) t -> p j t", p=128), "L" + mx.key,
                          writes=[mx.key])
                    cur = (g0, mx)
                mx = cur[1]; lo = tok0 - g0
                rs = rs_.next(); tm = tm_.next(); st = st_.next()
                K.dma("sp", rs[:, :], res_src[tok0:tok0 + 128, :], "L" + rs.key, writes=[rs.key])
                pks = [(2 * gi) % 6, (2 * gi + 1) % 6]; gi += 1
                for nb in range(2):
                    for j in range(KC):
                        K.mm(psum[pks[nb]][:, :], mx[:, j, lo:lo + 128], Wb[:, j, nb * 512:(nb + 1) * 512],
                             j == 0, j == KC - 1, [mx.key, "Wo"], [f"ps#{pks[nb]}"], j == KC - 1)
                for nb in range(2):
                    K.act(junk[:, :], psum[pks[nb]][:, :], AF.Square, [f"ps#{pks[nb]}"], ["ojunk", st.key] if nb == 0 else ["ojunk"],
                          accw=() if nb == 0 else [st.key], accum_out=st[:, nb:nb + 1])
                K.tt("dve", st[:, 2:3], st[:, 0:1], st[:, 1:2], ALU.add, [st.key], [st.key])
                K.act(st[:, 3:4], st[:, 2:3], AF.Sqrt, [st.key, "epsb"], [st.key], scale=1.0 / D, bias=epsb[:, 0:1])
                K.op("dve", lambda e, st=st: e.reciprocal(st[:, 2:3], st[:, 3:4]), [st.key], [st.key])
                for nb in range(2):
                    K.stt("dve", tm[:, nb * 512:(nb + 1) * 512], psum[pks[nb]][:, :], st[:, 2:3],
                          ggbc[:, layer, tmod, nb * 512:(nb + 1) * 512], ALU.mult, ALU.mult,
                          [f"ps#{pks[nb]}", st.key, "ggbc"], accw=[tm.key])
                K.tt("pool", tm[:, :], tm[:, :], rs[:, :], ALU.add, [tm.key, rs.key], [tm.key])
                K.dma("sp", dst[drow:drow + 128, :], tm[:, :], "S" + tm.key, reads=[tm.key], accw=[dst.name])
            K.barrier()

    nt0 = P.dbg.get("out0_tiles", NT)
    tiles0 = [(i * 128, i * 128, 1 if i < 2 else 0) for i in range(nt0)]
    phase_out(0, mixT0, 8, e_w_out, src0, h1, tiles0)
    if stop_after == "out0":
        K.barrier(); pes.close(); P.es.close(); return P


    def l1_alloc(alloc):
        c = {}
        c["sb"] = Ring(alloc, "sb1", 4, [128, 512], BF16)
        c["sf"] = Ring(alloc, "sf1", 2, [128, 64], F32)
        c["n"] = 0
        return c

    def epi_xbc(c, pks, bi, tok0, ntok):
        pk = pks[0]; b = c["sb"].next()
        c["n"] += 1
        K.cp("act" if c["n"] % 2 else "dve", b[:, 0:ntok], psum[pk][:, 0:ntok], [f"ps#{pk}"], [b.key])
        K.dma("sp", xbcT[bi * 128:(bi + 1) * 128, tok0:tok0 + ntok], b[:, 0:ntok], "S" + b.key,
              reads=[b.key], accw=["xbcT"])

    def epi_z(zc):
        def f(c, pk, tok0):
            b = c["sb"].next()
            K.act(b[:, :], psum[pk][:, :], AF.Silu, [f"ps#{pk}"], [b.key])
            K.dma("sp", zs[tok0:tok0 + 128, zc * 512:(zc + 1) * 512], b[:, :], "S" + b.key, reads=[b.key], accw=["zs"])
        return f

    def epi_dt(c, pk, tok0):
        b = c["sf"].next()
        K.cp("dve", b[:, :], psum[pk][:, 0:64], [f"ps#{pk}"], [b.key])
        K.dma("sp", dtraw[tok0:tok0 + 128, :], b[:, :], "S" + b.key, reads=[b.key], accw=["dtraw"])

    l1_f = [("xbc", [[2048 + f * 128] for f in range(24)], epi_xbc)]
    l1_t = [(zc * 512, 512, epi_z(zc)) for zc in range(4)] + [(5120, 64, epi_dt)]
    if "l1" not in P.dbg.get("skip", ()):
        phase_proj(1, h1, o_w_in, 5184, l1_f, l1_t, l1_alloc, None)
    if stop_after == "proj1":
        K.barrier(); pes.close(); P.es.close(); return P

    def phase_conv():
        with ExitStack() as es:
            def alloc(name, shape, dt):
                P.uid += 1
                return es.enter_context(nc.sbuf_tensor(f"{name}_{P.uid}", list(shape), dt))
            rows = alloc("cvrows", [120, 128], F32)
            cv = alloc("cv", [128, 5, 24], F32)
            dg = alloc("dg", [128, 4, 24, 128], BF16)
            xin_ = Ring(alloc, "cxin", 2, [128, 24, 516], BF16)
            sb_ = Ring(alloc, "csb", 3, [128, 512], BF16)
            rb_ = Ring(alloc, "crow", 8, [128, 2560], BF16)
            K.dma("sp", rows[:, :], ssd_cv.rearrange("v (f p) -> (v f) p", p=128), "Lcvrows", writes=["cvrows"])
            K.op("pe", lambda e: e.transpose(psum[0][:, 0:120], rows[:, :], idf[0:120, 0:120]),
                 reads=["cvrows", "idf"], writes=["ps#0"])
            K.cp("dve", cv[:].rearrange("p v f -> p (v f)"), psum[0][:, 0:120], ["ps#0"], ["cv"])
            for j in range(4):
                for f in range(24):
                    K.ts("dve" if (f % 2) else "pool", dg[:, j, f, :], idf[:, :], cv[:, j, f:f + 1], None, ALU.mult, None,
                         ["idf", "cv"], accw=["dg"])
            blocks = [(0, CTX, 0, CTX)] + [(CTX + 512 * g, 512, CTX, T) for g in range(SEQ // 512)]
            cpi = 0
            for (b0, bn, sa, sb_end) in blocks[:P.dbg.get("nconv", 99)]:
                xin = xin_.next()
                l0 = max(sa, b0 - 2); l1 = min(sb_end, b0 + bn + 1)
                K.dma("sp", xin[:, :, l0 - (b0 - 2):l1 - (b0 - 2)],
                      xbcT[:, l0:l1].rearrange("(f p) t -> p f t", p=128), "L" + xin.key, writes=[xin.key])
                nt_ = bn // 128
                rbs = [rb_.next() for _ in range(nt_)]
                for ft in range(24):
                    pk = 4 + (cpi % 2); cpi += 1
                    order = [2, 0, 1, 3]
                    for oi, j in enumerate(order):
                        sh = j - 2
                        lo = max(b0, sa - sh); hi = min(b0 + bn, sb_end - sh)
                        K.mm(psum[pk][:, lo - b0:hi - b0], dg[:, j, ft, :],
                             xin[:, ft, lo + sh - (b0 - 2):hi + sh - (b0 - 2)], oi == 0, oi == 3,
                             ["dg", xin.key], [f"ps#{pk}"], oi == 3)
                    sb = sb_.next()
                    K.act(sb[:, 0:bn], psum[pk][:, 0:bn], AF.Silu, [f"ps#{pk}", "cv"], [sb.key], bias=cv[:, 4, ft:ft + 1])
                    if ft >= 16:
                        K.dma("sp", bcT[(ft - 16) * 128:(ft - 15) * 128, b0:b0 + bn], sb[:, 0:bn], "S" + sb.key,
                              reads=[sb.key], accw=["bcT"])
                    if ft < 20:
                        for i in range(nt_):
                            tb = psum[i][:].bitcast(BF16)
                            K.op("pe", lambda e, tb=tb, sb=sb, i=i, ft=ft: e.transpose(
                                tb[:, (ft % 8) * 128:(ft % 8 + 1) * 128], sb[:, i * 128:(i + 1) * 128], idb[:]),
                                reads=[sb.key, "idb"], writes=[f"ps#{i}"], inc=True)
                        if ft % 8 == 7 or ft == 19:
                            ncol = (ft % 8 + 1) * 128
                            c0 = (ft // 8) * 1024
                            for i in range(nt_):
                                tb = psum[i][:].bitcast(BF16)
                                K.cp("dve" if i % 2 else "act", rbs[i][:, c0:c0 + ncol], tb[:, 0:ncol], [f"ps#{i}"],
                                     accw=[rbs[i].key])
                for i in range(nt_):
                    K.dma("sp", xsB[b0 + i * 128:b0 + (i + 1) * 128, :], rbs[i][:, :], "S" + rbs[i].key,
                          reads=[rbs[i].key], accw=["xsB"])
            K.barrier()

    if "conv" not in P.dbg.get("skip", ()):
        phase_conv()
    if stop_after == "conv":
        K.barrier(); pes.close(); P.es.close(); return P


    def phase_ssd():
        with ExitStack() as es:
            def alloc(name, shape, dt):
                P.uid += 1
                return es.enter_context(nc.sbuf_tensor(f"{name}_{P.uid}", list(shape), dt))
            vb = alloc("vb", [128, 160], F32)
            aneg = alloc("aneg", [128, 64], F32)
            nwbc = alloc("nwbc", [128, 2048], F32)
            mk = alloc("mk", [128, 2, 128], F32)
            self_ = alloc("self", [64, 4096], F32)
            selb = alloc("selb", [64, 32, 128], BF16)
            onesf = alloc("onesf2", [128, 128], F32)
            ones1 = alloc("ones1b", [128, 1], F32)
            state = alloc("state", [128, 2048], F32)
            stbf = alloc("stbf", [128, 2048], BF16)
            xb_ = Ring(alloc, "xb", 2, [128, 2560], BF16)
            bc_ = Ring(alloc, "bc", 2, [128, 8, 128], BF16)
            dr_ = Ring(alloc, "dr", 2, [128, 32], F32)
            sm_ = Ring(alloc, "sm", 2, [128, 8, 32], F32)
            cs4_ = Ring(alloc, "cs4", 2, [128, 64], F32)
            cst_ = Ring(alloc, "cst", 2, [64, 128], F32)
            hl_ = Ring(alloc, "hl", 2, [64, 3, 128], BF16)
            X_ = Ring(alloc, "Xd", 2, [128, 2048], BF16)
            Xc_ = Ring(alloc, "Xc", 2, [128, 2048], BF16)
            cbm_ = Ring(alloc, "cbm", 2, [128, 4, 128], F32)
            E_ = Ring(alloc, "sE", 2, [128, 512], F32)
            MT_ = Ring(alloc, "sMT", 3, [128, 4, 128], BF16)
            to_ = Ring(alloc, "sto", 2, [128, 512], F32)
            ysb_ = Ring(alloc, "ysb", 2, [128, 2048], F32)
            yf_ = Ring(alloc, "yfl", 1, [128, 2048], F32)
            zt_ = Ring(alloc, "ztl", 1, [128, 2048], BF16)
            ynb_ = Ring(alloc, "ynb", 1, [128, 2048], BF16)
            yT_ = Ring(alloc, "yTs", 2, [128, 16, 128], BF16)
            junk = alloc("sjunk", [128, 512], BF16)
            K.op("dve", lambda e: e.memset(onesf[:], 1.0), writes=["onesf2"])
            K.op("dve", lambda e: e.memset(ones1[:], 1.0), writes=["ones1b"])
            K.dma("sp", vb[:, :], ssd_vec[0, :].partition_broadcast(128), "Lvb", writes=["vb"])
            K.dma("sp", nwbc[:, :], ssd_norm.partition_broadcast(128), "Lnwbc", writes=["nwbc"])
            K.dma("sp", mk[:], maskT.rearrange("d s l -> s d l"), "Lmk", writes=["mk"])
            K.dma("sp", self_[:, :], selc[:, :], "Lself", writes=["self"])
            K.cp("pool", selb[:].rearrange("k h l -> k (h l)"), self_[:, :], ["self"], ["selb"])
            K.act(aneg[:, :], vb[:, 0:64], AF.Exp, ["vb"], ["aneg"])
            K.ts("dve", aneg[:, :], aneg[:, :], -1.0, None, ALU.mult, None, ["aneg"], ["aneg"])
            lat_chunks = list(range(2, NT))
            ncl = P.dbg.get("nchunk", len(lat_chunks))
            lat_chunks = lat_chunks[:ncl]
            for d in range(2):
                order = [0, 1] + lat_chunks if d == 0 else [1, 0] + lat_chunks[::-1]
                K.op("pool", lambda e: e.memset(state[:], 0.0), writes=["state"])
                K.op("pool", lambda e: e.memset(stbf[:], 0.0), writes=["stbf"])
                for c in order:
                    tok0 = c * 128
                    lat = c >= 2
                    xb = xb_.next(); bc = bc_.next(); dr = dr_.next(); sm = sm_.next()
                    K.dma("sp", xb[:, :], xsB[tok0:tok0 + 128, :], "L" + xb.key, writes=[xb.key])
                    K.dma("sp", bc[:], bcT[:, tok0:tok0 + 128].rearrange("(f p) t -> p f t", p=128), "L" + bc.key,
                          writes=[bc.key])
                    K.dma("sp", dr[:, :], dtraw[tok0:tok0 + 128, d * 32:(d + 1) * 32], "L" + dr.key, writes=[dr.key])
                    dt = sm[:, 0, :]; adt = sm[:, 1, :]; cs = sm[:, 2, :]; ecs = sm[:, 3, :]
                    etot = sm[:, 4, :]; w2 = sm[:, 5, :]; tmp = sm[:, 6, :]
                    k_ = sm.key
                    K.tt("dve", tmp, dr[:, :], vb[:, 64 + d * 32:96 + d * 32], ALU.add, [dr.key, "vb"], [k_])
                    K.act(tmp, tmp, AF.Exp, [k_], [k_])
                    K.act(dt, tmp, AF.Ln, [k_, "ones1b"], [k_], bias=ones1[:, 0:1])
                    K.tt("dve", adt, dt, aneg[:, d * 32:(d + 1) * 32], ALU.mult, [k_, "aneg"], [k_])
                    K.mm(psum[6][:, 0:32], mk[:, d, :], adt, True, True, ["mk", k_], ["ps#6"], True)
                    K.mm(psum[6][:, 32:64], onesf[:, :], adt, True, True, ["onesf2", k_], ["ps#6"], True)
                    K.cp("dve", cs, psum[6][:, 0:32], ["ps#6"], [k_])
                    K.act(etot, psum[6][:, 32:64], AF.Exp, ["ps#6"], [k_])
                    K.tt("dve", tmp, psum[6][:, 32:64], cs, ALU.subtract, ["ps#6", k_], [k_])
                    K.act(w2, tmp, AF.Exp, [k_], [k_])
                    K.tt("dve", w2, w2, dt, ALU.mult, [k_], [k_])
                    xs3 = xb[:, 0:2048].rearrange("p (h q) -> p h q", q=64)
                    Xc = Xc_.next()
                    K.tt("pool", Xc[:, :].rearrange("p (h q) -> p h q", q=64), xs3,
                         w2.unsqueeze(2).to_broadcast([128, 32, 64]), ALU.mult, [xb.key, k_], [Xc.key])
                    if lat:
                        K.act(ecs, cs, AF.Exp, [k_], [k_])
                        X = X_.next()
                        K.tt("dve", X[:, :].rearrange("p (h q) -> p h q", q=64), xs3,
                             dt.unsqueeze(2).to_broadcast([128, 32, 64]), ALU.mult, [xb.key, k_], [X.key])
                        cs4 = cs4_.next(); cst = cst_.next(); hl = hl_.next()
                        K.cp("dve", cs4[:, 0:32], cs, [k_], accw=[cs4.key])
                        K.cp("dve", cs4[:, 32:64], cs, [k_], accw=[cs4.key])
                        K.op("pe", lambda e, cs4=cs4: e.transpose(psum[6][0:64, 128:256], cs4[:, :], idf[:, :]),
                             reads=[cs4.key, "idf"], writes=["ps#6"])
                        K.cp("dve", cst[:, :], psum[6][0:64, 128:256], ["ps#6"], [cst.key])
                        K.cp("dve", hl[0:32, 0, :], cst[0:32, :], [cst.key], accw=[hl.key])
                        K.cp("dve", hl[32:64, 2, :], cst[32:64, :], [cst.key], accw=[hl.key])
                        K.tt("dve", hl[32:64, 0, :], cst[32:64, :], hl[32:64, 2, :], ALU.subtract, [cst.key, hl.key], accw=[hl.key])
                        K.ts("dve", hl[:, 1, :], hl[:, 0, :], -1.0, None, ALU.mult, None, [hl.key], accw=[hl.key])
                        for g in range(4):
                            K.mm(psum[7][:, g * 128:(g + 1) * 128], bc[:, g, :], bc[:, 4 + g, :], True, True,
                                 [bc.key], ["ps#7"], g == 3)
                        cbm = cbm_.next()
                        K.tt("dve", cbm[:], psum[7][:, :].rearrange("p (g l) -> p g l", l=128),
                             mk[:, d:d + 1, :].to_broadcast([128, 4, 128]), ALU.mult, ["ps#7", "mk"], [cbm.key])
                        for hq in range(8):
                            g = hq // 2
                            pk = 4 + hq % 2
                            for j in range(4):
                                h = hq * 4 + j
                                K.mm(psum[pk][:, j * 128:(j + 1) * 128], selb[:, h, :], hl[:, 0, :], True, False,
                                     ["selb", hl.key], [f"ps#{pk}"], False)
                                K.mm(psum[pk][:, j * 128:(j + 1) * 128], hl[:, 1, :], selb[:, h, :], False, True,
                                     ["selb", hl.key], [f"ps#{pk}"], j == 3)
                            E = E_.next(); MT = MT_.next()
                            K.act(E[:, :], psum[pk][:, :], AF.Exp, [f"ps#{pk}"], [E.key])
                            K.stt("dve", MT[:], E[:, :].rearrange("p (j l) -> p j l", l=128), 1e30,
                                  cbm[:, g:g + 1, :].to_broadcast([128, 4, 128]), ALU.min, ALU.mult,
                                  [E.key, cbm.key], [MT.key])
                            for j in range(4):
                                h = hq * 4 + j
                                K.mm(psum[h // 8][:, (h % 8) * 64:(h % 8 + 1) * 64], MT[:, j, :], X[:, h * 64:(h + 1) * 64],
                                     True, True, [MT.key, X.key], [f"ps#{h // 8}"], (h % 8 == 7))
                        ysb = ysb_.next()
                        for g in range(4):
                            K.mm(psum[7][:, :], bc[:, 4 + g, :], stbf[:, g * 512:(g + 1) * 512], True, True,
                                 [bc.key, "stbf"], ["ps#7"], True)
                            to = to_.next()
                            K.tt("dve", to[:, :].rearrange("p (h q) -> p h q", q=64),
                                 psum[7][:, :].rearrange("p (h q) -> p h q", q=64),
                                 ecs[:, g * 8:(g + 1) * 8].unsqueeze(2).to_broadcast([128, 8, 64]), ALU.mult,
                                 ["ps#7", k_], [to.key])
                            K.tt("dve", ysb[:, g * 512:(g + 1) * 512], psum[g][:, :], to[:, :], ALU.add,
                                 [f"ps#{g}", to.key], accw=[ysb.key])
                    for g in range(4):
                        K.mm(psum[7][:, :], xb[:, 2048 + g * 128:2048 + (g + 1) * 128], Xc[:, g * 512:(g + 1) * 512],
                             True, True, [xb.key, Xc.key], ["ps#7"], True)
                        sg = state[:, g * 512:(g + 1) * 512]
                        K.tt("pool", sg.rearrange("p (h q) -> p h q", q=64), sg.rearrange("p (h q) -> p h q", q=64),
                             etot[:, g * 8:(g + 1) * 8].unsqueeze(2).to_broadcast([128, 8, 64]), ALU.mult,
                             ["state", k_], ["state"])
                        K.tt("dve", sg, sg, psum[7][:, :], ALU.add, ["state", "ps#7"], ["state"])
                    K.cp("act", stbf[:, :], state[:, :], ["state"], ["stbf"])
                    if not lat:
                        continue
                    if d == 0:
                        K.dma("sp", yfw[tok0:tok0 + 128, :], ysb[:, :], "S" + ysb.key, reads=[ysb.key], accw=["yfw"])
                        continue
                    yf = yf_.next(); zt = zt_.next(); ynb = ynb_.next(); yT = yT_.next()
                    K.dma("sp", yf[:, :], yfw[tok0:tok0 + 128, :], "L" + yf.key, reads=["yfw"], writes=[yf.key])
                    K.dma("sp", zt[:, :], zs[tok0:tok0 + 128, :], "L" + zt.key, writes=[zt.key])
                    K.tt("pool", ysb[:, :], ysb[:, :], yf[:, :], ALU.add, [ysb.key, yf.key], [ysb.key])
                    K.tt("dve", yf[:, :].rearrange("p (h q) -> p h q", q=64), xs3,
                         vb[:, 128:160].unsqueeze(2).to_broadcast([128, 32, 64]), ALU.mult, [xb.key, "vb", yf.key], [yf.key])
                    K.tt("pool", ysb[:, :], ysb[:, :], yf[:, :], ALU.add, [ysb.key, yf.key], [ysb.key])
                    K.tt("dve", ysb[:, :], ysb[:, :], zt[:, :], ALU.mult, [ysb.key, zt.key], [ysb.key])
                    for g in range(4):
                        K.act(junk[:, :], ysb[:, g * 512:(g + 1) * 512], AF.Square, [ysb.key], ["sjunk"], accw=[k_],
                              accum_out=sm[:, 7, g:g + 1])
                    K.act(sm[:, 7, 4:8], sm[:, 7, 0:4], AF.Sqrt, [k_, "epsb"], [k_], scale=1.0 / 512, bias=epsb[:, 0:1])
                    K.op("dve", lambda e, sm=sm: e.reciprocal(sm[:, 7, 8:12], sm[:, 7, 4:8]), [k_], [k_])
                    for g in range(4):
                        K.stt("dve", ynb[:, g * 512:(g + 1) * 512], ysb[:, g * 512:(g + 1) * 512], sm[:, 7, 8 + g:9 + g],
                              nwbc[:, g * 512:(g + 1) * 512], ALU.mult, ALU.mult, [ysb.key, k_, "nwbc"], accw=[ynb.key])
                    for half in range(2):
                        pk = 4 + half
                        tb = psum[pk][:].bitcast(BF16)
                        for jj in range(8):
                            j = half * 8 + jj
                            K.op("pe", lambda e, tb=tb, jj=jj, j=j, ynb=ynb: e.transpose(
                                tb[:, jj * 128:(jj + 1) * 128], ynb[:, j * 128:(j + 1) * 128], idb[:]),
                                reads=[ynb.key, "idb"], writes=[f"ps#{pk}"], inc=(jj == 7))
                        K.cp("act", yT[:, half * 8:(half + 1) * 8, :].rearrange("p j t -> p (j t)"), tb[:, :], [f"ps#{pk}"],
                             accw=[yT.key])
                    K.dma("sp", mixT1[:, tok0:tok0 + 128].rearrange("(j p) t -> p j t", p=128), yT[:], "S" + yT.key,
                          reads=[yT.key], accw=["mixT1"])
            K.barrier()

    if "ssd" not in P.dbg.get("skip", ()):
        phase_ssd()
    if stop_after == "ssd":
        K.barrier(); pes.close(); P.es.close(); return P

    nt1 = P.dbg.get("out1_tiles", SEQ // 128)
    tiles1 = [(CTX + i * 128, i * 128, 0) for i in range(nt1)]
    phase_out(1, mixT1, 16, o_w_out, h1, out_h, tiles1)

    K.barrier()
    pes.close()
    P.es.close()
    return P


def _rope_tables():
    n_freq = 16
    inv = (10000.0 ** (-np.arange(n_freq, dtype=np.float32) / np.float32(n_freq))).astype(np.float32)
    t = np.arange(SEQ)
    row = (t // 64).astype(np.float32)
    col = (t % 64).astype(np.float32)
    ang = np.concatenate([row[:, None] * inv, col[:, None] * inv], axis=-1).astype(np.float32)
    cos, sin = np.cos(ang).astype(np.float32), np.sin(ang).astype(np.float32)
    tab = np.zeros((2, 128, T), np.float32)
    tab[0, :, :CTX] = 1.0
    for p in range(128):
        d = p % 64
        fi = (d % 16) + 16 * (d // 32)
        sgn = -1.0 if (d % 32) < 16 else 1.0
        tab[0, p, CTX:] = cos[:, fi]
        tab[1, p, CTX:] = sgn * sin[:, fi]
    return tab


def _rope_perm():
    perm = np.zeros(512, np.int64)
    for f in range(512):
        d = f % 64
        e = d % 32
        e2 = e + 16 if e < 16 else e - 16
        perm[f] = f - d + (d // 32) * 32 + e2
    return perm


def make_in_maps(inp):
    B = inp["x"].shape[0]
    perm = _rope_perm()
    w = np.asarray(inp["e_w_in"][0], np.float32)
    w_aug = np.ascontiguousarray(np.concatenate([w, w[:, 1024 + perm], w[:, 1536 + perm]], axis=1))
    tab = _rope_tables()
    ident = np.eye(128, dtype=np.float32)
    wbd = np.zeros((2, 2, 4, 128, 128), np.float32)
    for d in range(2):
        for g, nm in enumerate(("lru_w_r", "lru_w_i")):
            wsrc = np.asarray(inp[nm][0][d], np.float32)
            for cc in range(4):
                wbd[d, g, cc, 0:64, 0:64] = wsrc[2 * cc]
                wbd[d, g, cc, 64:128, 64:128] = wsrc[2 * cc + 1]
    wbd = np.ascontiguousarray(wbd.reshape(16, 128, 128))
    lvec = np.ascontiguousarray(np.concatenate([
        np.asarray(inp["lru_conv_w"][0], np.float32), np.asarray(inp["lru_conv_b"], np.float32).reshape(1, 512),
        np.asarray(inp["lru_b_r"][0], np.float32), np.asarray(inp["lru_b_i"][0], np.float32),
        np.asarray(inp["lru_lambda"][0], np.float32)], axis=0))
    ssd_cv = np.ascontiguousarray(np.concatenate([np.asarray(inp["ssd_conv_w"][0], np.float32),
                                                  np.asarray(inp["ssd_conv_b"], np.float32).reshape(1, 3072)], 0))
    ssd_vec = np.ascontiguousarray(np.concatenate([np.asarray(inp["ssd_a_log"][0], np.float32).reshape(-1),
                                                   np.asarray(inp["ssd_dt_bias"][0], np.float32).reshape(-1),
                                                   np.asarray(inp["ssd_d"][0], np.float32).reshape(-1)]).reshape(1, 160))
    ii = np.arange(128)
    maskT = np.stack([(ii[None, :] >= ii[:, None]), (ii[None, :] <= ii[:, None])], 0).astype(np.float32)
    selc = np.zeros((64, 32, 128), np.float32)
    for hh in range(32):
        selc[hh, hh, :] = 1.0
        selc[32 + hh, hh, :] = 1.0
    selc = np.ascontiguousarray(selc.reshape(64, 32 * 128))
    maps = []
    for b in range(B):
        m = {
            "src0": np.ascontiguousarray(np.concatenate([inp["ctx"][b], inp["x"][b]], axis=0), dtype=np.float32),
            "cvec": np.ascontiguousarray(np.stack([inp["c"][b], inp["c_ctx"]], 0), dtype=np.float32),
            "w_mod": np.asarray(inp["w_mod"], np.float32),
            "b_mod": np.asarray(inp["b_mod"], np.float32),
            "g_pre": np.asarray(inp["g_pre"], np.float32),
            "g_post": np.asarray(inp["g_post"], np.float32),
            "e_w_in_aug": w_aug,
            "e_w_out": np.asarray(inp["e_w_out"][0], np.float32),
            "ident": ident,
            "ropetab": tab,
            "lru_wbd": wbd,
            "lru_vec": lvec,
            "da_lam": np.ascontiguousarray(np.asarray(inp["da_lambda"][0], np.float32).reshape(1, 256)),
            "da_sub": np.ascontiguousarray(np.asarray(inp["da_subln"][0], np.float32).reshape(1, 128)),
            "o_w_in": np.asarray(inp["o_w_in"][0], np.float32),
            "o_w_out": np.asarray(inp["o_w_out"][0], np.float32),
            "ssd_cv": ssd_cv,
            "ssd_vec": ssd_vec,
            "ssd_norm": np.asarray(inp["ssd_norm"][0], np.float32),
            "maskT": maskT,
            "selc": selc,
        }
        maps.append(m)
    return maps


def kernel(**inp):
    P = build_program()
    maps = make_in_maps(inp)
    res = run_bass_kernel_spmd(P.nc, maps, core_ids=list(range(8)))
    return np.stack([np.asarray(r["out"], np.float32) for r in res.results], 0)
```

```python
import os
from contextlib import ExitStack
import numpy as np
import concourse.bass as bass
import concourse.mybir as mybir
from concourse.bass_utils import run_bass_kernel_spmd

F32, BF16 = mybir.dt.float32, mybir.dt.bfloat16
AF = mybir.ActivationFunctionType
ALU = mybir.AluOpType
AX = mybir.AxisListType

D = 1024
SEQ = 4096
CTX = 256
T = SEQ + CTX
NT = T // 128
EPS = 1e-6


class Sched:
    ENG = ("pe", "dve", "act", "pool", "sp")

    def __init__(self, nc, es):
        self.nc, self.es = nc, es
        self.e = {"pe": nc.tensor, "dve": nc.vector, "act": nc.scalar, "pool": nc.gpsimd, "sp": nc.sync}
        self.sem, self.cnt = {}, {}
        for n in self.ENG:
            self.sem[n] = es.enter_context(nc.semaphore("s_" + n))
            self.cnt[n] = 0
        self.seen = {n: {} for n in self.ENG}
        self.W, self.Rd = {}, {}
        self.pend = {n: [] for n in self.ENG}
        self.nwait = 0
        self.nins = 0

    def _wait(self, eng, need):
        for s, v in need.items():
            if s == "pe" and eng == "pe":
                continue
            if self.seen[eng].get(s, 0) >= v:
                continue
            self.e[eng].wait_ge(self.sem[s], v)
            self.seen[eng][s] = v
            self.nwait += 1

    def _deps(self, eng, reads, writes, accw):
        need = {}

        def add(d):
            for s, v in d.items():
                if need.get(s, 0) < v:
                    need[s] = v
        for r in reads:
            add(self.W.get(r, {}))
        for w in writes:
            add(self.W.get(w, {}))
            add(self.Rd.get(w, {}))
        for w in accw:
            add(self.Rd.get(w, {}))
        self._wait(eng, need)

    def _register(self, ev, reads, writes, accw):
        s, v = ev
        for r in reads:
            d = self.Rd.setdefault(r, {})
            d[s] = max(d.get(s, 0), v)
        for w in writes:
            self.W[w] = {s: v}
            self.Rd[w] = {}
        for w in accw:
            d = self.W.setdefault(w, {})
            d[s] = max(d.get(s, 0), v)

    def op(self, eng, fn, reads=(), writes=(), accw=(), inc=True):
        self._deps(eng, reads, writes, accw)
        ins = fn(self.e[eng])
        self.nins += 1
        if inc:
            self.cnt[eng] += 1
            ins.then_inc(self.sem[eng], 1)
            ev = (eng, self.cnt[eng])
            for (r, w, a) in self.pend[eng]:
                self._register(ev, r, w, a)
            self.pend[eng] = []
            self._register(ev, reads, writes, accw)
        else:
            self.pend[eng].append((tuple(reads), tuple(writes), tuple(accw)))

    def dma(self, q, out, in_, semkey, reads=(), writes=(), accw=(), **kw):
        if semkey not in self.sem:
            self.sem[semkey] = self.es.enter_context(self.nc.semaphore("d_" + semkey.replace("#", "_")))
            self.cnt[semkey] = 0
        self._deps(q, reads, writes, accw)
        ins = self.e[q].dma_start(out=out, in_=in_, **kw)
        ins.then_inc(self.sem[semkey], 16)
        self.cnt[semkey] += 16
        self.nins += 1
        self._register((semkey, self.cnt[semkey]), reads, writes, accw)

    def tt(self, eng, out, a, b, op, reads, writes=(), accw=()):
        self.op(eng, lambda e: e.tensor_tensor(out, a, b, op), reads, writes, accw)

    def ts(self, eng, out, a, s1, s2, op0, op1=None, reads=(), writes=(), accw=()):
        if op1 is None:
            self.op(eng, lambda e: e.tensor_scalar(out, a, s1, None, op0), reads, writes, accw)
        else:
            self.op(eng, lambda e: e.tensor_scalar(out, a, s1, s2, op0, op1), reads, writes, accw)

    def stt(self, eng, out, a, sc, b, op0, op1, reads, writes=(), accw=()):
        self.op(eng, lambda e: e.scalar_tensor_tensor(out, a, sc, b, op0, op1), reads, writes, accw)

    def act(self, out, in_, func, reads, writes=(), accw=(), **kw):
        self.op("act", lambda e: e.activation(out=out, in_=in_, func=func, **kw), reads, writes, accw)

    def cp(self, eng, out, in_, reads, writes=(), accw=()):
        if eng == "act":
            self.op("act", lambda e: e.copy(out, in_), reads, writes, accw)
        else:
            self.op(eng, lambda e: e.tensor_copy(out, in_), reads, writes, accw)

    def mm(self, out, lhsT, rhs, start, stop, reads, writes, inc):
        self.op("pe", lambda e: e.matmul(out, lhsT, rhs, start=start, stop=stop), reads, writes, inc=inc)

    def barrier(self):
        for n in self.ENG:
            assert not self.pend[n]
        allev = {s: c for s, c in self.cnt.items() if c > 0}
        for n in self.ENG:
            self._wait(n, allev)
        self.W, self.Rd = {}, {}


class Buf:
    def __init__(self, t, key):
        self.t, self.key = t, key

    def __getitem__(self, k):
        return self.t[k]


class Ring:
    def __init__(self, alloc, name, n, shape, dtype):
        self.bufs = [Buf(alloc(f"{name}{i}", shape, dtype), f"{name}#{i}") for i in range(n)]
        self.i = 0

    def next(self):
        b = self.bufs[self.i % len(self.bufs)]
        self.i += 1
        return b


class Prog:
    def __init__(self, dbg=None):
        self.dbg = dbg or {}
        self.nc = nc = bass.Bass("TRN2", target_bir_lowering=False)
        self.es = ExitStack()
        self.K = Sched(nc, self.es)
        self.dram = {}
        self.uid = 0

    def din(self, name, shape, dt=F32):
        self.dram[name] = self.nc.dram_tensor(name, list(shape), dt, kind="ExternalInput").ap()
        return self.dram[name]

    def dout(self, name, shape, dt=F32):
        self.dram[name] = self.nc.dram_tensor(name, list(shape), dt, kind="ExternalOutput").ap()
        return self.dram[name]

    def dscr(self, name, shape, dt):
        if name in self.dbg.get("dump", ()):
            return self.dout(name, shape, dt)
        self.dram[name] = self.nc.dram_tensor(name, list(shape), dt).ap()
        return self.dram[name]


def build_program(dbg=None):
    P = Prog(dbg)
    nc, K = P.nc, P.K
    stop_after = P.dbg.get("stop_after", "all")

    src0 = P.din("src0", [T, D])
    cvec = P.din("cvec", [2, D])
    w_mod = P.din("w_mod", [2, D, 3 * D])
    b_mod = P.din("b_mod", [2, 3 * D])
    g_pre = P.din("g_pre", [2, D])
    g_post = P.din("g_post", [2, D])
    e_w_in = P.din("e_w_in_aug", [D, 4096])
    e_w_out = P.din("e_w_out", [D, D])
    ident = P.din("ident", [128, 128])
    ropetab = P.din("ropetab", [2, 128, T])
    out_h = P.dout("out", [SEQ, D])
    lru_wbd = P.din("lru_wbd", [16, 128, 128])
    lru_vec = P.din("lru_vec", [11, 512])
    da_lam = P.din("da_lam", [1, 256])
    da_sub = P.din("da_sub", [1, 128])
    mixT0 = P.dscr("mixT0", [D, T], BF16)
    o_w_in = P.din("o_w_in", [D, 5184])
    o_w_out = P.din("o_w_out", [2048, D])
    ssd_cv = P.din("ssd_cv", [5, 3072])
    ssd_vec = P.din("ssd_vec", [1, 160])
    ssd_norm = P.din("ssd_norm", [2048])
    maskT = P.din("maskT", [2, 128, 128])
    selc = P.din("selc", [64, 32 * 128])
    xbcT = P.dscr("xbcT", [3072, T], BF16)
    zs = P.dscr("zs", [T, 2048], BF16)
    dtraw = P.dscr("dtraw", [T, 64], F32)
    xsB = P.dscr("xsB", [T, 2560], BF16)
    bcT = P.dscr("bcT", [1024, T], BF16)
    yfw = P.dscr("yfw", [T, 2048], F32)
    mixT1 = P.dscr("mixT1", [2048, T], BF16)
    h1 = P.dscr("h1", [T, D], F32)

    xrT = P.dscr("xrT", [512, T], F32)
    grT = P.dscr("grT", [512, T], BF16)
    gdT = P.dscr("gdT", [512, T], BF16)
    qT = P.dscr("qT", [512, T], BF16)
    kT = P.dscr("kT", [512, T], BF16)
    vtok = P.dscr("vtok", [T, 512], BF16)

    pes = ExitStack()
    def palloc(name, shape, dt):
        return pes.enter_context(nc.sbuf_tensor(name, list(shape), dt))
    psum = [pes.enter_context(nc.psum_tensor(f"psb{i}", [128, 512], F32)) for i in range(8)]
    idf = palloc("idf", [128, 128], F32)
    idb = palloc("idb", [128, 128], BF16)
    epsb = palloc("epsb", [128, 1], F32)
    modA = palloc("modA", [128, 2, 8, 2], F32)
    modS = palloc("modS", [128, 2, 8, 2], F32)
    ggbc = palloc("ggbc", [128, 2, 2, D], F32)

    K.dma("sp", idf[:], ident[:, :], "Lidf", writes=["idf"])
    K.op("dve", lambda e: e.tensor_copy(idb[:], idf[:]), reads=["idf"], writes=["idb"])
    K.op("dve", lambda e: e.memset(epsb[:], EPS), writes=["epsb"])

    def phase_mod():
        with ExitStack() as es:
            def alloc(name, shape, dt):
                P.uid += 1
                return es.enter_context(nc.sbuf_tensor(f"{name}_{P.uid}", list(shape), dt))
            cT = alloc("cT", [128, 2, 8], F32)
            sig = alloc("sig", [128, 2, 8], F32)
            srep = alloc("srep", [128, 2, 8, 128], F32)
            bT = alloc("bT", [128, 2, 24], F32)
            gpT = alloc("gpT", [128, 2, 8], F32)
            bgbc = alloc("bgbc", [128, 2, D], F32)
            gpbc = alloc("gpbc", [128, 2, D], F32)
            wst = Ring(alloc, "wst", 2, [128, 8, 512], F32)
            tmp = alloc("mtmp", [128, 16, 2], F32)

            rows = alloc("rows", [80, 128], F32)
            K.dma("sp", rows[0:16, :], cvec.rearrange("t (j p) -> (t j) p", p=128), "Lrows", accw=["rows"])
            K.dma("sp", rows[16:64, :], b_mod.rearrange("l (f p) -> (l f) p", p=128), "Lrows", accw=["rows"])
            K.dma("sp", rows[64:80, :], g_pre.rearrange("l (j p) -> (l j) p", p=128), "Lrows", accw=["rows"])
            K.op("pe", lambda e: e.transpose(psum[4][:, 0:80], rows[:, :], idf[0:80, 0:80]),
                 reads=["rows", "idf"], writes=["ps#4"])
            K.op("dve", lambda e: e.tensor_copy(cT[:].rearrange("p t j -> p (t j)"), psum[4][:, 0:16]),
                 reads=["ps#4"], writes=["cT"])
            K.op("dve", lambda e: e.tensor_copy(bT[:].rearrange("p l f -> p (l f)"), psum[4][:, 16:64]),
                 reads=["ps#4"], writes=["bT"])
            K.op("dve", lambda e: e.tensor_copy(gpT[:].rearrange("p l j -> p (l j)"), psum[4][:, 64:80]),
                 reads=["ps#4"], writes=["gpT"])
            for l in range(2):
                K.dma("sp", bgbc[:, l, :], b_mod[l, 2 * D:3 * D].partition_broadcast(128), "Lbgbc", accw=["bgbc"])
                K.dma("sp", gpbc[:, l, :], g_post[l, :].partition_broadcast(128), "Lgpbc", accw=["gpbc"])
            K.op("act", lambda e: e.activation(out=sig[:], in_=cT[:], func=AF.Sigmoid), reads=["cT"], writes=["sig"])
            K.op("dve", lambda e: e.tensor_tensor(cT[:], cT[:], sig[:], ALU.mult), reads=["sig", "cT"], writes=["cT"])
            for t in range(2):
                K.op("dve", lambda e, t=t: e.tensor_copy(
                    srep[:, t, :, :], cT[:, t, :].unsqueeze(2).to_broadcast([128, 8, 128])),
                    reads=["cT"], accw=["srep"])
            for l in range(2):
                for pc in range(6):
                    wb = wst.next()
                    K.dma("sp", wb[:], w_mod[l, :, pc * 512:(pc + 1) * 512].rearrange("(j p) n -> p j n", p=128),
                          "L" + wb.key, writes=[wb.key])
                    if pc < 4:
                        pst = psum[pc % 2]
                        for f in range(4):
                            for j in range(8):
                                K.op("pe", lambda e, f=f, j=j, pst=pst, wb=wb: e.matmul(
                                    pst[:, 2 * f:2 * f + 2], wb[:, j, f * 128:(f + 1) * 128], cT[:, :, j],
                                    start=(j == 0), stop=(j == 7)),
                                    reads=[wb.key, "cT"], writes=[f"ps#{pc % 2}"], inc=(j == 7 and f == 3))
                        K.op("dve", lambda e, pst=pst, pc=pc: e.tensor_copy(
                            tmp[:, pc * 4:(pc + 1) * 4, :], pst[:, 0:8].rearrange("p (f t) -> p f t", t=2)),
                            reads=[f"ps#{pc % 2}"], accw=["mtmp"])
                    else:
                        for t in range(2):
                            pst = psum[2 + t]
                            for j in range(8):
                                K.op("pe", lambda e, j=j, t=t, pst=pst, wb=wb: e.matmul(
                                    pst[:, :], srep[:, t, j, :], wb[:, j, :], start=(j == 0), stop=(j == 7)),
                                    reads=[wb.key, "srep"], writes=[f"ps#{2 + t}"], inc=(j == 7))
                            c0 = (pc - 4) * 512
                            K.op("dve", lambda e, t=t, l=l, c0=c0, pst=pst: e.tensor_tensor(
                                ggbc[:, l, t, c0:c0 + 512], pst[:, :], bgbc[:, l, c0:c0 + 512], ALU.add),
                                reads=[f"ps#{2 + t}", "bgbc"], accw=["ggbc"])
                for t in range(2):
                    K.op("dve", lambda e, t=t, l=l: e.tensor_tensor(
                        modS[:, l, :, t], tmp[:, 0:8, t], bT[:, l, 0:8], ALU.add),
                        reads=["mtmp", "bT"], accw=["modS"])
                    K.op("dve", lambda e, t=t, l=l: e.scalar_tensor_tensor(
                        modA[:, l, :, t], tmp[:, 8:16, t], 1.0, bT[:, l, 8:16], ALU.add, ALU.add),
                        reads=["mtmp", "bT"], accw=["modA"])
                    K.op("dve", lambda e, t=t, l=l: e.tensor_tensor(
                        modA[:, l, :, t], modA[:, l, :, t], gpT[:, l, :], ALU.mult),
                        reads=["modA", "gpT"], writes=["modA"])
                    K.op("dve", lambda e, t=t, l=l: e.tensor_tensor(
                        ggbc[:, l, t, :], ggbc[:, l, t, :], gpbc[:, l, :], ALU.mult),
                        reads=["ggbc", "gpbc"], writes=["ggbc"])
            K.barrier()

    phase_mod()
    if "mod" in P.dbg.get("dump", ()):
        dA = P.dout("dbg_modA", [128, 32]); dS = P.dout("dbg_modS", [128, 32]); dG = P.dout("dbg_gg", [128, 4 * D])
        K.dma("sp", dA[:, :], modA[:].rearrange("p l j t -> p (l j t)"), "Sdbg", reads=["modA"])
        K.dma("sp", dS[:, :], modS[:].rearrange("p l j t -> p (l j t)"), "Sdbg", reads=["modS"])
        K.dma("sp", dG[:, :], ggbc[:].rearrange("p l t f -> p (l t f)"), "Sdbg", reads=["ggbc"])
    if stop_after == "mod":
        K.barrier(); pes.close(); P.es.close(); return P

    def phase_proj(layer, src, Wd, ncols, fspecs, tspecs, extra_alloc=None, per_group=None):
        with ExitStack() as es:
            def alloc(name, shape, dt):
                P.uid += 1
                return es.enter_context(nc.sbuf_tensor(f"{name}_{P.uid}", list(shape), dt))
            Wb = alloc("Wb", [128, 8, ncols], BF16)
            wst = Ring(alloc, "wst", 2, [128, 8, 256], F32)
            xr_ = Ring(alloc, "xin", 3, [128, D], F32)
            xn_ = Ring(alloc, "xn", 2, [128, D], BF16)
            uT_ = Ring(alloc, "uT", 2, [128, 8, 512], BF16)
            junk = alloc("junk", [128, D], BF16)
            stat = Ring(alloc, "stat", 4, [128, 4], F32)
            ctxo = extra_alloc(alloc) if extra_alloc else None
            ceng = ["dve", "pool", "act"]
            for pc in range(ncols // 256 + (1 if ncols % 256 else 0)):
                c0 = pc * 256
                cw = min(256, ncols - c0)
                wb = wst.next()
                K.dma("sp", wb[:, :, 0:cw], Wd[:, c0:c0 + cw].rearrange("(j p) n -> p j n", p=128),
                      "L" + wb.key, writes=[wb.key])
                en = ceng[pc % 3]
                if en == "act":
                    K.op("act", lambda e, wb=wb, c0=c0, cw=cw: e.copy(Wb[:, :, c0:c0 + cw], wb[:, :, 0:cw]),
                         reads=[wb.key], accw=["Wb"])
                else:
                    K.op(en, lambda e, wb=wb, c0=c0, cw=cw: e.tensor_copy(Wb[:, :, c0:c0 + cw], wb[:, :, 0:cw]),
                         reads=[wb.key], accw=["Wb"])
            groups = [(0, CTX, 1)] + [(CTX + 512 * g, 512, 0) for g in range(SEQ // 512)]
            pst_i = [0]
            pso_i = [0]

            def front(gi):
                tok0, ntok, tmod = groups[gi]
                uT = uT_.next()
                for ti in range(ntok // 128):
                    xt = xr_.next(); xn = xn_.next(); st = stat.next()
                    K.dma("sp", xt[:], src[tok0 + ti * 128: tok0 + (ti + 1) * 128, :], "L" + xt.key, writes=[xt.key])
                    K.op("act", lambda e, xt=xt, st=st: e.activation(out=junk[:], in_=xt[:], func=AF.Square,
                                                                     accum_out=st[:, 0:1]),
                         reads=[xt.key], writes=["junk", st.key])
                    fl = P.dbg.get("fl", 9)
                    if fl < 2: continue
                    K.op("act", lambda e, st=st: e.activation(out=st[:, 1:2], in_=st[:, 0:1], func=AF.Sqrt,
                                                              scale=1.0 / D, bias=epsb[:, 0:1]),
                         reads=[st.key, "epsb"], writes=[st.key])
                    K.op("dve", lambda e, st=st: e.reciprocal(st[:, 2:3], st[:, 1:2]), reads=[st.key], writes=[st.key])
                    if fl < 3: continue
                    K.op("pool", lambda e, xt=xt, xn=xn, st=st: e.tensor_scalar(xn[:], xt[:], st[:, 2:3], None, ALU.mult),
                         reads=[xt.key, st.key], writes=[xn.key])
                    if fl < 4: continue
                    pk = 6 + (pst_i[0] % 2); pst_i[0] += 1
                    pst = psum[pk][:].bitcast(BF16)
                    for j in range(8):
                        K.op("pe", lambda e, j=j, pst=pst, xn=xn: e.transpose(
                            pst[:, j * 128:(j + 1) * 128], xn[:, j * 128:(j + 1) * 128], idb[:]),
                            reads=[xn.key, "idb"], writes=[f"ps#{pk}"], inc=(j == 7))
                    if fl < 5: continue
                    for j in range(8):
                        if True:
                            K.op("dve", lambda e, j=j, pst=pst, uT=uT, ti=ti, tmod=tmod: e.tensor_scalar(
                                uT[:, j, ti * 128:(ti + 1) * 128], pst[:, j * 128:(j + 1) * 128],
                                modA[:, layer, j, tmod:tmod + 1], modS[:, layer, j, tmod:tmod + 1], ALU.mult, ALU.add),
                                reads=[f"ps#{pk}", "modA", "modS"], accw=[uT.key])
                        else:
                            K.op("act", lambda e, j=j, pst=pst, uT=uT, ti=ti, tmod=tmod: e.activation(
                                out=uT[:, j, ti * 128:(ti + 1) * 128], in_=pst[:, j * 128:(j + 1) * 128],
                                func=AF.Identity, scale=modA[:, layer, j, tmod:tmod + 1],
                                bias=modS[:, layer, j, tmod:tmod + 1]),
                                reads=[f"ps#{pk}", "modA", "modS"], accw=[uT.key])
                return uT

            def mm(gi, uT):
                tok0, ntok, tmod = groups[gi]
                if per_group:
                    per_group(ctxo, gi, tok0, ntok)
                for (name, bundles, epi) in fspecs:
                    for bi, cols in enumerate(bundles):
                        pks = []
                        for c0 in cols:
                            pk = pso_i[0] % 6; pso_i[0] += 1
                            pks.append(pk)
                            for j in range(8):
                                K.op("pe", lambda e, j=j, pk=pk, c0=c0, uT=uT, ntok=ntok: e.matmul(
                                    psum[pk][:, 0:ntok], Wb[:, j, c0:c0 + 128], uT[:, j, 0:ntok],
                                    start=(j == 0), stop=(j == 7)),
                                    reads=["Wb", uT.key], writes=[f"ps#{pk}"], inc=(j == 7))
                        epi(ctxo, pks, bi, tok0, ntok)
                for (c0, cw, epi) in tspecs:
                    for ti in range(ntok // 128):
                        pk = pso_i[0] % 6; pso_i[0] += 1
                        for j in range(8):
                            K.op("pe", lambda e, j=j, pk=pk, uT=uT, ti=ti: e.matmul(
                                psum[pk][:, 0:cw], uT[:, j, ti * 128:(ti + 1) * 128], Wb[:, j, c0:c0 + cw],
                                start=(j == 0), stop=(j == 7)),
                                reads=["Wb", uT.key], writes=[f"ps#{pk}"], inc=(j == 7))
                        epi(ctxo, pk, tok0 + ti * 128)

            ng = P.dbg.get("ngroups", len(groups))
            lvl = P.dbg.get("lvl", 9)
            if lvl == 1:
                K.barrier(); return
            if lvl == 2:
                front(0); K.barrier(); return
            uts = {0: front(0)}
            for gi in range(ng):
                if gi + 1 < ng:
                    uts[gi + 1] = front(gi + 1)
                mm(gi, uts.pop(gi))
            K.barrier()

    def l0_alloc(alloc):
        c = {}
        c["sf"] = Ring(alloc, "sf", 3, [128, 512], F32)
        c["sb"] = Ring(alloc, "sb", 4, [128, 512], BF16)
        c["t1"] = Ring(alloc, "t1", 2, [128, 512], F32)
        c["t2"] = Ring(alloc, "t2", 2, [128, 512], F32)
        c["tab"] = Ring(alloc, "tab", 2, [128, 2, 512], F32)
        return c

    def l0_group(c, gi, tok0, ntok):
        tb = c["tab"].next()
        c["curtab"] = tb
        K.dma("sp", tb[:, :, 0:ntok], ropetab[:, :, tok0:tok0 + ntok].rearrange("c p t -> p c t"),
              "L" + tb.key, writes=[tb.key])

    def epi_copy_f32(dst):
        def f(c, pks, bi, tok0, ntok):
            pk = pks[0]; b = c["sf"].next()
            K.op("act", lambda e: e.copy(b[:, 0:ntok], psum[pk][:, 0:ntok]), reads=[f"ps#{pk}"], writes=[b.key])
            K.dma("sp", dst[bi * 128:(bi + 1) * 128, tok0:tok0 + ntok], b[:, 0:ntok], "S" + b.key,
                  reads=[b.key], accw=[dst.name])
        return f

    def epi_silu_bf(dst):
        def f(c, pks, bi, tok0, ntok):
            pk = pks[0]; b = c["sb"].next()
            K.op("act", lambda e: e.activation(out=b[:, 0:ntok], in_=psum[pk][:, 0:ntok], func=AF.Silu),
                 reads=[f"ps#{pk}"], writes=[b.key])
            K.dma("sp", dst[bi * 128:(bi + 1) * 128, tok0:tok0 + ntok], b[:, 0:ntok], "S" + b.key,
                  reads=[b.key], accw=[dst.name])
        return f

    def epi_rope(dst):
        def f(c, pks, bi, tok0, ntok):
            pa, pb = pks; t1 = c["t1"].next(); t2 = c["t2"].next(); b = c["sb"].next(); tb = c["curtab"]
            K.op("dve", lambda e: e.tensor_tensor(t1[:, 0:ntok], psum[pa][:, 0:ntok], tb[:, 0, 0:ntok], ALU.mult),
                 reads=[f"ps#{pa}", tb.key], writes=[t1.key])
            K.op("dve", lambda e: e.tensor_tensor(t2[:, 0:ntok], psum[pb][:, 0:ntok], tb[:, 1, 0:ntok], ALU.mult),
                 reads=[f"ps#{pb}", tb.key], writes=[t2.key])
            K.op("pool", lambda e: e.tensor_tensor(b[:, 0:ntok], t1[:, 0:ntok], t2[:, 0:ntok], ALU.add),
                 reads=[t1.key, t2.key], writes=[b.key])
            K.dma("sp", dst[bi * 128:(bi + 1) * 128, tok0:tok0 + ntok], b[:, 0:ntok], "S" + b.key,
                  reads=[b.key], accw=[dst.name])
        return f

    def epi_v(c, pk, tok0):
        b = c["sb"].next()
        K.op("act", lambda e: e.copy(b[:, :], psum[pk][:, :]), reads=[f"ps#{pk}"], writes=[b.key])
        K.dma("sp", vtok[tok0:tok0 + 128, :], b[:, :], "S" + b.key, reads=[b.key], accw=["vtok"])

    l0_f = [
        ("xr", [[f * 128] for f in range(0, 4)], epi_copy_f32(xrT)),
        ("gr", [[f * 128] for f in range(4, 8)], epi_silu_bf(grT)),
        ("q", [[1024 + f * 128, 3072 + f * 128] for f in range(4)], epi_rope(qT)),
        ("k", [[1536 + f * 128, 3584 + f * 128] for f in range(4)], epi_rope(kT)),
        ("gd", [[f * 128] for f in range(20, 24)], epi_silu_bf(gdT)),
    ]
    l0_t = [(2048, 512, epi_v)]
    phase_proj(0, src0, e_w_in, 4096, l0_f, l0_t, l0_alloc, l0_group)
    if stop_after == "proj0":
        K.barrier(); pes.close(); P.es.close(); return P


    LAMBDA_INIT0 = 0.8 - 0.6 * 1.0

    def phase_lru():
        with ExitStack() as es:
            def alloc(name, shape, dt):
                P.uid += 1
                return es.enter_context(nc.sbuf_tensor(f"{name}_{P.uid}", list(shape), dt))
            rows = alloc("lrows", [44, 128], F32)
            pv = alloc("lpv", [128, 11, 4], F32)
            coef = alloc("lcoef", [128, 2, 4], F32)
            ones1 = alloc("ones1", [128, 1], F32)
            wbf = alloc("wbf", [128, 16, 128], F32)
            wbb = alloc("wbb", [128, 16, 128], BF16)
            big = Ring(alloc, "big", 7, [128, T], F32)
            xcb = alloc("xcb", [128, T], BF16)
            grs = alloc("grs", [128, T], BF16)
            mo = alloc("mixo", [128, T], BF16)
            K.op("dve", lambda e: e.memset(ones1[:], 1.0), writes=["ones1"])
            K.dma("sp", rows[:, :], lru_vec.rearrange("v (c p) -> (v c) p", p=128), "Lrows", writes=["lrows"])
            K.op("pe", lambda e: e.transpose(psum[0][:, 0:44], rows[:, :], idf[0:44, 0:44]),
                 reads=["lrows", "idf"], writes=["ps#0"])
            K.cp("dve", pv[:].rearrange("p v c -> p (v c)"), psum[0][:, 0:44], ["ps#0"], ["lpv"])
            K.act(coef[:].rearrange("p d c -> p (d c)"), pv[:, 9:11, :].rearrange("p d c -> p (d c)"), AF.Exp,
                  ["lpv"], ["lcoef"], scale=-1.0)
            K.act(coef[:].rearrange("p d c -> p (d c)"), coef[:].rearrange("p d c -> p (d c)"), AF.Ln,
                  ["lcoef", "ones1"], ["lcoef"], bias=ones1[:, 0:1])
            K.ts("dve", coef[:].rearrange("p d c -> p (d c)"), coef[:].rearrange("p d c -> p (d c)"), -8.0, None,
                 ALU.mult, None, ["lcoef"], ["lcoef"])
            K.dma("sp", wbf[:], lru_wbd.rearrange("n k m -> k n m"), "Lwbf", writes=["wbf"])
            K.cp("dve", wbb[:], wbf[:], ["wbf"], ["wbb"])
            segs = [(0, CTX), (CTX, T)]
            blocks = [(b0, min(512, T - b0)) for b0 in range(0, T, 512)]
            for cc in range(4):
                x = big.next(); xc = big.next()
                K.dma("sp", x[:, :], xrT[cc * 128:(cc + 1) * 128, :], "L" + x.key, writes=[x.key])
                K.dma("sp", grs[:, :], grT[cc * 128:(cc + 1) * 128, :], "Lgrs", writes=["grs"])
                K.ts("dve", xc[:, :], x[:, :], pv[:, 2, cc:cc + 1], pv[:, 4, cc:cc + 1], ALU.mult, ALU.add,
                     [x.key, "lpv"], [xc.key])
                for (a, b) in segs:
                    for tap, sh in ((0, -2), (1, -1), (3, 1)):
                        lo = max(a, a - sh); hi = min(b, b - sh)
                        K.stt("dve", xc[:, lo:hi], x[:, lo + sh:hi + sh], pv[:, tap, cc:cc + 1],
                              xc[:, lo:hi], ALU.mult, ALU.add, [x.key, xc.key, "lpv"], [xc.key])
                K.cp("pool", xcb[:, :], xc[:, :], [xc.key], ["xcb"])
                hs = []
                for d in range(2):
                    rb = big.next(); ib = big.next()
                    for (b0, bn) in blocks:
                        for g, dstb, brow in ((0, rb, 5 + d), (1, ib, 7 + d)):
                            pk = (2 * (b0 // 512) + g) % 6
                            K.mm(psum[pk][:, 0:bn], wbb[:, (d * 2 + g) * 4 + cc, :], xcb[:, b0:b0 + bn], True, True,
                                 ["wbb", "xcb"], [f"ps#{pk}"], True)
                            K.act(dstb[:, b0:b0 + bn], psum[pk][:, 0:bn], AF.Sigmoid, [f"ps#{pk}", "lpv"], accw=[dstb.key],
                                  bias=pv[:, brow, cc:cc + 1])
                    K.ts("dve", rb[:, :], rb[:, :], coef[:, d, cc:cc + 1], None, ALU.mult, None, [rb.key, "lcoef"], [rb.key])
                    K.act(rb[:, :], rb[:, :], AF.Exp, [rb.key], [rb.key])
                    sq = big.next()
                    K.tt("pool", sq[:, :], rb[:, :], rb[:, :], ALU.mult, [rb.key], [sq.key])
                    K.act(sq[:, :], sq[:, :], AF.Sqrt, [sq.key, "ones1"], [sq.key], scale=-1.0, bias=ones1[:, 0:1])
                    K.tt("pool", ib[:, :], ib[:, :], xc[:, :], ALU.mult, [ib.key, xc.key], [ib.key])
                    K.tt("dve", ib[:, :], ib[:, :], sq[:, :], ALU.mult, [ib.key, sq.key], [ib.key])
                    h = sq
                    if d == 0:
                        K.op("dve", lambda e, h=h, rb=rb, ib=ib: e.tensor_tensor_scan(
                            h[:, :], rb[:, :], ib[:, :], 0.0, ALU.mult, ALU.add), [rb.key, ib.key, h.key], [h.key])
                    else:
                        K.op("dve", lambda e, h=h, rb=rb, ib=ib: e.tensor_tensor_scan(
                            h[:, CTX - 1::-1] if False else h[:, 0:CTX][:, ::-1], rb[:, 0:CTX][:, ::-1], ib[:, 0:CTX][:, ::-1],
                            0.0, ALU.mult, ALU.add), [rb.key, ib.key, h.key], [h.key])
                        K.op("dve", lambda e, h=h, rb=rb, ib=ib: e.tensor_tensor_scan(
                            h[:, CTX:T][:, ::-1], rb[:, CTX:T][:, ::-1], ib[:, CTX:T][:, ::-1],
                            h[:, 0:1], ALU.mult, ALU.add), [rb.key, ib.key, h.key], [h.key])
                    hs.append(h)
                K.tt("pool", hs[0][:, :], hs[0][:, :], hs[1][:, :], ALU.add, [hs[0].key, hs[1].key], [hs[0].key])
                K.tt("dve", mo[:, :], hs[0][:, :], grs[:, :], ALU.mult, [hs[0].key, "grs"], ["mixo"])
                K.dma("sp", mixT0[cc * 128:(cc + 1) * 128, :], mo[:, :], "Smixo", reads=["mixo"], accw=["mixT0"])
            K.barrier()

    if "lru" not in P.dbg.get("skip", ()):
        phase_lru()
    if stop_after == "lru":
        K.barrier(); pes.close(); P.es.close(); return P

    def phase_attn():
        with ExitStack() as es:
            def alloc(name, shape, dt):
                P.uid += 1
                return es.enter_context(nc.sbuf_tensor(f"{name}_{P.uid}", list(shape), dt))
            kres = alloc("kres", [128, 4, T], BF16)
            vres = alloc("vres", [128, NT, 512], BF16)
            onesb = alloc("onesb", [128, 128], BF16)
            onesf = alloc("onesf", [128, 128], F32)
            lrow = alloc("lamrow", [1, 260], F32)
            lamc = alloc("lamc", [128, 2], F32)
            subc = alloc("subc", [128, 2], F32)
            subrow = alloc("subrow", [1, 128], F32)
            qb_ = Ring(alloc, "qblk", 2, [128, 4, 512], BF16)
            gd_ = Ring(alloc, "gdblk", 2, [128, 512], BF16)
            E_ = Ring(alloc, "Eb", 6, [128, 512], BF16)
            f_ = Ring(alloc, "af", 6, [128, 512], F32)
            ob_ = Ring(alloc, "aob", 4, [128, 512], BF16)
            K.op("dve", lambda e: e.memset(onesb[:], 1.0), writes=["onesb"])
            K.op("dve", lambda e: e.memset(onesf[:], 1.0), writes=["onesf"])
            K.dma("sp", lrow[:, 0:256], da_lam[:, :], "Llam", writes=["lamrow"])
            K.dma("sp", subrow[:, :], da_sub[:, :], "Lsub", writes=["subrow"])
            K.tt("dve", lrow[:, 0:64], lrow[:, 0:64], lrow[:, 64:128], ALU.mult, ["lamrow"], ["lamrow"])
            K.tt("dve", lrow[:, 128:192], lrow[:, 128:192], lrow[:, 192:256], ALU.mult, ["lamrow"], ["lamrow"])
            K.op("dve", lambda e: e.reduce_sum(lrow[:, 256:257], lrow[:, 0:64], AX.X), ["lamrow"], ["lamrow"])
            K.op("dve", lambda e: e.reduce_sum(lrow[:, 257:258], lrow[:, 128:192], AX.X), ["lamrow"], ["lamrow"])
            K.act(lrow[:, 256:258], lrow[:, 256:258], AF.Exp, ["lamrow"], ["lamrow"])
            K.tt("dve", lrow[:, 258:259], lrow[:, 256:257], lrow[:, 257:258], ALU.subtract, ["lamrow"], ["lamrow"])
            K.ts("dve", lrow[:, 258:259], lrow[:, 258:259], -1.0, -LAMBDA_INIT0, ALU.mult, ALU.add, ["lamrow"], ["lamrow"])
            K.mm(psum[0][:, 0:1], onesf[0:1, :], lrow[0:1, 258:259], True, True, ["onesf", "lamrow"], ["ps#0"], True)
            K.cp("dve", lamc[:, 0:1], psum[0][:, 0:1], ["ps#0"], ["lamc"])
            K.op("pe", lambda e: e.transpose(psum[1][:, 0:1], subrow[0:1, :], idf[0:1, 0:1]),
                 reads=["subrow", "idf"], writes=["ps#1"])
            K.ts("dve", subc[:, 0:1], psum[1][:, 0:1], 1.0 - LAMBDA_INIT0, None, ALU.mult, None, ["ps#1"], ["subc"])
            for h in range(4):
                K.dma("sp", kres[:, h, :], kT[h * 128:(h + 1) * 128, :], "Lkres", accw=["kres"])
            for n0 in range(0, NT, 2):
                K.dma("sp", vres[:, n0:n0 + 2, :], vtok[n0 * 128:(n0 + 2) * 128, :].rearrange("(n p) e -> p n e", p=128),
                      "Lvres", accw=["vres"])
            qblocks = [(0, CTX, 0, 2)] + [(CTX + 512 * g, 512, 0, NT) for g in range(SEQ // 512)]
            nqb = P.dbg.get("nqb", len(qblocks))
            sti = 0
            acc_ = Ring(alloc, "dacc", 4, [128, 512], F32)
            for (q0, nq, kt0, kt1) in qblocks[:nqb]:
                qb = qb_.next()
                for h in range(4):
                    K.dma("sp", qb[:, h, 0:nq], qT[h * 128:(h + 1) * 128, q0:q0 + nq], "L" + qb.key, accw=[qb.key])
                for h in range(4):
                    gd = gd_.next()
                    K.dma("sp", gd[:, 0:nq], gdT[h * 128:(h + 1) * 128, q0:q0 + nq], "L" + gd.key, writes=[gd.key])
                    accs = [acc_.next(), acc_.next()]
                    kts = list(range(kt0, kt1))
                    pkmap = {}

                    def emit_qk(kt):
                        nonlocal sti
                        for c in range(2):
                            pk = sti % 4; sti += 1
                            pkmap[(kt, c)] = pk
                            K.mm(psum[pk][:, 0:nq], kres[c * 64:(c + 1) * 64, h, kt * 128:(kt + 1) * 128],
                                 qb[c * 64:(c + 1) * 64, h, 0:nq], True, True, ["kres", qb.key], [f"ps#{pk}"], True)

                    def emit_rest(kt):
                        for c in range(2):
                            pk = pkmap[(kt, c)]
                            E = E_.next()
                            K.act(E[:, 0:nq], psum[pk][:, 0:nq], AF.Exp, [f"ps#{pk}"], [E.key], scale=0.125)
                            K.mm(psum[4 + c][:, 0:nq], vres[:, kt, h * 128:(h + 1) * 128], E[:, 0:nq],
                                 kt == kt0, kt == kt1 - 1, ["vres", E.key], [f"ps#{4 + c}"], kt == kt1 - 1)
                            eng = "pool" if c == 0 else "dve"
                            if kt == kt0:
                                K.cp(eng, accs[c][:, 0:nq], E[:, 0:nq], [E.key], [accs[c].key])
                            else:
                                K.tt(eng, accs[c][:, 0:nq], accs[c][:, 0:nq], E[:, 0:nq], ALU.add,
                                     [E.key, accs[c].key], [accs[c].key])

                    emit_qk(kts[0])
                    for i, kt in enumerate(kts):
                        if i + 1 < len(kts):
                            emit_qk(kts[i + 1])
                        emit_rest(kt)
                    r0 = f_.next(); r1 = f_.next(); t0 = f_.next(); t1 = f_.next()
                    K.cp("act", t0[:, 0:nq], psum[4][:, 0:nq], ["ps#4"], [t0.key])
                    K.cp("act", t1[:, 0:nq], psum[5][:, 0:nq], ["ps#5"], [t1.key])
                    for c in range(2):
                        ab = ob_.next()
                        K.cp("pool", ab[:, 0:nq], accs[c][:, 0:nq], [accs[c].key], [ab.key])
                        K.mm(psum[6 + c][:, 0:nq], onesb[:, :], ab[:, 0:nq], True, True, ["onesb", ab.key], [f"ps#{6 + c}"], True)
                    K.op("dve", lambda e, r0=r0: e.reciprocal(r0[:, 0:nq], psum[6][:, 0:nq]), ["ps#6"], [r0.key])
                    K.op("dve", lambda e, r1=r1: e.reciprocal(r1[:, 0:nq], psum[7][:, 0:nq]), ["ps#7"], [r1.key])
                    K.tt("dve", t0[:, 0:nq], t0[:, 0:nq], r0[:, 0:nq], ALU.mult, [t0.key, r0.key], [t0.key])
                    K.tt("pool", t1[:, 0:nq], t1[:, 0:nq], r1[:, 0:nq], ALU.mult, [t1.key, r1.key], [t1.key])
                    K.stt("dve", t0[:, 0:nq], t1[:, 0:nq], lamc[:, 0:1], t0[:, 0:nq], ALU.mult, ALU.add,
                          [t0.key, t1.key, "lamc"], [t0.key])
                    osq = ob_.next()
                    K.tt("pool", osq[:, 0:nq], t0[:, 0:nq], t0[:, 0:nq], ALU.mult, [t0.key], [osq.key])
                    K.mm(psum[6][:, 0:nq], onesb[:, :], osq[:, 0:nq], True, True, ["onesb", osq.key], ["ps#6"], True)
                    K.act(r0[:, 0:nq], psum[6][:, 0:nq], AF.Sqrt, ["ps#6", "epsb"], [r0.key], scale=1.0 / 128, bias=epsb[:, 0:1])
                    K.op("dve", lambda e, r0=r0, r1=r1: e.reciprocal(r1[:, 0:nq], r0[:, 0:nq]), [r0.key], [r1.key])
                    K.tt("dve", t0[:, 0:nq], t0[:, 0:nq], r1[:, 0:nq], ALU.mult, [t0.key, r1.key], [t0.key])
                    mo = ob_.next()
                    K.stt("dve", mo[:, 0:nq], t0[:, 0:nq], subc[:, 0:1], gd[:, 0:nq], ALU.mult, ALU.mult,
                          [t0.key, "subc", gd.key], [mo.key])
                    K.dma("sp", mixT0[(4 + h) * 128:(5 + h) * 128, q0:q0 + nq], mo[:, 0:nq], "S" + mo.key,
                          reads=[mo.key], accw=["mixT0"])
            K.barrier()

    if "attn" not in P.dbg.get("skip", ()):
        phase_attn()
    if stop_after == "attn":
        K.barrier(); pes.close(); P.es.close(); return P

    def phase_out(layer, mixT, KC, Wd, res_src, dst, tiles):
        with ExitStack() as es:
            def alloc(name, shape, dt):
                P.uid += 1
                return es.enter_context(nc.sbuf_tensor(f"{name}_{P.uid}", list(shape), dt))
            Wb = alloc("Wo", [128, KC, D], BF16)
            wst = Ring(alloc, "wost", 2, [128, KC, 256], F32)
            mx_ = Ring(alloc, "mxin", 2, [128, KC, 512], BF16)
            rs_ = Ring(alloc, "resin", 3, [128, D], F32)
            tm_ = Ring(alloc, "otmp", 2, [128, D], F32)
            st_ = Ring(alloc, "ostat", 4, [128, 4], F32)
            junk = alloc("ojunk", [128, 512], BF16)
            for pc in range(4):
                wb = wst.next()
                K.dma("sp", wb[:], Wd[:, pc * 256:(pc + 1) * 256].rearrange("(j p) n -> p j n", p=128), "L" + wb.key,
                      writes=[wb.key])
                K.cp(["dve", "pool"][pc % 2], Wb[:, :, pc * 256:(pc + 1) * 256], wb[:], [wb.key], accw=["Wo"])
            gi = 0
            cur = None
            for (tok0, drow, tmod) in tiles:
                g0 = (tok0 // 512) * 512 if tok0 >= CTX else 0
                if tok0 >= CTX:
                    g0 = CTX + ((tok0 - CTX) // 512) * 512
                gn = CTX if tok0 < CTX else 512
                if cur is None or cur[0] != g0:
                    mx = mx_.next()
                    K.dma("sp", mx[:, :, 0:gn], mixT[:, g0:g0 + gn].rearrange("(j p) t -> p j t", p=128), "L" + mx.key,
                          writes=[mx.key])
                    cur = (g0, mx)
                mx = cur[1]; lo = tok0 - g0
                rs = rs_.next(); tm = tm_.next(); st = st_.next()
                K.dma("sp", rs[:, :], res_src[tok0:tok0 + 128, :], "L" + rs.key, writes=[rs.key])
                pks = [(2 * gi) % 6, (2 * gi + 1) % 6]; gi += 1
                for nb in range(2):
                    for j in range(KC):
                        K.mm(psum[pks[nb]][:, :], mx[:, j, lo:lo + 128], Wb[:, j, nb * 512:(nb + 1) * 512],
                             j == 0, j == KC - 1, [mx.key, "Wo"], [f"ps#{pks[nb]}"], j == KC - 1)
                for nb in range(2):
                    K.act(junk[:, :], psum[pks[nb]][:, :], AF.Square, [f"ps#{pks[nb]}"], ["ojunk", st.key] if nb == 0 else ["ojunk"],
                          accw=() if nb == 0 else [st.key], accum_out=st[:, nb:nb + 1])
                K.tt("dve", st[:, 2:3], st[:, 0:1], st[:, 1:2], ALU.add, [st.key], [st.key])
                K.act(st[:, 3:4], st[:, 2:3], AF.Sqrt, [st.key, "epsb"], [st.key], scale=1.0 / D, bias=epsb[:, 0:1])
                K.op("dve", lambda e, st=st: e.reciprocal(st[:, 2:3], st[:, 3:4]), [st.key], [st.key])
                for nb in range(2):
                    K.stt("dve", tm[:, nb * 512:(nb + 1) * 512], psum[pks[nb]][:, :], st[:, 2:3],
                          ggbc[:, layer, tmod, nb * 512:(nb + 1) * 512], ALU.mult, ALU.mult,
                          [f"ps#{pks[nb]}", st.key, "ggbc"], accw=[tm.key])
                K.tt("pool", tm[:, :], tm[:, :], rs[:, :], ALU.add, [tm.key, rs.key], [tm.key])
                K.dma("sp", dst[drow:drow + 128, :], tm[:, :], "S" + tm.key, reads=[tm.key], accw=[dst.name])
            K.barrier()

    nt0 = P.dbg.get("out0_tiles", NT)
    tiles0 = [(i * 128, i * 128, 1 if i < 2 else 0) for i in range(nt0)]
    phase_out(0, mixT0, 8, e_w_out, src0, h1, tiles0)
    if stop_after == "out0":
        K.barrier(); pes.close(); P.es.close(); return P


    def l1_alloc(alloc):
        c = {}
        c["sb"] = Ring(alloc, "sb1", 4, [128, 512], BF16)
        c["sf"] = Ring(alloc, "sf1", 2, [128, 64], F32)
        c["n"] = 0
        return c

    def epi_xbc(c, pks, bi, tok0, ntok):
        pk = pks[0]; b = c["sb"].next()
        c["n"] += 1
        K.cp("act" if c["n"] % 2 else "dve", b[:, 0:ntok], psum[pk][:, 0:ntok], [f"ps#{pk}"], [b.key])
        K.dma("sp", xbcT[bi * 128:(bi + 1) * 128, tok0:tok0 + ntok], b[:, 0:ntok], "S" + b.key,
              reads=[b.key], accw=["xbcT"])

    def epi_z(zc):
        def f(c, pk, tok0):
            b = c["sb"].next()
            K.act(b[:, :], psum[pk][:, :], AF.Silu, [f"ps#{pk}"], [b.key])
            K.dma("sp", zs[tok0:tok0 + 128, zc * 512:(zc + 1) * 512], b[:, :], "S" + b.key, reads=[b.key], accw=["zs"])
        return f

    def epi_dt(c, pk, tok0):
        b = c["sf"].next()
        K.cp("dve", b[:, :], psum[pk][:, 0:64], [f"ps#{pk}"], [b.key])
        K.dma("sp", dtraw[tok0:tok0 + 128, :], b[:, :], "S" + b.key, reads=[b.key], accw=["dtraw"])

    l1_f = [("xbc", [[2048 + f * 128] for f in range(24)], epi_xbc)]
    l1_t = [(zc * 512, 512, epi_z(zc)) for zc in range(4)] + [(5120, 64, epi_dt)]
    if "l1" not in P.dbg.get("skip", ()):
        phase_proj(1, h1, o_w_in, 5184, l1_f, l1_t, l1_alloc, None)
    if stop_after == "proj1":
        K.barrier(); pes.close(); P.es.close(); return P

    def phase_conv():
        with ExitStack() as es:
            def alloc(name, shape, dt):
                P.uid += 1
                return es.enter_context(nc.sbuf_tensor(f"{name}_{P.uid}", list(shape), dt))
            rows = alloc("cvrows", [120, 128], F32)
            cv = alloc("cv", [128, 5, 24], F32)
            dg = alloc("dg", [128, 4, 24, 128], BF16)
            xin_ = Ring(alloc, "cxin", 2, [128, 24, 516], BF16)
            sb_ = Ring(alloc, "csb", 3, [128, 512], BF16)
            rb_ = Ring(alloc, "crow", 8, [128, 2560], BF16)
            K.dma("sp", rows[:, :], ssd_cv.rearrange("v (f p) -> (v f) p", p=128), "Lcvrows", writes=["cvrows"])
            K.op("pe", lambda e: e.transpose(psum[0][:, 0:120], rows[:, :], idf[0:120, 0:120]),
                 reads=["cvrows", "idf"], writes=["ps#0"])
            K.cp("dve", cv[:].rearrange("p v f -> p (v f)"), psum[0][:, 0:120], ["ps#0"], ["cv"])
            for j in range(4):
                for f in range(24):
                    K.ts("dve" if (f % 2) else "pool", dg[:, j, f, :], idf[:, :], cv[:, j, f:f + 1], None, ALU.mult, None,
                         ["idf", "cv"], accw=["dg"])
            blocks = [(0, CTX, 0, CTX)] + [(CTX + 512 * g, 512, CTX, T) for g in range(SEQ // 512)]
            cpi = 0
            for (b0, bn, sa, sb_end) in blocks[:P.dbg.get("nconv", 99)]:
                xin = xin_.next()
                l0 = max(sa, b0 - 2); l1 = min(sb_end, b0 + bn + 1)
                K.dma("sp", xin[:, :, l0 - (b0 - 2):l1 - (b0 - 2)],
                      xbcT[:, l0:l1].rearrange("(f p) t -> p f t", p=128), "L" + xin.key, writes=[xin.key])
                nt_ = bn // 128
                rbs = [rb_.next() for _ in range(nt_)]
                for ft in range(24):
                    pk = 4 + (cpi % 2); cpi += 1
                    order = [2, 0, 1, 3]
                    for oi, j in enumerate(order):
                        sh = j - 2
                        lo = max(b0, sa - sh); hi = min(b0 + bn, sb_end - sh)
                        K.mm(psum[pk][:, lo - b0:hi - b0], dg[:, j, ft, :],
                             xin[:, ft, lo + sh - (b0 - 2):hi + sh - (b0 - 2)], oi == 0, oi == 3,
                             ["dg", xin.key], [f"ps#{pk}"], oi == 3)
                    sb = sb_.next()
                    K.act(sb[:, 0:bn], psum[pk][:, 0:bn], AF.Silu, [f"ps#{pk}", "cv"], [sb.key], bias=cv[:, 4, ft:ft + 1])
                    if ft >= 16:
                        K.dma("sp", bcT[(ft - 16) * 128:(ft - 15) * 128, b0:b0 + bn], sb[:, 0:bn], "S" + sb.key,
                              reads=[sb.key], accw=["bcT"])
                    if ft < 20:
                        for i in range(nt_):
                            tb = psum[i][:].bitcast(BF16)
                            K.op("pe", lambda e, tb=tb, sb=sb, i=i, ft=ft: e.transpose(
                                tb[:, (ft % 8) * 128:(ft % 8 + 1) * 128], sb[:, i * 128:(i + 1) * 128], idb[:]),
                                reads=[sb.key, "idb"], writes=[f"ps#{i}"], inc=True)
                        if ft % 8 == 7 or ft == 19:
                            ncol = (ft % 8 + 1) * 128
                            c0 = (ft // 8) * 1024
                            for i in range(nt_):
                                tb = psum[i][:].bitcast(BF16)
                                K.cp("dve" if i % 2 else "act", rbs[i][:, c0:c0 + ncol], tb[:, 0:ncol], [f"ps#{i}"],
                                     accw=[rbs[i].key])
                for i in range(nt_):
                    K.dma("sp", xsB[b0 + i * 128:b0 + (i + 1) * 128, :], rbs[i][:, :], "S" + rbs[i].key,
                          reads=[rbs[i].key], accw=["xsB"])
            K.barrier()

    if "conv" not in P.dbg.get("skip", ()):
        phase_conv()
    if stop_after == "conv":
        K.barrier(); pes.close(); P.es.close(); return P


    def phase_ssd():
        with ExitStack() as es:
            def alloc(name, shape, dt):
                P.uid += 1
                return es.enter_context(nc.sbuf_tensor(f"{name}_{P.uid}", list(shape), dt))
            vb = alloc("vb", [128, 160], F32)
            aneg = alloc("aneg", [128, 64], F32)
            nwbc = alloc("nwbc", [128, 2048], F32)
            mk = alloc("mk", [128, 2, 128], F32)
            self_ = alloc("self", [64, 4096], F32)
            selb = alloc("selb", [64, 32, 128], BF16)
            onesf = alloc("onesf2", [128, 128], F32)
            ones1 = alloc("ones1b", [128, 1], F32)
            state = alloc("state", [128, 2048], F32)
            stbf = alloc("stbf", [128, 2048], BF16)
            xb_ = Ring(alloc, "xb", 2, [128, 2560], BF16)
            bc_ = Ring(alloc, "bc", 2, [128, 8, 128], BF16)
            dr_ = Ring(alloc, "dr", 2, [128, 32], F32)
            sm_ = Ring(alloc, "sm", 2, [128, 8, 32], F32)
            cs4_ = Ring(alloc, "cs4", 2, [128, 64], F32)
            cst_ = Ring(alloc, "cst", 2, [64, 128], F32)
            hl_ = Ring(alloc, "hl", 2, [64, 3, 128], BF16)
            X_ = Ring(alloc, "Xd", 2, [128, 2048], BF16)
            Xc_ = Ring(alloc, "Xc", 2, [128, 2048], BF16)
            cbm_ = Ring(alloc, "cbm", 2, [128, 4, 128], F32)
            E_ = Ring(alloc, "sE", 2, [128, 512], F32)
            MT_ = Ring(alloc, "sMT", 3, [128, 4, 128], BF16)
            to_ = Ring(alloc, "sto", 2, [128, 512], F32)
            ysb_ = Ring(alloc, "ysb", 2, [128, 2048], F32)
            yf_ = Ring(alloc, "yfl", 1, [128, 2048], F32)
            zt_ = Ring(alloc, "ztl", 1, [128, 2048], BF16)
            ynb_ = Ring(alloc, "ynb", 1, [128, 2048], BF16)
            yT_ = Ring(alloc, "yTs", 2, [128, 16, 128], BF16)
            junk = alloc("sjunk", [128, 512], BF16)
            K.op("dve", lambda e: e.memset(onesf[:], 1.0), writes=["onesf2"])
            K.op("dve", lambda e: e.memset(ones1[:], 1.0), writes=["ones1b"])
            K.dma("sp", vb[:, :], ssd_vec[0, :].partition_broadcast(128), "Lvb", writes=["vb"])
            K.dma("sp", nwbc[:, :], ssd_norm.partition_broadcast(128), "Lnwbc", writes=["nwbc"])
            K.dma("sp", mk[:], maskT.rearrange("d s l -> s d l"), "Lmk", writes=["mk"])
            K.dma("sp", self_[:, :], selc[:, :], "Lself", writes=["self"])
            K.cp("pool", selb[:].rearrange("k h l -> k (h l)"), self_[:, :], ["self"], ["selb"])
            K.act(aneg[:, :], vb[:, 0:64], AF.Exp, ["vb"], ["aneg"])
            K.ts("dve", aneg[:, :], aneg[:, :], -1.0, None, ALU.mult, None, ["aneg"], ["aneg"])
            lat_chunks = list(range(2, NT))
            ncl = P.dbg.get("nchunk", len(lat_chunks))
            lat_chunks = lat_chunks[:ncl]
            for d in range(2):
                order = [0, 1] + lat_chunks if d == 0 else [1, 0] + lat_chunks[::-1]
                K.op("pool", lambda e: e.memset(state[:], 0.0), writes=["state"])
                K.op("pool", lambda e: e.memset(stbf[:], 0.0), writes=["stbf"])
                for c in order:
                    tok0 = c * 128
                    lat = c >= 2
                    xb = xb_.next(); bc = bc_.next(); dr = dr_.next(); sm = sm_.next()
                    K.dma("sp", xb[:, :], xsB[tok0:tok0 + 128, :], "L" + xb.key, writes=[xb.key])
                    K.dma("sp", bc[:], bcT[:, tok0:tok0 + 128].rearrange("(f p) t -> p f t", p=128), "L" + bc.key,
                          writes=[bc.key])
                    K.dma("sp", dr[:, :], dtraw[tok0:tok0 + 128, d * 32:(d + 1) * 32], "L" + dr.key, writes=[dr.key])
                    dt = sm[:, 0, :]; adt = sm[:, 1, :]; cs = sm[:, 2, :]; ecs = sm[:, 3, :]
                    etot = sm[:, 4, :]; w2 = sm[:, 5, :]; tmp = sm[:, 6, :]
                    k_ = sm.key
                    K.tt("dve", tmp, dr[:, :], vb[:, 64 + d * 32:96 + d * 32], ALU.add, [dr.key, "vb"], [k_])
                    K.act(tmp, tmp, AF.Exp, [k_], [k_])
                    K.act(dt, tmp, AF.Ln, [k_, "ones1b"], [k_], bias=ones1[:, 0:1])
                    K.tt("dve", adt, dt, aneg[:, d * 32:(d + 1) * 32], ALU.mult, [k_, "aneg"], [k_])
                    K.mm(psum[6][:, 0:32], mk[:, d, :], adt, True, True, ["mk", k_], ["ps#6"], True)
                    K.mm(psum[6][:, 32:64], onesf[:, :], adt, True, True, ["onesf2", k_], ["ps#6"], True)
                    K.cp("dve", cs, psum[6][:, 0:32], ["ps#6"], [k_])
                    K.act(etot, psum[6][:, 32:64], AF.Exp, ["ps#6"], [k_])
                    K.tt("dve", tmp, psum[6][:, 32:64], cs, ALU.subtract, ["ps#6", k_], [k_])
                    K.act(w2, tmp, AF.Exp, [k_], [k_])
                    K.tt("dve", w2, w2, dt, ALU.mult, [k_], [k_])
                    xs3 = xb[:, 0:2048].rearrange("p (h q) -> p h q", q=64)
                    Xc = Xc_.next()
                    K.tt("pool", Xc[:, :].rearrange("p (h q) -> p h q", q=64), xs3,
                         w2.unsqueeze(2).to_broadcast([128, 32, 64]), ALU.mult, [xb.key, k_], [Xc.key])
                    if lat:
                        K.act(ecs, cs, AF.Exp, [k_], [k_])
                        X = X_.next()
                        K.tt("dve", X[:, :].rearrange("p (h q) -> p h q", q=64), xs3,
                             dt.unsqueeze(2).to_broadcast([128, 32, 64]), ALU.mult, [xb.key, k_], [X.key])
                        cs4 = cs4_.next(); cst = cst_.next(); hl = hl_.next()
                        K.cp("dve", cs4[:, 0:32], cs, [k_], accw=[cs4.key])
                        K.cp("dve", cs4[:, 32:64], cs, [k_], accw=[cs4.key])
                        K.op("pe", lambda e, cs4=cs4: e.transpose(psum[6][0:64, 128:256], cs4[:, :], idf[:, :]),
                             reads=[cs4.key, "idf"], writes=["ps#6"])
                        K.cp("dve", cst[:, :], psum[6][0:64, 128:256], ["ps#6"], [cst.key])
                        K.cp("dve", hl[0:32, 0, :], cst[0:32, :], [cst.key], accw=[hl.key])
                        K.cp("dve", hl[32:64, 2, :], cst[32:64, :], [cst.key], accw=[hl.key])
                        K.tt("dve", hl[32:64, 0, :], cst[32:64, :], hl[32:64, 2, :], ALU.subtract, [cst.key, hl.key], accw=[hl.key])
                        K.ts("dve", hl[:, 1, :], hl[:, 0, :], -1.0, None, ALU.mult, None, [hl.key], accw=[hl.key])
                        for g in range(4):
                            K.mm(psum[7][:, g * 128:(g + 1) * 128], bc[:, g, :], bc[:, 4 + g, :], True, True,
                                 [bc.key], ["ps#7"], g == 3)
                        cbm = cbm_.next()
                        K.tt("dve", cbm[:], psum[7][:, :].rearrange("p (g l) -> p g l", l=128),
                             mk[:, d:d + 1, :].to_broadcast([128, 4, 128]), ALU.mult, ["ps#7", "mk"], [cbm.key])
                        for hq in range(8):
                            g = hq // 2
                            pk = 4 + hq % 2
                            for j in range(4):
                                h = hq * 4 + j
                                K.mm(psum[pk][:, j * 128:(j + 1) * 128], selb[:, h, :], hl[:, 0, :], True, False,
                                     ["selb", hl.key], [f"ps#{pk}"], False)
                                K.mm(psum[pk][:, j * 128:(j + 1) * 128], hl[:, 1, :], selb[:, h, :], False, True,
                                     ["selb", hl.key], [f"ps#{pk}"], j == 3)
                            E = E_.next(); MT = MT_.next()
                            K.act(E[:, :], psum[pk][:, :], AF.Exp, [f"ps#{pk}"], [E.key])
                            K.stt("dve", MT[:], E[:, :].rearrange("p (j l) -> p j l", l=128), 1e30,
                                  cbm[:, g:g + 1, :].to_broadcast([128, 4, 128]), ALU.min, ALU.mult,
                                  [E.key, cbm.key], [MT.key])
                            for j in range(4):
                                h = hq * 4 + j
                                K.mm(psum[h // 8][:, (h % 8) * 64:(h % 8 + 1) * 64], MT[:, j, :], X[:, h * 64:(h + 1) * 64],
                                     True, True, [MT.key, X.key], [f"ps#{h // 8}"], (h % 8 == 7))
                        ysb = ysb_.next()
                        for g in range(4):
                            K.mm(psum[7][:, :], bc[:, 4 + g, :], stbf[:, g * 512:(g + 1) * 512], True, True,
                                 [bc.key, "stbf"], ["ps#7"], True)
                            to = to_.next()
                            K.tt("dve", to[:, :].rearrange("p (h q) -> p h q", q=64),
                                 psum[7][:, :].rearrange("p (h q) -> p h q", q=64),
                                 ecs[:, g * 8:(g + 1) * 8].unsqueeze(2).to_broadcast([128, 8, 64]), ALU.mult,
                                 ["ps#7", k_], [to.key])
                            K.tt("dve", ysb[:, g * 512:(g + 1) * 512], psum[g][:, :], to[:, :], ALU.add,
                                 [f"ps#{g}", to.key], accw=[ysb.key])
                    for g in range(4):
                        K.mm(psum[7][:, :], xb[:, 2048 + g * 128:2048 + (g + 1) * 128], Xc[:, g * 512:(g + 1) * 512],
                             True, True, [xb.key, Xc.key], ["ps#7"], True)
                        sg = state[:, g * 512:(g + 1) * 512]
                        K.tt("pool", sg.rearrange("p (h q) -> p h q", q=64), sg.rearrange("p (h q) -> p h q", q=64),
                             etot[:, g * 8:(g + 1) * 8].unsqueeze(2).to_broadcast([128, 8, 64]), ALU.mult,
                             ["state", k_], ["state"])
                        K.tt("dve", sg, sg, psum[7][:, :], ALU.add, ["state", "ps#7"], ["state"])
                    K.cp("act", stbf[:, :], state[:, :], ["state"], ["stbf"])
                    if not lat:
                        continue
                    if d == 0:
                        K.dma("sp", yfw[tok0:tok0 + 128, :], ysb[:, :], "S" + ysb.key, reads=[ysb.key], accw=["yfw"])
                        continue
                    yf = yf_.next(); zt = zt_.next(); ynb = ynb_.next(); yT = yT_.next()
                    K.dma("sp", yf[:, :], yfw[tok0:tok0 + 128, :], "L" + yf.key, reads=["yfw"], writes=[yf.key])
                    K.dma("sp", zt[:, :], zs[tok0:tok0 + 128, :], "L" + zt.key, writes=[zt.key])
                    K.tt("pool", ysb[:, :], ysb[:, :], yf[:, :], ALU.add, [ysb.key, yf.key], [ysb.key])
                    K.tt("dve", yf[:, :].rearrange("p (h q) -> p h q", q=64), xs3,
                         vb[:, 128:160].unsqueeze(2).to_broadcast([128, 32, 64]), ALU.mult, [xb.key, "vb", yf.key], [yf.key])
                    K.tt("pool", ysb[:, :], ysb[:, :], yf[:, :], ALU.add, [ysb.key, yf.key], [ysb.key])
                    K.tt("dve", ysb[:, :], ysb[:, :], zt[:, :], ALU.mult, [ysb.key, zt.key], [ysb.key])
                    for g in range(4):
                        K.act(junk[:, :], ysb[:, g * 512:(g + 1) * 512], AF.Square, [ysb.key], ["sjunk"], accw=[k_],
                              accum_out=sm[:, 7, g:g + 1])
                    K.act(sm[:, 7, 4:8], sm[:, 7, 0:4], AF.Sqrt, [k_, "epsb"], [k_], scale=1.0 / 512, bias=epsb[:, 0:1])
                    K.op("dve", lambda e, sm=sm: e.reciprocal(sm[:, 7, 8:12], sm[:, 7, 4:8]), [k_], [k_])
                    for g in range(4):
                        K.stt("dve", ynb[:, g * 512:(g + 1) * 512], ysb[:, g * 512:(g + 1) * 512], sm[:, 7, 8 + g:9 + g],
                              nwbc[:, g * 512:(g + 1) * 512], ALU.mult, ALU.mult, [ysb.key, k_, "nwbc"], accw=[ynb.key])
                    for half in range(2):
                        pk = 4 + half
                        tb = psum[pk][:].bitcast(BF16)
                        for jj in range(8):
                            j = half * 8 + jj
                            K.op("pe", lambda e, tb=tb, jj=jj, j=j, ynb=ynb: e.transpose(
                                tb[:, jj * 128:(jj + 1) * 128], ynb[:, j * 128:(j + 1) * 128], idb[:]),
                                reads=[ynb.key, "idb"], writes=[f"ps#{pk}"], inc=(jj == 7))
                        K.cp("act", yT[:, half * 8:(half + 1) * 8, :].rearrange("p j t -> p (j t)"), tb[:, :], [f"ps#{pk}"],
                             accw=[yT.key])
                    K.dma("sp", mixT1[:, tok0:tok0 + 128].rearrange("(j p) t -> p j t", p=128), yT[:], "S" + yT.key,
                          reads=[yT.key], accw=["mixT1"])
            K.barrier()

    if "ssd" not in P.dbg.get("skip", ()):
        phase_ssd()
    if stop_after == "ssd":
        K.barrier(); pes.close(); P.es.close(); return P

    nt1 = P.dbg.get("out1_tiles", SEQ // 128)
    tiles1 = [(CTX + i * 128, i * 128, 0) for i in range(nt1)]
    phase_out(1, mixT1, 16, o_w_out, h1, out_h, tiles1)

    K.barrier()
    pes.close()
    P.es.close()
    return P


def _rope_tables():
    n_freq = 16
    inv = (10000.0 ** (-np.arange(n_freq, dtype=np.float32) / np.float32(n_freq))).astype(np.float32)
    t = np.arange(SEQ)
    row = (t // 64).astype(np.float32)
    col = (t % 64).astype(np.float32)
    ang = np.concatenate([row[:, None] * inv, col[:, None] * inv], axis=-1).astype(np.float32)
    cos, sin = np.cos(ang).astype(np.float32), np.sin(ang).astype(np.float32)
    tab = np.zeros((2, 128, T), np.float32)
    tab[0, :, :CTX] = 1.0
    for p in range(128):
        d = p % 64
        fi = (d % 16) + 16 * (d // 32)
        sgn = -1.0 if (d % 32) < 16 else 1.0
        tab[0, p, CTX:] = cos[:, fi]
        tab[1, p, CTX:] = sgn * sin[:, fi]
    return tab


def _rope_perm():
    perm = np.zeros(512, np.int64)
    for f in range(512):
        d = f % 64
        e = d % 32
        e2 = e + 16 if e < 16 else e - 16
        perm[f] = f - d + (d // 32) * 32 + e2
    return perm


def make_in_maps(inp):
    B = inp["x"].shape[0]
    perm = _rope_perm()
    w = np.asarray(inp["e_w_in"][0], np.float32)
    w_aug = np.ascontiguousarray(np.concatenate([w, w[:, 1024 + perm], w[:, 1536 + perm]], axis=1))
    tab = _rope_tables()
    ident = np.eye(128, dtype=np.float32)
    wbd = np.zeros((2, 2, 4, 128, 128), np.float32)
    for d in range(2):
        for g, nm in enumerate(("lru_w_r", "lru_w_i")):
            wsrc = np.asarray(inp[nm][0][d], np.float32)
            for cc in range(4):
                wbd[d, g, cc, 0:64, 0:64] = wsrc[2 * cc]
                wbd[d, g, cc, 64:128, 64:128] = wsrc[2 * cc + 1]
    wbd = np.ascontiguousarray(wbd.reshape(16, 128, 128))
    lvec = np.ascontiguousarray(np.concatenate([
        np.asarray(inp["lru_conv_w"][0], np.float32), np.asarray(inp["lru_conv_b"], np.float32).reshape(1, 512),
        np.asarray(inp["lru_b_r"][0], np.float32), np.asarray(inp["lru_b_i"][0], np.float32),
        np.asarray(inp["lru_lambda"][0], np.float32)], axis=0))
    ssd_cv = np.ascontiguousarray(np.concatenate([np.asarray(inp["ssd_conv_w"][0], np.float32),
                                                  np.asarray(inp["ssd_conv_b"], np.float32).reshape(1, 3072)], 0))
    ssd_vec = np.ascontiguousarray(np.concatenate([np.asarray(inp["ssd_a_log"][0], np.float32).reshape(-1),
                                                   np.asarray(inp["ssd_dt_bias"][0], np.float32).reshape(-1),
                                                   np.asarray(inp["ssd_d"][0], np.float32).reshape(-1)]).reshape(1, 160))
    ii = np.arange(128)
    maskT = np.stack([(ii[None, :] >= ii[:, None]), (ii[None, :] <= ii[:, None])], 0).astype(np.float32)
    selc = np.zeros((64, 32, 128), np.float32)
    for hh in range(32):
        selc[hh, hh, :] = 1.0
        selc[32 + hh, hh, :] = 1.0
    selc = np.ascontiguousarray(selc.reshape(64, 32 * 128))
    maps = []
    for b in range(B):
        m = {
            "src0": np.ascontiguousarray(np.concatenate([inp["ctx"][b], inp["x"][b]], axis=0), dtype=np.float32),
            "cvec": np.ascontiguousarray(np.stack([inp["c"][b], inp["c_ctx"]], 0), dtype=np.float32),
            "w_mod": np.asarray(inp["w_mod"], np.float32),
            "b_mod": np.asarray(inp["b_mod"], np.float32),
            "g_pre": np.asarray(inp["g_pre"], np.float32),
            "g_post": np.asarray(inp["g_post"], np.float32),
            "e_w_in_aug": w_aug,
            "e_w_out": np.asarray(inp["e_w_out"][0], np.float32),
            "ident": ident,
            "ropetab": tab,
            "lru_wbd": wbd,
            "lru_vec": lvec,
            "da_lam": np.ascontiguousarray(np.asarray(inp["da_lambda"][0], np.float32).reshape(1, 256)),
            "da_sub": np.ascontiguousarray(np.asarray(inp["da_subln"][0], np.float32).reshape(1, 128)),
            "o_w_in": np.asarray(inp["o_w_in"][0], np.float32),
            "o_w_out": np.asarray(inp["o_w_out"][0], np.float32),
            "ssd_cv": ssd_cv,
            "ssd_vec": ssd_vec,
            "ssd_norm": np.asarray(inp["ssd_norm"][0], np.float32),
            "maskT": maskT,
            "selc": selc,
        }
        maps.append(m)
    return maps


def kernel(**inp):
    P = build_program()
    maps = make_in_maps(inp)
    res = run_bass_kernel_spmd(P.nc, maps, core_ids=list(range(8)))
    return np.stack([np.asarray(r["out"], np.float32) for r in res.results], 0)
```

```python
import os
from contextlib import ExitStack
import numpy as np
import concourse.bass as bass
import concourse.mybir as mybir
from concourse.bass_utils import run_bass_kernel_spmd

F32, BF16 = mybir.dt.float32, mybir.dt.bfloat16
AF = mybir.ActivationFunctionType
ALU = mybir.AluOpType
AX = mybir.AxisListType

D = 1024
SEQ = 4096
CTX = 256
T = SEQ + CTX
NT = T // 128
EPS = 1e-6


class Sched:
    ENG = ("pe", "dve", "act", "pool", "sp")

    def __init__(self, nc, es):
        self.nc, self.es = nc, es
        self.e = {"pe": nc.tensor, "dve": nc.vector, "act": nc.scalar, "pool": nc.gpsimd, "sp": nc.sync}
        self.sem, self.cnt = {}, {}
        for n in self.ENG:
            self.sem[n] = es.enter_context(nc.semaphore("s_" + n))
            self.cnt[n] = 0
        self.seen = {n: {} for n in self.ENG}
        self.W, self.Rd = {}, {}
        self.pend = {n: [] for n in self.ENG}
        self.nwait = 0
        self.nins = 0

    def _wait(self, eng, need):
        for s, v in need.items():
            if s == "pe" and eng == "pe":
                continue
            if self.seen[eng].get(s, 0) >= v:
                continue
            self.e[eng].wait_ge(self.sem[s], v)
            self.seen[eng][s] = v
            self.nwait += 1

    def _deps(self, eng, reads, writes, accw):
        need = {}

        def add(d):
            for s, v in d.items():
                if need.get(s, 0) < v:
                    need[s] = v
        for r in reads:
            add(self.W.get(r, {}))
        for w in writes:
            add(self.W.get(w, {}))
            add(self.Rd.get(w, {}))
        for w in accw:
            add(self.Rd.get(w, {}))
        self._wait(eng, need)

    def _register(self, ev, reads, writes, accw):
        s, v = ev
        for r in reads:
            d = self.Rd.setdefault(r, {})
            d[s] = max(d.get(s, 0), v)
        for w in writes:
            self.W[w] = {s: v}
            self.Rd[w] = {}
        for w in accw:
            d = self.W.setdefault(w, {})
            d[s] = max(d.get(s, 0), v)

    def op(self, eng, fn, reads=(), writes=(), accw=(), inc=True):
        self._deps(eng, reads, writes, accw)
        ins = fn(self.e[eng])
        self.nins += 1
        if inc:
            self.cnt[eng] += 1
            ins.then_inc(self.sem[eng], 1)
            ev = (eng, self.cnt[eng])
            for (r, w, a) in self.pend[eng]:
                self._register(ev, r, w, a)
            self.pend[eng] = []
            self._register(ev, reads, writes, accw)
        else:
            self.pend[eng].append((tuple(reads), tuple(writes), tuple(accw)))

    def dma(self, q, out, in_, semkey, reads=(), writes=(), accw=(), **kw):
        if semkey not in self.sem:
            self.sem[semkey] = self.es.enter_context(self.nc.semaphore("d_" + semkey.replace("#", "_")))
            self.cnt[semkey] = 0
        self._deps(q, reads, writes, accw)
        ins = self.e[q].dma_start(out=out, in_=in_, **kw)
        ins.then_inc(self.sem[semkey], 16)
        self.cnt[semkey] += 16
        self.nins += 1
        self._register((semkey, self.cnt[semkey]), reads, writes, accw)

    def tt(self, eng, out, a, b, op, reads, writes=(), accw=()):
        self.op(eng, lambda e: e.tensor_tensor(out, a, b, op), reads, writes, accw)

    def ts(self, eng, out, a, s1, s2, op0, op1=None, reads=(), writes=(), accw=()):
        if op1 is None:
            self.op(eng, lambda e: e.tensor_scalar(out, a, s1, None, op0), reads, writes, accw)
        else:
            self.op(eng, lambda e: e.tensor_scalar(out, a, s1, s2, op0, op1), reads, writes, accw)

    def stt(self, eng, out, a, sc, b, op0, op1, reads, writes=(), accw=()):
        self.op(eng, lambda e: e.scalar_tensor_tensor(out, a, sc, b, op0, op1), reads, writes, accw)

    def act(self, out, in_, func, reads, writes=(), accw=(), **kw):
        self.op("act", lambda e: e.activation(out=out, in_=in_, func=func, **kw), reads, writes, accw)

    def cp(self, eng, out, in_, reads, writes=(), accw=()):
        if eng == "act":
            self.op("act", lambda e: e.copy(out, in_), reads, writes, accw)
        else:
            self.op(eng, lambda e: e.tensor_copy(out, in_), reads, writes, accw)

    def mm(self, out, lhsT, rhs, start, stop, reads, writes, inc):
        self.op("pe", lambda e: e.matmul(out, lhsT, rhs, start=start, stop=stop), reads, writes, inc=inc)

    def barrier(self):
        for n in self.ENG:
            assert not self.pend[n]
        allev = {s: c for s, c in self.cnt.items() if c > 0}
        for n in self.ENG:
            self._wait(n, allev)
        self.W, self.Rd = {}, {}


class Buf:
    def __init__(self, t, key):
        self.t, self.key = t, key

    def __getitem__(self, k):
        return self.t[k]


class Ring:
    def __init__(self, alloc, name, n, shape, dtype):
        self.bufs = [Buf(alloc(f"{name}{i}", shape, dtype), f"{name}#{i}") for i in range(n)]
        self.i = 0

    def next(self):
        b = self.bufs[self.i % len(self.bufs)]
        self.i += 1
        return b


class Prog:
    def __init__(self, dbg=None):
        self.dbg = dbg or {}
        self.nc = nc = bass.Bass("TRN2", target_bir_lowering=False)
        self.es = ExitStack()
        self.K = Sched(nc, self.es)
        self.dram = {}
        self.uid = 0

    def din(self, name, shape, dt=F32):
        self.dram[name] = self.nc.dram_tensor(name, list(shape), dt, kind="ExternalInput").ap()
        return self.dram[name]

    def dout(self, name, shape, dt=F32):
        self.dram[name] = self.nc.dram_tensor(name, list(shape), dt, kind="ExternalOutput").ap()
        return self.dram[name]

    def dscr(self, name, shape, dt):
        if name in self.dbg.get("dump", ()):
            return self.dout(name, shape, dt)
        self.dram[name] = self.nc.dram_tensor(name, list(shape), dt).ap()
        return self.dram[name]


def build_program(dbg=None):
    P = Prog(dbg)
    nc, K = P.nc, P.K
    stop_after = P.dbg.get("stop_after", "all")

    src0 = P.din("src0", [T, D])
    cvec = P.din("cvec", [2, D])
    w_mod = P.din("w_mod", [2, D, 3 * D])
    b_mod = P.din("b_mod", [2, 3 * D])
    g_pre = P.din("g_pre", [2, D])
    g_post = P.din("g_post", [2, D])
    e_w_in = P.din("e_w_in_aug", [D, 4096])
    e_w_out = P.din("e_w_out", [D, D])
    ident = P.din("ident", [128, 128])
    ropetab = P.din("ropetab", [2, 128, T])
    out_h = P.dout("out", [SEQ, D])
    lru_wbd = P.din("lru_wbd", [16, 128, 128])
    lru_vec = P.din("lru_vec", [11, 512])
    da_lam = P.din("da_lam", [1, 256])
    da_sub = P.din("da_sub", [1, 128])
    mixT0 = P.dscr("mixT0", [D, T], BF16)
    o_w_in = P.din("o_w_in", [D, 5184])
    o_w_out = P.din("o_w_out", [2048, D])
    ssd_cv = P.din("ssd_cv", [5, 3072])
    ssd_vec = P.din("ssd_vec", [1, 160])
    ssd_norm = P.din("ssd_norm", [2048])
    maskT = P.din("maskT", [2, 128, 128])
    selc = P.din("selc", [64, 32 * 128])
    xbcT = P.dscr("xbcT", [3072, T], BF16)
    zs = P.dscr("zs", [T, 2048], BF16)
    dtraw = P.dscr("dtraw", [T, 64], F32)
    xsB = P.dscr("xsB", [T, 2560], BF16)
    bcT = P.dscr("bcT", [1024, T], BF16)
    yfw = P.dscr("yfw", [T, 2048], F32)
    mixT1 = P.dscr("mixT1", [2048, T], BF16)
    h1 = P.dscr("h1", [T, D], F32)

    xrT = P.dscr("xrT", [512, T], F32)
    grT = P.dscr("grT", [512, T], BF16)
    gdT = P.dscr("gdT", [512, T], BF16)
    qT = P.dscr("qT", [512, T], BF16)
    kT = P.dscr("kT", [512, T], BF16)
    vtok = P.dscr("vtok", [T, 512], BF16)

    pes = ExitStack()
    def palloc(name, shape, dt):
        return pes.enter_context(nc.sbuf_tensor(name, list(shape), dt))
    psum = [pes.enter_context(nc.psum_tensor(f"psb{i}", [128, 512], F32)) for i in range(8)]
    idf = palloc("idf", [128, 128], F32)
    idb = palloc("idb", [128, 128], BF16)
    epsb = palloc("epsb", [128, 1], F32)
    modA = palloc("modA", [128, 2, 8, 2], F32)
    modS = palloc("modS", [128, 2, 8, 2], F32)
    ggbc = palloc("ggbc", [128, 2, 2, D], F32)

    K.dma("sp", idf[:], ident[:, :], "Lidf", writes=["idf"])
    K.op("dve", lambda e: e.tensor_copy(idb[:], idf[:]), reads=["idf"], writes=["idb"])
    K.op("dve", lambda e: e.memset(epsb[:], EPS), writes=["epsb"])

    def phase_mod():
        with ExitStack() as es:
            def alloc(name, shape, dt):
                P.uid += 1
                return es.enter_context(nc.sbuf_tensor(f"{name}_{P.uid}", list(shape), dt))
            cT = alloc("cT", [128, 2, 8], F32)
            sig = alloc("sig", [128, 2, 8], F32)
            srep = alloc("srep", [128, 2, 8, 128], F32)
            bT = alloc("bT", [128, 2, 24], F32)
            gpT = alloc("gpT", [128, 2, 8], F32)
            bgbc = alloc("bgbc", [128, 2, D], F32)
            gpbc = alloc("gpbc", [128, 2, D], F32)
            wst = Ring(alloc, "wst", 2, [128, 8, 512], F32)
            tmp = alloc("mtmp", [128, 16, 2], F32)

            rows = alloc("rows", [80, 128], F32)
            K.dma("sp", rows[0:16, :], cvec.rearrange("t (j p) -> (t j) p", p=128), "Lrows", accw=["rows"])
            K.dma("sp", rows[16:64, :], b_mod.rearrange("l (f p) -> (l f) p", p=128), "Lrows", accw=["rows"])
            K.dma("sp", rows[64:80, :], g_pre.rearrange("l (j p) -> (l j) p", p=128), "Lrows", accw=["rows"])
            K.op("pe", lambda e: e.transpose(psum[4][:, 0:80], rows[:, :], idf[0:80, 0:80]),
                 reads=["rows", "idf"], writes=["ps#4"])
            K.op("dve", lambda e: e.tensor_copy(cT[:].rearrange("p t j -> p (t j)"), psum[4][:, 0:16]),
                 reads=["ps#4"], writes=["cT"])
            K.op("dve", lambda e: e.tensor_copy(bT[:].rearrange("p l f -> p (l f)"), psum[4][:, 16:64]),
                 reads=["ps#4"], writes=["bT"])
            K.op("dve", lambda e: e.tensor_copy(gpT[:].rearrange("p l j -> p (l j)"), psum[4][:, 64:80]),
                 reads=["ps#4"], writes=["gpT"])
            for l in range(2):
                K.dma("sp", bgbc[:, l, :], b_mod[l, 2 * D:3 * D].partition_broadcast(128), "Lbgbc", accw=["bgbc"])
                K.dma("sp", gpbc[:, l, :], g_post[l, :].partition_broadcast(128), "Lgpbc", accw=["gpbc"])
            K.op("act", lambda e: e.activation(out=sig[:], in_=cT[:], func=AF.Sigmoid), reads=["cT"], writes=["sig"])
            K.op("dve", lambda e: e.tensor_tensor(cT[:], cT[:], sig[:], ALU.mult), reads=["sig", "cT"], writes=["cT"])
            for t in range(2):
                K.op("dve", lambda e, t=t: e.tensor_copy(
                    srep[:, t, :, :], cT[:, t, :].unsqueeze(2).to_broadcast([128, 8, 128])),
                    reads=["cT"], accw=["srep"])
            for l in range(2):
                for pc in range(6):
                    wb = wst.next()
                    K.dma("sp", wb[:], w_mod[l, :, pc * 512:(pc + 1) * 512].rearrange("(j p) n -> p j n", p=128),
                          "L" + wb.key, writes=[wb.key])
                    if pc < 4:
                        pst = psum[pc % 2]
                        for f in range(4):
                            for j in range(8):
                                K.op("pe", lambda e, f=f, j=j, pst=pst, wb=wb: e.matmul(
                                    pst[:, 2 * f:2 * f + 2], wb[:, j, f * 128:(f + 1) * 128], cT[:, :, j],
                                    start=(j == 0), stop=(j == 7)),
                                    reads=[wb.key, "cT"], writes=[f"ps#{pc % 2}"], inc=(j == 7 and f == 3))
                        K.op("dve", lambda e, pst=pst, pc=pc: e.tensor_copy(
                            tmp[:, pc * 4:(pc + 1) * 4, :], pst[:, 0:8].rearrange("p (f t) -> p f t", t=2)),
                            reads=[f"ps#{pc % 2}"], accw=["mtmp"])
                    else:
                        for t in range(2):
                            pst = psum[2 + t]
                            for j in range(8):
                                K.op("pe", lambda e, j=j, t=t, pst=pst, wb=wb: e.matmul(
                                    pst[:, :], srep[:, t, j, :], wb[:, j, :], start=(j == 0), stop=(j == 7)),
                                    reads=[wb.key, "srep"], writes=[f"ps#{2 + t}"], inc=(j == 7))
                            c0 = (pc - 4) * 512
                            K.op("dve", lambda e, t=t, l=l, c0=c0, pst=pst: e.tensor_tensor(
                                ggbc[:, l, t, c0:c0 + 512], pst[:, :], bgbc[:, l, c0:c0 + 512], ALU.add),
                                reads=[f"ps#{2 + t}", "bgbc"], accw=["ggbc"])
                for t in range(2):
                    K.op("dve", lambda e, t=t, l=l: e.tensor_tensor(
                        modS[:, l, :, t], tmp[:, 0:8, t], bT[:, l, 0:8], ALU.add),
                        reads=["mtmp", "bT"], accw=["modS"])
                    K.op("dve", lambda e, t=t, l=l: e.scalar_tensor_tensor(
                        modA[:, l, :, t], tmp[:, 8:16, t], 1.0, bT[:, l, 8:16], ALU.add, ALU.add),
                        reads=["mtmp", "bT"], accw=["modA"])
                    K.op("dve", lambda e, t=t, l=l: e.tensor_tensor(
                        modA[:, l, :, t], modA[:, l, :, t], gpT[:, l, :], ALU.mult),
                        reads=["modA", "gpT"], writes=["modA"])
                    K.op("dve", lambda e, t=t, l=l: e.tensor_tensor(
                        ggbc[:, l, t, :], ggbc[:, l, t, :], gpbc[:, l, :], ALU.mult),
                        reads=["ggbc", "gpbc"], writes=["ggbc"])
            K.barrier()

    phase_mod()
    if "mod" in P.dbg.get("dump", ()):
        dA = P.dout("dbg_modA", [128, 32]); dS = P.dout("dbg_modS", [128, 32]); dG = P.dout("dbg_gg", [128, 4 * D])
        K.dma("sp", dA[:, :], modA[:].rearrange("p l j t -> p (l j t)"), "Sdbg", reads=["modA"])
        K.dma("sp", dS[:, :], modS[:].rearrange("p l j t -> p (l j t)"), "Sdbg", reads=["modS"])
        K.dma("sp", dG[:, :], ggbc[:].rearrange("p l t f -> p (l t f)"), "Sdbg", reads=["ggbc"])
    if stop_after == "mod":
        K.barrier(); pes.close(); P.es.close(); return P

    def phase_proj(layer, src, Wd, ncols, fspecs, tspecs, extra_alloc=None, per_group=None):
        with ExitStack() as es:
            def alloc(name, shape, dt):
                P.uid += 1
                return es.enter_context(nc.sbuf_tensor(f"{name}_{P.uid}", list(shape), dt))
            Wb = alloc("Wb", [128, 8, ncols], BF16)
            wst = Ring(alloc, "wst", 2, [128, 8, 256], F32)
            xr_ = Ring(alloc, "xin", 4, [128, D], F32)
            xn_ = Ring(alloc, "xn", 4, [128, D], BF16)
            uT_ = Ring(alloc, "uT", 2, [128, 8, 512], BF16)
            junk = alloc("junk", [128, D], BF16)
            stat = Ring(alloc, "stat", 4, [128, 4], F32)
            ctxo = extra_alloc(alloc) if extra_alloc else None
            ceng = ["dve", "pool", "act"]
            for pc in range(ncols // 256 + (1 if ncols % 256 else 0)):
                c0 = pc * 256
                cw = min(256, ncols - c0)
                wb = wst.next()
                K.dma("sp", wb[:, :, 0:cw], Wd[:, c0:c0 + cw].rearrange("(j p) n -> p j n", p=128),
                      "L" + wb.key, writes=[wb.key])
                en = ceng[pc % 3]
                if en == "act":
                    K.op("act", lambda e, wb=wb, c0=c0, cw=cw: e.copy(Wb[:, :, c0:c0 + cw], wb[:, :, 0:cw]),
                         reads=[wb.key], accw=["Wb"])
                else:
                    K.op(en, lambda e, wb=wb, c0=c0, cw=cw: e.tensor_copy(Wb[:, :, c0:c0 + cw], wb[:, :, 0:cw]),
                         reads=[wb.key], accw=["Wb"])
            groups = [(0, CTX, 1)] + [(CTX + 512 * g, 512, 0) for g in range(SEQ // 512)]
            pst_i = [0]
            pso_i = [0]

            def front_parts(gi):
                tok0, ntok, tmod = groups[gi]
                uT = uT_.next()
                p1s, p2s = [], []
                for ti in range(ntok // 128):
                    def part1(ti=ti):
                        box = {}
                        xt = xr_.next(); xn = xn_.next(); st = stat.next()
                        box["xn"] = xn
                        K.dma("sp", xt[:], src[tok0 + ti * 128: tok0 + (ti + 1) * 128, :], "L" + xt.key, writes=[xt.key])
                        K.op("act", lambda e: e.activation(out=junk[:], in_=xt[:], func=AF.Square, accum_out=st[:, 0:1]),
                             reads=[xt.key], writes=["junk", st.key])
                        K.op("act", lambda e: e.activation(out=st[:, 1:2], in_=st[:, 0:1], func=AF.Sqrt,
                                                           scale=1.0 / D, bias=epsb[:, 0:1]),
                             reads=[st.key, "epsb"], writes=[st.key])
                        K.op("dve", lambda e: e.reciprocal(st[:, 2:3], st[:, 1:2]), reads=[st.key], writes=[st.key])
                        K.op("pool", lambda e: e.tensor_scalar(xn[:], xt[:], st[:, 2:3], None, ALU.mult),
                             reads=[xt.key, st.key], writes=[xn.key])
                        return box

                    def part2(box, ti=ti):
                        xn = box["xn"]
                        pk = 6 + (pst_i[0] % 2); pst_i[0] += 1
                        pst = psum[pk][:].bitcast(BF16)
                        for j in range(8):
                            K.op("pe", lambda e, j=j: e.transpose(
                                pst[:, j * 128:(j + 1) * 128], xn[:, j * 128:(j + 1) * 128], idb[:]),
                                reads=[xn.key, "idb"], writes=[f"ps#{pk}"], inc=(j == 7))
                        for j in range(8):
                            K.op("dve", lambda e, j=j: e.tensor_scalar(
                                uT[:, j, ti * 128:(ti + 1) * 128], pst[:, j * 128:(j + 1) * 128],
                                modA[:, layer, j, tmod:tmod + 1], modS[:, layer, j, tmod:tmod + 1], ALU.mult, ALU.add),
                                reads=[f"ps#{pk}", "modA", "modS"], accw=[uT.key])
                    p1s.append(part1); p2s.append(part2)
                return uT, p1s, p2s

            def mm(gi, uT, hooks):
                tok0, ntok, tmod = groups[gi]
                if per_group:
                    per_group(ctxo, gi, tok0, ntok)
                nb_tot = sum(len(b) for (_, b, _) in fspecs)
                nhk = max(1, len(hooks))
                step = max(1, nb_tot // nhk)
                bcount = 0
                hooks = list(hooks)
                for (name, bundles, epi) in fspecs:
                    for bi, cols in enumerate(bundles):
                        if hooks and bcount % step == 0:
                            hooks.pop(0)()
                        bcount += 1
                        pks = []
                        for c0 in cols:
                            pk = pso_i[0] % 6; pso_i[0] += 1
                            pks.append(pk)
                            for j in range(8):
                                K.op("pe", lambda e, j=j, pk=pk, c0=c0, uT=uT, ntok=ntok: e.matmul(
                                    psum[pk][:, 0:ntok], Wb[:, j, c0:c0 + 128], uT[:, j, 0:ntok],
                                    start=(j == 0), stop=(j == 7)),
                                    reads=["Wb", uT.key], writes=[f"ps#{pk}"], inc=(j == 7))
                        epi(ctxo, pks, bi, tok0, ntok)
                for (c0, cw, epi) in tspecs:
                    for ti in range(ntok // 128):
                        pk = pso_i[0] % 6; pso_i[0] += 1
                        for j in range(8):
                            K.op("pe", lambda e, j=j, pk=pk, uT=uT, ti=ti: e.matmul(
                                psum[pk][:, 0:cw], uT[:, j, ti * 128:(ti + 1) * 128], Wb[:, j, c0:c0 + cw],
                                start=(j == 0), stop=(j == 7)),
                                reads=["Wb", uT.key], writes=[f"ps#{pk}"], inc=(j == 7))
                        epi(ctxo, pk, tok0 + ti * 128)
                while hooks:
                    hooks.pop(0)()

            ng = P.dbg.get("ngroups", len(groups))
            uT0, p1s, p2s = front_parts(0)
            for p1, p2 in zip(p1s, p2s):
                p2(p1())
            cur = uT0
            for gi in range(ng):
                hooks = []
                nxt = None
                if gi + 1 < ng:
                    nxt, p1s, p2s = front_parts(gi + 1)
                    boxes = {}
                    def mk(k, p1s=p1s, p2s=p2s, boxes=boxes):
                        def h():
                            if k == 0:
                                for kk in range(min(2, len(p1s))):
                                    boxes[kk] = p1s[kk]()
                                return
                            if 0 <= k - 1 < len(p1s):
                                p2s[k - 1](boxes[k - 1])
                            if k + 1 < len(p1s):
                                boxes[k + 1] = p1s[k + 1]()
                        return h
                    hooks = [mk(k) for k in range(len(p1s) + 1)]
                mm(gi, cur, hooks)
                cur = nxt
            K.barrier()

    def l0_alloc(alloc):
        c = {}
        c["sf"] = Ring(alloc, "sf", 3, [128, 512], F32)
        c["sb"] = Ring(alloc, "sb", 4, [128, 512], BF16)
        c["t1"] = Ring(alloc, "t1", 2, [128, 512], F32)
        c["t2"] = Ring(alloc, "t2", 2, [128, 512], F32)
        c["tab"] = Ring(alloc, "tab", 2, [128, 2, 512], F32)
        return c

    def l0_group(c, gi, tok0, ntok):
        tb = c["tab"].next()
        c["curtab"] = tb
        K.dma("sp", tb[:, :, 0:ntok], ropetab[:, :, tok0:tok0 + ntok].rearrange("c p t -> p c t"),
              "L" + tb.key, writes=[tb.key])

    def epi_copy_f32(dst):
        def f(c, pks, bi, tok0, ntok):
            pk = pks[0]; b = c["sf"].next()
            K.op("act", lambda e: e.copy(b[:, 0:ntok], psum[pk][:, 0:ntok]), reads=[f"ps#{pk}"], writes=[b.key])
            K.dma("sp", dst[bi * 128:(bi + 1) * 128, tok0:tok0 + ntok], b[:, 0:ntok], "S" + b.key,
                  reads=[b.key], accw=[dst.name])
        return f

    def epi_silu_bf(dst):
        def f(c, pks, bi, tok0, ntok):
            pk = pks[0]; b = c["sb"].next()
            K.op("act", lambda e: e.activation(out=b[:, 0:ntok], in_=psum[pk][:, 0:ntok], func=AF.Silu),
                 reads=[f"ps#{pk}"], writes=[b.key])
            K.dma("sp", dst[bi * 128:(bi + 1) * 128, tok0:tok0 + ntok], b[:, 0:ntok], "S" + b.key,
                  reads=[b.key], accw=[dst.name])
        return f

    def epi_rope(dst):
        def f(c, pks, bi, tok0, ntok):
            pa, pb = pks; t1 = c["t1"].next(); t2 = c["t2"].next(); b = c["sb"].next(); tb = c["curtab"]
            K.op("dve", lambda e: e.tensor_tensor(t1[:, 0:ntok], psum[pa][:, 0:ntok], tb[:, 0, 0:ntok], ALU.mult),
                 reads=[f"ps#{pa}", tb.key], writes=[t1.key])
            K.op("dve", lambda e: e.tensor_tensor(t2[:, 0:ntok], psum[pb][:, 0:ntok], tb[:, 1, 0:ntok], ALU.mult),
                 reads=[f"ps#{pb}", tb.key], writes=[t2.key])
            K.op("pool", lambda e: e.tensor_tensor(b[:, 0:ntok], t1[:, 0:ntok], t2[:, 0:ntok], ALU.add),
                 reads=[t1.key, t2.key], writes=[b.key])
            K.dma("sp", dst[bi * 128:(bi + 1) * 128, tok0:tok0 + ntok], b[:, 0:ntok], "S" + b.key,
                  reads=[b.key], accw=[dst.name])
        return f

    def epi_v(c, pk, tok0):
        b = c["sb"].next()
        K.op("act", lambda e: e.copy(b[:, :], psum[pk][:, :]), reads=[f"ps#{pk}"], writes=[b.key])
        K.dma("sp", vtok[tok0:tok0 + 128, :], b[:, :], "S" + b.key, reads=[b.key], accw=["vtok"])

    l0_f = [
        ("xr", [[f * 128] for f in range(0, 4)], epi_copy_f32(xrT)),
        ("gr", [[f * 128] for f in range(4, 8)], epi_silu_bf(grT)),
        ("q", [[1024 + f * 128, 3072 + f * 128] for f in range(4)], epi_rope(qT)),
        ("k", [[1536 + f * 128, 3584 + f * 128] for f in range(4)], epi_rope(kT)),
        ("gd", [[f * 128] for f in range(20, 24)], epi_silu_bf(gdT)),
    ]
    l0_t = [(2048, 512, epi_v)]
    phase_proj(0, src0, e_w_in, 4096, l0_f, l0_t, l0_alloc, l0_group)
    if stop_after == "proj0":
        K.barrier(); pes.close(); P.es.close(); return P


    LAMBDA_INIT0 = 0.8 - 0.6 * 1.0

    def phase_lru():
        with ExitStack() as es:
            def alloc(name, shape, dt):
                P.uid += 1
                return es.enter_context(nc.sbuf_tensor(f"{name}_{P.uid}", list(shape), dt))
            rows = alloc("lrows", [44, 128], F32)
            pv = alloc("lpv", [128, 11, 4], F32)
            coef = alloc("lcoef", [128, 2, 4], F32)
            ones1 = alloc("ones1", [128, 1], F32)
            wbf = alloc("wbf", [128, 16, 128], F32)
            wbb = alloc("wbb", [128, 16, 128], BF16)
            big = Ring(alloc, "big", 7, [128, T], F32)
            xcb = alloc("xcb", [128, T], BF16)
            grs = alloc("grs", [128, T], BF16)
            mo = alloc("mixo", [128, T], BF16)
            K.op("dve", lambda e: e.memset(ones1[:], 1.0), writes=["ones1"])
            K.dma("sp", rows[:, :], lru_vec.rearrange("v (c p) -> (v c) p", p=128), "Lrows", writes=["lrows"])
            K.op("pe", lambda e: e.transpose(psum[0][:, 0:44], rows[:, :], idf[0:44, 0:44]),
                 reads=["lrows", "idf"], writes=["ps#0"])
            K.cp("dve", pv[:].rearrange("p v c -> p (v c)"), psum[0][:, 0:44], ["ps#0"], ["lpv"])
            K.act(coef[:].rearrange("p d c -> p (d c)"), pv[:, 9:11, :].rearrange("p d c -> p (d c)"), AF.Exp,
                  ["lpv"], ["lcoef"], scale=-1.0)
            K.act(coef[:].rearrange("p d c -> p (d c)"), coef[:].rearrange("p d c -> p (d c)"), AF.Ln,
                  ["lcoef", "ones1"], ["lcoef"], bias=ones1[:, 0:1])
            K.ts("dve", coef[:].rearrange("p d c -> p (d c)"), coef[:].rearrange("p d c -> p (d c)"), -8.0, None,
                 ALU.mult, None, ["lcoef"], ["lcoef"])
            K.dma("sp", wbf[:], lru_wbd.rearrange("n k m -> k n m"), "Lwbf", writes=["wbf"])
            K.cp("dve", wbb[:], wbf[:], ["wbf"], ["wbb"])
            segs = [(0, CTX), (CTX, T)]
            blocks = [(b0, min(512, T - b0)) for b0 in range(0, T, 512)]
            for cc in range(4):
                x = big.next(); xc = big.next()
                K.dma("sp", x[:, :], xrT[cc * 128:(cc + 1) * 128, :], "L" + x.key, writes=[x.key])
                K.dma("sp", grs[:, :], grT[cc * 128:(cc + 1) * 128, :], "Lgrs", writes=["grs"])
                K.ts("dve", xc[:, :], x[:, :], pv[:, 2, cc:cc + 1], pv[:, 4, cc:cc + 1], ALU.mult, ALU.add,
                     [x.key, "lpv"], [xc.key])
                for (a, b) in segs:
                    for tap, sh in ((0, -2), (1, -1), (3, 1)):
                        lo = max(a, a - sh); hi = min(b, b - sh)
                        K.stt("dve", xc[:, lo:hi], x[:, lo + sh:hi + sh], pv[:, tap, cc:cc + 1],
                              xc[:, lo:hi], ALU.mult, ALU.add, [x.key, xc.key, "lpv"], [xc.key])
                K.cp("pool", xcb[:, :], xc[:, :], [xc.key], ["xcb"])
                hs = []
                for d in range(2):
                    rb = big.next(); ib = big.next()
                    for (b0, bn) in blocks:
                        for g, dstb, brow in ((0, rb, 5 + d), (1, ib, 7 + d)):
                            pk = (2 * (b0 // 512) + g) % 6
                            K.mm(psum[pk][:, 0:bn], wbb[:, (d * 2 + g) * 4 + cc, :], xcb[:, b0:b0 + bn], True, True,
                                 ["wbb", "xcb"], [f"ps#{pk}"], True)
                            K.act(dstb[:, b0:b0 + bn], psum[pk][:, 0:bn], AF.Sigmoid, [f"ps#{pk}", "lpv"], accw=[dstb.key],
                                  bias=pv[:, brow, cc:cc + 1])
                    K.ts("dve", rb[:, :], rb[:, :], coef[:, d, cc:cc + 1], None, ALU.mult, None, [rb.key, "lcoef"], [rb.key])
                    K.act(rb[:, :], rb[:, :], AF.Exp, [rb.key], [rb.key])
                    sq = big.next()
                    K.tt("pool", sq[:, :], rb[:, :], rb[:, :], ALU.mult, [rb.key], [sq.key])
                    K.act(sq[:, :], sq[:, :], AF.Sqrt, [sq.key, "ones1"], [sq.key], scale=-1.0, bias=ones1[:, 0:1])
                    K.tt("pool", ib[:, :], ib[:, :], xc[:, :], ALU.mult, [ib.key, xc.key], [ib.key])
                    K.tt("dve", ib[:, :], ib[:, :], sq[:, :], ALU.mult, [ib.key, sq.key], [ib.key])
                    h = sq
                    if d == 0:
                        K.op("dve", lambda e, h=h, rb=rb, ib=ib: e.tensor_tensor_scan(
                            h[:, :], rb[:, :], ib[:, :], 0.0, ALU.mult, ALU.add), [rb.key, ib.key, h.key], [h.key])
                    else:
                        K.op("dve", lambda e, h=h, rb=rb, ib=ib: e.tensor_tensor_scan(
                            h[:, CTX - 1::-1] if False else h[:, 0:CTX][:, ::-1], rb[:, 0:CTX][:, ::-1], ib[:, 0:CTX][:, ::-1],
                            0.0, ALU.mult, ALU.add), [rb.key, ib.key, h.key], [h.key])
                        K.op("dve", lambda e, h=h, rb=rb, ib=ib: e.tensor_tensor_scan(
                            h[:, CTX:T][:, ::-1], rb[:, CTX:T][:, ::-1], ib[:, CTX:T][:, ::-1],
                            h[:, 0:1], ALU.mult, ALU.add), [rb.key, ib.key, h.key], [h.key])
                    hs.append(h)
                K.tt("pool", hs[0][:, :], hs[0][:, :], hs[1][:, :], ALU.add, [hs[0].key, hs[1].key], [hs[0].key])
                K.tt("dve", mo[:, :], hs[0][:, :], grs[:, :], ALU.mult, [hs[0].key, "grs"], ["mixo"])
                K.dma("sp", mixT0[cc * 128:(cc + 1) * 128, :], mo[:, :], "Smixo", reads=["mixo"], accw=["mixT0"])
            K.barrier()

    if "lru" not in P.dbg.get("skip", ()):
        phase_lru()
    if stop_after == "lru":
        K.barrier(); pes.close(); P.es.close(); return P

    def phase_attn():
        with ExitStack() as es:
            def alloc(name, shape, dt):
                P.uid += 1
                return es.enter_context(nc.sbuf_tensor(f"{name}_{P.uid}", list(shape), dt))
            kres = alloc("kres", [128, 4, T], BF16)
            vres = alloc("vres", [128, NT, 512], BF16)
            onesb = alloc("onesb", [128, 128], BF16)
            onesf = alloc("onesf", [128, 128], F32)
            lrow = alloc("lamrow", [1, 260], F32)
            lamc = alloc("lamc", [128, 2], F32)
            subc = alloc("subc", [128, 2], F32)
            subrow = alloc("subrow", [1, 128], F32)
            qb_ = Ring(alloc, "qblk", 2, [128, 4, 512], BF16)
            gd_ = Ring(alloc, "gdblk", 2, [128, 512], BF16)
            E_ = Ring(alloc, "Eb", 6, [128, 512], BF16)
            f_ = Ring(alloc, "af", 6, [128, 512], F32)
            ob_ = Ring(alloc, "aob", 4, [128, 512], BF16)
            K.op("dve", lambda e: e.memset(onesb[:], 1.0), writes=["onesb"])
            K.op("dve", lambda e: e.memset(onesf[:], 1.0), writes=["onesf"])
            K.dma("sp", lrow[:, 0:256], da_lam[:, :], "Llam", writes=["lamrow"])
            K.dma("sp", subrow[:, :], da_sub[:, :], "Lsub", writes=["subrow"])
            K.tt("dve", lrow[:, 0:64], lrow[:, 0:64], lrow[:, 64:128], ALU.mult, ["lamrow"], ["lamrow"])
            K.tt("dve", lrow[:, 128:192], lrow[:, 128:192], lrow[:, 192:256], ALU.mult, ["lamrow"], ["lamrow"])
            K.op("dve", lambda e: e.reduce_sum(lrow[:, 256:257], lrow[:, 0:64], AX.X), ["lamrow"], ["lamrow"])
            K.op("dve", lambda e: e.reduce_sum(lrow[:, 257:258], lrow[:, 128:192], AX.X), ["lamrow"], ["lamrow"])
            K.act(lrow[:, 256:258], lrow[:, 256:258], AF.Exp, ["lamrow"], ["lamrow"])
            K.tt("dve", lrow[:, 258:259], lrow[:, 256:257], lrow[:, 257:258], ALU.subtract, ["lamrow"], ["lamrow"])
            K.ts("dve", lrow[:, 258:259], lrow[:, 258:259], -1.0, -LAMBDA_INIT0, ALU.mult, ALU.add, ["lamrow"], ["lamrow"])
            K.mm(psum[0][:, 0:1], onesf[0:1, :], lrow[0:1, 258:259], True, True, ["onesf", "lamrow"], ["ps#0"], True)
            K.cp("dve", lamc[:, 0:1], psum[0][:, 0:1], ["ps#0"], ["lamc"])
            K.op("pe", lambda e: e.transpose(psum[1][:, 0:1], subrow[0:1, :], idf[0:1, 0:1]),
                 reads=["subrow", "idf"], writes=["ps#1"])
            K.ts("dve", subc[:, 0:1], psum[1][:, 0:1], 1.0 - LAMBDA_INIT0, None, ALU.mult, None, ["ps#1"], ["subc"])
            for h in range(4):
                K.dma("sp", kres[:, h, :], kT[h * 128:(h + 1) * 128, :], "Lkres", accw=["kres"])
            for n0 in range(0, NT, 2):
                K.dma("sp", vres[:, n0:n0 + 2, :], vtok[n0 * 128:(n0 + 2) * 128, :].rearrange("(n p) e -> p n e", p=128),
                      "Lvres", accw=["vres"])
            qblocks = [(0, CTX, 0, 2)] + [(CTX + 512 * g, 512, 0, NT) for g in range(SEQ // 512)]
            nqb = P.dbg.get("nqb", len(qblocks))
            sti = 0
            acc_ = Ring(alloc, "dacc", 4, [128, 512], F32)
            for (q0, nq, kt0, kt1) in qblocks[:nqb]:
                qb = qb_.next()
                for h in range(4):
                    K.dma("sp", qb[:, h, 0:nq], qT[h * 128:(h + 1) * 128, q0:q0 + nq], "L" + qb.key, accw=[qb.key])
                for h in range(4):
                    gd = gd_.next()
                    K.dma("sp", gd[:, 0:nq], gdT[h * 128:(h + 1) * 128, q0:q0 + nq], "L" + gd.key, writes=[gd.key])
                    accs = [acc_.next(), acc_.next()]
                    kts = list(range(kt0, kt1))
                    pkmap = {}

                    def emit_qk(kt):
                        nonlocal sti
                        for c in range(2):
                            pk = sti % 4; sti += 1
                            pkmap[(kt, c)] = pk
                            K.mm(psum[pk][:, 0:nq], kres[c * 64:(c + 1) * 64, h, kt * 128:(kt + 1) * 128],
                                 qb[c * 64:(c + 1) * 64, h, 0:nq], True, True, ["kres", qb.key], [f"ps#{pk}"], True)

                    def emit_rest(kt):
                        for c in range(2):
                            pk = pkmap[(kt, c)]
                            E = E_.next()
                            K.act(E[:, 0:nq], psum[pk][:, 0:nq], AF.Exp, [f"ps#{pk}"], [E.key], scale=0.125)
                            K.mm(psum[4 + c][:, 0:nq], vres[:, kt, h * 128:(h + 1) * 128], E[:, 0:nq],
                                 kt == kt0, kt == kt1 - 1, ["vres", E.key], [f"ps#{4 + c}"], kt == kt1 - 1)
                            eng = "pool" if c == 0 else "dve"
                            if kt == kt0:
                                K.cp(eng, accs[c][:, 0:nq], E[:, 0:nq], [E.key], [accs[c].key])
                            else:
                                K.tt(eng, accs[c][:, 0:nq], accs[c][:, 0:nq], E[:, 0:nq], ALU.add,
                                     [E.key, accs[c].key], [accs[c].key])

                    emit_qk(kts[0])
                    for i, kt in enumerate(kts):
                        if i + 1 < len(kts):
                            emit_qk(kts[i + 1])
                        emit_rest(kt)
                    r0 = f_.next(); r1 = f_.next(); t0 = f_.next(); t1 = f_.next()
                    K.cp("act", t0[:, 0:nq], psum[4][:, 0:nq], ["ps#4"], [t0.key])
                    K.cp("act", t1[:, 0:nq], psum[5][:, 0:nq], ["ps#5"], [t1.key])
                    for c in range(2):
                        ab = ob_.next()
                        K.cp("pool", ab[:, 0:nq], accs[c][:, 0:nq], [accs[c].key], [ab.key])
                        K.mm(psum[6 + c][:, 0:nq], onesb[:, :], ab[:, 0:nq], True, True, ["onesb", ab.key], [f"ps#{6 + c}"], True)
                    K.op("dve", lambda e, r0=r0: e.reciprocal(r0[:, 0:nq], psum[6][:, 0:nq]), ["ps#6"], [r0.key])
                    K.op("dve", lambda e, r1=r1: e.reciprocal(r1[:, 0:nq], psum[7][:, 0:nq]), ["ps#7"], [r1.key])
                    K.tt("dve", t0[:, 0:nq], t0[:, 0:nq], r0[:, 0:nq], ALU.mult, [t0.key, r0.key], [t0.key])
                    K.tt("pool", t1[:, 0:nq], t1[:, 0:nq], r1[:, 0:nq], ALU.mult, [t1.key, r1.key], [t1.key])
                    K.stt("dve", t0[:, 0:nq], t1[:, 0:nq], lamc[:, 0:1], t0[:, 0:nq], ALU.mult, ALU.add,
                          [t0.key, t1.key, "lamc"], [t0.key])
                    osq = ob_.next()
                    K.tt("pool", osq[:, 0:nq], t0[:, 0:nq], t0[:, 0:nq], ALU.mult, [t0.key], [osq.key])
                    K.mm(psum[6][:, 0:nq], onesb[:, :], osq[:, 0:nq], True, True, ["onesb", osq.key], ["ps#6"], True)
                    K.act(r0[:, 0:nq], psum[6][:, 0:nq], AF.Sqrt, ["ps#6", "epsb"], [r0.key], scale=1.0 / 128, bias=epsb[:, 0:1])
                    K.op("dve", lambda e, r0=r0, r1=r1: e.reciprocal(r1[:, 0:nq], r0[:, 0:nq]), [r0.key], [r1.key])
                    K.tt("dve", t0[:, 0:nq], t0[:, 0:nq], r1[:, 0:nq], ALU.mult, [t0.key, r1.key], [t0.key])
                    mo = ob_.next()
                    K.stt("dve", mo[:, 0:nq], t0[:, 0:nq], subc[:, 0:1], gd[:, 0:nq], ALU.mult, ALU.mult,
                          [t0.key, "subc", gd.key], [mo.key])
                    K.dma("sp", mixT0[(4 + h) * 128:(5 + h) * 128, q0:q0 + nq], mo[:, 0:nq], "S" + mo.key,
                          reads=[mo.key], accw=["mixT0"])
            K.barrier()

    if "attn" not in P.dbg.get("skip", ()):
        phase_attn()
    if stop_after == "attn":
        K.barrier(); pes.close(); P.es.close(); return P

    def phase_out(layer, mixT, KC, Wd, res_src, dst, tiles):
        with ExitStack() as es:
            def alloc(name, shape, dt):
                P.uid += 1
                return es.enter_context(nc.sbuf_tensor(f"{name}_{P.uid}", list(shape), dt))
            Wb = alloc("Wo", [128, KC, D], BF16)
            wst = Ring(alloc, "wost", 2, [128, KC, 256], F32)
            mx_ = Ring(alloc, "mxin", 2, [128, KC, 512], BF16)
            rs_ = Ring(alloc, "resin", 3, [128, D], F32)
            tm_ = Ring(alloc, "otmp", 2, [128, D], F32)
            st_ = Ring(alloc, "ostat", 4, [128, 4], F32)
            junk = alloc("ojunk", [128, 512], BF16)
            for pc in range(4):
                wb = wst.next()
                K.dma("sp", wb[:], Wd[:, pc * 256:(pc + 1) * 256].rearrange("(j p) n -> p j n", p=128), "L" + wb.key,
                      writes=[wb.key])
                K.cp(["dve", "pool"][pc % 2], Wb[:, :, pc * 256:(pc + 1) * 256], wb[:], [wb.key], accw=["Wo"])
            gi = 0
            cur = None
            for (tok0, drow, tmod) in tiles:
                g0 = (tok0 // 512) * 512 if tok0 >= CTX else 0
                if tok0 >= CTX:
                    g0 = CTX + ((tok0 - CTX) // 512) * 512
                gn = CTX if tok0 < CTX else 512
                if cur is None or cur[0] != g0:
                    mx = mx_.next()
                    K.dma("sp", mx[:, :, 0:gn], mixT[:, g0:g0 + gn].rearrange("(j p) t -> p j t", p=128), "L" + mx.key,
                          writes=[mx.key])
                    cur = (g0, mx)
                mx = cur[1]; lo = tok0 - g0
                rs = rs_.next(); tm = tm_.next(); st = st_.next()
                K.dma("sp", rs[:, :], res_src[tok0:tok0 + 128, :], "L" + rs.key, writes=[rs.key])
                pks = [(2 * gi) % 6, (2 * gi + 1) % 6]; gi += 1
                for nb in range(2):
                    for j in range(KC):
                        K.mm(psum[pks[nb]][:, :], mx[:, j, lo:lo + 128], Wb[:, j, nb * 512:(nb + 1) * 512],
                             j == 0, j == KC - 1, [mx.key, "Wo"], [f"ps#{pks[nb]}"], j == KC - 1)
                for nb in range(2):
                    K.act(junk[:, :], psum[pks[nb]][:, :], AF.Square, [f"ps#{pks[nb]}"], ["ojunk", st.key] if nb == 0 else ["ojunk"],
                          accw=() if nb == 0 else [st.key], accum_out=st[:, nb:nb + 1])
                K.tt("dve", st[:, 2:3], st[:, 0:1], st[:, 1:2], ALU.add, [st.key], [st.key])
                K.act(st[:, 3:4], st[:, 2:3], AF.Sqrt, [st.key, "epsb"], [st.key], scale=1.0 / D, bias=epsb[:, 0:1])
                K.op("dve", lambda e, st=st: e.reciprocal(st[:, 2:3], st[:, 3:4]), [st.key], [st.key])
                for nb in range(2):
                    K.stt("dve", tm[:, nb * 512:(nb + 1) * 512], psum[pks[nb]][:, :], st[:, 2:3],
                          ggbc[:, layer, tmod, nb * 512:(nb + 1) * 512], ALU.mult, ALU.mult,
                          [f"ps#{pks[nb]}", st.key, "ggbc"], accw=[tm.key])
                K.tt("pool", tm[:, :], tm[:, :], rs[:, :], ALU.add, [tm.key, rs.key], [tm.key])
                K.dma("sp", dst[drow:drow + 128, :], tm[:, :], "S" + tm.key, reads=[tm.key], accw=[dst.name])
            K.barrier()

    nt0 = P.dbg.get("out0_tiles", NT)
    tiles0 = [(i * 128, i * 128, 1 if i < 2 else 0) for i in range(nt0)]
    phase_out(0, mixT0, 8, e_w_out, src0, h1, tiles0)
    if stop_after == "out0":
        K.barrier(); pes.close(); P.es.close(); return P


    def l1_alloc(alloc):
        c = {}
        c["sb"] = Ring(alloc, "sb1", 4, [128, 512], BF16)
        c["sf"] = Ring(alloc, "sf1", 2, [128, 64], F32)
        c["n"] = 0
        return c

    def epi_xbc(c, pks, bi, tok0, ntok):
        pk = pks[0]; b = c["sb"].next()
        c["n"] += 1
        K.cp("act" if c["n"] % 2 else "dve", b[:, 0:ntok], psum[pk][:, 0:ntok], [f"ps#{pk}"], [b.key])
        K.dma("sp", xbcT[bi * 128:(bi + 1) * 128, tok0:tok0 + ntok], b[:, 0:ntok], "S" + b.key,
              reads=[b.key], accw=["xbcT"])

    def epi_z(zc):
        def f(c, pk, tok0):
            b = c["sb"].next()
            K.act(b[:, :], psum[pk][:, :], AF.Silu, [f"ps#{pk}"], [b.key])
            K.dma("sp", zs[tok0:tok0 + 128, zc * 512:(zc + 1) * 512], b[:, :], "S" + b.key, reads=[b.key], accw=["zs"])
        return f

    def epi_dt(c, pk, tok0):
        b = c["sf"].next()
        K.cp("dve", b[:, :], psum[pk][:, 0:64], [f"ps#{pk}"], [b.key])
        K.dma("sp", dtraw[tok0:tok0 + 128, :], b[:, :], "S" + b.key, reads=[b.key], accw=["dtraw"])

    l1_f = [("xbc", [[2048 + f * 128] for f in range(24)], epi_xbc)]
    l1_t = [(zc * 512, 512, epi_z(zc)) for zc in range(4)] + [(5120, 64, epi_dt)]
    if "l1" not in P.dbg.get("skip", ()):
        phase_proj(1, h1, o_w_in, 5184, l1_f, l1_t, l1_alloc, None)
    if stop_after == "proj1":
        K.barrier(); pes.close(); P.es.close(); return P

    def phase_conv():
        with ExitStack() as es:
            def alloc(name, shape, dt):
                P.uid += 1
                return es.enter_context(nc.sbuf_tensor(f"{name}_{P.uid}", list(shape), dt))
            rows = alloc("cvrows", [120, 128], F32)
            cv = alloc("cv", [128, 5, 24], F32)
            dg = alloc("dg", [128, 4, 24, 128], BF16)
            xin_ = Ring(alloc, "cxin", 2, [128, 24, 516], BF16)
            sb_ = Ring(alloc, "csb", 3, [128, 512], BF16)
            rb_ = Ring(alloc, "crow", 8, [128, 2560], BF16)
            K.dma("sp", rows[:, :], ssd_cv.rearrange("v (f p) -> (v f) p", p=128), "Lcvrows", writes=["cvrows"])
            K.op("pe", lambda e: e.transpose(psum[0][:, 0:120], rows[:, :], idf[0:120, 0:120]),
                 reads=["cvrows", "idf"], writes=["ps#0"])
            K.cp("dve", cv[:].rearrange("p v f -> p (v f)"), psum[0][:, 0:120], ["ps#0"], ["cv"])
            for j in range(4):
                for f in range(24):
                    K.ts("dve" if (f % 2) else "pool", dg[:, j, f, :], idf[:, :], cv[:, j, f:f + 1], None, ALU.mult, None,
                         ["idf", "cv"], accw=["dg"])
            blocks = [(0, CTX, 0, CTX)] + [(CTX + 512 * g, 512, CTX, T) for g in range(SEQ // 512)]
            cpi = 0
            for (b0, bn, sa, sb_end) in blocks[:P.dbg.get("nconv", 99)]:
                xin = xin_.next()
                l0 = max(sa, b0 - 2); l1 = min(sb_end, b0 + bn + 1)
                K.dma("sp", xin[:, :, l0 - (b0 - 2):l1 - (b0 - 2)],
                      xbcT[:, l0:l1].rearrange("(f p) t -> p f t", p=128), "L" + xin.key, writes=[xin.key])
                nt_ = bn // 128
                rbs = [rb_.next() for _ in range(nt_)]
                for ft in range(24):
                    pk = 4 + (cpi % 2); cpi += 1
                    order = [2, 0, 1, 3]
                    for oi, j in enumerate(order):
                        sh = j - 2
                        lo = max(b0, sa - sh); hi = min(b0 + bn, sb_end - sh)
                        K.mm(psum[pk][:, lo - b0:hi - b0], dg[:, j, ft, :],
                             xin[:, ft, lo + sh - (b0 - 2):hi + sh - (b0 - 2)], oi == 0, oi == 3,
                             ["dg", xin.key], [f"ps#{pk}"], oi == 3)
                    sb = sb_.next()
                    K.act(sb[:, 0:bn], psum[pk][:, 0:bn], AF.Silu, [f"ps#{pk}", "cv"], [sb.key], bias=cv[:, 4, ft:ft + 1])
                    if ft >= 16:
                        K.dma("sp", bcT[(ft - 16) * 128:(ft - 15) * 128, b0:b0 + bn], sb[:, 0:bn], "S" + sb.key,
                              reads=[sb.key], accw=["bcT"])
                    if ft < 20:
                        for i in range(nt_):
                            tb = psum[i][:].bitcast(BF16)
                            K.op("pe", lambda e, tb=tb, sb=sb, i=i, ft=ft: e.transpose(
                                tb[:, (ft % 8) * 128:(ft % 8 + 1) * 128], sb[:, i * 128:(i + 1) * 128], idb[:]),
                                reads=[sb.key, "idb"], writes=[f"ps#{i}"], inc=True)
                        if ft % 8 == 7 or ft == 19:
                            ncol = (ft % 8 + 1) * 128
                            c0 = (ft // 8) * 1024
                            for i in range(nt_):
                                tb = psum[i][:].bitcast(BF16)
                                K.cp("dve" if i % 2 else "act", rbs[i][:, c0:c0 + ncol], tb[:, 0:ncol], [f"ps#{i}"],
                                     accw=[rbs[i].key])
                for i in range(nt_):
                    K.dma("sp", xsB[b0 + i * 128:b0 + (i + 1) * 128, :], rbs[i][:, :], "S" + rbs[i].key,
                          reads=[rbs[i].key], accw=["xsB"])
            K.barrier()

    if "conv" not in P.dbg.get("skip", ()):
        phase_conv()
    if stop_after == "conv":
        K.barrier(); pes.close(); P.es.close(); return P


    def phase_ssd():
        with ExitStack() as es:
            def alloc(name, shape, dt):
                P.uid += 1
                return es.enter_context(nc.sbuf_tensor(f"{name}_{P.uid}", list(shape), dt))
            vb = alloc("vb", [128, 160], F32)
            aneg = alloc("aneg", [128, 64], F32)
            nwbc = alloc("nwbc", [128, 2048], F32)
            mk = alloc("mk", [128, 2, 128], F32)
            self_ = alloc("self", [64, 1024], F32)
            selb = alloc("selb", [64, 32, 128], BF16)
            onesf = alloc("onesf2", [128, 128], F32)
            ones1 = alloc("ones1b", [128, 1], F32)
            state = alloc("state", [128, 2048], F32)
            stbf = alloc("stbf", [128, 2048], BF16)
            xb_ = Ring(alloc, "xb", 4, [128, 2560], BF16)
            bc_ = Ring(alloc, "bc", 4, [128, 8, 128], BF16)
            dr_ = Ring(alloc, "dr", 4, [128, 32], F32)
            sm_ = Ring(alloc, "sm", 4, [128, 8, 32], F32)
            cs4_ = Ring(alloc, "cs4", 3, [128, 64], F32)
            cst_ = Ring(alloc, "cst", 3, [64, 128], F32)
            hl_ = Ring(alloc, "hl", 3, [64, 3, 128], BF16)
            X_ = Ring(alloc, "Xd", 3, [128, 2048], BF16)
            Xc_ = Ring(alloc, "Xc", 3, [128, 2048], BF16)
            xd_ = Ring(alloc, "xdk", 1, [128, 2048], F32)
            cbm_ = Ring(alloc, "cbm", 2, [128, 4, 128], F32)
            E_ = Ring(alloc, "sE", 3, [128, 512], F32)
            MT_ = Ring(alloc, "sMT", 4, [128, 4, 128], BF16)
            to_ = Ring(alloc, "sto", 2, [128, 512], F32)
            ysb_ = Ring(alloc, "ysb", 2, [128, 2048], F32)
            yf_ = Ring(alloc, "yfl", 2, [128, 2048], F32)
            zt_ = Ring(alloc, "ztl", 1, [128, 2048], BF16)
            ynb_ = Ring(alloc, "ynb", 1, [128, 2048], BF16)
            yT_ = Ring(alloc, "yTs", 2, [128, 16, 128], BF16)
            junk = alloc("sjunk", [128, 512], BF16)
            K.op("dve", lambda e: e.memset(onesf[:], 1.0), writes=["onesf2"])
            K.op("dve", lambda e: e.memset(ones1[:], 1.0), writes=["ones1b"])
            K.dma("sp", vb[:, :], ssd_vec[0, :].partition_broadcast(128), "Lvb", writes=["vb"])
            K.dma("sp", nwbc[:, :], ssd_norm.partition_broadcast(128), "Lnwbc", writes=["nwbc"])
            K.dma("sp", mk[:], maskT.rearrange("d s l -> s d l"), "Lmk", writes=["mk"])
            for q4 in range(4):
                K.dma("sp", self_[:, :], selc[:, q4 * 1024:(q4 + 1) * 1024], "Lself", writes=["self"])
                K.cp("dve", selb[:, q4 * 8:(q4 + 1) * 8, :].rearrange("k h l -> k (h l)"), self_[:, :], ["self"], accw=["selb"])
            K.act(aneg[:, :], vb[:, 0:64], AF.Exp, ["vb"], ["aneg"])
            K.ts("dve", aneg[:, :], aneg[:, :], -1.0, None, ALU.mult, None, ["aneg"], ["aneg"])
            lat_chunks = list(range(2, NT))
            ncl = P.dbg.get("nchunk", len(lat_chunks))
            lat_chunks = lat_chunks[:ncl]
            PB = {"yd": 0, "seg": 0, "so": 0}

            def stage_a(d, c):
                tok0 = c * 128
                lat = c >= 2
                xb = xb_.next(); bc = bc_.next(); dr = dr_.next(); sm = sm_.next()
                ctx_ = {"xb": xb, "bc": bc, "sm": sm, "lat": lat, "tok0": tok0}
                K.dma("sp", xb[:, :], xsB[tok0:tok0 + 128, :], "L" + xb.key, writes=[xb.key])
                K.dma("sp", bc[:], bcT[:, tok0:tok0 + 128].rearrange("(f p) t -> p f t", p=128), "L" + bc.key,
                      writes=[bc.key])
                K.dma("sp", dr[:, :], dtraw[tok0:tok0 + 128, d * 32:(d + 1) * 32], "L" + dr.key, writes=[dr.key])
                dt = sm[:, 0, :]; adt = sm[:, 1, :]; cs = sm[:, 2, :]; ecs = sm[:, 3, :]
                etot = sm[:, 4, :]; w2 = sm[:, 5, :]; tmp = sm[:, 6, :]
                k_ = sm.key
                K.tt("dve", tmp, dr[:, :], vb[:, 64 + d * 32:96 + d * 32], ALU.add, [dr.key, "vb"], [k_])
                K.act(tmp, tmp, AF.Exp, [k_], [k_])
                K.act(dt, tmp, AF.Ln, [k_, "ones1b"], [k_], bias=ones1[:, 0:1])
                K.tt("dve", adt, dt, aneg[:, d * 32:(d + 1) * 32], ALU.mult, [k_, "aneg"], [k_])
                K.mm(psum[4][:, 0:32], mk[:, d, :], adt, True, True, ["mk", k_], ["ps#4"], True)
                K.mm(psum[4][:, 32:64], onesf[:, :], adt, True, True, ["onesf2", k_], ["ps#4"], True)
                K.cp("dve", cs, psum[4][:, 0:32], ["ps#4"], [k_])
                K.act(etot, psum[4][:, 32:64], AF.Exp, ["ps#4"], [k_])
                K.tt("dve", tmp, psum[4][:, 32:64], cs, ALU.subtract, ["ps#4", k_], [k_])
                K.act(w2, tmp, AF.Exp, [k_], [k_])
                K.tt("dve", w2, w2, dt, ALU.mult, [k_], [k_])
                xs3 = xb[:, 0:2048].rearrange("p (h q) -> p h q", q=64)
                Xc = Xc_.next()
                ctx_["Xc"] = Xc
                K.tt("pool", Xc[:, :].rearrange("p (h q) -> p h q", q=64), xs3,
                     w2.unsqueeze(2).to_broadcast([128, 32, 64]), ALU.mult, [xb.key, k_], [Xc.key])
                if not lat:
                    return ctx_
                K.act(ecs, cs, AF.Exp, [k_], [k_])
                X = X_.next()
                K.tt("pool", X[:, :].rearrange("p (h q) -> p h q", q=64), xs3,
                     dt.unsqueeze(2).to_broadcast([128, 32, 64]), ALU.mult, [xb.key, k_], [X.key])
                cs4 = cs4_.next(); cst = cst_.next(); hl = hl_.next()
                K.cp("dve", cs4[:, 0:32], cs, [k_], accw=[cs4.key])
                K.cp("dve", cs4[:, 32:64], cs, [k_], accw=[cs4.key])
                K.op("pe", lambda e, cs4=cs4: e.transpose(psum[4][0:64, 128:256], cs4[:, :], idf[:, :]),
                     reads=[cs4.key, "idf"], writes=["ps#4"])
                K.cp("dve", cst[:, :], psum[4][0:64, 128:256], ["ps#4"], [cst.key])
                K.cp("dve", hl[0:32, 0, :], cst[0:32, :], [cst.key], accw=[hl.key])
                K.cp("dve", hl[32:64, 2, :], cst[32:64, :], [cst.key], accw=[hl.key])
                K.tt("dve", hl[32:64, 0, :], cst[32:64, :], hl[32:64, 2, :], ALU.subtract, [cst.key, hl.key], accw=[hl.key])
                K.ts("dve", hl[:, 1, :], hl[:, 0, :], -1.0, None, ALU.mult, None, [hl.key], accw=[hl.key])
                ctx_["X"] = X; ctx_["hl"] = hl
                return ctx_

            def stage_a1(d, cx):
                if not cx["lat"]:
                    return
                xb, bc, sm, tok0, X, hl = cx["xb"], cx["bc"], cx["sm"], cx["tok0"], cx["X"], cx["hl"]
                ctx_ = cx
                for g in range(4):
                    K.mm(psum[5][:, g * 128:(g + 1) * 128], bc[:, g, :], bc[:, 4 + g, :], True, True,
                         [bc.key], ["ps#5"], g == 3)
                cbm = cbm_.next()
                K.tt("dve", cbm[:], psum[5][:, :].rearrange("p (g l) -> p g l", l=128),
                     mk[:, d:d + 1, :].to_broadcast([128, 4, 128]), ALU.mult, ["ps#5", "mk"], [cbm.key])
                ysb = ysb_.next()
                ctx_["ysb"] = ysb
                if d == 1:
                    yf = yf_.next()
                    K.dma("sp", yf[:, :], yfw[tok0:tok0 + 128, :], "L" + yf.key, reads=["yfw"], writes=[yf.key])
                segbank = {}

                def emit_seg(hq):
                    pk = 2 + PB["seg"] % 2; PB["seg"] += 1
                    segbank[hq] = pk
                    for j in range(4):
                        h = hq * 4 + j
                        K.mm(psum[pk][:, j * 128:(j + 1) * 128], selb[:, h, :], hl[:, 0, :], True, False,
                             ["selb", hl.key], [f"ps#{pk}"], False)
                        K.mm(psum[pk][:, j * 128:(j + 1) * 128], hl[:, 1, :], selb[:, h, :], False, True,
                             ["selb", hl.key], [f"ps#{pk}"], j == 3)

                pyd = 0
                emit_seg(0)
                for hq in range(8):
                    g = hq // 2
                    if hq + 1 < 8:
                        emit_seg(hq + 1)
                    pk = segbank[hq]
                    if hq % 2 == 0:
                        pyd = PB["yd"] % 2; PB["yd"] += 1
                    E = E_.next(); MT = MT_.next()
                    K.act(E[:, :], psum[pk][:, :], AF.Exp, [f"ps#{pk}"], [E.key])
                    K.stt("dve", MT[:], E[:, :].rearrange("p (j l) -> p j l", l=128), 1e30,
                          cbm[:, g:g + 1, :].to_broadcast([128, 4, 128]), ALU.min, ALU.mult,
                          [E.key, cbm.key], [MT.key])
                    for j in range(4):
                        h = hq * 4 + j
                        K.mm(psum[pyd][:, (h % 8) * 64:(h % 8 + 1) * 64], MT[:, j, :], X[:, h * 64:(h + 1) * 64],
                             True, True, [MT.key, X.key], [f"ps#{pyd}"], (h % 8 == 7))
                    if hq % 2 == 1:
                        if d == 0:
                            K.cp("act", ysb[:, g * 512:(g + 1) * 512], psum[pyd][:, :], [f"ps#{pyd}"], accw=[ysb.key])
                        else:
                            K.tt("dve", ysb[:, g * 512:(g + 1) * 512], psum[pyd][:, :], yf[:, g * 512:(g + 1) * 512],
                                 ALU.add, [f"ps#{pyd}", yf.key], accw=[ysb.key])
                return ctx_

            def stage_b(d, cx):
                xb, bc, sm, lat, tok0, Xc = cx["xb"], cx["bc"], cx["sm"], cx["lat"], cx["tok0"], cx["Xc"]
                k_ = sm.key
                ecs = sm[:, 3, :]; etot = sm[:, 4, :]
                xs3 = xb[:, 0:2048].rearrange("p (h q) -> p h q", q=64)
                if lat:
                    ysb = cx["ysb"]
                    for g in range(4):
                        pk = 6 + PB["so"] % 2; PB["so"] += 1
                        K.mm(psum[pk][:, :], bc[:, 4 + g, :], stbf[:, g * 512:(g + 1) * 512], True, True,
                             [bc.key, "stbf"], [f"ps#{pk}"], True)
                        to = to_.next()
                        K.tt("dve", to[:, :].rearrange("p (h q) -> p h q", q=64),
                             psum[pk][:, :].rearrange("p (h q) -> p h q", q=64),
                             ecs[:, g * 8:(g + 1) * 8].unsqueeze(2).to_broadcast([128, 8, 64]), ALU.mult,
                             [f"ps#{pk}", k_], [to.key])
                        K.tt("dve", ysb[:, g * 512:(g + 1) * 512], ysb[:, g * 512:(g + 1) * 512], to[:, :], ALU.add,
                             [ysb.key, to.key], [ysb.key])
                for g in range(4):
                    pk = 6 + PB["so"] % 2; PB["so"] += 1
                    K.mm(psum[pk][:, :], xb[:, 2048 + g * 128:2048 + (g + 1) * 128], Xc[:, g * 512:(g + 1) * 512],
                         True, True, [xb.key, Xc.key], [f"ps#{pk}"], True)
                    sg = state[:, g * 512:(g + 1) * 512]
                    K.tt("pool", sg.rearrange("p (h q) -> p h q", q=64), sg.rearrange("p (h q) -> p h q", q=64),
                         etot[:, g * 8:(g + 1) * 8].unsqueeze(2).to_broadcast([128, 8, 64]), ALU.mult,
                         ["state", k_], ["state"])
                    K.tt("dve", sg, sg, psum[pk][:, :], ALU.add, ["state", f"ps#{pk}"], ["state"])
                K.cp("act", stbf[:, :], state[:, :], ["state"], ["stbf"])
                if not lat:
                    return
                if d == 0:
                    K.dma("sp", yfw[tok0:tok0 + 128, :], ysb[:, :], "S" + ysb.key, reads=[ysb.key], accw=["yfw"])
                    return
                zt = zt_.next(); ynb = ynb_.next(); yT = yT_.next(); xd = xd_.next()
                K.dma("sp", zt[:, :], zs[tok0:tok0 + 128, :], "L" + zt.key, writes=[zt.key])
                K.tt("pool", xd[:, :].rearrange("p (h q) -> p h q", q=64), xs3,
                     vb[:, 128:160].unsqueeze(2).to_broadcast([128, 32, 64]), ALU.mult, [xb.key, "vb"], [xd.key])
                K.tt("dve", ysb[:, :], ysb[:, :], xd[:, :], ALU.add, [ysb.key, xd.key], [ysb.key])
                K.tt("dve", ysb[:, :], ysb[:, :], zt[:, :], ALU.mult, [ysb.key, zt.key], [ysb.key])
                for g in range(4):
                    K.act(junk[:, :], ysb[:, g * 512:(g + 1) * 512], AF.Square, [ysb.key], ["sjunk"], accw=[k_],
                          accum_out=sm[:, 7, g:g + 1])
                K.act(sm[:, 7, 4:8], sm[:, 7, 0:4], AF.Sqrt, [k_, "epsb"], [k_], scale=1.0 / 512, bias=epsb[:, 0:1])
                K.op("dve", lambda e, sm=sm: e.reciprocal(sm[:, 7, 8:12], sm[:, 7, 4:8]), [k_], [k_])
                for g in range(4):
                    K.stt("dve", ynb[:, g * 512:(g + 1) * 512], ysb[:, g * 512:(g + 1) * 512], sm[:, 7, 8 + g:9 + g],
                          nwbc[:, g * 512:(g + 1) * 512], ALU.mult, ALU.mult, [ysb.key, k_, "nwbc"], accw=[ynb.key])
                for half in range(2):
                    pk = half
                    tb = psum[pk][:].bitcast(BF16)
                    for jj in range(8):
                        j = half * 8 + jj
                        K.op("pe", lambda e, tb=tb, jj=jj, j=j, ynb=ynb: e.transpose(
                            tb[:, jj * 128:(jj + 1) * 128], ynb[:, j * 128:(j + 1) * 128], idb[:]),
                            reads=[ynb.key, "idb"], writes=[f"ps#{pk}"], inc=(jj == 7))
                    K.cp("act", yT[:, half * 8:(half + 1) * 8, :].rearrange("p j t -> p (j t)"), tb[:, :], [f"ps#{pk}"],
                         accw=[yT.key])
                K.dma("sp", mixT1[:, tok0:tok0 + 128].rearrange("(j p) t -> p j t", p=128), yT[:], "S" + yT.key,
                      reads=[yT.key], accw=["mixT1"])

            for d in range(2):
                order = [0, 1] + lat_chunks if d == 0 else [1, 0] + lat_chunks[::-1]
                K.op("pool", lambda e: e.memset(state[:], 0.0), writes=["state"])
                K.op("pool", lambda e: e.memset(stbf[:], 0.0), writes=["stbf"])
                n_ = len(order)
                cxs = {0: stage_a(d, order[0])}
                if n_ > 1:
                    cxs[1] = stage_a(d, order[1])
                stage_a1(d, cxs[0])
                for i in range(n_):
                    if i + 2 < n_:
                        cxs[i + 2] = stage_a(d, order[i + 2])
                    if i + 1 < n_:
                        stage_a1(d, cxs[i + 1])
                    stage_b(d, cxs.pop(i))
            K.barrier()

    if "ssd" not in P.dbg.get("skip", ()):
        phase_ssd()
    if stop_after == "ssd":
        K.barrier(); pes.close(); P.es.close(); return P

    nt1 = P.dbg.get("out1_tiles", SEQ // 128)
    tiles1 = [(CTX + i * 128, i * 128, 0) for i in range(nt1)]
    phase_out(1, mixT1, 16, o_w_out, h1, out_h, tiles1)

    K.barrier()
    pes.close()
    P.es.close()
    return P


def _rope_tables():
    n_freq = 16
    inv = (10000.0 ** (-np.arange(n_freq, dtype=np.float32) / np.float32(n_freq))).astype(np.float32)
    t = np.arange(SEQ)
    row = (t // 64).astype(np.float32)
    col = (t % 64).astype(np.float32)
    ang = np.concatenate([row[:, None] * inv, col[:, None] * inv], axis=-1).astype(np.float32)
    cos, sin = np.cos(ang).astype(np.float32), np.sin(ang).astype(np.float32)
    tab = np.zeros((2, 128, T), np.float32)
    tab[0, :, :CTX] = 1.0
    for p in range(128):
        d = p % 64
        fi = (d % 16) + 16 * (d // 32)
        sgn = -1.0 if (d % 32) < 16 else 1.0
        tab[0, p, CTX:] = cos[:, fi]
        tab[1, p, CTX:] = sgn * sin[:, fi]
    return tab


def _rope_perm():
    perm = np.zeros(512, np.int64)
    for f in range(512):
        d = f % 64
        e = d % 32
        e2 = e + 16 if e < 16 else e - 16
        perm[f] = f - d + (d // 32) * 32 + e2
    return perm


def make_in_maps(inp):
    B = inp["x"].shape[0]
    perm = _rope_perm()
    w = np.asarray(inp["e_w_in"][0], np.float32)
    w_aug = np.ascontiguousarray(np.concatenate([w, w[:, 1024 + perm], w[:, 1536 + perm]], axis=1))
    tab = _rope_tables()
    ident = np.eye(128, dtype=np.float32)
    wbd = np.zeros((2, 2, 4, 128, 128), np.float32)
    for d in range(2):
        for g, nm in enumerate(("lru_w_r", "lru_w_i")):
            wsrc = np.asarray(inp[nm][0][d], np.float32)
            for cc in range(4):
                wbd[d, g, cc, 0:64, 0:64] = wsrc[2 * cc]
                wbd[d, g, cc, 64:128, 64:128] = wsrc[2 * cc + 1]
    wbd = np.ascontiguousarray(wbd.reshape(16, 128, 128))
    lvec = np.ascontiguousarray(np.concatenate([
        np.asarray(inp["lru_conv_w"][0], np.float32), np.asarray(inp["lru_conv_b"], np.float32).reshape(1, 512),
        np.asarray(inp["lru_b_r"][0], np.float32), np.asarray(inp["lru_b_i"][0], np.float32),
        np.asarray(inp["lru_lambda"][0], np.float32)], axis=0))
    ssd_cv = np.ascontiguousarray(np.concatenate([np.asarray(inp["ssd_conv_w"][0], np.float32),
                                                  np.asarray(inp["ssd_conv_b"], np.float32).reshape(1, 3072)], 0))
    ssd_vec = np.ascontiguousarray(np.concatenate([np.asarray(inp["ssd_a_log"][0], np.float32).reshape(-1),
                                                   np.asarray(inp["ssd_dt_bias"][0], np.float32).reshape(-1),
                                                   np.asarray(inp["ssd_d"][0], np.float32).reshape(-1)]).reshape(1, 160))
    ii = np.arange(128)
    maskT = np.stack([(ii[None, :] >= ii[:, None]), (ii[None, :] <= ii[:, None])], 0).astype(np.float32)
    selc = np.zeros((64, 32, 128), np.float32)
    for hh in range(32):
        selc[hh, hh, :] = 1.0
        selc[32 + hh, hh, :] = 1.0
    selc = np.ascontiguousarray(selc.reshape(64, 32 * 128))
    maps = []
    for b in range(B):
        m = {
            "src0": np.ascontiguousarray(np.concatenate([inp["ctx"][b], inp["x"][b]], axis=0), dtype=np.float32),
            "cvec": np.ascontiguousarray(np.stack([inp["c"][b], inp["c_ctx"]], 0), dtype=np.float32),
            "w_mod": np.asarray(inp["w_mod"], np.float32),
            "b_mod": np.asarray(inp["b_mod"], np.float32),
            "g_pre": np.asarray(inp["g_pre"], np.float32),
            "g_post": np.asarray(inp["g_post"], np.float32),
            "e_w_in_aug": w_aug,
            "e_w_out": np.asarray(inp["e_w_out"][0], np.float32),
            "ident": ident,
            "ropetab": tab,
            "lru_wbd": wbd,
            "lru_vec": lvec,
            "da_lam": np.ascontiguousarray(np.asarray(inp["da_lambda"][0], np.float32).reshape(1, 256)),
            "da_sub": np.ascontiguousarray(np.asarray(inp["da_subln"][0], np.float32).reshape(1, 128)),
            "o_w_in": np.asarray(inp["o_w_in"][0], np.float32),
            "o_w_out": np.asarray(inp["o_w_out"][0], np.float32),
            "ssd_cv": ssd_cv,
            "ssd_vec": ssd_vec,
            "ssd_norm": np.asarray(inp["ssd_norm"][0], np.float32),
            "maskT": maskT,
            "selc": selc,
        }
        maps.append(m)
    return maps


def kernel(**inp):
    P = build_program()
    maps = make_in_maps(inp)
    res = run_bass_kernel_spmd(P.nc, maps, core_ids=list(range(8)))
    return np.stack([np.asarray(r["out"], np.float32) for r in res.results], 0)
```

```python
import os
from contextlib import ExitStack
import numpy as np
import concourse.bass as bass
import concourse.mybir as mybir
from concourse.bass_utils import run_bass_kernel_spmd

F32, BF16 = mybir.dt.float32, mybir.dt.bfloat16
AF = mybir.ActivationFunctionType
ALU = mybir.AluOpType
AX = mybir.AxisListType

D = 1024
SEQ = 4096
CTX = 256
T = SEQ + CTX
NT = T // 128
EPS = 1e-6


class Sched:
    ENG = ("pe", "dve", "act", "pool", "sp")

    def __init__(self, nc, es):
        self.nc, self.es = nc, es
        self.e = {"pe": nc.tensor, "dve": nc.vector, "act": nc.scalar, "pool": nc.gpsimd, "sp": nc.sync}
        self.sem, self.cnt = {}, {}
        for n in self.ENG:
            self.sem[n] = es.enter_context(nc.semaphore("s_" + n))
            self.cnt[n] = 0
        self.seen = {n: {} for n in self.ENG}
        self.W, self.Rd = {}, {}
        self.pend = {n: [] for n in self.ENG}
        self.nwait = 0
        self.nins = 0

    def _wait(self, eng, need):
        for s, v in need.items():
            if s == "pe" and eng == "pe":
                continue
            if self.seen[eng].get(s, 0) >= v:
                continue
            self.e[eng].wait_ge(self.sem[s], v)
            self.seen[eng][s] = v
            self.nwait += 1

    def _deps(self, eng, reads, writes, accw):
        need = {}

        def add(d):
            for s, v in d.items():
                if need.get(s, 0) < v:
                    need[s] = v
        for r in reads:
            add(self.W.get(r, {}))
        for w in writes:
            add(self.W.get(w, {}))
            add(self.Rd.get(w, {}))
        for w in accw:
            add(self.Rd.get(w, {}))
        self._wait(eng, need)

    def _register(self, ev, reads, writes, accw):
        s, v = ev
        for r in reads:
            d = self.Rd.setdefault(r, {})
            d[s] = max(d.get(s, 0), v)
        for w in writes:
            self.W[w] = {s: v}
            self.Rd[w] = {}
        for w in accw:
            d = self.W.setdefault(w, {})
            d[s] = max(d.get(s, 0), v)

    def op(self, eng, fn, reads=(), writes=(), accw=(), inc=True):
        self._deps(eng, reads, writes, accw)
        ins = fn(self.e[eng])
        self.nins += 1
        if inc:
            self.cnt[eng] += 1
            ins.then_inc(self.sem[eng], 1)
            ev = (eng, self.cnt[eng])
            for (r, w, a) in self.pend[eng]:
                self._register(ev, r, w, a)
            self.pend[eng] = []
            self._register(ev, reads, writes, accw)
        else:
            self.pend[eng].append((tuple(reads), tuple(writes), tuple(accw)))

    def dma(self, q, out, in_, semkey, reads=(), writes=(), accw=(), **kw):
        if semkey not in self.sem:
            self.sem[semkey] = self.es.enter_context(self.nc.semaphore("d_" + semkey.replace("#", "_")))
            self.cnt[semkey] = 0
        self._deps(q, reads, writes, accw)
        ins = self.e[q].dma_start(out=out, in_=in_, **kw)
        ins.then_inc(self.sem[semkey], 16)
        self.cnt[semkey] += 16
        self.nins += 1
        self._register((semkey, self.cnt[semkey]), reads, writes, accw)

    def tt(self, eng, out, a, b, op, reads, writes=(), accw=()):
        self.op(eng, lambda e: e.tensor_tensor(out, a, b, op), reads, writes, accw)

    def ts(self, eng, out, a, s1, s2, op0, op1=None, reads=(), writes=(), accw=()):
        if op1 is None:
            self.op(eng, lambda e: e.tensor_scalar(out, a, s1, None, op0), reads, writes, accw)
        else:
            self.op(eng, lambda e: e.tensor_scalar(out, a, s1, s2, op0, op1), reads, writes, accw)

    def stt(self, eng, out, a, sc, b, op0, op1, reads, writes=(), accw=()):
        self.op(eng, lambda e: e.scalar_tensor_tensor(out, a, sc, b, op0, op1), reads, writes, accw)

    def act(self, out, in_, func, reads, writes=(), accw=(), **kw):
        self.op("act", lambda e: e.activation(out=out, in_=in_, func=func, **kw), reads, writes, accw)

    def cp(self, eng, out, in_, reads, writes=(), accw=()):
        if eng == "act":
            self.op("act", lambda e: e.copy(out, in_), reads, writes, accw)
        else:
            self.op(eng, lambda e: e.tensor_copy(out, in_), reads, writes, accw)

    def mm(self, out, lhsT, rhs, start, stop, reads, writes, inc):
        self.op("pe", lambda e: e.matmul(out, lhsT, rhs, start=start, stop=stop), reads, writes, inc=inc)

    def barrier(self):
        for n in self.ENG:
            assert not self.pend[n]
        allev = {s: c for s, c in self.cnt.items() if c > 0}
        for n in self.ENG:
            self._wait(n, allev)
        self.W, self.Rd = {}, {}


class Buf:
    def __init__(self, t, key):
        self.t, self.key = t, key

    def __getitem__(self, k):
        return self.t[k]


class Ring:
    def __init__(self, alloc, name, n, shape, dtype):
        self.bufs = [Buf(alloc(f"{name}{i}", shape, dtype), f"{name}#{i}") for i in range(n)]
        self.i = 0

    def next(self):
        b = self.bufs[self.i % len(self.bufs)]
        self.i += 1
        return b


class Prog:
    def __init__(self, dbg=None):
        self.dbg = dbg or {}
        self.nc = nc = bass.Bass("TRN2", target_bir_lowering=False)
        self.es = ExitStack()
        self.K = Sched(nc, self.es)
        self.dram = {}
        self.uid = 0

    def din(self, name, shape, dt=F32):
        self.dram[name] = self.nc.dram_tensor(name, list(shape), dt, kind="ExternalInput").ap()
        return self.dram[name]

    def dout(self, name, shape, dt=F32):
        self.dram[name] = self.nc.dram_tensor(name, list(shape), dt, kind="ExternalOutput").ap()
        return self.dram[name]

    def dscr(self, name, shape, dt):
        if name in self.dbg.get("dump", ()):
            return self.dout(name, shape, dt)
        self.dram[name] = self.nc.dram_tensor(name, list(shape), dt).ap()
        return self.dram[name]


def build_program(dbg=None):
    P = Prog(dbg)
    nc, K = P.nc, P.K
    stop_after = P.dbg.get("stop_after", "all")

    src0 = P.din("src0", [T, D])
    cvec = P.din("cvec", [2, D])
    w_mod = P.din("w_mod", [2, D, 3 * D])
    b_mod = P.din("b_mod", [2, 3 * D])
    g_pre = P.din("g_pre", [2, D])
    g_post = P.din("g_post", [2, D])
    e_w_in = P.din("e_w_in_aug", [D, 4096])
    e_w_out = P.din("e_w_out", [D, D])
    ident = P.din("ident", [128, 128])
    ropetab = P.din("ropetab", [2, 128, T])
    out_h = P.dout("out", [SEQ, D])
    lru_wbd = P.din("lru_wbd", [16, 128, 128])
    lru_vec = P.din("lru_vec", [11, 512])
    da_lam = P.din("da_lam", [1, 256])
    da_sub = P.din("da_sub", [1, 128])
    mixT0 = P.dscr("mixT0", [D, T], BF16)
    o_w_in = P.din("o_w_in", [D, 5184])
    o_w_out = P.din("o_w_out", [2048, D])
    ssd_cv = P.din("ssd_cv", [5, 3072])
    ssd_vec = P.din("ssd_vec", [1, 160])
    ssd_norm = P.din("ssd_norm", [2048])
    maskT = P.din("maskT", [2, 128, 128])
    selc = P.din("selc", [64, 32 * 128])
    xbcT = P.dscr("xbcT", [3072, T], BF16)
    zs = P.dscr("zs", [T, 2048], BF16)
    dtraw = P.dscr("dtraw", [T, 64], F32)
    xsB = P.dscr("xsB", [T, 2560], BF16)
    bcT = P.dscr("bcT", [1024, T], BF16)
    yfw = P.dscr("yfw", [T, 2048], F32)
    mixT1 = P.dscr("mixT1", [2048, T], BF16)
    h1 = P.dscr("h1", [T, D], F32)

    xrT = P.dscr("xrT", [512, T], F32)
    grT = P.dscr("grT", [512, T], BF16)
    gdT = P.dscr("gdT", [512, T], BF16)
    qT = P.dscr("qT", [512, T], BF16)
    kT = P.dscr("kT", [512, T], BF16)
    vtok = P.dscr("vtok", [T, 512], BF16)

    pes = ExitStack()
    def palloc(name, shape, dt):
        return pes.enter_context(nc.sbuf_tensor(name, list(shape), dt))
    psum = [pes.enter_context(nc.psum_tensor(f"psb{i}", [128, 512], F32)) for i in range(8)]
    idf = palloc("idf", [128, 128], F32)
    idb = palloc("idb", [128, 128], BF16)
    epsb = palloc("epsb", [128, 1], F32)
    modA = palloc("modA", [128, 2, 8, 2], F32)
    modS = palloc("modS", [128, 2, 8, 2], F32)
    ggbc = palloc("ggbc", [128, 2, 2, D], F32)

    K.dma("sp", idf[:], ident[:, :], "Lidf", writes=["idf"])
    K.op("dve", lambda e: e.tensor_copy(idb[:], idf[:]), reads=["idf"], writes=["idb"])
    K.op("dve", lambda e: e.memset(epsb[:], EPS), writes=["epsb"])

    def phase_mod():
        with ExitStack() as es:
            def alloc(name, shape, dt):
                P.uid += 1
                return es.enter_context(nc.sbuf_tensor(f"{name}_{P.uid}", list(shape), dt))
            cT = alloc("cT", [128, 2, 8], F32)
            sig = alloc("sig", [128, 2, 8], F32)
            srep = alloc("srep", [128, 2, 8, 128], F32)
            bT = alloc("bT", [128, 2, 24], F32)
            gpT = alloc("gpT", [128, 2, 8], F32)
            bgbc = alloc("bgbc", [128, 2, D], F32)
            gpbc = alloc("gpbc", [128, 2, D], F32)
            wst = Ring(alloc, "wst", 2, [128, 8, 512], F32)
            tmp = alloc("mtmp", [128, 16, 2], F32)

            rows = alloc("rows", [80, 128], F32)
            K.dma("sp", rows[0:16, :], cvec.rearrange("t (j p) -> (t j) p", p=128), "Lrows", accw=["rows"])
            K.dma("sp", rows[16:64, :], b_mod.rearrange("l (f p) -> (l f) p", p=128), "Lrows", accw=["rows"])
            K.dma("sp", rows[64:80, :], g_pre.rearrange("l (j p) -> (l j) p", p=128), "Lrows", accw=["rows"])
            K.op("pe", lambda e: e.transpose(psum[4][:, 0:80], rows[:, :], idf[0:80, 0:80]),
                 reads=["rows", "idf"], writes=["ps#4"])
            K.op("dve", lambda e: e.tensor_copy(cT[:].rearrange("p t j -> p (t j)"), psum[4][:, 0:16]),
                 reads=["ps#4"], writes=["cT"])
            K.op("dve", lambda e: e.tensor_copy(bT[:].rearrange("p l f -> p (l f)"), psum[4][:, 16:64]),
                 reads=["ps#4"], writes=["bT"])
            K.op("dve", lambda e: e.tensor_copy(gpT[:].rearrange("p l j -> p (l j)"), psum[4][:, 64:80]),
                 reads=["ps#4"], writes=["gpT"])
            for l in range(2):
                K.dma("sp", bgbc[:, l, :], b_mod[l, 2 * D:3 * D].partition_broadcast(128), "Lbgbc", accw=["bgbc"])
                K.dma("sp", gpbc[:, l, :], g_post[l, :].partition_broadcast(128), "Lgpbc", accw=["gpbc"])
            K.op("act", lambda e: e.activation(out=sig[:], in_=cT[:], func=AF.Sigmoid), reads=["cT"], writes=["sig"])
            K.op("dve", lambda e: e.tensor_tensor(cT[:], cT[:], sig[:], ALU.mult), reads=["sig", "cT"], writes=["cT"])
            for t in range(2):
                K.op("dve", lambda e, t=t: e.tensor_copy(
                    srep[:, t, :, :], cT[:, t, :].unsqueeze(2).to_broadcast([128, 8, 128])),
                    reads=["cT"], accw=["srep"])
            for l in range(2):
                for pc in range(6):
                    wb = wst.next()
                    K.dma("sp", wb[:], w_mod[l, :, pc * 512:(pc + 1) * 512].rearrange("(j p) n -> p j n", p=128),
                          "L" + wb.key, writes=[wb.key])
                    if pc < 4:
                        pst = psum[pc % 2]
                        for f in range(4):
                            for j in range(8):
                                K.op("pe", lambda e, f=f, j=j, pst=pst, wb=wb: e.matmul(
                                    pst[:, 2 * f:2 * f + 2], wb[:, j, f * 128:(f + 1) * 128], cT[:, :, j],
                                    start=(j == 0), stop=(j == 7)),
                                    reads=[wb.key, "cT"], writes=[f"ps#{pc % 2}"], inc=(j == 7 and f == 3))
                        K.op("dve", lambda e, pst=pst, pc=pc: e.tensor_copy(
                            tmp[:, pc * 4:(pc + 1) * 4, :], pst[:, 0:8].rearrange("p (f t) -> p f t", t=2)),
                            reads=[f"ps#{pc % 2}"], accw=["mtmp"])
                    else:
                        for t in range(2):
                            pst = psum[2 + t]
                            for j in range(8):
                                K.op("pe", lambda e, j=j, t=t, pst=pst, wb=wb: e.matmul(
                                    pst[:, :], srep[:, t, j, :], wb[:, j, :], start=(j == 0), stop=(j == 7)),
                                    reads=[wb.key, "srep"], writes=[f"ps#{2 + t}"], inc=(j == 7))
                            c0 = (pc - 4) * 512
                            K.op("dve", lambda e, t=t, l=l, c0=c0, pst=pst: e.tensor_tensor(
                                ggbc[:, l, t, c0:c0 + 512], pst[:, :], bgbc[:, l, c0:c0 + 512], ALU.add),
                                reads=[f"ps#{2 + t}", "bgbc"], accw=["ggbc"])
                for t in range(2):
                    K.op("dve", lambda e, t=t, l=l: e.tensor_tensor(
                        modS[:, l, :, t], tmp[:, 0:8, t], bT[:, l, 0:8], ALU.add),
                        reads=["mtmp", "bT"], accw=["modS"])
                    K.op("dve", lambda e, t=t, l=l: e.scalar_tensor_tensor(
                        modA[:, l, :, t], tmp[:, 8:16, t], 1.0, bT[:, l, 8:16], ALU.add, ALU.add),
                        reads=["mtmp", "bT"], accw=["modA"])
                    K.op("dve", lambda e, t=t, l=l: e.tensor_tensor(
                        modA[:, l, :, t], modA[:, l, :, t], gpT[:, l, :], ALU.mult),
                        reads=["modA", "gpT"], writes=["modA"])
                    K.op("dve", lambda e, t=t, l=l: e.tensor_tensor(
                        ggbc[:, l, t, :], ggbc[:, l, t, :], gpbc[:, l, :], ALU.mult),
                        reads=["ggbc", "gpbc"], writes=["ggbc"])
            K.barrier()

    phase_mod()
    if "mod" in P.dbg.get("dump", ()):
        dA = P.dout("dbg_modA", [128, 32]); dS = P.dout("dbg_modS", [128, 32]); dG = P.dout("dbg_gg", [128, 4 * D])
        K.dma("sp", dA[:, :], modA[:].rearrange("p l j t -> p (l j t)"), "Sdbg", reads=["modA"])
        K.dma("sp", dS[:, :], modS[:].rearrange("p l j t -> p (l j t)"), "Sdbg", reads=["modS"])
        K.dma("sp", dG[:, :], ggbc[:].rearrange("p l t f -> p (l t f)"), "Sdbg", reads=["ggbc"])
    if stop_after == "mod":
        K.barrier(); pes.close(); P.es.close(); return P

    def phase_proj(layer, src, Wd, ncols, fspecs, tspecs, extra_alloc=None, per_group=None):
        with ExitStack() as es:
            def alloc(name, shape, dt):
                P.uid += 1
                return es.enter_context(nc.sbuf_tensor(f"{name}_{P.uid}", list(shape), dt))
            Wb = alloc("Wb", [128, 8, ncols], BF16)
            wst = Ring(alloc, "wst", 2, [128, 8, 256], F32)
            xr_ = Ring(alloc, "xin", 4, [128, D], F32)
            xn_ = Ring(alloc, "xn", 4, [128, D], BF16)
            uT_ = Ring(alloc, "uT", 2, [128, 8, 512], BF16)
            junk = alloc("junk", [128, D], BF16)
            stat = Ring(alloc, "stat", 4, [128, 4], F32)
            ctxo = extra_alloc(alloc) if extra_alloc else None
            ceng = ["dve", "pool", "act"]
            for pc in range(ncols // 256 + (1 if ncols % 256 else 0)):
                c0 = pc * 256
                cw = min(256, ncols - c0)
                wb = wst.next()
                K.dma("sp", wb[:, :, 0:cw], Wd[:, c0:c0 + cw].rearrange("(j p) n -> p j n", p=128),
                      "L" + wb.key, writes=[wb.key])
                en = ceng[pc % 3]
                if en == "act":
                    K.op("act", lambda e, wb=wb, c0=c0, cw=cw: e.copy(Wb[:, :, c0:c0 + cw], wb[:, :, 0:cw]),
                         reads=[wb.key], accw=["Wb"])
                else:
                    K.op(en, lambda e, wb=wb, c0=c0, cw=cw: e.tensor_copy(Wb[:, :, c0:c0 + cw], wb[:, :, 0:cw]),
                         reads=[wb.key], accw=["Wb"])
            groups = [(0, CTX, 1)] + [(CTX + 512 * g, 512, 0) for g in range(SEQ // 512)]
            pst_i = [0]
            pso_i = [0]

            def front_parts(gi):
                tok0, ntok, tmod = groups[gi]
                uT = uT_.next()
                p1s, p2s = [], []
                for ti in range(ntok // 128):
                    def part1(ti=ti):
                        box = {}
                        xt = xr_.next(); xn = xn_.next(); st = stat.next()
                        box["xn"] = xn
                        K.dma("sp", xt[:], src[tok0 + ti * 128: tok0 + (ti + 1) * 128, :], "L" + xt.key, writes=[xt.key])
                        K.op("act", lambda e: e.activation(out=junk[:], in_=xt[:], func=AF.Square, accum_out=st[:, 0:1]),
                             reads=[xt.key], writes=["junk", st.key])
                        K.op("act", lambda e: e.activation(out=st[:, 1:2], in_=st[:, 0:1], func=AF.Sqrt,
                                                           scale=1.0 / D, bias=epsb[:, 0:1]),
                             reads=[st.key, "epsb"], writes=[st.key])
                        K.op("dve", lambda e: e.reciprocal(st[:, 2:3], st[:, 1:2]), reads=[st.key], writes=[st.key])
                        K.op("pool", lambda e: e.tensor_scalar(xn[:], xt[:], st[:, 2:3], None, ALU.mult),
                             reads=[xt.key, st.key], writes=[xn.key])
                        return box

                    def part2(box, ti=ti):
                        xn = box["xn"]
                        pk = 6 + (pst_i[0] % 2); pst_i[0] += 1
                        pst = psum[pk][:].bitcast(BF16)
                        for j in range(8):
                            K.op("pe", lambda e, j=j: e.transpose(
                                pst[:, j * 128:(j + 1) * 128], xn[:, j * 128:(j + 1) * 128], idb[:]),
                                reads=[xn.key, "idb"], writes=[f"ps#{pk}"], inc=(j == 7))
                        for j in range(8):
                            K.op("dve", lambda e, j=j: e.tensor_scalar(
                                uT[:, j, ti * 128:(ti + 1) * 128], pst[:, j * 128:(j + 1) * 128],
                                modA[:, layer, j, tmod:tmod + 1], modS[:, layer, j, tmod:tmod + 1], ALU.mult, ALU.add),
                                reads=[f"ps#{pk}", "modA", "modS"], accw=[uT.key])
                    p1s.append(part1); p2s.append(part2)
                return uT, p1s, p2s

            def mm(gi, uT, hooks):
                tok0, ntok, tmod = groups[gi]
                if per_group:
                    per_group(ctxo, gi, tok0, ntok)
                nb_tot = sum(len(b) for (_, b, _) in fspecs)
                nhk = max(1, len(hooks))
                step = max(1, nb_tot // nhk)
                bcount = 0
                hooks = list(hooks)
                for (name, bundles, epi) in fspecs:
                    for bi, cols in enumerate(bundles):
                        if hooks and bcount % step == 0:
                            hooks.pop(0)()
                        bcount += 1
                        pks = []
                        for c0 in cols:
                            pk = pso_i[0] % 6; pso_i[0] += 1
                            pks.append(pk)
                            for j in range(8):
                                K.op("pe", lambda e, j=j, pk=pk, c0=c0, uT=uT, ntok=ntok: e.matmul(
                                    psum[pk][:, 0:ntok], Wb[:, j, c0:c0 + 128], uT[:, j, 0:ntok],
                                    start=(j == 0), stop=(j == 7)),
                                    reads=["Wb", uT.key], writes=[f"ps#{pk}"], inc=(j == 7))
                        epi(ctxo, pks, bi, tok0, ntok)
                for (c0, cw, epi) in tspecs:
                    for ti in range(ntok // 128):
                        pk = pso_i[0] % 6; pso_i[0] += 1
                        for j in range(8):
                            K.op("pe", lambda e, j=j, pk=pk, uT=uT, ti=ti: e.matmul(
                                psum[pk][:, 0:cw], uT[:, j, ti * 128:(ti + 1) * 128], Wb[:, j, c0:c0 + cw],
                                start=(j == 0), stop=(j == 7)),
                                reads=["Wb", uT.key], writes=[f"ps#{pk}"], inc=(j == 7))
                        epi(ctxo, pk, tok0 + ti * 128)
                while hooks:
                    hooks.pop(0)()

            ng = P.dbg.get("ngroups", len(groups))
            uT0, p1s, p2s = front_parts(0)
            for p1, p2 in zip(p1s, p2s):
                p2(p1())
            cur = uT0
            for gi in range(ng):
                hooks = []
                nxt = None
                if gi + 1 < ng:
                    nxt, p1s, p2s = front_parts(gi + 1)
                    boxes = {}
                    def mk(k, p1s=p1s, p2s=p2s, boxes=boxes):
                        def h():
                            if k == 0:
                                for kk in range(min(2, len(p1s))):
                                    boxes[kk] = p1s[kk]()
                                return
                            if 0 <= k - 1 < len(p1s):
                                p2s[k - 1](boxes[k - 1])
                            if k + 1 < len(p1s):
                                boxes[k + 1] = p1s[k + 1]()
                        return h
                    hooks = [mk(k) for k in range(len(p1s) + 1)]
                mm(gi, cur, hooks)
                cur = nxt
            K.barrier()

    def l0_alloc(alloc):
        c = {}
        c["sf"] = Ring(alloc, "sf", 3, [128, 512], F32)
        c["sb"] = Ring(alloc, "sb", 4, [128, 512], BF16)
        c["t1"] = Ring(alloc, "t1", 2, [128, 512], F32)
        c["t2"] = Ring(alloc, "t2", 2, [128, 512], F32)
        c["tab"] = Ring(alloc, "tab", 2, [128, 2, 512], F32)
        return c

    def l0_group(c, gi, tok0, ntok):
        tb = c["tab"].next()
        c["curtab"] = tb
        K.dma("sp", tb[:, :, 0:ntok], ropetab[:, :, tok0:tok0 + ntok].rearrange("c p t -> p c t"),
              "L" + tb.key, writes=[tb.key])

    def epi_copy_f32(dst):
        def f(c, pks, bi, tok0, ntok):
            pk = pks[0]; b = c["sf"].next()
            K.op("act", lambda e: e.copy(b[:, 0:ntok], psum[pk][:, 0:ntok]), reads=[f"ps#{pk}"], writes=[b.key])
            K.dma("sp", dst[bi * 128:(bi + 1) * 128, tok0:tok0 + ntok], b[:, 0:ntok], "S" + b.key,
                  reads=[b.key], accw=[dst.name])
        return f

    def epi_silu_bf(dst):
        def f(c, pks, bi, tok0, ntok):
            pk = pks[0]; b = c["sb"].next()
            K.op("act", lambda e: e.activation(out=b[:, 0:ntok], in_=psum[pk][:, 0:ntok], func=AF.Silu),
                 reads=[f"ps#{pk}"], writes=[b.key])
            K.dma("sp", dst[bi * 128:(bi + 1) * 128, tok0:tok0 + ntok], b[:, 0:ntok], "S" + b.key,
                  reads=[b.key], accw=[dst.name])
        return f

    def epi_rope(dst):
        def f(c, pks, bi, tok0, ntok):
            pa, pb = pks; t1 = c["t1"].next(); t2 = c["t2"].next(); b = c["sb"].next(); tb = c["curtab"]
            K.op("dve", lambda e: e.tensor_tensor(t1[:, 0:ntok], psum[pa][:, 0:ntok], tb[:, 0, 0:ntok], ALU.mult),
                 reads=[f"ps#{pa}", tb.key], writes=[t1.key])
            K.op("dve", lambda e: e.tensor_tensor(t2[:, 0:ntok], psum[pb][:, 0:ntok], tb[:, 1, 0:ntok], ALU.mult),
                 reads=[f"ps#{pb}", tb.key], writes=[t2.key])
            K.op("pool", lambda e: e.tensor_tensor(b[:, 0:ntok], t1[:, 0:ntok], t2[:, 0:ntok], ALU.add),
                 reads=[t1.key, t2.key], writes=[b.key])
            K.dma("sp", dst[bi * 128:(bi + 1) * 128, tok0:tok0 + ntok], b[:, 0:ntok], "S" + b.key,
                  reads=[b.key], accw=[dst.name])
        return f

    def epi_v(c, pk, tok0):
        b = c["sb"].next()
        K.op("act", lambda e: e.copy(b[:, :], psum[pk][:, :]), reads=[f"ps#{pk}"], writes=[b.key])
        K.dma("sp", vtok[tok0:tok0 + 128, :], b[:, :], "S" + b.key, reads=[b.key], accw=["vtok"])

    l0_f = [
        ("xr", [[f * 128] for f in range(0, 4)], epi_copy_f32(xrT)),
        ("gr", [[f * 128] for f in range(4, 8)], epi_silu_bf(grT)),
        ("q", [[1024 + f * 128, 3072 + f * 128] for f in range(4)], epi_rope(qT)),
        ("k", [[1536 + f * 128, 3584 + f * 128] for f in range(4)], epi_rope(kT)),
        ("gd", [[f * 128] for f in range(20, 24)], epi_silu_bf(gdT)),
    ]
    l0_t = [(2048, 512, epi_v)]
    phase_proj(0, src0, e_w_in, 4096, l0_f, l0_t, l0_alloc, l0_group)
    if stop_after == "proj0":
        K.barrier(); pes.close(); P.es.close(); return P


    LAMBDA_INIT0 = 0.8 - 0.6 * 1.0

    def phase_lru():
        with ExitStack() as es:
            def alloc(name, shape, dt):
                P.uid += 1
                return es.enter_context(nc.sbuf_tensor(f"{name}_{P.uid}", list(shape), dt))
            rows = alloc("lrows", [44, 128], F32)
            pv = alloc("lpv", [128, 11, 4], F32)
            coef = alloc("lcoef", [128, 2, 4], F32)
            ones1 = alloc("ones1", [128, 1], F32)
            wbf = alloc("wbf", [128, 16, 128], F32)
            wbb = alloc("wbb", [128, 16, 128], BF16)
            big = Ring(alloc, "big", 7, [128, T], F32)
            xcb = alloc("xcb", [128, T], BF16)
            grs = alloc("grs", [128, T], BF16)
            mo = alloc("mixo", [128, T], BF16)
            K.op("dve", lambda e: e.memset(ones1[:], 1.0), writes=["ones1"])
            K.dma("sp", rows[:, :], lru_vec.rearrange("v (c p) -> (v c) p", p=128), "Lrows", writes=["lrows"])
            K.op("pe", lambda e: e.transpose(psum[0][:, 0:44], rows[:, :], idf[0:44, 0:44]),
                 reads=["lrows", "idf"], writes=["ps#0"])
            K.cp("dve", pv[:].rearrange("p v c -> p (v c)"), psum[0][:, 0:44], ["ps#0"], ["lpv"])
            K.act(coef[:].rearrange("p d c -> p (d c)"), pv[:, 9:11, :].rearrange("p d c -> p (d c)"), AF.Exp,
                  ["lpv"], ["lcoef"], scale=-1.0)
            K.act(coef[:].rearrange("p d c -> p (d c)"), coef[:].rearrange("p d c -> p (d c)"), AF.Ln,
                  ["lcoef", "ones1"], ["lcoef"], bias=ones1[:, 0:1])
            K.ts("dve", coef[:].rearrange("p d c -> p (d c)"), coef[:].rearrange("p d c -> p (d c)"), -8.0, None,
                 ALU.mult, None, ["lcoef"], ["lcoef"])
            K.dma("sp", wbf[:], lru_wbd.rearrange("n k m -> k n m"), "Lwbf", writes=["wbf"])
            K.cp("dve", wbb[:], wbf[:], ["wbf"], ["wbb"])
            segs = [(0, CTX), (CTX, T)]
            blocks = [(b0, min(512, T - b0)) for b0 in range(0, T, 512)]
            for cc in range(4):
                x = big.next(); xc = big.next()
                K.dma("sp", x[:, :], xrT[cc * 128:(cc + 1) * 128, :], "L" + x.key, writes=[x.key])
                K.dma("sp", grs[:, :], grT[cc * 128:(cc + 1) * 128, :], "Lgrs", writes=["grs"])
                K.ts("dve", xc[:, :], x[:, :], pv[:, 2, cc:cc + 1], pv[:, 4, cc:cc + 1], ALU.mult, ALU.add,
                     [x.key, "lpv"], [xc.key])
                for (a, b) in segs:
                    for tap, sh in ((0, -2), (1, -1), (3, 1)):
                        lo = max(a, a - sh); hi = min(b, b - sh)
                        K.stt("dve", xc[:, lo:hi], x[:, lo + sh:hi + sh], pv[:, tap, cc:cc + 1],
                              xc[:, lo:hi], ALU.mult, ALU.add, [x.key, xc.key, "lpv"], [xc.key])
                K.cp("pool", xcb[:, :], xc[:, :], [xc.key], ["xcb"])
                hs = []
                for d in range(2):
                    rb = big.next(); ib = big.next()
                    for (b0, bn) in blocks:
                        for g, dstb, brow in ((0, rb, 5 + d), (1, ib, 7 + d)):
                            pk = (2 * (b0 // 512) + g) % 6
                            K.mm(psum[pk][:, 0:bn], wbb[:, (d * 2 + g) * 4 + cc, :], xcb[:, b0:b0 + bn], True, True,
                                 ["wbb", "xcb"], [f"ps#{pk}"], True)
                            K.act(dstb[:, b0:b0 + bn], psum[pk][:, 0:bn], AF.Sigmoid, [f"ps#{pk}", "lpv"], accw=[dstb.key],
                                  bias=pv[:, brow, cc:cc + 1])
                    K.ts("dve", rb[:, :], rb[:, :], coef[:, d, cc:cc + 1], None, ALU.mult, None, [rb.key, "lcoef"], [rb.key])
                    K.act(rb[:, :], rb[:, :], AF.Exp, [rb.key], [rb.key])
                    sq = big.next()
                    K.tt("pool", sq[:, :], rb[:, :], rb[:, :], ALU.mult, [rb.key], [sq.key])
                    K.act(sq[:, :], sq[:, :], AF.Sqrt, [sq.key, "ones1"], [sq.key], scale=-1.0, bias=ones1[:, 0:1])
                    K.tt("pool", ib[:, :], ib[:, :], xc[:, :], ALU.mult, [ib.key, xc.key], [ib.key])
                    K.tt("dve", ib[:, :], ib[:, :], sq[:, :], ALU.mult, [ib.key, sq.key], [ib.key])
                    h = sq
                    if d == 0:
                        K.op("dve", lambda e, h=h, rb=rb, ib=ib: e.tensor_tensor_scan(
                            h[:, :], rb[:, :], ib[:, :], 0.0, ALU.mult, ALU.add), [rb.key, ib.key, h.key], [h.key])
                    else:
                        K.op("dve", lambda e, h=h, rb=rb, ib=ib: e.tensor_tensor_scan(
                            h[:, CTX - 1::-1] if False else h[:, 0:CTX][:, ::-1], rb[:, 0:CTX][:, ::-1], ib[:, 0:CTX][:, ::-1],
                            0.0, ALU.mult, ALU.add), [rb.key, ib.key, h.key], [h.key])
                        K.op("dve", lambda e, h=h, rb=rb, ib=ib: e.tensor_tensor_scan(
                            h[:, CTX:T][:, ::-1], rb[:, CTX:T][:, ::-1], ib[:, CTX:T][:, ::-1],
                            h[:, 0:1], ALU.mult, ALU.add), [rb.key, ib.key, h.key], [h.key])
                    hs.append(h)
                K.tt("pool", hs[0][:, :], hs[0][:, :], hs[1][:, :], ALU.add, [hs[0].key, hs[1].key], [hs[0].key])
                K.tt("dve", mo[:, :], hs[0][:, :], grs[:, :], ALU.mult, [hs[0].key, "grs"], ["mixo"])
                K.dma("sp", mixT0[cc * 128:(cc + 1) * 128, :], mo[:, :], "Smixo", reads=["mixo"], accw=["mixT0"])
            K.barrier()

    if "lru" not in P.dbg.get("skip", ()):
        phase_lru()
    if stop_after == "lru":
        K.barrier(); pes.close(); P.es.close(); return P

    def phase_attn():
        with ExitStack() as es:
            def alloc(name, shape, dt):
                P.uid += 1
                return es.enter_context(nc.sbuf_tensor(f"{name}_{P.uid}", list(shape), dt))
            kres = alloc("kres", [128, 4, T], BF16)
            vres = alloc("vres", [128, NT, 512], BF16)
            onesb = alloc("onesb", [128, 128], BF16)
            onesf = alloc("onesf", [128, 128], F32)
            lrow = alloc("lamrow", [1, 260], F32)
            lamc = alloc("lamc", [128, 2], F32)
            subc = alloc("subc", [128, 2], F32)
            subrow = alloc("subrow", [1, 128], F32)
            qb_ = Ring(alloc, "qblk", 2, [128, 4, 512], BF16)
            gd_ = Ring(alloc, "gdblk", 2, [128, 512], BF16)
            E_ = Ring(alloc, "Eb", 6, [128, 512], BF16)
            f_ = Ring(alloc, "af", 6, [128, 512], F32)
            ob_ = Ring(alloc, "aob", 4, [128, 512], BF16)
            K.op("dve", lambda e: e.memset(onesb[:], 1.0), writes=["onesb"])
            K.op("dve", lambda e: e.memset(onesf[:], 1.0), writes=["onesf"])
            K.dma("sp", lrow[:, 0:256], da_lam[:, :], "Llam", writes=["lamrow"])
            K.dma("sp", subrow[:, :], da_sub[:, :], "Lsub", writes=["subrow"])
            K.tt("dve", lrow[:, 0:64], lrow[:, 0:64], lrow[:, 64:128], ALU.mult, ["lamrow"], ["lamrow"])
            K.tt("dve", lrow[:, 128:192], lrow[:, 128:192], lrow[:, 192:256], ALU.mult, ["lamrow"], ["lamrow"])
            K.op("dve", lambda e: e.reduce_sum(lrow[:, 256:257], lrow[:, 0:64], AX.X), ["lamrow"], ["lamrow"])
            K.op("dve", lambda e: e.reduce_sum(lrow[:, 257:258], lrow[:, 128:192], AX.X), ["lamrow"], ["lamrow"])
            K.act(lrow[:, 256:258], lrow[:, 256:258], AF.Exp, ["lamrow"], ["lamrow"])
            K.tt("dve", lrow[:, 258:259], lrow[:, 256:257], lrow[:, 257:258], ALU.subtract, ["lamrow"], ["lamrow"])
            K.ts("dve", lrow[:, 258:259], lrow[:, 258:259], -1.0, -LAMBDA_INIT0, ALU.mult, ALU.add, ["lamrow"], ["lamrow"])
            K.mm(psum[0][:, 0:1], onesf[0:1, :], lrow[0:1, 258:259], True, True, ["onesf", "lamrow"], ["ps#0"], True)
            K.cp("dve", lamc[:, 0:1], psum[0][:, 0:1], ["ps#0"], ["lamc"])
            K.op("pe", lambda e: e.transpose(psum[1][:, 0:1], subrow[0:1, :], idf[0:1, 0:1]),
                 reads=["subrow", "idf"], writes=["ps#1"])
            K.ts("dve", subc[:, 0:1], psum[1][:, 0:1], 1.0 - LAMBDA_INIT0, None, ALU.mult, None, ["ps#1"], ["subc"])
            for h in range(4):
                K.dma("sp", kres[:, h, :], kT[h * 128:(h + 1) * 128, :], "Lkres", accw=["kres"])
            for n0 in range(0, NT, 2):
                K.dma("sp", vres[:, n0:n0 + 2, :], vtok[n0 * 128:(n0 + 2) * 128, :].rearrange("(n p) e -> p n e", p=128),
                      "Lvres", accw=["vres"])
            qblocks = [(0, CTX, 0, 2)] + [(CTX + 512 * g, 512, 0, NT) for g in range(SEQ // 512)]
            nqb = P.dbg.get("nqb", len(qblocks))
            sti = 0
            acc_ = Ring(alloc, "dacc", 4, [128, 512], F32)
            for (q0, nq, kt0, kt1) in qblocks[:nqb]:
                qb = qb_.next()
                for h in range(4):
                    K.dma("sp", qb[:, h, 0:nq], qT[h * 128:(h + 1) * 128, q0:q0 + nq], "L" + qb.key, accw=[qb.key])
                for h in range(4):
                    gd = gd_.next()
                    K.dma("sp", gd[:, 0:nq], gdT[h * 128:(h + 1) * 128, q0:q0 + nq], "L" + gd.key, writes=[gd.key])
                    accs = [acc_.next(), acc_.next()]
                    kts = list(range(kt0, kt1))
                    pkmap = {}

                    def emit_qk(kt):
                        nonlocal sti
                        for c in range(2):
                            pk = sti % 4; sti += 1
                            pkmap[(kt, c)] = pk
                            K.mm(psum[pk][:, 0:nq], kres[c * 64:(c + 1) * 64, h, kt * 128:(kt + 1) * 128],
                                 qb[c * 64:(c + 1) * 64, h, 0:nq], True, True, ["kres", qb.key], [f"ps#{pk}"], True)

                    def emit_rest(kt):
                        for c in range(2):
                            pk = pkmap[(kt, c)]
                            E = E_.next()
                            K.act(E[:, 0:nq], psum[pk][:, 0:nq], AF.Exp, [f"ps#{pk}"], [E.key], scale=0.125)
                            K.mm(psum[4 + c][:, 0:nq], vres[:, kt, h * 128:(h + 1) * 128], E[:, 0:nq],
                                 kt == kt0, kt == kt1 - 1, ["vres", E.key], [f"ps#{4 + c}"], kt == kt1 - 1)
                            if c == 1:
                                K.mm(psum[7][:, 0:nq], onesb[:, :], E[:, 0:nq], kt == kt0, kt == kt1 - 1,
                                     ["onesb", E.key], ["ps#7"], kt == kt1 - 1)
                            elif kt == kt0:
                                K.cp("dve", accs[c][:, 0:nq], E[:, 0:nq], [E.key], [accs[c].key])
                            else:
                                K.tt("dve", accs[c][:, 0:nq], accs[c][:, 0:nq], E[:, 0:nq], ALU.add,
                                     [E.key, accs[c].key], [accs[c].key])

                    emit_qk(kts[0])
                    for i, kt in enumerate(kts):
                        if i + 1 < len(kts):
                            emit_qk(kts[i + 1])
                        emit_rest(kt)
                    r0 = f_.next(); r1 = f_.next(); t0 = f_.next(); t1 = f_.next()
                    K.cp("act", t0[:, 0:nq], psum[4][:, 0:nq], ["ps#4"], [t0.key])
                    K.cp("act", t1[:, 0:nq], psum[5][:, 0:nq], ["ps#5"], [t1.key])
                    ab = ob_.next()
                    K.cp("pool", ab[:, 0:nq], accs[0][:, 0:nq], [accs[0].key], [ab.key])
                    K.mm(psum[6][:, 0:nq], onesb[:, :], ab[:, 0:nq], True, True, ["onesb", ab.key], ["ps#6"], True)
                    K.cp("act", r0[:, 0:nq], psum[6][:, 0:nq], ["ps#6"], [r0.key])
                    K.tt("dve", t0[:, 0:nq], t0[:, 0:nq], psum[7][:, 0:nq], ALU.mult, [t0.key, "ps#7"], [t0.key])
                    K.tt("dve", t1[:, 0:nq], t1[:, 0:nq], r0[:, 0:nq], ALU.mult, [t1.key, r0.key], [t1.key])
                    K.stt("dve", t0[:, 0:nq], t1[:, 0:nq], lamc[:, 0:1], t0[:, 0:nq], ALU.mult, ALU.add,
                          [t0.key, t1.key, "lamc"], [t0.key])
                    K.tt("dve", r0[:, 0:nq], r0[:, 0:nq], psum[7][:, 0:nq], ALU.mult, [r0.key, "ps#7"], [r0.key])
                    K.stt("dve", r1[:, 0:nq], r0[:, 0:nq], EPS, r0[:, 0:nq], ALU.mult, ALU.mult, [r0.key], [r1.key])
                    osq = ob_.next()
                    K.tt("pool", osq[:, 0:nq], t0[:, 0:nq], t0[:, 0:nq], ALU.mult, [t0.key], [osq.key])
                    K.mm(psum[6][:, 0:nq], onesb[:, :], osq[:, 0:nq], True, True, ["onesb", osq.key], ["ps#6"], True)
                    K.stt("dve", r1[:, 0:nq], psum[6][:, 0:nq], 1.0 / 128, r1[:, 0:nq], ALU.mult, ALU.add,
                          ["ps#6", r1.key], [r1.key])
                    K.act(r1[:, 0:nq], r1[:, 0:nq], AF.Ln, [r1.key], [r1.key])
                    K.act(r1[:, 0:nq], r1[:, 0:nq], AF.Exp, [r1.key], [r1.key], scale=-0.5)
                    K.tt("dve", t0[:, 0:nq], t0[:, 0:nq], r1[:, 0:nq], ALU.mult, [t0.key, r1.key], [t0.key])
                    mo = ob_.next()
                    K.stt("dve", mo[:, 0:nq], t0[:, 0:nq], subc[:, 0:1], gd[:, 0:nq], ALU.mult, ALU.mult,
                          [t0.key, "subc", gd.key], [mo.key])
                    K.dma("sp", mixT0[(4 + h) * 128:(5 + h) * 128, q0:q0 + nq], mo[:, 0:nq], "S" + mo.key,
                          reads=[mo.key], accw=["mixT0"])
            K.barrier()

    if "attn" not in P.dbg.get("skip", ()):
        phase_attn()
    if stop_after == "attn":
        K.barrier(); pes.close(); P.es.close(); return P

    def phase_out(layer, mixT, KC, Wd, res_src, dst, tiles):
        with ExitStack() as es:
            def alloc(name, shape, dt):
                P.uid += 1
                return es.enter_context(nc.sbuf_tensor(f"{name}_{P.uid}", list(shape), dt))
            Wb = alloc("Wo", [128, KC, D], BF16)
            wst = Ring(alloc, "wost", 2, [128, KC, 256], F32)
            mx_ = Ring(alloc, "mxin", 2, [128, KC, 512], BF16)
            rs_ = Ring(alloc, "resin", 3, [128, D], F32)
            tm_ = Ring(alloc, "otmp", 2, [128, D], F32)
            st_ = Ring(alloc, "ostat", 4, [128, 4], F32)
            junk = alloc("ojunk", [128, 512], BF16)
            for pc in range(4):
                wb = wst.next()
                K.dma("sp", wb[:], Wd[:, pc * 256:(pc + 1) * 256].rearrange("(j p) n -> p j n", p=128), "L" + wb.key,
                      writes=[wb.key])
                K.cp(["dve", "pool"][pc % 2], Wb[:, :, pc * 256:(pc + 1) * 256], wb[:], [wb.key], accw=["Wo"])
            gi = 0
            cur = None
            for (tok0, drow, tmod) in tiles:
                g0 = (tok0 // 512) * 512 if tok0 >= CTX else 0
                if tok0 >= CTX:
                    g0 = CTX + ((tok0 - CTX) // 512) * 512
                gn = CTX if tok0 < CTX else 512
                if cur is None or cur[0] != g0:
                    mx = mx_.next()
                    K.dma("sp", mx[:, :, 0:gn], mixT[:, g0:g0 + gn].rearrange("(j p) t -> p j t", p=128), "L" + mx.key,
                          writes=[mx.key])
                    cur = (g0, mx)
                mx = cur[1]; lo = tok0 - g0
                rs = rs_.next(); tm = tm_.next(); st = st_.next()
                K.dma("sp", rs[:, :], res_src[tok0:tok0 + 128, :], "L" + rs.key, writes=[rs.key])
                pks = [(2 * gi) % 6, (2 * gi + 1) % 6]; gi += 1
                for nb in range(2):
                    for j in range(KC):
                        K.mm(psum[pks[nb]][:, :], mx[:, j, lo:lo + 128], Wb[:, j, nb * 512:(nb + 1) * 512],
                             j == 0, j == KC - 1, [mx.key, "Wo"], [f"ps#{pks[nb]}"], j == KC - 1)
                for nb in range(2):
                    K.act(junk[:, :], psum[pks[nb]][:, :], AF.Square, [f"ps#{pks[nb]}"], ["ojunk", st.key] if nb == 0 else ["ojunk"],
                          accw=() if nb == 0 else [st.key], accum_out=st[:, nb:nb + 1])
                K.tt("dve", st[:, 2:3], st[:, 0:1], st[:, 1:2], ALU.add, [st.key], [st.key])
                K.act(st[:, 3:4], st[:, 2:3], AF.Sqrt, [st.key, "epsb"], [st.key], scale=1.0 / D, bias=epsb[:, 0:1])
                K.op("dve", lambda e, st=st: e.reciprocal(st[:, 2:3], st[:, 3:4]), [st.key], [st.key])
                for nb in range(2):
                    K.stt("dve", tm[:, nb * 512:(nb + 1) * 512], psum[pks[nb]][:, :], st[:, 2:3],
                          ggbc[:, layer, tmod, nb * 512:(nb + 1) * 512], ALU.mult, ALU.mult,
                          [f"ps#{pks[nb]}", st.key, "ggbc"], accw=[tm.key])
                K.tt("pool", tm[:, :], tm[:, :], rs[:, :], ALU.add, [tm.key, rs.key], [tm.key])
                K.dma("sp", dst[drow:drow + 128, :], tm[:, :], "S" + tm.key, reads=[tm.key], accw=[dst.name])
            K.barrier()

    nt0 = P.dbg.get("out0_tiles", NT)
    tiles0 = [(i * 128, i * 128, 1 if i < 2 else 0) for i in range(nt0)]
    phase_out(0, mixT0, 8, e_w_out, src0, h1, tiles0)
    if stop_after == "out0":
        K.barrier(); pes.close(); P.es.close(); return P


    def l1_alloc(alloc):
        c = {}
        c["sb"] = Ring(alloc, "sb1", 4, [128, 512], BF16)
        c["sf"] = Ring(alloc, "sf1", 2, [128, 64], F32)
        c["n"] = 0
        return c

    def epi_xbc(c, pks, bi, tok0, ntok):
        pk = pks[0]; b = c["sb"].next()
        c["n"] += 1
        K.cp("act" if c["n"] % 2 else "dve", b[:, 0:ntok], psum[pk][:, 0:ntok], [f"ps#{pk}"], [b.key])
        K.dma("sp", xbcT[bi * 128:(bi + 1) * 128, tok0:tok0 + ntok], b[:, 0:ntok], "S" + b.key,
              reads=[b.key], accw=["xbcT"])

    def epi_z(zc):
        def f(c, pk, tok0):
            b = c["sb"].next()
            K.act(b[:, :], psum[pk][:, :], AF.Silu, [f"ps#{pk}"], [b.key])
            K.dma("sp", zs[tok0:tok0 + 128, zc * 512:(zc + 1) * 512], b[:, :], "S" + b.key, reads=[b.key], accw=["zs"])
        return f

    def epi_dt(c, pk, tok0):
        b = c["sf"].next()
        K.cp("dve", b[:, :], psum[pk][:, 0:64], [f"ps#{pk}"], [b.key])
        K.dma("sp", dtraw[tok0:tok0 + 128, :], b[:, :], "S" + b.key, reads=[b.key], accw=["dtraw"])

    l1_f = [("xbc", [[2048 + f * 128] for f in range(24)], epi_xbc)]
    l1_t = [(zc * 512, 512, epi_z(zc)) for zc in range(4)] + [(5120, 64, epi_dt)]
    if "l1" not in P.dbg.get("skip", ()):
        phase_proj(1, h1, o_w_in, 5184, l1_f, l1_t, l1_alloc, None)
    if stop_after == "proj1":
        K.barrier(); pes.close(); P.es.close(); return P

    def phase_conv():
        with ExitStack() as es:
            def alloc(name, shape, dt):
                P.uid += 1
                return es.enter_context(nc.sbuf_tensor(f"{name}_{P.uid}", list(shape), dt))
            rows = alloc("cvrows", [120, 128], F32)
            cv = alloc("cv", [128, 5, 24], F32)
            dg = alloc("dg", [128, 4, 24, 128], BF16)
            xin_ = Ring(alloc, "cxin", 2, [128, 24, 516], BF16)
            sb_ = Ring(alloc, "csb", 3, [128, 512], BF16)
            rb_ = Ring(alloc, "crow", 8, [128, 2560], BF16)
            K.dma("sp", rows[:, :], ssd_cv.rearrange("v (f p) -> (v f) p", p=128), "Lcvrows", writes=["cvrows"])
            K.op("pe", lambda e: e.transpose(psum[0][:, 0:120], rows[:, :], idf[0:120, 0:120]),
                 reads=["cvrows", "idf"], writes=["ps#0"])
            K.cp("dve", cv[:].rearrange("p v f -> p (v f)"), psum[0][:, 0:120], ["ps#0"], ["cv"])
            for j in range(4):
                for f in range(24):
                    K.ts("dve" if (f % 2) else "pool", dg[:, j, f, :], idf[:, :], cv[:, j, f:f + 1], None, ALU.mult, None,
                         ["idf", "cv"], accw=["dg"])
            blocks = [(0, CTX, 0, CTX)] + [(CTX + 512 * g, 512, CTX, T) for g in range(SEQ // 512)]
            cpi = 0
            for (b0, bn, sa, sb_end) in blocks[:P.dbg.get("nconv", 99)]:
                xin = xin_.next()
                l0 = max(sa, b0 - 2); l1 = min(sb_end, b0 + bn + 1)
                K.dma("sp", xin[:, :, l0 - (b0 - 2):l1 - (b0 - 2)],
                      xbcT[:, l0:l1].rearrange("(f p) t -> p f t", p=128), "L" + xin.key, writes=[xin.key])
                nt_ = bn // 128
                rbs = [rb_.next() for _ in range(nt_)]
                for ft in range(24):
                    pk = 4 + (cpi % 2); cpi += 1
                    order = [2, 0, 1, 3]
                    for oi, j in enumerate(order):
                        sh = j - 2
                        lo = max(b0, sa - sh); hi = min(b0 + bn, sb_end - sh)
                        K.mm(psum[pk][:, lo - b0:hi - b0], dg[:, j, ft, :],
                             xin[:, ft, lo + sh - (b0 - 2):hi + sh - (b0 - 2)], oi == 0, oi == 3,
                             ["dg", xin.key], [f"ps#{pk}"], oi == 3)
                    sb = sb_.next()
                    K.act(sb[:, 0:bn], psum[pk][:, 0:bn], AF.Silu, [f"ps#{pk}", "cv"], [sb.key], bias=cv[:, 4, ft:ft + 1])
                    if ft >= 16:
                        K.dma("sp", bcT[(ft - 16) * 128:(ft - 15) * 128, b0:b0 + bn], sb[:, 0:bn], "S" + sb.key,
                              reads=[sb.key], accw=["bcT"])
                    if ft < 20:
                        for i in range(nt_):
                            tb = psum[i][:].bitcast(BF16)
                            K.op("pe", lambda e, tb=tb, sb=sb, i=i, ft=ft: e.transpose(
                                tb[:, (ft % 8) * 128:(ft % 8 + 1) * 128], sb[:, i * 128:(i + 1) * 128], idb[:]),
                                reads=[sb.key, "idb"], writes=[f"ps#{i}"], inc=True)
                        if ft % 8 == 7 or ft == 19:
                            ncol = (ft % 8 + 1) * 128
                            c0 = (ft // 8) * 1024
                            for i in range(nt_):
                                tb = psum[i][:].bitcast(BF16)
                                K.cp("dve" if i % 2 else "act", rbs[i][:, c0:c0 + ncol], tb[:, 0:ncol], [f"ps#{i}"],
                                     accw=[rbs[i].key])
                for i in range(nt_):
                    K.dma("sp", xsB[b0 + i * 128:b0 + (i + 1) * 128, :], rbs[i][:, :], "S" + rbs[i].key,
                          reads=[rbs[i].key], accw=["xsB"])
            K.barrier()

    if "conv" not in P.dbg.get("skip", ()):
        phase_conv()
    if stop_after == "conv":
        K.barrier(); pes.close(); P.es.close(); return P


    def phase_ssd():
        with ExitStack() as es:
            def alloc(name, shape, dt):
                P.uid += 1
                return es.enter_context(nc.sbuf_tensor(f"{name}_{P.uid}", list(shape), dt))
            vb = alloc("vb", [128, 160], F32)
            aneg = alloc("aneg", [128, 64], F32)
            nwbc = alloc("nwbc", [128, 2048], F32)
            mk = alloc("mk", [128, 2, 128], F32)
            self_ = alloc("self", [64, 1024], F32)
            selb = alloc("selb", [64, 32, 128], BF16)
            onesf = alloc("onesf2", [128, 128], F32)
            ones1 = alloc("ones1b", [128, 1], F32)
            state = alloc("state", [128, 2048], F32)
            stbf = alloc("stbf", [128, 2048], BF16)
            xb_ = Ring(alloc, "xb", 4, [128, 2560], BF16)
            bc_ = Ring(alloc, "bc", 4, [128, 8, 128], BF16)
            dr_ = Ring(alloc, "dr", 4, [128, 32], F32)
            sm_ = Ring(alloc, "sm", 4, [128, 8, 32], F32)
            cs4_ = Ring(alloc, "cs4", 3, [128, 64], F32)
            cst_ = Ring(alloc, "cst", 3, [64, 128], F32)
            hl_ = Ring(alloc, "hl", 3, [64, 3, 128], BF16)
            X_ = Ring(alloc, "Xd", 3, [128, 2048], BF16)
            Xc_ = Ring(alloc, "Xc", 3, [128, 2048], BF16)
            xd_ = Ring(alloc, "xdk", 1, [128, 2048], F32)
            cbm_ = Ring(alloc, "cbm", 2, [128, 4, 128], F32)
            E_ = Ring(alloc, "sE", 3, [128, 512], F32)
            MT_ = Ring(alloc, "sMT", 4, [128, 4, 128], BF16)
            to_ = Ring(alloc, "sto", 2, [128, 512], F32)
            ysb_ = Ring(alloc, "ysb", 2, [128, 2048], F32)
            yf_ = Ring(alloc, "yfl", 2, [128, 2048], F32)
            zt_ = Ring(alloc, "ztl", 1, [128, 2048], BF16)
            ynb_ = Ring(alloc, "ynb", 1, [128, 2048], BF16)
            yT_ = Ring(alloc, "yTs", 2, [128, 16, 128], BF16)
            junk = alloc("sjunk", [128, 512], BF16)
            K.op("dve", lambda e: e.memset(onesf[:], 1.0), writes=["onesf2"])
            K.op("dve", lambda e: e.memset(ones1[:], 1.0), writes=["ones1b"])
            K.dma("sp", vb[:, :], ssd_vec[0, :].partition_broadcast(128), "Lvb", writes=["vb"])
            K.dma("sp", nwbc[:, :], ssd_norm.partition_broadcast(128), "Lnwbc", writes=["nwbc"])
            K.dma("sp", mk[:], maskT.rearrange("d s l -> s d l"), "Lmk", writes=["mk"])
            for q4 in range(4):
                K.dma("sp", self_[:, :], selc[:, q4 * 1024:(q4 + 1) * 1024], "Lself", writes=["self"])
                K.cp("dve", selb[:, q4 * 8:(q4 + 1) * 8, :].rearrange("k h l -> k (h l)"), self_[:, :], ["self"], accw=["selb"])
            K.act(aneg[:, :], vb[:, 0:64], AF.Exp, ["vb"], ["aneg"])
            K.ts("dve", aneg[:, :], aneg[:, :], -1.0, None, ALU.mult, None, ["aneg"], ["aneg"])
            lat_chunks = list(range(2, NT))
            ncl = P.dbg.get("nchunk", len(lat_chunks))
            lat_chunks = lat_chunks[:ncl]
            PB = {"yd": 0, "seg": 0, "so": 0}

            def stage_a(d, c):
                tok0 = c * 128
                lat = c >= 2
                xb = xb_.next(); bc = bc_.next(); dr = dr_.next(); sm = sm_.next()
                ctx_ = {"xb": xb, "bc": bc, "sm": sm, "lat": lat, "tok0": tok0}
                K.dma("sp", xb[:, :], xsB[tok0:tok0 + 128, :], "L" + xb.key, writes=[xb.key])
                K.dma("sp", bc[:], bcT[:, tok0:tok0 + 128].rearrange("(f p) t -> p f t", p=128), "L" + bc.key,
                      writes=[bc.key])
                K.dma("sp", dr[:, :], dtraw[tok0:tok0 + 128, d * 32:(d + 1) * 32], "L" + dr.key, writes=[dr.key])
                dt = sm[:, 0, :]; adt = sm[:, 1, :]; cs = sm[:, 2, :]; ecs = sm[:, 3, :]
                etot = sm[:, 4, :]; w2 = sm[:, 5, :]; tmp = sm[:, 6, :]
                k_ = sm.key
                K.tt("dve", tmp, dr[:, :], vb[:, 64 + d * 32:96 + d * 32], ALU.add, [dr.key, "vb"], [k_])
                K.act(tmp, tmp, AF.Exp, [k_], [k_])
                K.act(dt, tmp, AF.Ln, [k_, "ones1b"], [k_], bias=ones1[:, 0:1])
                K.tt("dve", adt, dt, aneg[:, d * 32:(d + 1) * 32], ALU.mult, [k_, "aneg"], [k_])
                K.mm(psum[4][:, 0:32], mk[:, d, :], adt, True, True, ["mk", k_], ["ps#4"], True)
                K.mm(psum[4][:, 32:64], onesf[:, :], adt, True, True, ["onesf2", k_], ["ps#4"], True)
                K.cp("dve", cs, psum[4][:, 0:32], ["ps#4"], [k_])
                K.act(etot, psum[4][:, 32:64], AF.Exp, ["ps#4"], [k_])
                K.tt("dve", tmp, psum[4][:, 32:64], cs, ALU.subtract, ["ps#4", k_], [k_])
                K.act(w2, tmp, AF.Exp, [k_], [k_])
                K.tt("dve", w2, w2, dt, ALU.mult, [k_], [k_])
                xs3 = xb[:, 0:2048].rearrange("p (h q) -> p h q", q=64)
                Xc = Xc_.next()
                ctx_["Xc"] = Xc
                K.tt("pool", Xc[:, :].rearrange("p (h q) -> p h q", q=64), xs3,
                     w2.unsqueeze(2).to_broadcast([128, 32, 64]), ALU.mult, [xb.key, k_], [Xc.key])
                if not lat:
                    return ctx_
                K.act(ecs, cs, AF.Exp, [k_], [k_])
                X = X_.next()
                K.tt("pool", X[:, :].rearrange("p (h q) -> p h q", q=64), xs3,
                     dt.unsqueeze(2).to_broadcast([128, 32, 64]), ALU.mult, [xb.key, k_], [X.key])
                cs4 = cs4_.next(); cst = cst_.next(); hl = hl_.next()
                K.cp("dve", cs4[:, 0:32], cs, [k_], accw=[cs4.key])
                K.cp("dve", cs4[:, 32:64], cs, [k_], accw=[cs4.key])
                K.op("pe", lambda e, cs4=cs4: e.transpose(psum[4][0:64, 128:256], cs4[:, :], idf[:, :]),
                     reads=[cs4.key, "idf"], writes=["ps#4"])
                K.cp("dve", cst[:, :], psum[4][0:64, 128:256], ["ps#4"], [cst.key])
                K.cp("dve", hl[0:32, 0, :], cst[0:32, :], [cst.key], accw=[hl.key])
                K.cp("dve", hl[32:64, 2, :], cst[32:64, :], [cst.key], accw=[hl.key])
                K.tt("dve", hl[32:64, 0, :], cst[32:64, :], hl[32:64, 2, :], ALU.subtract, [cst.key, hl.key], accw=[hl.key])
                K.ts("dve", hl[:, 1, :], hl[:, 0, :], -1.0, None, ALU.mult, None, [hl.key], accw=[hl.key])
                ctx_["X"] = X; ctx_["hl"] = hl
                return ctx_

            def stage_a1(d, cx):
                if not cx["lat"]:
                    return
                xb, bc, sm, tok0, X, hl = cx["xb"], cx["bc"], cx["sm"], cx["tok0"], cx["X"], cx["hl"]
                ctx_ = cx
                for g in range(4):
                    K.mm(psum[5][:, g * 128:(g + 1) * 128], bc[:, g, :], bc[:, 4 + g, :], True, True,
                         [bc.key], ["ps#5"], g == 3)
                cbm = cbm_.next()
                K.tt("dve", cbm[:], psum[5][:, :].rearrange("p (g l) -> p g l", l=128),
                     mk[:, d:d + 1, :].to_broadcast([128, 4, 128]), ALU.mult, ["ps#5", "mk"], [cbm.key])
                ysb = ysb_.next()
                ctx_["ysb"] = ysb
                if d == 1:
                    yf = yf_.next()
                    K.dma("sp", yf[:, :], yfw[tok0:tok0 + 128, :], "L" + yf.key, reads=["yfw"], writes=[yf.key])
                segbank = {}

                def emit_seg(hq):
                    pk = 2 + PB["seg"] % 2; PB["seg"] += 1
                    segbank[hq] = pk
                    for j in range(4):
                        h = hq * 4 + j
                        K.mm(psum[pk][:, j * 128:(j + 1) * 128], selb[:, h, :], hl[:, 0, :], True, False,
                             ["selb", hl.key], [f"ps#{pk}"], False)
                        K.mm(psum[pk][:, j * 128:(j + 1) * 128], hl[:, 1, :], selb[:, h, :], False, True,
                             ["selb", hl.key], [f"ps#{pk}"], j == 3)

                pyd = 0
                emit_seg(0)
                for hq in range(8):
                    g = hq // 2
                    if hq + 1 < 8:
                        emit_seg(hq + 1)
                    pk = segbank[hq]
                    if hq % 2 == 0:
                        pyd = PB["yd"] % 2; PB["yd"] += 1
                    E = E_.next(); MT = MT_.next()
                    K.act(E[:, :], psum[pk][:, :], AF.Exp, [f"ps#{pk}"], [E.key])
                    K.stt("dve", MT[:], E[:, :].rearrange("p (j l) -> p j l", l=128), 1e30,
                          cbm[:, g:g + 1, :].to_broadcast([128, 4, 128]), ALU.min, ALU.mult,
                          [E.key, cbm.key], [MT.key])
                    for j in range(4):
                        h = hq * 4 + j
                        K.mm(psum[pyd][:, (h % 8) * 64:(h % 8 + 1) * 64], MT[:, j, :], X[:, h * 64:(h + 1) * 64],
                             True, True, [MT.key, X.key], [f"ps#{pyd}"], (h % 8 == 7))
                    if hq % 2 == 1:
                        if d == 0:
                            K.cp("act", ysb[:, g * 512:(g + 1) * 512], psum[pyd][:, :], [f"ps#{pyd}"], accw=[ysb.key])
                        else:
                            K.tt("dve", ysb[:, g * 512:(g + 1) * 512], psum[pyd][:, :], yf[:, g * 512:(g + 1) * 512],
                                 ALU.add, [f"ps#{pyd}", yf.key], accw=[ysb.key])
                return ctx_

            def stage_b(d, cx):
                xb, bc, sm, lat, tok0, Xc = cx["xb"], cx["bc"], cx["sm"], cx["lat"], cx["tok0"], cx["Xc"]
                k_ = sm.key
                ecs = sm[:, 3, :]; etot = sm[:, 4, :]
                xs3 = xb[:, 0:2048].rearrange("p (h q) -> p h q", q=64)
                if lat:
                    ysb = cx["ysb"]
                    for g in range(4):
                        pk = 6 + PB["so"] % 2; PB["so"] += 1
                        K.mm(psum[pk][:, :], bc[:, 4 + g, :], stbf[:, g * 512:(g + 1) * 512], True, True,
                             [bc.key, "stbf"], [f"ps#{pk}"], True)
                        to = to_.next()
                        K.tt("dve", to[:, :].rearrange("p (h q) -> p h q", q=64),
                             psum[pk][:, :].rearrange("p (h q) -> p h q", q=64),
                             ecs[:, g * 8:(g + 1) * 8].unsqueeze(2).to_broadcast([128, 8, 64]), ALU.mult,
                             [f"ps#{pk}", k_], [to.key])
                        K.tt("dve", ysb[:, g * 512:(g + 1) * 512], ysb[:, g * 512:(g + 1) * 512], to[:, :], ALU.add,
                             [ysb.key, to.key], [ysb.key])
                for g in range(4):
                    pk = 6 + PB["so"] % 2; PB["so"] += 1
                    K.mm(psum[pk][:, :], xb[:, 2048 + g * 128:2048 + (g + 1) * 128], Xc[:, g * 512:(g + 1) * 512],
                         True, True, [xb.key, Xc.key], [f"ps#{pk}"], True)
                    sg = state[:, g * 512:(g + 1) * 512]
                    K.tt("pool", sg.rearrange("p (h q) -> p h q", q=64), sg.rearrange("p (h q) -> p h q", q=64),
                         etot[:, g * 8:(g + 1) * 8].unsqueeze(2).to_broadcast([128, 8, 64]), ALU.mult,
                         ["state", k_], ["state"])
                    K.tt("dve", sg, sg, psum[pk][:, :], ALU.add, ["state", f"ps#{pk}"], ["state"])
                K.cp("act", stbf[:, :], state[:, :], ["state"], ["stbf"])
                if not lat:
                    return
                if d == 0:
                    K.dma("sp", yfw[tok0:tok0 + 128, :], ysb[:, :], "S" + ysb.key, reads=[ysb.key], accw=["yfw"])
                    return
                zt = zt_.next(); ynb = ynb_.next(); yT = yT_.next(); xd = xd_.next()
                K.dma("sp", zt[:, :], zs[tok0:tok0 + 128, :], "L" + zt.key, writes=[zt.key])
                K.tt("pool", xd[:, :].rearrange("p (h q) -> p h q", q=64), xs3,
                     vb[:, 128:160].unsqueeze(2).to_broadcast([128, 32, 64]), ALU.mult, [xb.key, "vb"], [xd.key])
                K.tt("dve", ysb[:, :], ysb[:, :], xd[:, :], ALU.add, [ysb.key, xd.key], [ysb.key])
                K.tt("dve", ysb[:, :], ysb[:, :], zt[:, :], ALU.mult, [ysb.key, zt.key], [ysb.key])
                for g in range(4):
                    K.act(junk[:, :], ysb[:, g * 512:(g + 1) * 512], AF.Square, [ysb.key], ["sjunk"], accw=[k_],
                          accum_out=sm[:, 7, g:g + 1])
                K.act(sm[:, 7, 4:8], sm[:, 7, 0:4], AF.Sqrt, [k_, "epsb"], [k_], scale=1.0 / 512, bias=epsb[:, 0:1])
                K.op("dve", lambda e, sm=sm: e.reciprocal(sm[:, 7, 8:12], sm[:, 7, 4:8]), [k_], [k_])
                for g in range(4):
                    K.stt("dve", ynb[:, g * 512:(g + 1) * 512], ysb[:, g * 512:(g + 1) * 512], sm[:, 7, 8 + g:9 + g],
                          nwbc[:, g * 512:(g + 1) * 512], ALU.mult, ALU.mult, [ysb.key, k_, "nwbc"], accw=[ynb.key])
                for half in range(2):
                    pk = half
                    tb = psum[pk][:].bitcast(BF16)
                    for jj in range(8):
                        j = half * 8 + jj
                        K.op("pe", lambda e, tb=tb, jj=jj, j=j, ynb=ynb: e.transpose(
                            tb[:, jj * 128:(jj + 1) * 128], ynb[:, j * 128:(j + 1) * 128], idb[:]),
                            reads=[ynb.key, "idb"], writes=[f"ps#{pk}"], inc=(jj == 7))
                    K.cp("act", yT[:, half * 8:(half + 1) * 8, :].rearrange("p j t -> p (j t)"), tb[:, :], [f"ps#{pk}"],
                         accw=[yT.key])
                K.dma("sp", mixT1[:, tok0:tok0 + 128].rearrange("(j p) t -> p j t", p=128), yT[:], "S" + yT.key,
                      reads=[yT.key], accw=["mixT1"])

            for d in range(2):
                order = [0, 1] + lat_chunks if d == 0 else [1, 0] + lat_chunks[::-1]
                K.op("pool", lambda e: e.memset(state[:], 0.0), writes=["state"])
                K.op("pool", lambda e: e.memset(stbf[:], 0.0), writes=["stbf"])
                n_ = len(order)
                cxs = {0: stage_a(d, order[0])}
                if n_ > 1:
                    cxs[1] = stage_a(d, order[1])
                stage_a1(d, cxs[0])
                for i in range(n_):
                    if i + 2 < n_:
                        cxs[i + 2] = stage_a(d, order[i + 2])
                    if i + 1 < n_:
                        stage_a1(d, cxs[i + 1])
                    stage_b(d, cxs.pop(i))
            K.barrier()

    if "ssd" not in P.dbg.get("skip", ()):
        phase_ssd()
    if stop_after == "ssd":
        K.barrier(); pes.close(); P.es.close(); return P

    nt1 = P.dbg.get("out1_tiles", SEQ // 128)
    tiles1 = [(CTX + i * 128, i * 128, 0) for i in range(nt1)]
    phase_out(1, mixT1, 16, o_w_out, h1, out_h, tiles1)

    K.barrier()
    pes.close()
    P.es.close()
    return P


def _rope_tables():
    n_freq = 16
    inv = (10000.0 ** (-np.arange(n_freq, dtype=np.float32) / np.float32(n_freq))).astype(np.float32)
    t = np.arange(SEQ)
    row = (t // 64).astype(np.float32)
    col = (t % 64).astype(np.float32)
    ang = np.concatenate([row[:, None] * inv, col[:, None] * inv], axis=-1).astype(np.float32)
    cos, sin = np.cos(ang).astype(np.float32), np.sin(ang).astype(np.float32)
    tab = np.zeros((2, 128, T), np.float32)
    tab[0, :, :CTX] = 1.0
    for p in range(128):
        d = p % 64
        fi = (d % 16) + 16 * (d // 32)
        sgn = -1.0 if (d % 32) < 16 else 1.0
        tab[0, p, CTX:] = cos[:, fi]
        tab[1, p, CTX:] = sgn * sin[:, fi]
    return tab


def _rope_perm():
    perm = np.zeros(512, np.int64)
    for f in range(512):
        d = f % 64
        e = d % 32
        e2 = e + 16 if e < 16 else e - 16
        perm[f] = f - d + (d // 32) * 32 + e2
    return perm


def make_in_maps(inp):
    B = inp["x"].shape[0]
    perm = _rope_perm()
    w = np.asarray(inp["e_w_in"][0], np.float32)
    w_aug = np.ascontiguousarray(np.concatenate([w, w[:, 1024 + perm], w[:, 1536 + perm]], axis=1))
    tab = _rope_tables()
    ident = np.eye(128, dtype=np.float32)
    wbd = np.zeros((2, 2, 4, 128, 128), np.float32)
    for d in range(2):
        for g, nm in enumerate(("lru_w_r", "lru_w_i")):
            wsrc = np.asarray(inp[nm][0][d], np.float32)
            for cc in range(4):
                wbd[d, g, cc, 0:64, 0:64] = wsrc[2 * cc]
                wbd[d, g, cc, 64:128, 64:128] = wsrc[2 * cc + 1]
    wbd = np.ascontiguousarray(wbd.reshape(16, 128, 128))
    lvec = np.ascontiguousarray(np.concatenate([
        np.asarray(inp["lru_conv_w"][0], np.float32), np.asarray(inp["lru_conv_b"], np.float32).reshape(1, 512),
        np.asarray(inp["lru_b_r"][0], np.float32), np.asarray(inp["lru_b_i"][0], np.float32),
        np.asarray(inp["lru_lambda"][0], np.float32)], axis=0))
    ssd_cv = np.ascontiguousarray(np.concatenate([np.asarray(inp["ssd_conv_w"][0], np.float32),
                                                  np.asarray(inp["ssd_conv_b"], np.float32).reshape(1, 3072)], 0))
    ssd_vec = np.ascontiguousarray(np.concatenate([np.asarray(inp["ssd_a_log"][0], np.float32).reshape(-1),
                                                   np.asarray(inp["ssd_dt_bias"][0], np.float32).reshape(-1),
                                                   np.asarray(inp["ssd_d"][0], np.float32).reshape(-1)]).reshape(1, 160))
    ii = np.arange(128)
    maskT = np.stack([(ii[None, :] >= ii[:, None]), (ii[None, :] <= ii[:, None])], 0).astype(np.float32)
    selc = np.zeros((64, 32, 128), np.float32)
    for hh in range(32):
        selc[hh, hh, :] = 1.0
        selc[32 + hh, hh, :] = 1.0
    selc = np.ascontiguousarray(selc.reshape(64, 32 * 128))
    maps = []
    for b in range(B):
        m = {
            "src0": np.ascontiguousarray(np.concatenate([inp["ctx"][b], inp["x"][b]], axis=0), dtype=np.float32),
            "cvec": np.ascontiguousarray(np.stack([inp["c"][b], inp["c_ctx"]], 0), dtype=np.float32),
            "w_mod": np.asarray(inp["w_mod"], np.float32),
            "b_mod": np.asarray(inp["b_mod"], np.float32),
            "g_pre": np.asarray(inp["g_pre"], np.float32),
            "g_post": np.asarray(inp["g_post"], np.float32),
            "e_w_in_aug": w_aug,
            "e_w_out": np.asarray(inp["e_w_out"][0], np.float32),
            "ident": ident,
            "ropetab": tab,
            "lru_wbd": wbd,
            "lru_vec": lvec,
            "da_lam": np.ascontiguousarray(np.asarray(inp["da_lambda"][0], np.float32).reshape(1, 256)),
            "da_sub": np.ascontiguousarray(np.asarray(inp["da_subln"][0], np.float32).reshape(1, 128)),
            "o_w_in": np.asarray(inp["o_w_in"][0], np.float32),
            "o_w_out": np.asarray(inp["o_w_out"][0], np.float32),
            "ssd_cv": ssd_cv,
            "ssd_vec": ssd_vec,
            "ssd_norm": np.asarray(inp["ssd_norm"][0], np.float32),
            "maskT": maskT,
            "selc": selc,
        }
        maps.append(m)
    return maps


def kernel(**inp):
    P = build_program()
    maps = make_in_maps(inp)
    res = run_bass_kernel_spmd(P.nc, maps, core_ids=list(range(8)))
    return np.stack([np.asarray(r["out"], np.float32) for r in res.results], 0)
```

```python
import os
from contextlib import ExitStack
import numpy as np
import concourse.bass as bass
import concourse.mybir as mybir
from concourse.bass_utils import run_bass_kernel_spmd

F32, BF16 = mybir.dt.float32, mybir.dt.bfloat16
AF = mybir.ActivationFunctionType
ALU = mybir.AluOpType
AX = mybir.AxisListType

D = 1024
SEQ = 4096
CTX = 256
T = SEQ + CTX
NT = T // 128
EPS = 1e-6


class Sched:
    ENG = ("pe", "dve", "act", "pool", "sp")

    def __init__(self, nc, es):
        self.nc, self.es = nc, es
        self.e = {"pe": nc.tensor, "dve": nc.vector, "act": nc.scalar, "pool": nc.gpsimd, "sp": nc.sync}
        self.sem, self.cnt = {}, {}
        for n in self.ENG:
            self.sem[n] = es.enter_context(nc.semaphore("s_" + n))
            self.cnt[n] = 0
        self.seen = {n: {} for n in self.ENG}
        self.W, self.Rd = {}, {}
        self.pend = {n: [] for n in self.ENG}
        self.nwait = 0
        self.nins = 0

    def _wait(self, eng, need):
        for s, v in need.items():
            if s == "pe" and eng == "pe":
                continue
            if self.seen[eng].get(s, 0) >= v:
                continue
            self.e[eng].wait_ge(self.sem[s], v)
            self.seen[eng][s] = v
            self.nwait += 1

    def _deps(self, eng, reads, writes, accw):
        need = {}

        def add(d):
            for s, v in d.items():
                if need.get(s, 0) < v:
                    need[s] = v
        for r in reads:
            add(self.W.get(r, {}))
        for w in writes:
            add(self.W.get(w, {}))
            add(self.Rd.get(w, {}))
        for w in accw:
            add(self.Rd.get(w, {}))
        self._wait(eng, need)

    def _register(self, ev, reads, writes, accw):
        s, v = ev
        for r in reads:
            d = self.Rd.setdefault(r, {})
            d[s] = max(d.get(s, 0), v)
        for w in writes:
            self.W[w] = {s: v}
            self.Rd[w] = {}
        for w in accw:
            d = self.W.setdefault(w, {})
            d[s] = max(d.get(s, 0), v)

    def op(self, eng, fn, reads=(), writes=(), accw=(), inc=True):
        self._deps(eng, reads, writes, accw)
        ins = fn(self.e[eng])
        self.nins += 1
        if inc:
            self.cnt[eng] += 1
            ins.then_inc(self.sem[eng], 1)
            ev = (eng, self.cnt[eng])
            for (r, w, a) in self.pend[eng]:
                self._register(ev, r, w, a)
            self.pend[eng] = []
            self._register(ev, reads, writes, accw)
        else:
            self.pend[eng].append((tuple(reads), tuple(writes), tuple(accw)))

    def dma(self, q, out, in_, semkey, reads=(), writes=(), accw=(), **kw):
        if semkey not in self.sem:
            self.sem[semkey] = self.es.enter_context(self.nc.semaphore("d_" + semkey.replace("#", "_")))
            self.cnt[semkey] = 0
        self._deps(q, reads, writes, accw)
        ins = self.e[q].dma_start(out=out, in_=in_, **kw)
        ins.then_inc(self.sem[semkey], 16)
        self.cnt[semkey] += 16
        self.nins += 1
        self._register((semkey, self.cnt[semkey]), reads, writes, accw)

    def tt(self, eng, out, a, b, op, reads, writes=(), accw=()):
        self.op(eng, lambda e: e.tensor_tensor(out, a, b, op), reads, writes, accw)

    def ts(self, eng, out, a, s1, s2, op0, op1=None, reads=(), writes=(), accw=()):
        if op1 is None:
            self.op(eng, lambda e: e.tensor_scalar(out, a, s1, None, op0), reads, writes, accw)
        else:
            self.op(eng, lambda e: e.tensor_scalar(out, a, s1, s2, op0, op1), reads, writes, accw)

    def stt(self, eng, out, a, sc, b, op0, op1, reads, writes=(), accw=()):
        self.op(eng, lambda e: e.scalar_tensor_tensor(out, a, sc, b, op0, op1), reads, writes, accw)

    def act(self, out, in_, func, reads, writes=(), accw=(), **kw):
        self.op("act", lambda e: e.activation(out=out, in_=in_, func=func, **kw), reads, writes, accw)

    def cp(self, eng, out, in_, reads, writes=(), accw=()):
        if eng == "act":
            self.op("act", lambda e: e.copy(out, in_), reads, writes, accw)
        else:
            self.op(eng, lambda e: e.tensor_copy(out, in_), reads, writes, accw)

    def mm(self, out, lhsT, rhs, start, stop, reads, writes, inc):
        self.op("pe", lambda e: e.matmul(out, lhsT, rhs, start=start, stop=stop), reads, writes, inc=inc)

    def barrier(self):
        for n in self.ENG:
            assert not self.pend[n]
        allev = {s: c for s, c in self.cnt.items() if c > 0}
        for n in self.ENG:
            self._wait(n, allev)
        self.W, self.Rd = {}, {}


class Buf:
    def __init__(self, t, key):
        self.t, self.key = t, key

    def __getitem__(self, k):
        return self.t[k]


class Ring:
    def __init__(self, alloc, name, n, shape, dtype):
        self.bufs = [Buf(alloc(f"{name}{i}", shape, dtype), f"{name}#{i}") for i in range(n)]
        self.i = 0

    def next(self):
        b = self.bufs[self.i % len(self.bufs)]
        self.i += 1
        return b


class Prog:
    def __init__(self, dbg=None):
        self.dbg = dbg or {}
        self.nc = nc = bass.Bass("TRN2", target_bir_lowering=False)
        self.es = ExitStack()
        self.K = Sched(nc, self.es)
        self.dram = {}
        self.uid = 0

    def din(self, name, shape, dt=F32):
        self.dram[name] = self.nc.dram_tensor(name, list(shape), dt, kind="ExternalInput").ap()
        return self.dram[name]

    def dout(self, name, shape, dt=F32):
        self.dram[name] = self.nc.dram_tensor(name, list(shape), dt, kind="ExternalOutput").ap()
        return self.dram[name]

    def dscr(self, name, shape, dt):
        if name in self.dbg.get("dump", ()):
            return self.dout(name, shape, dt)
        self.dram[name] = self.nc.dram_tensor(name, list(shape), dt).ap()
        return self.dram[name]


def build_program(dbg=None):
    P = Prog(dbg)
    nc, K = P.nc, P.K
    stop_after = P.dbg.get("stop_after", "all")

    src0 = P.din("src0", [T, D])
    cvec = P.din("cvec", [2, D])
    w_mod = P.din("w_mod", [2, D, 3 * D])
    b_mod = P.din("b_mod", [2, 3 * D])
    g_pre = P.din("g_pre", [2, D])
    g_post = P.din("g_post", [2, D])
    e_w_in = P.din("e_w_in_aug", [D, 4096])
    e_w_out = P.din("e_w_out", [D, D])
    ident = P.din("ident", [128, 128])
    ropetab = P.din("ropetab", [2, 128, T])
    out_h = P.dout("out", [SEQ, D])
    lru_wbd = P.din("lru_wbd", [16, 128, 128])
    lru_vec = P.din("lru_vec", [11, 512])
    da_lam = P.din("da_lam", [1, 256])
    da_sub = P.din("da_sub", [1, 128])
    mixT0 = P.dscr("mixT0", [D, T], BF16)
    o_w_in = P.din("o_w_in", [D, 5184])
    o_w_out = P.din("o_w_out", [2048, D])
    ssd_cv = P.din("ssd_cv", [5, 3072])
    ssd_vec = P.din("ssd_vec", [1, 160])
    ssd_norm = P.din("ssd_norm", [2048])
    maskT = P.din("maskT", [2, 128, 128])
    selc = P.din("selc", [64, 32 * 128])
    xbcT = P.dscr("xbcT", [3072, T], BF16)
    zs = P.dscr("zs", [T, 2048], BF16)
    dtraw = P.dscr("dtraw", [T, 64], F32)
    xsB = P.dscr("xsB", [T, 2560], BF16)
    bcT = P.dscr("bcT", [1024, T], BF16)
    yfw = P.dscr("yfw", [T, 2048], F32)
    mixT1 = P.dscr("mixT1", [2048, T], BF16)
    h1 = P.dscr("h1", [T, D], F32)

    xrT = P.dscr("xrT", [512, T], F32)
    grT = P.dscr("grT", [512, T], BF16)
    gdT = P.dscr("gdT", [512, T], BF16)
    qT = P.dscr("qT", [512, T], BF16)
    kT = P.dscr("kT", [512, T], BF16)
    vtok = P.dscr("vtok", [T, 512], BF16)

    pes = ExitStack()
    def palloc(name, shape, dt):
        return pes.enter_context(nc.sbuf_tensor(name, list(shape), dt))
    psum = [pes.enter_context(nc.psum_tensor(f"psb{i}", [128, 512], F32)) for i in range(8)]
    idf = palloc("idf", [128, 128], F32)
    idb = palloc("idb", [128, 128], BF16)
    epsb = palloc("epsb", [128, 1], F32)
    modA = palloc("modA", [128, 2, 8, 2], F32)
    modS = palloc("modS", [128, 2, 8, 2], F32)
    ggbc = palloc("ggbc", [128, 2, 2, D], F32)

    K.dma("sp", idf[:], ident[:, :], "Lidf", writes=["idf"])
    K.op("dve", lambda e: e.tensor_copy(idb[:], idf[:]), reads=["idf"], writes=["idb"])
    K.op("dve", lambda e: e.memset(epsb[:], EPS), writes=["epsb"])

    def phase_mod():
        with ExitStack() as es:
            def alloc(name, shape, dt):
                P.uid += 1
                return es.enter_context(nc.sbuf_tensor(f"{name}_{P.uid}", list(shape), dt))
            cT = alloc("cT", [128, 2, 8], F32)
            sig = alloc("sig", [128, 2, 8], F32)
            srep = alloc("srep", [128, 2, 8, 128], F32)
            bT = alloc("bT", [128, 2, 24], F32)
            gpT = alloc("gpT", [128, 2, 8], F32)
            bgbc = alloc("bgbc", [128, 2, D], F32)
            gpbc = alloc("gpbc", [128, 2, D], F32)
            wst = Ring(alloc, "wst", 2, [128, 8, 512], F32)
            tmp = alloc("mtmp", [128, 16, 2], F32)

            rows = alloc("rows", [80, 128], F32)
            K.dma("sp", rows[0:16, :], cvec.rearrange("t (j p) -> (t j) p", p=128), "Lrows", accw=["rows"])
            K.dma("sp", rows[16:64, :], b_mod.rearrange("l (f p) -> (l f) p", p=128), "Lrows", accw=["rows"])
            K.dma("sp", rows[64:80, :], g_pre.rearrange("l (j p) -> (l j) p", p=128), "Lrows", accw=["rows"])
            K.op("pe", lambda e: e.transpose(psum[4][:, 0:80], rows[:, :], idf[0:80, 0:80]),
                 reads=["rows", "idf"], writes=["ps#4"])
            K.op("dve", lambda e: e.tensor_copy(cT[:].rearrange("p t j -> p (t j)"), psum[4][:, 0:16]),
                 reads=["ps#4"], writes=["cT"])
            K.op("dve", lambda e: e.tensor_copy(bT[:].rearrange("p l f -> p (l f)"), psum[4][:, 16:64]),
                 reads=["ps#4"], writes=["bT"])
            K.op("dve", lambda e: e.tensor_copy(gpT[:].rearrange("p l j -> p (l j)"), psum[4][:, 64:80]),
                 reads=["ps#4"], writes=["gpT"])
            for l in range(2):
                K.dma("sp", bgbc[:, l, :], b_mod[l, 2 * D:3 * D].partition_broadcast(128), "Lbgbc", accw=["bgbc"])
                K.dma("sp", gpbc[:, l, :], g_post[l, :].partition_broadcast(128), "Lgpbc", accw=["gpbc"])
            K.op("act", lambda e: e.activation(out=sig[:], in_=cT[:], func=AF.Sigmoid), reads=["cT"], writes=["sig"])
            K.op("dve", lambda e: e.tensor_tensor(cT[:], cT[:], sig[:], ALU.mult), reads=["sig", "cT"], writes=["cT"])
            for t in range(2):
                K.op("dve", lambda e, t=t: e.tensor_copy(
                    srep[:, t, :, :], cT[:, t, :].unsqueeze(2).to_broadcast([128, 8, 128])),
                    reads=["cT"], accw=["srep"])
            for l in range(2):
                for pc in range(6):
                    wb = wst.next()
                    K.dma("sp", wb[:], w_mod[l, :, pc * 512:(pc + 1) * 512].rearrange("(j p) n -> p j n", p=128),
                          "L" + wb.key, writes=[wb.key])
                    if pc < 4:
                        pst = psum[pc % 2]
                        for f in range(4):
                            for j in range(8):
                                K.op("pe", lambda e, f=f, j=j, pst=pst, wb=wb: e.matmul(
                                    pst[:, 2 * f:2 * f + 2], wb[:, j, f * 128:(f + 1) * 128], cT[:, :, j],
                                    start=(j == 0), stop=(j == 7)),
                                    reads=[wb.key, "cT"], writes=[f"ps#{pc % 2}"], inc=(j == 7 and f == 3))
                        K.op("dve", lambda e, pst=pst, pc=pc: e.tensor_copy(
                            tmp[:, pc * 4:(pc + 1) * 4, :], pst[:, 0:8].rearrange("p (f t) -> p f t", t=2)),
                            reads=[f"ps#{pc % 2}"], accw=["mtmp"])
                    else:
                        for t in range(2):
                            pst = psum[2 + t]
                            for j in range(8):
                                K.op("pe", lambda e, j=j, t=t, pst=pst, wb=wb: e.matmul(
                                    pst[:, :], srep[:, t, j, :], wb[:, j, :], start=(j == 0), stop=(j == 7)),
                                    reads=[wb.key, "srep"], writes=[f"ps#{2 + t}"], inc=(j == 7))
                            c0 = (pc - 4) * 512
                            K.op("dve", lambda e, t=t, l=l, c0=c0, pst=pst: e.tensor_tensor(
                                ggbc[:, l, t, c0:c0 + 512], pst[:, :], bgbc[:, l, c0:c0 + 512], ALU.add),
                                reads=[f"ps#{2 + t}", "bgbc"], accw=["ggbc"])
                for t in range(2):
                    K.op("dve", lambda e, t=t, l=l: e.tensor_tensor(
                        modS[:, l, :, t], tmp[:, 0:8, t], bT[:, l, 0:8], ALU.add),
                        reads=["mtmp", "bT"], accw=["modS"])
                    K.op("dve", lambda e, t=t, l=l: e.scalar_tensor_tensor(
                        modA[:, l, :, t], tmp[:, 8:16, t], 1.0, bT[:, l, 8:16], ALU.add, ALU.add),
                        reads=["mtmp", "bT"], accw=["modA"])
                    K.op("dve", lambda e, t=t, l=l: e.tensor_tensor(
                        modA[:, l, :, t], modA[:, l, :, t], gpT[:, l, :], ALU.mult),
                        reads=["modA", "gpT"], writes=["modA"])
                    K.op("dve", lambda e, t=t, l=l: e.tensor_tensor(
                        ggbc[:, l, t, :], ggbc[:, l, t, :], gpbc[:, l, :], ALU.mult),
                        reads=["ggbc", "gpbc"], writes=["ggbc"])
            K.barrier()

    phase_mod()
    if "mod" in P.dbg.get("dump", ()):
        dA = P.dout("dbg_modA", [128, 32]); dS = P.dout("dbg_modS", [128, 32]); dG = P.dout("dbg_gg", [128, 4 * D])
        K.dma("sp", dA[:, :], modA[:].rearrange("p l j t -> p (l j t)"), "Sdbg", reads=["modA"])
        K.dma("sp", dS[:, :], modS[:].rearrange("p l j t -> p (l j t)"), "Sdbg", reads=["modS"])
        K.dma("sp", dG[:, :], ggbc[:].rearrange("p l t f -> p (l t f)"), "Sdbg", reads=["ggbc"])
    if stop_after == "mod":
        K.barrier(); pes.close(); P.es.close(); return P

    def phase_proj(layer, src, Wd, ncols, fspecs, tspecs, extra_alloc=None, per_group=None):
        with ExitStack() as es:
            def alloc(name, shape, dt):
                P.uid += 1
                return es.enter_context(nc.sbuf_tensor(f"{name}_{P.uid}", list(shape), dt))
            Wb = alloc("Wb", [128, 8, ncols], BF16)
            wst = Ring(alloc, "wst", 2, [128, 8, 256], F32)
            xr_ = Ring(alloc, "xin", 4, [128, D], F32)
            xn_ = Ring(alloc, "xn", 4, [128, D], BF16)
            uT_ = Ring(alloc, "uT", 2, [128, 8, 512], BF16)
            junk = alloc("junk", [128, D], BF16)
            stat = Ring(alloc, "stat", 4, [128, 4], F32)
            ctxo = extra_alloc(alloc) if extra_alloc else None
            ceng = ["dve", "pool", "act"]
            for pc in range(ncols // 256 + (1 if ncols % 256 else 0)):
                c0 = pc * 256
                cw = min(256, ncols - c0)
                wb = wst.next()
                K.dma("sp", wb[:, :, 0:cw], Wd[:, c0:c0 + cw].rearrange("(j p) n -> p j n", p=128),
                      "L" + wb.key, writes=[wb.key])
                en = ceng[pc % 3]
                if en == "act":
                    K.op("act", lambda e, wb=wb, c0=c0, cw=cw: e.copy(Wb[:, :, c0:c0 + cw], wb[:, :, 0:cw]),
                         reads=[wb.key], accw=["Wb"])
                else:
                    K.op(en, lambda e, wb=wb, c0=c0, cw=cw: e.tensor_copy(Wb[:, :, c0:c0 + cw], wb[:, :, 0:cw]),
                         reads=[wb.key], accw=["Wb"])
            groups = [(0, CTX, 1)] + [(CTX + 512 * g, 512, 0) for g in range(SEQ // 512)]
            pst_i = [0]
            pso_i = [0]

            def front_parts(gi):
                tok0, ntok, tmod = groups[gi]
                uT = uT_.next()
                p1s, p2s = [], []
                for ti in range(ntok // 128):
                    def part1(ti=ti):
                        box = {}
                        xt = xr_.next(); xn = xn_.next(); st = stat.next()
                        box["xn"] = xn
                        K.dma("sp", xt[:], src[tok0 + ti * 128: tok0 + (ti + 1) * 128, :], "L" + xt.key, writes=[xt.key])
                        K.op("act", lambda e: e.activation(out=junk[:], in_=xt[:], func=AF.Square, accum_out=st[:, 0:1]),
                             reads=[xt.key], writes=["junk", st.key])
                        K.op("act", lambda e: e.activation(out=st[:, 1:2], in_=st[:, 0:1], func=AF.Sqrt,
                                                           scale=1.0 / D, bias=epsb[:, 0:1]),
                             reads=[st.key, "epsb"], writes=[st.key])
                        K.op("dve", lambda e: e.reciprocal(st[:, 2:3], st[:, 1:2]), reads=[st.key], writes=[st.key])
                        K.op("pool", lambda e: e.tensor_scalar(xn[:], xt[:], st[:, 2:3], None, ALU.mult),
                             reads=[xt.key, st.key], writes=[xn.key])
                        return box

                    def part2(box, ti=ti):
                        xn = box["xn"]
                        pk = 6 + (pst_i[0] % 2); pst_i[0] += 1
                        pst = psum[pk][:].bitcast(BF16)
                        for j in range(8):
                            K.op("pe", lambda e, j=j: e.transpose(
                                pst[:, j * 128:(j + 1) * 128], xn[:, j * 128:(j + 1) * 128], idb[:]),
                                reads=[xn.key, "idb"], writes=[f"ps#{pk}"], inc=(j == 7))
                        for j in range(8):
                            K.op("dve", lambda e, j=j: e.tensor_scalar(
                                uT[:, j, ti * 128:(ti + 1) * 128], pst[:, j * 128:(j + 1) * 128],
                                modA[:, layer, j, tmod:tmod + 1], modS[:, layer, j, tmod:tmod + 1], ALU.mult, ALU.add),
                                reads=[f"ps#{pk}", "modA", "modS"], accw=[uT.key])
                    p1s.append(part1); p2s.append(part2)
                return uT, p1s, p2s

            def mm(gi, uT, hooks):
                tok0, ntok, tmod = groups[gi]
                if per_group:
                    per_group(ctxo, gi, tok0, ntok)
                nb_tot = sum(len(b) for (_, b, _) in fspecs)
                nhk = max(1, len(hooks))
                step = max(1, nb_tot // nhk)
                bcount = 0
                hooks = list(hooks)
                for (name, bundles, epi) in fspecs:
                    for bi, cols in enumerate(bundles):
                        if hooks and bcount % step == 0:
                            hooks.pop(0)()
                        bcount += 1
                        pks = []
                        for c0 in cols:
                            pk = pso_i[0] % 6; pso_i[0] += 1
                            pks.append(pk)
                            for j in range(8):
                                K.op("pe", lambda e, j=j, pk=pk, c0=c0, uT=uT, ntok=ntok: e.matmul(
                                    psum[pk][:, 0:ntok], Wb[:, j, c0:c0 + 128], uT[:, j, 0:ntok],
                                    start=(j == 0), stop=(j == 7)),
                                    reads=["Wb", uT.key], writes=[f"ps#{pk}"], inc=(j == 7))
                        epi(ctxo, pks, bi, tok0, ntok)
                for (c0, cw, epi) in tspecs:
                    for ti in range(ntok // 128):
                        pk = pso_i[0] % 6; pso_i[0] += 1
                        for j in range(8):
                            K.op("pe", lambda e, j=j, pk=pk, uT=uT, ti=ti: e.matmul(
                                psum[pk][:, 0:cw], uT[:, j, ti * 128:(ti + 1) * 128], Wb[:, j, c0:c0 + cw],
                                start=(j == 0), stop=(j == 7)),
                                reads=["Wb", uT.key], writes=[f"ps#{pk}"], inc=(j == 7))
                        epi(ctxo, pk, tok0 + ti * 128)
                while hooks:
                    hooks.pop(0)()

            ng = P.dbg.get("ngroups", len(groups))
            uT0, p1s, p2s = front_parts(0)
            for p1, p2 in zip(p1s, p2s):
                p2(p1())
            cur = uT0
            for gi in range(ng):
                hooks = []
                nxt = None
                if gi + 1 < ng:
                    nxt, p1s, p2s = front_parts(gi + 1)
                    boxes = {}
                    def mk(k, p1s=p1s, p2s=p2s, boxes=boxes):
                        def h():
                            if k == 0:
                                for kk in range(min(2, len(p1s))):
                                    boxes[kk] = p1s[kk]()
                                return
                            if 0 <= k - 1 < len(p1s):
                                p2s[k - 1](boxes[k - 1])
                            if k + 1 < len(p1s):
                                boxes[k + 1] = p1s[k + 1]()
                        return h
                    hooks = [mk(k) for k in range(len(p1s) + 1)]
                mm(gi, cur, hooks)
                cur = nxt
            K.barrier()

    def l0_alloc(alloc):
        c = {}
        c["sf"] = Ring(alloc, "sf", 3, [128, 512], F32)
        c["sb"] = Ring(alloc, "sb", 4, [128, 512], BF16)
        c["t1"] = Ring(alloc, "t1", 2, [128, 512], F32)
        c["t2"] = Ring(alloc, "t2", 2, [128, 512], F32)
        c["tab"] = Ring(alloc, "tab", 2, [128, 2, 512], F32)
        return c

    def l0_group(c, gi, tok0, ntok):
        tb = c["tab"].next()
        c["curtab"] = tb
        K.dma("sp", tb[:, :, 0:ntok], ropetab[:, :, tok0:tok0 + ntok].rearrange("c p t -> p c t"),
              "L" + tb.key, writes=[tb.key])

    def epi_copy_f32(dst):
        def f(c, pks, bi, tok0, ntok):
            pk = pks[0]; b = c["sf"].next()
            K.op("act", lambda e: e.copy(b[:, 0:ntok], psum[pk][:, 0:ntok]), reads=[f"ps#{pk}"], writes=[b.key])
            K.dma("sp", dst[bi * 128:(bi + 1) * 128, tok0:tok0 + ntok], b[:, 0:ntok], "S" + b.key,
                  reads=[b.key], accw=[dst.name])
        return f

    def epi_silu_bf(dst):
        def f(c, pks, bi, tok0, ntok):
            pk = pks[0]; b = c["sb"].next()
            K.op("act", lambda e: e.activation(out=b[:, 0:ntok], in_=psum[pk][:, 0:ntok], func=AF.Silu),
                 reads=[f"ps#{pk}"], writes=[b.key])
            K.dma("sp", dst[bi * 128:(bi + 1) * 128, tok0:tok0 + ntok], b[:, 0:ntok], "S" + b.key,
                  reads=[b.key], accw=[dst.name])
        return f

    def epi_rope(dst):
        def f(c, pks, bi, tok0, ntok):
            pa, pb = pks; t1 = c["t1"].next(); t2 = c["t2"].next(); b = c["sb"].next(); tb = c["curtab"]
            K.op("dve", lambda e: e.tensor_tensor(t1[:, 0:ntok], psum[pa][:, 0:ntok], tb[:, 0, 0:ntok], ALU.mult),
                 reads=[f"ps#{pa}", tb.key], writes=[t1.key])
            K.op("dve", lambda e: e.tensor_tensor(t2[:, 0:ntok], psum[pb][:, 0:ntok], tb[:, 1, 0:ntok], ALU.mult),
                 reads=[f"ps#{pb}", tb.key], writes=[t2.key])
            K.op("pool", lambda e: e.tensor_tensor(b[:, 0:ntok], t1[:, 0:ntok], t2[:, 0:ntok], ALU.add),
                 reads=[t1.key, t2.key], writes=[b.key])
            K.dma("sp", dst[bi * 128:(bi + 1) * 128, tok0:tok0 + ntok], b[:, 0:ntok], "S" + b.key,
                  reads=[b.key], accw=[dst.name])
        return f

    def epi_v(c, pk, tok0):
        b = c["sb"].next()
        K.op("act", lambda e: e.copy(b[:, :], psum[pk][:, :]), reads=[f"ps#{pk}"], writes=[b.key])
        K.dma("sp", vtok[tok0:tok0 + 128, :], b[:, :], "S" + b.key, reads=[b.key], accw=["vtok"])

    l0_f = [
        ("xr", [[f * 128] for f in range(0, 4)], epi_copy_f32(xrT)),
        ("gr", [[f * 128] for f in range(4, 8)], epi_silu_bf(grT)),
        ("q", [[1024 + f * 128, 3072 + f * 128] for f in range(4)], epi_rope(qT)),
        ("k", [[1536 + f * 128, 3584 + f * 128] for f in range(4)], epi_rope(kT)),
        ("gd", [[f * 128] for f in range(20, 24)], epi_silu_bf(gdT)),
    ]
    l0_t = [(2048, 512, epi_v)]
    phase_proj(0, src0, e_w_in, 4096, l0_f, l0_t, l0_alloc, l0_group)
    if stop_after == "proj0":
        K.barrier(); pes.close(); P.es.close(); return P


    LAMBDA_INIT0 = 0.8 - 0.6 * 1.0

    def phase_lru():
        with ExitStack() as es:
            def alloc(name, shape, dt):
                P.uid += 1
                return es.enter_context(nc.sbuf_tensor(f"{name}_{P.uid}", list(shape), dt))
            rows = alloc("lrows", [44, 128], F32)
            pv = alloc("lpv", [128, 11, 4], F32)
            coef = alloc("lcoef", [128, 2, 4], F32)
            ones1 = alloc("ones1", [128, 1], F32)
            wbf = alloc("wbf", [128, 16, 128], F32)
            wbb = alloc("wbb", [128, 16, 128], BF16)
            big = Ring(alloc, "big", 7, [128, T], F32)
            xcb = alloc("xcb", [128, T], BF16)
            grs = alloc("grs", [128, T], BF16)
            mo = alloc("mixo", [128, T], BF16)
            K.op("dve", lambda e: e.memset(ones1[:], 1.0), writes=["ones1"])
            K.dma("sp", rows[:, :], lru_vec.rearrange("v (c p) -> (v c) p", p=128), "Lrows", writes=["lrows"])
            K.op("pe", lambda e: e.transpose(psum[0][:, 0:44], rows[:, :], idf[0:44, 0:44]),
                 reads=["lrows", "idf"], writes=["ps#0"])
            K.cp("dve", pv[:].rearrange("p v c -> p (v c)"), psum[0][:, 0:44], ["ps#0"], ["lpv"])
            K.act(coef[:].rearrange("p d c -> p (d c)"), pv[:, 9:11, :].rearrange("p d c -> p (d c)"), AF.Exp,
                  ["lpv"], ["lcoef"], scale=-1.0)
            K.act(coef[:].rearrange("p d c -> p (d c)"), coef[:].rearrange("p d c -> p (d c)"), AF.Ln,
                  ["lcoef", "ones1"], ["lcoef"], bias=ones1[:, 0:1])
            K.ts("dve", coef[:].rearrange("p d c -> p (d c)"), coef[:].rearrange("p d c -> p (d c)"), -8.0, None,
                 ALU.mult, None, ["lcoef"], ["lcoef"])
            K.dma("sp", wbf[:], lru_wbd.rearrange("n k m -> k n m"), "Lwbf", writes=["wbf"])
            K.cp("dve", wbb[:], wbf[:], ["wbf"], ["wbb"])
            segs = [(0, CTX), (CTX, T)]
            blocks = [(b0, min(512, T - b0)) for b0 in range(0, T, 512)]
            for cc in range(4):
                x = big.next(); xc = big.next()
                K.dma("sp", x[:, :], xrT[cc * 128:(cc + 1) * 128, :], "L" + x.key, writes=[x.key])
                K.dma("sp", grs[:, :], grT[cc * 128:(cc + 1) * 128, :], "Lgrs", writes=["grs"])
                K.ts("dve", xc[:, :], x[:, :], pv[:, 2, cc:cc + 1], pv[:, 4, cc:cc + 1], ALU.mult, ALU.add,
                     [x.key, "lpv"], [xc.key])
                for (a, b) in segs:
                    for tap, sh in ((0, -2), (1, -1), (3, 1)):
                        lo = max(a, a - sh); hi = min(b, b - sh)
                        K.stt("dve", xc[:, lo:hi], x[:, lo + sh:hi + sh], pv[:, tap, cc:cc + 1],
                              xc[:, lo:hi], ALU.mult, ALU.add, [x.key, xc.key, "lpv"], [xc.key])
                K.cp("pool", xcb[:, :], xc[:, :], [xc.key], ["xcb"])
                hs = []
                for d in range(2):
                    rb = big.next(); ib = big.next()
                    for (b0, bn) in blocks:
                        for g, dstb, brow in ((0, rb, 5 + d), (1, ib, 7 + d)):
                            pk = (2 * (b0 // 512) + g) % 6
                            K.mm(psum[pk][:, 0:bn], wbb[:, (d * 2 + g) * 4 + cc, :], xcb[:, b0:b0 + bn], True, True,
                                 ["wbb", "xcb"], [f"ps#{pk}"], True)
                            K.act(dstb[:, b0:b0 + bn], psum[pk][:, 0:bn], AF.Sigmoid, [f"ps#{pk}", "lpv"], accw=[dstb.key],
                                  bias=pv[:, brow, cc:cc + 1])
                    K.ts("dve", rb[:, :], rb[:, :], coef[:, d, cc:cc + 1], None, ALU.mult, None, [rb.key, "lcoef"], [rb.key])
                    K.act(rb[:, :], rb[:, :], AF.Exp, [rb.key], [rb.key])
                    sq = big.next()
                    K.tt("pool", sq[:, :], rb[:, :], rb[:, :], ALU.mult, [rb.key], [sq.key])
                    K.act(sq[:, :], sq[:, :], AF.Sqrt, [sq.key, "ones1"], [sq.key], scale=-1.0, bias=ones1[:, 0:1])
                    K.tt("pool", ib[:, :], ib[:, :], xc[:, :], ALU.mult, [ib.key, xc.key], [ib.key])
                    K.tt("dve", ib[:, :], ib[:, :], sq[:, :], ALU.mult, [ib.key, sq.key], [ib.key])
                    h = sq
                    if d == 0:
                        K.op("dve", lambda e, h=h, rb=rb, ib=ib: e.tensor_tensor_scan(
                            h[:, :], rb[:, :], ib[:, :], 0.0, ALU.mult, ALU.add), [rb.key, ib.key, h.key], [h.key])
                    else:
                        K.op("dve", lambda e, h=h, rb=rb, ib=ib: e.tensor_tensor_scan(
                            h[:, CTX - 1::-1] if False else h[:, 0:CTX][:, ::-1], rb[:, 0:CTX][:, ::-1], ib[:, 0:CTX][:, ::-1],
                            0.0, ALU.mult, ALU.add), [rb.key, ib.key, h.key], [h.key])
                        K.op("dve", lambda e, h=h, rb=rb, ib=ib: e.tensor_tensor_scan(
                            h[:, CTX:T][:, ::-1], rb[:, CTX:T][:, ::-1], ib[:, CTX:T][:, ::-1],
                            h[:, 0:1], ALU.mult, ALU.add), [rb.key, ib.key, h.key], [h.key])
                    hs.append(h)
                K.tt("pool", hs[0][:, :], hs[0][:, :], hs[1][:, :], ALU.add, [hs[0].key, hs[1].key], [hs[0].key])
                K.tt("dve", mo[:, :], hs[0][:, :], grs[:, :], ALU.mult, [hs[0].key, "grs"], ["mixo"])
                K.dma("sp", mixT0[cc * 128:(cc + 1) * 128, :], mo[:, :], "Smixo", reads=["mixo"], accw=["mixT0"])
            K.barrier()

    if "lru" not in P.dbg.get("skip", ()):
        phase_lru()
    if stop_after == "lru":
        K.barrier(); pes.close(); P.es.close(); return P

    def phase_attn():
        with ExitStack() as es:
            def alloc(name, shape, dt):
                P.uid += 1
                return es.enter_context(nc.sbuf_tensor(f"{name}_{P.uid}", list(shape), dt))
            kres = alloc("kres", [128, 4, T], BF16)
            vres = alloc("vres", [128, NT, 512], BF16)
            onesb = alloc("onesb", [128, 128], BF16)
            onesf = alloc("onesf", [128, 128], F32)
            lrow = alloc("lamrow", [1, 260], F32)
            lamc = alloc("lamc", [128, 2], F32)
            subc = alloc("subc", [128, 2], F32)
            subrow = alloc("subrow", [1, 128], F32)
            qb_ = Ring(alloc, "qblk", 2, [128, 4, 512], BF16)
            gd_ = Ring(alloc, "gdblk", 2, [128, 512], BF16)
            E_ = Ring(alloc, "Eb", 6, [128, 512], BF16)
            f_ = Ring(alloc, "af", 6, [128, 512], F32)
            ob_ = Ring(alloc, "aob", 4, [128, 512], BF16)
            K.op("dve", lambda e: e.memset(onesb[:], 1.0), writes=["onesb"])
            K.op("dve", lambda e: e.memset(onesf[:], 1.0), writes=["onesf"])
            K.dma("sp", lrow[:, 0:256], da_lam[:, :], "Llam", writes=["lamrow"])
            K.dma("sp", subrow[:, :], da_sub[:, :], "Lsub", writes=["subrow"])
            K.tt("dve", lrow[:, 0:64], lrow[:, 0:64], lrow[:, 64:128], ALU.mult, ["lamrow"], ["lamrow"])
            K.tt("dve", lrow[:, 128:192], lrow[:, 128:192], lrow[:, 192:256], ALU.mult, ["lamrow"], ["lamrow"])
            K.op("dve", lambda e: e.reduce_sum(lrow[:, 256:257], lrow[:, 0:64], AX.X), ["lamrow"], ["lamrow"])
            K.op("dve", lambda e: e.reduce_sum(lrow[:, 257:258], lrow[:, 128:192], AX.X), ["lamrow"], ["lamrow"])
            K.act(lrow[:, 256:258], lrow[:, 256:258], AF.Exp, ["lamrow"], ["lamrow"])
            K.tt("dve", lrow[:, 258:259], lrow[:, 256:257], lrow[:, 257:258], ALU.subtract, ["lamrow"], ["lamrow"])
            K.ts("dve", lrow[:, 258:259], lrow[:, 258:259], -1.0, -LAMBDA_INIT0, ALU.mult, ALU.add, ["lamrow"], ["lamrow"])
            K.mm(psum[0][:, 0:1], onesf[0:1, :], lrow[0:1, 258:259], True, True, ["onesf", "lamrow"], ["ps#0"], True)
            K.cp("dve", lamc[:, 0:1], psum[0][:, 0:1], ["ps#0"], ["lamc"])
            K.op("pe", lambda e: e.transpose(psum[1][:, 0:1], subrow[0:1, :], idf[0:1, 0:1]),
                 reads=["subrow", "idf"], writes=["ps#1"])
            K.ts("dve", subc[:, 0:1], psum[1][:, 0:1], 1.0 - LAMBDA_INIT0, None, ALU.mult, None, ["ps#1"], ["subc"])
            for h in range(4):
                K.dma("sp", kres[:, h, :], kT[h * 128:(h + 1) * 128, :], "Lkres", accw=["kres"])
            for n0 in range(0, NT, 2):
                K.dma("sp", vres[:, n0:n0 + 2, :], vtok[n0 * 128:(n0 + 2) * 128, :].rearrange("(n p) e -> p n e", p=128),
                      "Lvres", accw=["vres"])
            qblocks = [(0, CTX, 0, 2)] + [(CTX + 512 * g, 512, 0, NT) for g in range(SEQ // 512)]
            nqb = P.dbg.get("nqb", len(qblocks))
            sti = 0
            acc_ = Ring(alloc, "dacc", 4, [128, 512], F32)
            for (q0, nq, kt0, kt1) in qblocks[:nqb]:
                qb = qb_.next()
                for h in range(4):
                    K.dma("sp", qb[:, h, 0:nq], qT[h * 128:(h + 1) * 128, q0:q0 + nq], "L" + qb.key, accw=[qb.key])
                for h in range(4):
                    gd = gd_.next()
                    K.dma("sp", gd[:, 0:nq], gdT[h * 128:(h + 1) * 128, q0:q0 + nq], "L" + gd.key, writes=[gd.key])
                    accs = [acc_.next(), acc_.next()]
                    kts = list(range(kt0, kt1))
                    pkmap = {}

                    def emit_qk(kt):
                        nonlocal sti
                        for c in range(2):
                            pk = sti % 4; sti += 1
                            pkmap[(kt, c)] = pk
                            K.mm(psum[pk][:, 0:nq], kres[c * 64:(c + 1) * 64, h, kt * 128:(kt + 1) * 128],
                                 qb[c * 64:(c + 1) * 64, h, 0:nq], True, True, ["kres", qb.key], [f"ps#{pk}"], True)

                    def emit_rest(kt):
                        for c in range(2):
                            pk = pkmap[(kt, c)]
                            E = E_.next()
                            K.act(E[:, 0:nq], psum[pk][:, 0:nq], AF.Exp, [f"ps#{pk}"], [E.key], scale=0.125)
                            K.mm(psum[4 + c][:, 0:nq], vres[:, kt, h * 128:(h + 1) * 128], E[:, 0:nq],
                                 kt == kt0, kt == kt1 - 1, ["vres", E.key], [f"ps#{4 + c}"], kt == kt1 - 1)
                            if c == 1:
                                K.mm(psum[7][:, 0:nq], onesb[:, :], E[:, 0:nq], kt == kt0, kt == kt1 - 1,
                                     ["onesb", E.key], ["ps#7"], kt == kt1 - 1)
                            elif kt == kt0:
                                K.cp("dve", accs[c][:, 0:nq], E[:, 0:nq], [E.key], [accs[c].key])
                            else:
                                K.tt("dve", accs[c][:, 0:nq], accs[c][:, 0:nq], E[:, 0:nq], ALU.add,
                                     [E.key, accs[c].key], [accs[c].key])

                    emit_qk(kts[0])
                    for i, kt in enumerate(kts):
                        if i + 1 < len(kts):
                            emit_qk(kts[i + 1])
                        emit_rest(kt)
                    r0 = f_.next(); r1 = f_.next(); t0 = f_.next(); t1 = f_.next()
                    K.cp("act", t0[:, 0:nq], psum[4][:, 0:nq], ["ps#4"], [t0.key])
                    K.cp("act", t1[:, 0:nq], psum[5][:, 0:nq], ["ps#5"], [t1.key])
                    ab = ob_.next()
                    K.cp("pool", ab[:, 0:nq], accs[0][:, 0:nq], [accs[0].key], [ab.key])
                    K.mm(psum[6][:, 0:nq], onesb[:, :], ab[:, 0:nq], True, True, ["onesb", ab.key], ["ps#6"], True)
                    K.cp("act", r0[:, 0:nq], psum[6][:, 0:nq], ["ps#6"], [r0.key])
                    K.tt("dve", t0[:, 0:nq], t0[:, 0:nq], psum[7][:, 0:nq], ALU.mult, [t0.key, "ps#7"], [t0.key])
                    K.tt("dve", t1[:, 0:nq], t1[:, 0:nq], r0[:, 0:nq], ALU.mult, [t1.key, r0.key], [t1.key])
                    K.stt("dve", t0[:, 0:nq], t1[:, 0:nq], lamc[:, 0:1], t0[:, 0:nq], ALU.mult, ALU.add,
                          [t0.key, t1.key, "lamc"], [t0.key])
                    K.tt("dve", r0[:, 0:nq], r0[:, 0:nq], psum[7][:, 0:nq], ALU.mult, [r0.key, "ps#7"], [r0.key])
                    K.stt("dve", r1[:, 0:nq], r0[:, 0:nq], EPS, r0[:, 0:nq], ALU.mult, ALU.mult, [r0.key], [r1.key])
                    osq = ob_.next()
                    K.tt("pool", osq[:, 0:nq], t0[:, 0:nq], t0[:, 0:nq], ALU.mult, [t0.key], [osq.key])
                    K.mm(psum[6][:, 0:nq], onesb[:, :], osq[:, 0:nq], True, True, ["onesb", osq.key], ["ps#6"], True)
                    K.stt("dve", r1[:, 0:nq], psum[6][:, 0:nq], 1.0 / 128, r1[:, 0:nq], ALU.mult, ALU.add,
                          ["ps#6", r1.key], [r1.key])
                    K.act(r1[:, 0:nq], r1[:, 0:nq], AF.Ln, [r1.key], [r1.key])
                    K.act(r1[:, 0:nq], r1[:, 0:nq], AF.Exp, [r1.key], [r1.key], scale=-0.5)
                    K.tt("dve", t0[:, 0:nq], t0[:, 0:nq], r1[:, 0:nq], ALU.mult, [t0.key, r1.key], [t0.key])
                    mo = ob_.next()
                    K.stt("dve", mo[:, 0:nq], t0[:, 0:nq], subc[:, 0:1], gd[:, 0:nq], ALU.mult, ALU.mult,
                          [t0.key, "subc", gd.key], [mo.key])
                    K.dma("sp", mixT0[(4 + h) * 128:(5 + h) * 128, q0:q0 + nq], mo[:, 0:nq], "S" + mo.key,
                          reads=[mo.key], accw=["mixT0"])
            K.barrier()

    if "attn" not in P.dbg.get("skip", ()):
        phase_attn()
    if stop_after == "attn":
        K.barrier(); pes.close(); P.es.close(); return P

    def phase_out(layer, mixT, KC, Wd, res_src, dst, tiles):
        with ExitStack() as es:
            def alloc(name, shape, dt):
                P.uid += 1
                return es.enter_context(nc.sbuf_tensor(f"{name}_{P.uid}", list(shape), dt))
            Wb = alloc("Wo", [128, KC, D], BF16)
            wst = Ring(alloc, "wost", 2, [128, KC, 256], F32)
            mx_ = Ring(alloc, "mxin", 2, [128, KC, 512], BF16)
            rs_ = Ring(alloc, "resin", 3, [128, D], F32)
            tm_ = Ring(alloc, "otmp", 2, [128, D], F32)
            st_ = Ring(alloc, "ostat", 4, [128, 4], F32)
            junk = alloc("ojunk", [128, 512], BF16)
            for pc in range(4):
                wb = wst.next()
                K.dma("sp", wb[:], Wd[:, pc * 256:(pc + 1) * 256].rearrange("(j p) n -> p j n", p=128), "L" + wb.key,
                      writes=[wb.key])
                K.cp(["dve", "pool"][pc % 2], Wb[:, :, pc * 256:(pc + 1) * 256], wb[:], [wb.key], accw=["Wo"])
            gi = 0
            cur = None
            for (tok0, drow, tmod) in tiles:
                g0 = (tok0 // 512) * 512 if tok0 >= CTX else 0
                if tok0 >= CTX:
                    g0 = CTX + ((tok0 - CTX) // 512) * 512
                gn = CTX if tok0 < CTX else 512
                if cur is None or cur[0] != g0:
                    mx = mx_.next()
                    K.dma("sp", mx[:, :, 0:gn], mixT[:, g0:g0 + gn].rearrange("(j p) t -> p j t", p=128), "L" + mx.key,
                          writes=[mx.key])
                    cur = (g0, mx)
                mx = cur[1]; lo = tok0 - g0
                rs = rs_.next(); tm = tm_.next(); st = st_.next()
                K.dma("sp", rs[:, :], res_src[tok0:tok0 + 128, :], "L" + rs.key, writes=[rs.key])
                pks = [(2 * gi) % 6, (2 * gi + 1) % 6]; gi += 1
                for nb in range(2):
                    for j in range(KC):
                        K.mm(psum[pks[nb]][:, :], mx[:, j, lo:lo + 128], Wb[:, j, nb * 512:(nb + 1) * 512],
                             j == 0, j == KC - 1, [mx.key, "Wo"], [f"ps#{pks[nb]}"], j == KC - 1)
                for nb in range(2):
                    K.act(junk[:, :], psum[pks[nb]][:, :], AF.Square, [f"ps#{pks[nb]}"], ["ojunk", st.key] if nb == 0 else ["ojunk"],
                          accw=() if nb == 0 else [st.key], accum_out=st[:, nb:nb + 1])
                K.tt("dve", st[:, 2:3], st[:, 0:1], st[:, 1:2], ALU.add, [st.key], [st.key])
                K.act(st[:, 3:4], st[:, 2:3], AF.Sqrt, [st.key, "epsb"], [st.key], scale=1.0 / D, bias=epsb[:, 0:1])
                K.op("dve", lambda e, st=st: e.reciprocal(st[:, 2:3], st[:, 3:4]), [st.key], [st.key])
                for nb in range(2):
                    K.stt("dve", tm[:, nb * 512:(nb + 1) * 512], psum[pks[nb]][:, :], st[:, 2:3],
                          ggbc[:, layer, tmod, nb * 512:(nb + 1) * 512], ALU.mult, ALU.mult,
                          [f"ps#{pks[nb]}", st.key, "ggbc"], accw=[tm.key])
                K.tt("pool", tm[:, :], tm[:, :], rs[:, :], ALU.add, [tm.key, rs.key], [tm.key])
                K.dma("sp", dst[drow:drow + 128, :], tm[:, :], "S" + tm.key, reads=[tm.key], accw=[dst.name])
            K.barrier()

    nt0 = P.dbg.get("out0_tiles", NT)
    tiles0 = [(i * 128, i * 128, 1 if i < 2 else 0) for i in range(nt0)]
    phase_out(0, mixT0, 8, e_w_out, src0, h1, tiles0)
    if stop_after == "out0":
        K.barrier(); pes.close(); P.es.close(); return P


    def l1_alloc(alloc):
        c = {}
        c["sb"] = Ring(alloc, "sb1", 4, [128, 512], BF16)
        c["sf"] = Ring(alloc, "sf1", 2, [128, 64], F32)
        c["n"] = 0
        return c

    def epi_xbc(c, pks, bi, tok0, ntok):
        pk = pks[0]; b = c["sb"].next()
        c["n"] += 1
        K.cp("act" if c["n"] % 2 else "dve", b[:, 0:ntok], psum[pk][:, 0:ntok], [f"ps#{pk}"], [b.key])
        K.dma("sp", xbcT[bi * 128:(bi + 1) * 128, tok0:tok0 + ntok], b[:, 0:ntok], "S" + b.key,
              reads=[b.key], accw=["xbcT"])

    def epi_z(zc):
        def f(c, pk, tok0):
            b = c["sb"].next()
            K.act(b[:, :], psum[pk][:, :], AF.Silu, [f"ps#{pk}"], [b.key])
            K.dma("sp", zs[tok0:tok0 + 128, zc * 512:(zc + 1) * 512], b[:, :], "S" + b.key, reads=[b.key], accw=["zs"])
        return f

    def epi_dt(c, pk, tok0):
        b = c["sf"].next()
        K.cp("dve", b[:, :], psum[pk][:, 0:64], [f"ps#{pk}"], [b.key])
        K.dma("sp", dtraw[tok0:tok0 + 128, :], b[:, :], "S" + b.key, reads=[b.key], accw=["dtraw"])

    l1_f = [("xbc", [[2048 + f * 128] for f in range(24)], epi_xbc)]
    l1_t = [(zc * 512, 512, epi_z(zc)) for zc in range(4)] + [(5120, 64, epi_dt)]
    if "l1" not in P.dbg.get("skip", ()):
        phase_proj(1, h1, o_w_in, 5184, l1_f, l1_t, l1_alloc, None)
    if stop_after == "proj1":
        K.barrier(); pes.close(); P.es.close(); return P

    def phase_conv():
        with ExitStack() as es:
            def alloc(name, shape, dt):
                P.uid += 1
                return es.enter_context(nc.sbuf_tensor(f"{name}_{P.uid}", list(shape), dt))
            rows = alloc("cvrows", [120, 128], F32)
            cv = alloc("cv", [128, 5, 24], F32)
            dg = alloc("dg", [128, 4, 24, 128], BF16)
            xin_ = Ring(alloc, "cxin", 2, [128, 24, 516], BF16)
            sb_ = Ring(alloc, "csb", 3, [128, 512], BF16)
            rb_ = Ring(alloc, "crow", 8, [128, 2560], BF16)
            K.dma("sp", rows[:, :], ssd_cv.rearrange("v (f p) -> (v f) p", p=128), "Lcvrows", writes=["cvrows"])
            K.op("pe", lambda e: e.transpose(psum[0][:, 0:120], rows[:, :], idf[0:120, 0:120]),
                 reads=["cvrows", "idf"], writes=["ps#0"])
            K.cp("dve", cv[:].rearrange("p v f -> p (v f)"), psum[0][:, 0:120], ["ps#0"], ["cv"])
            for j in range(4):
                for f in range(24):
                    K.ts("dve" if (f % 2) else "pool", dg[:, j, f, :], idf[:, :], cv[:, j, f:f + 1], None, ALU.mult, None,
                         ["idf", "cv"], accw=["dg"])
            blocks = [(0, CTX, 0, CTX)] + [(CTX + 512 * g, 512, CTX, T) for g in range(SEQ // 512)]
            cpi = 0
            for (b0, bn, sa, sb_end) in blocks[:P.dbg.get("nconv", 99)]:
                xin = xin_.next()
                l0 = max(sa, b0 - 2); l1 = min(sb_end, b0 + bn + 1)
                K.dma("sp", xin[:, :, l0 - (b0 - 2):l1 - (b0 - 2)],
                      xbcT[:, l0:l1].rearrange("(f p) t -> p f t", p=128), "L" + xin.key, writes=[xin.key])
                nt_ = bn // 128
                rbs = [rb_.next() for _ in range(nt_)]
                for ft in range(24):
                    pk = 4 + (cpi % 2); cpi += 1
                    order = [2, 0, 1, 3]
                    for oi, j in enumerate(order):
                        sh = j - 2
                        lo = max(b0, sa - sh); hi = min(b0 + bn, sb_end - sh)
                        K.mm(psum[pk][:, lo - b0:hi - b0], dg[:, j, ft, :],
                             xin[:, ft, lo + sh - (b0 - 2):hi + sh - (b0 - 2)], oi == 0, oi == 3,
                             ["dg", xin.key], [f"ps#{pk}"], oi == 3)
                    sb = sb_.next()
                    K.act(sb[:, 0:bn], psum[pk][:, 0:bn], AF.Silu, [f"ps#{pk}", "cv"], [sb.key], bias=cv[:, 4, ft:ft + 1])
                    if ft >= 16:
                        K.dma("sp", bcT[(ft - 16) * 128:(ft - 15) * 128, b0:b0 + bn], sb[:, 0:bn], "S" + sb.key,
                              reads=[sb.key], accw=["bcT"])
                    if ft < 20:
                        for i in range(nt_):
                            tb = psum[i][:].bitcast(BF16)
                            K.op("pe", lambda e, tb=tb, sb=sb, i=i, ft=ft: e.transpose(
                                tb[:, (ft % 8) * 128:(ft % 8 + 1) * 128], sb[:, i * 128:(i + 1) * 128], idb[:]),
                                reads=[sb.key, "idb"], writes=[f"ps#{i}"], inc=True)
                        if ft % 8 == 7 or ft == 19:
                            ncol = (ft % 8 + 1) * 128
                            c0 = (ft // 8) * 1024
                            for i in range(nt_):
                                tb = psum[i][:].bitcast(BF16)
                                K.cp("dve" if i % 2 else "act", rbs[i][:, c0:c0 + ncol], tb[:, 0:ncol], [f"ps#{i}"],
                                     accw=[rbs[i].key])
                for i in range(nt_):
                    K.dma("sp", xsB[b0 + i * 128:b0 + (i + 1) * 128, :], rbs[i][:, :], "S" + rbs[i].key,
                          reads=[rbs[i].key], accw=["xsB"])
            K.barrier()

    if "conv" not in P.dbg.get("skip", ()):
        phase_conv()
    if stop_after == "conv":
        K.barrier(); pes.close(); P.es.close(); return P


    def phase_ssd():
        with ExitStack() as es:
            def alloc(name, shape, dt):
                P.uid += 1
                return es.enter_context(nc.sbuf_tensor(f"{name}_{P.uid}", list(shape), dt))
            vb = alloc("vb", [128, 160], F32)
            aneg = alloc("aneg", [128, 64], F32)
            nwbc = alloc("nwbc", [128, 2048], F32)
            mk = alloc("mk", [128, 2, 128], F32)
            self_ = alloc("self", [64, 1024], F32)
            selb = alloc("selb", [64, 32, 128], BF16)
            onesf = alloc("onesf2", [128, 128], F32)
            ones1 = alloc("ones1b", [128, 1], F32)
            state = alloc("state", [128, 2048], F32)
            stbf_ = Ring(alloc, "stbf", 2, [128, 2048], BF16)
            S_ = Ring(alloc, "Ssb", 2, [128, 2048], F32)
            ST = {}
            xb_ = Ring(alloc, "xb", 5, [128, 2560], BF16)
            bc_ = Ring(alloc, "bc", 5, [128, 8, 128], BF16)
            dr_ = Ring(alloc, "dr", 3, [128, 32], F32)
            sm_ = Ring(alloc, "sm", 5, [128, 8, 32], F32)
            cs4_ = Ring(alloc, "cs4", 3, [128, 64], F32)
            cst_ = Ring(alloc, "cst", 3, [64, 128], F32)
            hl_ = Ring(alloc, "hl", 3, [64, 3, 128], BF16)
            X_ = Ring(alloc, "Xd", 3, [128, 2048], BF16)
            Xc_ = Ring(alloc, "Xc", 3, [128, 2048], BF16)
            xd_ = Ring(alloc, "xdk", 1, [128, 2048], F32)
            cbm_ = Ring(alloc, "cbm", 2, [128, 4, 128], F32)
            E_ = Ring(alloc, "sE", 2, [128, 512], F32)
            MT_ = Ring(alloc, "sMT", 4, [128, 4, 128], BF16)
            to_ = Ring(alloc, "sto", 2, [128, 512], F32)
            ysb_ = Ring(alloc, "ysb", 2, [128, 2048], F32)
            yf_ = Ring(alloc, "yfl", 1, [128, 2048], F32)
            zt_ = Ring(alloc, "ztl", 1, [128, 2048], BF16)
            ynb_ = Ring(alloc, "ynb", 1, [128, 2048], BF16)
            yT_ = Ring(alloc, "yTs", 1, [128, 16, 128], BF16)
            junk = alloc("sjunk", [128, 512], BF16)
            K.op("dve", lambda e: e.memset(onesf[:], 1.0), writes=["onesf2"])
            K.op("dve", lambda e: e.memset(ones1[:], 1.0), writes=["ones1b"])
            K.dma("sp", vb[:, :], ssd_vec[0, :].partition_broadcast(128), "Lvb", writes=["vb"])
            K.dma("sp", nwbc[:, :], ssd_norm.partition_broadcast(128), "Lnwbc", writes=["nwbc"])
            K.dma("sp", mk[:], maskT.rearrange("d s l -> s d l"), "Lmk", writes=["mk"])
            for q4 in range(4):
                K.dma("sp", self_[:, :], selc[:, q4 * 1024:(q4 + 1) * 1024], "Lself", writes=["self"])
                K.cp("dve", selb[:, q4 * 8:(q4 + 1) * 8, :].rearrange("k h l -> k (h l)"), self_[:, :], ["self"], accw=["selb"])
            K.act(aneg[:, :], vb[:, 0:64], AF.Exp, ["vb"], ["aneg"])
            K.ts("dve", aneg[:, :], aneg[:, :], -1.0, None, ALU.mult, None, ["aneg"], ["aneg"])
            lat_chunks = list(range(2, NT))
            ncl = P.dbg.get("nchunk", len(lat_chunks))
            lat_chunks = lat_chunks[:ncl]
            PB = {"yd": 0, "seg": 0, "so": 0}

            def stage_a(d, c):
                tok0 = c * 128
                lat = c >= 2
                xb = xb_.next(); bc = bc_.next(); dr = dr_.next(); sm = sm_.next()
                ctx_ = {"xb": xb, "bc": bc, "sm": sm, "lat": lat, "tok0": tok0}
                K.dma("sp", xb[:, :], xsB[tok0:tok0 + 128, :], "L" + xb.key, writes=[xb.key])
                K.dma("sp", bc[:], bcT[:, tok0:tok0 + 128].rearrange("(f p) t -> p f t", p=128), "L" + bc.key,
                      writes=[bc.key])
                K.dma("sp", dr[:, :], dtraw[tok0:tok0 + 128, d * 32:(d + 1) * 32], "L" + dr.key, writes=[dr.key])
                dt = sm[:, 0, :]; adt = sm[:, 1, :]; cs = sm[:, 2, :]; ecs = sm[:, 3, :]
                etot = sm[:, 4, :]; w2 = sm[:, 5, :]; tmp = sm[:, 6, :]
                k_ = sm.key
                K.tt("dve", tmp, dr[:, :], vb[:, 64 + d * 32:96 + d * 32], ALU.add, [dr.key, "vb"], [k_])
                K.act(tmp, tmp, AF.Exp, [k_], [k_])
                K.act(dt, tmp, AF.Ln, [k_, "ones1b"], [k_], bias=ones1[:, 0:1])
                K.tt("dve", adt, dt, aneg[:, d * 32:(d + 1) * 32], ALU.mult, [k_, "aneg"], [k_])
                return ctx_

            def stage_s2(d, ctx_):
                xb, sm, lat = ctx_["xb"], ctx_["sm"], ctx_["lat"]
                dt = sm[:, 0, :]; adt = sm[:, 1, :]; cs = sm[:, 2, :]; ecs = sm[:, 3, :]
                etot = sm[:, 4, :]; w2 = sm[:, 5, :]; tmp = sm[:, 6, :]
                k_ = sm.key
                K.mm(psum[4][:, 0:32], mk[:, d, :], adt, True, True, ["mk", k_], ["ps#4a"], True)
                K.mm(psum[4][:, 32:64], onesf[:, :], adt, True, True, ["onesf2", k_], ["ps#4a"], True)
                K.cp("dve", cs, psum[4][:, 0:32], ["ps#4a"], [k_])
                K.act(etot, psum[4][:, 32:64], AF.Exp, ["ps#4a"], [k_])
                K.tt("dve", tmp, psum[4][:, 32:64], cs, ALU.subtract, ["ps#4a", k_], [k_])
                K.act(w2, tmp, AF.Exp, [k_], [k_])
                K.tt("dve", w2, w2, dt, ALU.mult, [k_], [k_])
                xs3 = xb[:, 0:2048].rearrange("p (h q) -> p h q", q=64)
                Xc = Xc_.next()
                ctx_["Xc"] = Xc
                K.tt("pool", Xc[:, :].rearrange("p (h q) -> p h q", q=64), xs3,
                     w2.unsqueeze(2).to_broadcast([128, 32, 64]), ALU.mult, [xb.key, k_], [Xc.key])
                if not lat:
                    return ctx_
                K.act(ecs, cs, AF.Exp, [k_], [k_])
                X = X_.next()
                K.tt("pool", X[:, :].rearrange("p (h q) -> p h q", q=64), xs3,
                     dt.unsqueeze(2).to_broadcast([128, 32, 64]), ALU.mult, [xb.key, k_], [X.key])
                cs4 = cs4_.next(); cst = cst_.next(); hl = hl_.next()
                K.cp("dve", cs4[:, 0:32], cs, [k_], accw=[cs4.key])
                K.cp("dve", cs4[:, 32:64], cs, [k_], accw=[cs4.key])
                ctx_["X"] = X; ctx_["cs4"] = cs4; ctx_["cst"] = cst; ctx_["hl"] = hl
                return ctx_

            def stage_s3(d, ctx_):
                if not ctx_["lat"]:
                    return ctx_
                cs4, cst, hl, X = ctx_["cs4"], ctx_["cst"], ctx_["hl"], ctx_["X"]
                K.op("pe", lambda e, cs4=cs4: e.transpose(psum[4][0:64, 128:256], cs4[:, :], idf[:, :]),
                     reads=[cs4.key, "idf"], writes=["ps#4b"])
                K.cp("dve", cst[:, :], psum[4][0:64, 128:256], ["ps#4b"], [cst.key])
                K.cp("dve", hl[0:32, 0, :], cst[0:32, :], [cst.key], accw=[hl.key])
                K.cp("dve", hl[32:64, 2, :], cst[32:64, :], [cst.key], accw=[hl.key])
                K.tt("dve", hl[32:64, 0, :], cst[32:64, :], hl[32:64, 2, :], ALU.subtract, [cst.key, hl.key], accw=[hl.key])
                K.ts("dve", hl[:, 1, :], hl[:, 0, :], -1.0, None, ALU.mult, None, [hl.key], accw=[hl.key])
                ctx_["X"] = X; ctx_["hl"] = hl
                return ctx_

            def stage_a1(d, cx, bgen=None):
                if cx["lat"]:
                    stage_a1_lat(d, cx, bgen)
                xb, Xc = cx["xb"], cx["Xc"]
                Ssb = S_.next()
                cx["Ssb"] = Ssb
                for g in range(4):
                    pk = 6 + PB["so"] % 2; PB["so"] += 1
                    K.mm(psum[pk][:, :], xb[:, 2048 + g * 128:2048 + (g + 1) * 128], Xc[:, g * 512:(g + 1) * 512],
                         True, True, [xb.key, Xc.key], [f"ps#{pk}"], True)
                    K.cp("act", Ssb[:, g * 512:(g + 1) * 512], psum[pk][:, :], [f"ps#{pk}"], accw=[Ssb.key])

            def stage_a1_lat(d, cx, bgen=None):
                xb, bc, sm, tok0, X, hl = cx["xb"], cx["bc"], cx["sm"], cx["tok0"], cx["X"], cx["hl"]
                ctx_ = cx
                for g in range(4):
                    K.mm(psum[5][:, g * 128:(g + 1) * 128], bc[:, g, :], bc[:, 4 + g, :], True, True,
                         [bc.key], ["ps#5"], g == 3)
                cbm = cbm_.next()
                K.tt("dve", cbm[:], psum[5][:, :].rearrange("p (g l) -> p g l", l=128),
                     mk[:, d:d + 1, :].to_broadcast([128, 4, 128]), ALU.mult, ["ps#5", "mk"], [cbm.key])
                ysb = ysb_.next()
                ctx_["ysb"] = ysb
                if d == 1:
                    yf = yf_.next()
                    K.dma("sp", yf[:, :], yfw[tok0:tok0 + 128, :], "L" + yf.key, reads=["yfw"], writes=[yf.key])
                segbank = {}

                def emit_seg(hq):
                    pk = 2 + PB["seg"] % 2; PB["seg"] += 1
                    segbank[hq] = pk
                    for j in range(4):
                        h = hq * 4 + j
                        K.mm(psum[pk][:, j * 128:(j + 1) * 128], selb[:, h, :], hl[:, 0, :], True, False,
                             ["selb", hl.key], [f"ps#{pk}"], False)
                        K.mm(psum[pk][:, j * 128:(j + 1) * 128], hl[:, 1, :], selb[:, h, :], False, True,
                             ["selb", hl.key], [f"ps#{pk}"], j == 3)

                pyd = 0
                emit_seg(0)
                for hq in range(8):
                    g = hq // 2
                    if hq + 1 < 8:
                        emit_seg(hq + 1)
                    pk = segbank[hq]
                    if hq % 2 == 0:
                        pyd = PB["yd"] % 2; PB["yd"] += 1
                    E = E_.next(); MT = MT_.next()
                    K.act(E[:, :], psum[pk][:, :], AF.Exp, [f"ps#{pk}"], [E.key])
                    K.stt("dve", MT[:], E[:, :].rearrange("p (j l) -> p j l", l=128), 1e30,
                          cbm[:, g:g + 1, :].to_broadcast([128, 4, 128]), ALU.min, ALU.mult,
                          [E.key, cbm.key], [MT.key])
                    for j in range(4):
                        h = hq * 4 + j
                        K.mm(psum[pyd][:, (h % 8) * 64:(h % 8 + 1) * 64], MT[:, j, :], X[:, h * 64:(h + 1) * 64],
                             True, True, [MT.key, X.key], [f"ps#{pyd}"], (h % 8 == 7))
                    if hq % 2 == 1:
                        if d == 0:
                            K.cp("act", ysb[:, g * 512:(g + 1) * 512], psum[pyd][:, :], [f"ps#{pyd}"], accw=[ysb.key])
                        else:
                            K.tt("dve", ysb[:, g * 512:(g + 1) * 512], psum[pyd][:, :], yf[:, g * 512:(g + 1) * 512],
                                 ALU.add, [f"ps#{pyd}", yf.key], accw=[ysb.key])
                    if bgen is not None:
                        next(bgen, None)
                return ctx_

            def stage_b(d, cx):
                xb, bc, sm, lat, tok0, Xc = cx["xb"], cx["bc"], cx["sm"], cx["lat"], cx["tok0"], cx["Xc"]
                k_ = sm.key
                ecs = sm[:, 3, :]; etot = sm[:, 4, :]
                xs3 = xb[:, 0:2048].rearrange("p (h q) -> p h q", q=64)
                Ssb = cx["Ssb"]
                stprev = ST["cur"]
                stnew = stbf_.next()
                ST["cur"] = stnew
                K.tt("dve", state[:, :].rearrange("p (h q) -> p h q", q=64), state[:, :].rearrange("p (h q) -> p h q", q=64),
                     etot.unsqueeze(2).to_broadcast([128, 32, 64]), ALU.mult, ["state", k_], ["state"])
                yield
                K.tt("dve", state[:, :], state[:, :], Ssb[:, :], ALU.add, ["state", Ssb.key], ["state"])
                K.cp("act", stnew[:, :], state[:, :], ["state"], [stnew.key])
                yield
                if lat:
                    ysb = cx["ysb"]
                    for g in range(4):
                        pk = 6 + PB["so"] % 2; PB["so"] += 1
                        K.mm(psum[pk][:, :], bc[:, 4 + g, :], stprev[:, g * 512:(g + 1) * 512], True, True,
                             [bc.key, stprev.key], [f"ps#{pk}"], True)
                        to = to_.next()
                        K.tt("dve", to[:, :].rearrange("p (h q) -> p h q", q=64),
                             psum[pk][:, :].rearrange("p (h q) -> p h q", q=64),
                             ecs[:, g * 8:(g + 1) * 8].unsqueeze(2).to_broadcast([128, 8, 64]), ALU.mult,
                             [f"ps#{pk}", k_], [to.key])
                        K.tt("dve", ysb[:, g * 512:(g + 1) * 512], ysb[:, g * 512:(g + 1) * 512], to[:, :], ALU.add,
                             [ysb.key, to.key], [ysb.key])
                        yield
                if not lat:
                    return
                if d == 0:
                    K.dma("sp", yfw[tok0:tok0 + 128, :], ysb[:, :], "S" + ysb.key, reads=[ysb.key], accw=["yfw"])
                    return
                zt = zt_.next(); ynb = ynb_.next(); yT = yT_.next(); xd = xd_.next()
                K.dma("sp", zt[:, :], zs[tok0:tok0 + 128, :], "L" + zt.key, writes=[zt.key])
                K.tt("pool", xd[:, :].rearrange("p (h q) -> p h q", q=64), xs3,
                     vb[:, 128:160].unsqueeze(2).to_broadcast([128, 32, 64]), ALU.mult, [xb.key, "vb"], [xd.key])
                K.tt("dve", ysb[:, :], ysb[:, :], xd[:, :], ALU.add, [ysb.key, xd.key], [ysb.key])
                K.tt("dve", ysb[:, :], ysb[:, :], zt[:, :], ALU.mult, [ysb.key, zt.key], [ysb.key])
                yield
                for g in range(4):
                    K.act(junk[:, :], ysb[:, g * 512:(g + 1) * 512], AF.Square, [ysb.key], ["sjunk"], accw=[k_],
                          accum_out=sm[:, 7, g:g + 1])
                K.act(sm[:, 7, 4:8], sm[:, 7, 0:4], AF.Sqrt, [k_, "epsb"], [k_], scale=1.0 / 512, bias=epsb[:, 0:1])
                K.op("dve", lambda e, sm=sm: e.reciprocal(sm[:, 7, 8:12], sm[:, 7, 4:8]), [k_], [k_])
                for g in range(4):
                    K.stt("dve", ynb[:, g * 512:(g + 1) * 512], ysb[:, g * 512:(g + 1) * 512], sm[:, 7, 8 + g:9 + g],
                          nwbc[:, g * 512:(g + 1) * 512], ALU.mult, ALU.mult, [ysb.key, k_, "nwbc"], accw=[ynb.key])
                yield
                for half in range(2):
                    pk = half
                    tb = psum[pk][:].bitcast(BF16)
                    for jj in range(8):
                        j = half * 8 + jj
                        K.op("pe", lambda e, tb=tb, jj=jj, j=j, ynb=ynb: e.transpose(
                            tb[:, jj * 128:(jj + 1) * 128], ynb[:, j * 128:(j + 1) * 128], idb[:]),
                            reads=[ynb.key, "idb"], writes=[f"ps#{pk}"], inc=(jj == 7))
                    K.cp("act", yT[:, half * 8:(half + 1) * 8, :].rearrange("p j t -> p (j t)"), tb[:, :], [f"ps#{pk}"],
                         accw=[yT.key])
                K.dma("sp", mixT1[:, tok0:tok0 + 128].rearrange("(j p) t -> p j t", p=128), yT[:], "S" + yT.key,
                      reads=[yT.key], accw=["mixT1"])

            for d in range(2):
                order = [0, 1] + lat_chunks if d == 0 else [1, 0] + lat_chunks[::-1]
                K.op("pool", lambda e: e.memset(state[:], 0.0), writes=["state"])
                st0 = stbf_.next()
                ST["cur"] = st0
                K.op("pool", lambda e, st0=st0: e.memset(st0[:], 0.0), writes=[st0.key])
                n_ = len(order)
                cxs = {}
                def run(stage, idx):
                    if 0 <= idx < n_:
                        if stage == 1:
                            cxs[idx] = stage_a(d, order[idx])
                        elif stage == 2:
                            stage_s2(d, cxs[idx])
                        elif stage == 3:
                            stage_s3(d, cxs[idx])
                        elif stage == 4:
                            stage_a1(d, cxs[idx])
                        else:
                            stage_b(d, cxs.pop(idx))
                for it in range(-3, n_):
                    run(3, it + 1)
                    bgen = stage_b(d, cxs.pop(it)) if 0 <= it < n_ else None
                    if 0 <= it + 1 < n_:
                        stage_a1(d, cxs[it + 1], bgen)
                    if bgen is not None:
                        for _ in bgen:
                            pass
                    run(1, it + 3); run(2, it + 2)
            K.barrier()

    if "ssd" not in P.dbg.get("skip", ()):
        phase_ssd()
    if stop_after == "ssd":
        K.barrier(); pes.close(); P.es.close(); return P

    nt1 = P.dbg.get("out1_tiles", SEQ // 128)
    tiles1 = [(CTX + i * 128, i * 128, 0) for i in range(nt1)]
    phase_out(1, mixT1, 16, o_w_out, h1, out_h, tiles1)

    K.barrier()
    pes.close()
    P.es.close()
    return P


def _rope_tables():
    n_freq = 16
    inv = (10000.0 ** (-np.arange(n_freq, dtype=np.float32) / np.float32(n_freq))).astype(np.float32)
    t = np.arange(SEQ)
    row = (t // 64).astype(np.float32)
    col = (t % 64).astype(np.float32)
    ang = np.concatenate([row[:, None] * inv, col[:, None] * inv], axis=-1).astype(np.float32)
    cos, sin = np.cos(ang).astype(np.float32), np.sin(ang).astype(np.float32)
    tab = np.zeros((2, 128, T), np.float32)
    tab[0, :, :CTX] = 1.0
    for p in range(128):
        d = p % 64
        fi = (d % 16) + 16 * (d // 32)
        sgn = -1.0 if (d % 32) < 16 else 1.0
        tab[0, p, CTX:] = cos[:, fi]
        tab[1, p, CTX:] = sgn * sin[:, fi]
    return tab


def _rope_perm():
    perm = np.zeros(512, np.int64)
    for f in range(512):
        d = f % 64
        e = d % 32
        e2 = e + 16 if e < 16 else e - 16
        perm[f] = f - d + (d // 32) * 32 + e2
    return perm


def make_in_maps(inp):
    B = inp["x"].shape[0]
    perm = _rope_perm()
    w = np.asarray(inp["e_w_in"][0], np.float32)
    w_aug = np.ascontiguousarray(np.concatenate([w, w[:, 1024 + perm], w[:, 1536 + perm]], axis=1))
    tab = _rope_tables()
    ident = np.eye(128, dtype=np.float32)
    wbd = np.zeros((2, 2, 4, 128, 128), np.float32)
    for d in range(2):
        for g, nm in enumerate(("lru_w_r", "lru_w_i")):
            wsrc = np.asarray(inp[nm][0][d], np.float32)
            for cc in range(4):
                wbd[d, g, cc, 0:64, 0:64] = wsrc[2 * cc]
                wbd[d, g, cc, 64:128, 64:128] = wsrc[2 * cc + 1]
    wbd = np.ascontiguousarray(wbd.reshape(16, 128, 128))
    lvec = np.ascontiguousarray(np.concatenate([
        np.asarray(inp["lru_conv_w"][0], np.float32), np.asarray(inp["lru_conv_b"], np.float32).reshape(1, 512),
        np.asarray(inp["lru_b_r"][0], np.float32), np.asarray(inp["lru_b_i"][0], np.float32),
        np.asarray(inp["lru_lambda"][0], np.float32)], axis=0))
    ssd_cv = np.ascontiguousarray(np.concatenate([np.asarray(inp["ssd_conv_w"][0], np.float32),
                                                  np.asarray(inp["ssd_conv_b"], np.float32).reshape(1, 3072)], 0))
    ssd_vec = np.ascontiguousarray(np.concatenate([np.asarray(inp["ssd_a_log"][0], np.float32).reshape(-1),
                                                   np.asarray(inp["ssd_dt_bias"][0], np.float32).reshape(-1),
                                                   np.asarray(inp["ssd_d"][0], np.float32).reshape(-1)]).reshape(1, 160))
    ii = np.arange(128)
    maskT = np.stack([(ii[None, :] >= ii[:, None]), (ii[None, :] <= ii[:, None])], 0).astype(np.float32)
    selc = np.zeros((64, 32, 128), np.float32)
    for hh in range(32):
        selc[hh, hh, :] = 1.0
        selc[32 + hh, hh, :] = 1.0
    selc = np.ascontiguousarray(selc.reshape(64, 32 * 128))
    maps = []
    for b in range(B):
        m = {
            "src0": np.ascontiguousarray(np.concatenate([inp["ctx"][b], inp["x"][b]], axis=0), dtype=np.float32),
            "cvec": np.ascontiguousarray(np.stack([inp["c"][b], inp["c_ctx"]], 0), dtype=np.float32),
            "w_mod": np.asarray(inp["w_mod"], np.float32),
            "b_mod": np.asarray(inp["b_mod"], np.float32),
            "g_pre": np.asarray(inp["g_pre"], np.float32),
            "g_post": np.asarray(inp["g_post"], np.float32),
            "e_w_in_aug": w_aug,
            "e_w_out": np.asarray(inp["e_w_out"][0], np.float32),
            "ident": ident,
            "ropetab": tab,
            "lru_wbd": wbd,
            "lru_vec": lvec,
            "da_lam": np.ascontiguousarray(np.asarray(inp["da_lambda"][0], np.float32).reshape(1, 256)),
            "da_sub": np.ascontiguousarray(np.asarray(inp["da_subln"][0], np.float32).reshape(1, 128)),
            "o_w_in": np.asarray(inp["o_w_in"][0], np.float32),
            "o_w_out": np.asarray(inp["o_w_out"][0], np.float32),
            "ssd_cv": ssd_cv,
            "ssd_vec": ssd_vec,
            "ssd_norm": np.asarray(inp["ssd_norm"][0], np.float32),
            "maskT": maskT,
            "selc": selc,
        }
        maps.append(m)
    return maps


def kernel(**inp):
    P = build_program()
    maps = make_in_maps(inp)
    res = run_bass_kernel_spmd(P.nc, maps, core_ids=list(range(8)))
    return np.stack([np.asarray(r["out"], np.float32) for r in res.results], 0)
```

```python
import os
from contextlib import ExitStack
import numpy as np
import concourse.bass as bass
import concourse.mybir as mybir
from concourse.bass_utils import run_bass_kernel_spmd

F32, BF16 = mybir.dt.float32, mybir.dt.bfloat16
AF = mybir.ActivationFunctionType
ALU = mybir.AluOpType
AX = mybir.AxisListType

D = 1024
SEQ = 4096
CTX = 256
T = SEQ + CTX
NT = T // 128
EPS = 1e-6


class Sched:
    ENG = ("pe", "dve", "act", "pool", "sp")

    def __init__(self, nc, es):
        self.nc, self.es = nc, es
        self.e = {"pe": nc.tensor, "dve": nc.vector, "act": nc.scalar, "pool": nc.gpsimd, "sp": nc.sync}
        self.sem, self.cnt = {}, {}
        for n in self.ENG:
            self.sem[n] = es.enter_context(nc.semaphore("s_" + n))
            self.cnt[n] = 0
        self.seen = {n: {} for n in self.ENG}
        self.W, self.Rd = {}, {}
        self.pend = {n: [] for n in self.ENG}
        self.nwait = 0
        self.nins = 0

    def _wait(self, eng, need):
        for s, v in need.items():
            if s == "pe" and eng == "pe":
                continue
            if self.seen[eng].get(s, 0) >= v:
                continue
            self.e[eng].wait_ge(self.sem[s], v)
            self.seen[eng][s] = v
            self.nwait += 1

    def _deps(self, eng, reads, writes, accw):
        need = {}

        def add(d):
            for s, v in d.items():
                if need.get(s, 0) < v:
                    need[s] = v
        for r in reads:
            add(self.W.get(r, {}))
        for w in writes:
            add(self.W.get(w, {}))
            add(self.Rd.get(w, {}))
        for w in accw:
            add(self.Rd.get(w, {}))
        self._wait(eng, need)

    def _register(self, ev, reads, writes, accw):
        s, v = ev
        for r in reads:
            d = self.Rd.setdefault(r, {})
            d[s] = max(d.get(s, 0), v)
        for w in writes:
            self.W[w] = {s: v}
            self.Rd[w] = {}
        for w in accw:
            d = self.W.setdefault(w, {})
            d[s] = max(d.get(s, 0), v)

    def op(self, eng, fn, reads=(), writes=(), accw=(), inc=True):
        self._deps(eng, reads, writes, accw)
        ins = fn(self.e[eng])
        self.nins += 1
        if inc:
            self.cnt[eng] += 1
            ins.then_inc(self.sem[eng], 1)
            ev = (eng, self.cnt[eng])
            for (r, w, a) in self.pend[eng]:
                self._register(ev, r, w, a)
            self.pend[eng] = []
            self._register(ev, reads, writes, accw)
        else:
            self.pend[eng].append((tuple(reads), tuple(writes), tuple(accw)))

    def dma(self, q, out, in_, semkey, reads=(), writes=(), accw=(), **kw):
        if semkey not in self.sem:
            self.sem[semkey] = self.es.enter_context(self.nc.semaphore("d_" + semkey.replace("#", "_")))
            self.cnt[semkey] = 0
        self._deps(q, reads, writes, accw)
        ins = self.e[q].dma_start(out=out, in_=in_, **kw)
        ins.then_inc(self.sem[semkey], 16)
        self.cnt[semkey] += 16
        self.nins += 1
        self._register((semkey, self.cnt[semkey]), reads, writes, accw)

    def tt(self, eng, out, a, b, op, reads, writes=(), accw=()):
        self.op(eng, lambda e: e.tensor_tensor(out, a, b, op), reads, writes, accw)

    def ts(self, eng, out, a, s1, s2, op0, op1=None, reads=(), writes=(), accw=()):
        if op1 is None:
            self.op(eng, lambda e: e.tensor_scalar(out, a, s1, None, op0), reads, writes, accw)
        else:
            self.op(eng, lambda e: e.tensor_scalar(out, a, s1, s2, op0, op1), reads, writes, accw)

    def stt(self, eng, out, a, sc, b, op0, op1, reads, writes=(), accw=()):
        self.op(eng, lambda e: e.scalar_tensor_tensor(out, a, sc, b, op0, op1), reads, writes, accw)

    def act(self, out, in_, func, reads, writes=(), accw=(), **kw):
        self.op("act", lambda e: e.activation(out=out, in_=in_, func=func, **kw), reads, writes, accw)

    def cp(self, eng, out, in_, reads, writes=(), accw=()):
        if eng == "act":
            self.op("act", lambda e: e.copy(out, in_), reads, writes, accw)
        else:
            self.op(eng, lambda e: e.tensor_copy(out, in_), reads, writes, accw)

    def mm(self, out, lhsT, rhs, start, stop, reads, writes, inc):
        self.op("pe", lambda e: e.matmul(out, lhsT, rhs, start=start, stop=stop), reads, writes, inc=inc)

    def barrier(self):
        for n in self.ENG:
            assert not self.pend[n]
        allev = {s: c for s, c in self.cnt.items() if c > 0}
        for n in self.ENG:
            self._wait(n, allev)
        self.W, self.Rd = {}, {}


class Buf:
    def __init__(self, t, key):
        self.t, self.key = t, key

    def __getitem__(self, k):
        return self.t[k]


class Ring:
    def __init__(self, alloc, name, n, shape, dtype):
        self.bufs = [Buf(alloc(f"{name}{i}", shape, dtype), f"{name}#{i}") for i in range(n)]
        self.i = 0

    def next(self):
        b = self.bufs[self.i % len(self.bufs)]
        self.i += 1
        return b


class Prog:
    def __init__(self, dbg=None):
        self.dbg = dbg or {}
        self.nc = nc = bass.Bass("TRN2", target_bir_lowering=False)
        self.es = ExitStack()
        self.K = Sched(nc, self.es)
        self.dram = {}
        self.uid = 0

    def din(self, name, shape, dt=F32):
        self.dram[name] = self.nc.dram_tensor(name, list(shape), dt, kind="ExternalInput").ap()
        return self.dram[name]

    def dout(self, name, shape, dt=F32):
        self.dram[name] = self.nc.dram_tensor(name, list(shape), dt, kind="ExternalOutput").ap()
        return self.dram[name]

    def dscr(self, name, shape, dt):
        if name in self.dbg.get("dump", ()):
            return self.dout(name, shape, dt)
        self.dram[name] = self.nc.dram_tensor(name, list(shape), dt).ap()
        return self.dram[name]


def build_program(dbg=None):
    P = Prog(dbg)
    nc, K = P.nc, P.K
    stop_after = P.dbg.get("stop_after", "all")

    src0 = P.din("src0", [T, D])
    cvec = P.din("cvec", [2, D])
    w_mod = P.din("w_mod", [2, D, 3 * D])
    b_mod = P.din("b_mod", [2, 3 * D])
    g_pre = P.din("g_pre", [2, D])
    g_post = P.din("g_post", [2, D])
    e_w_in = P.din("e_w_in_aug", [D, 4096])
    e_w_out = P.din("e_w_out", [D, D])
    ident = P.din("ident", [128, 128])
    ropetab = P.din("ropetab", [2, 128, T])
    out_h = P.dout("out", [SEQ, D])
    lru_wbd = P.din("lru_wbd", [16, 128, 128])
    lru_vec = P.din("lru_vec", [11, 512])
    da_lam = P.din("da_lam", [1, 256])
    da_sub = P.din("da_sub", [1, 128])
    mixT0 = P.dscr("mixT0", [D, T], BF16)
    o_w_in = P.din("o_w_in", [D, 5184])
    o_w_out = P.din("o_w_out", [2048, D])
    ssd_cv = P.din("ssd_cv", [5, 3072])
    ssd_vec = P.din("ssd_vec", [1, 160])
    ssd_norm = P.din("ssd_norm", [2048])
    maskT = P.din("maskT", [2, 128, 128])
    selc = P.din("selc", [64, 32 * 128])
    negm = P.din("negm", [2, 128, 512])
    xbcT = P.dscr("xbcT", [3072, T], BF16)
    zs = P.dscr("zs", [T, 2048], BF16)
    dtraw = P.dscr("dtraw", [T, 64], F32)
    xsB = P.dscr("xsB", [T, 2560], BF16)
    bcT = P.dscr("bcT", [1024, T], BF16)
    yfw = P.dscr("yfw", [T, 2048], F32)
    mixT1 = P.dscr("mixT1", [2048, T], BF16)
    h1 = P.dscr("h1", [T, D], F32)

    xrT = P.dscr("xrT", [512, T], F32)
    grT = P.dscr("grT", [512, T], BF16)
    gdT = P.dscr("gdT", [512, T], BF16)
    qT = P.dscr("qT", [512, T], BF16)
    kT = P.dscr("kT", [512, T], BF16)
    vtok = P.dscr("vtok", [T, 512], BF16)

    pes = ExitStack()
    def palloc(name, shape, dt):
        return pes.enter_context(nc.sbuf_tensor(name, list(shape), dt))
    psum = [pes.enter_context(nc.psum_tensor(f"psb{i}", [128, 512], F32)) for i in range(8)]
    idf = palloc("idf", [128, 128], F32)
    idb = palloc("idb", [128, 128], BF16)
    epsb = palloc("epsb", [128, 1], F32)
    modA = palloc("modA", [128, 2, 8, 2], F32)
    modS = palloc("modS", [128, 2, 8, 2], F32)
    ggbc = palloc("ggbc", [128, 2, 2, D], F32)

    K.dma("sp", idf[:], ident[:, :], "Lidf", writes=["idf"])
    K.op("dve", lambda e: e.tensor_copy(idb[:], idf[:]), reads=["idf"], writes=["idb"])
    K.op("dve", lambda e: e.memset(epsb[:], EPS), writes=["epsb"])

    def phase_mod():
        with ExitStack() as es:
            def alloc(name, shape, dt):
                P.uid += 1
                return es.enter_context(nc.sbuf_tensor(f"{name}_{P.uid}", list(shape), dt))
            cT = alloc("cT", [128, 2, 8], F32)
            sig = alloc("sig", [128, 2, 8], F32)
            srep = alloc("srep", [128, 2, 8, 128], F32)
            bT = alloc("bT", [128, 2, 24], F32)
            gpT = alloc("gpT", [128, 2, 8], F32)
            bgbc = alloc("bgbc", [128, 2, D], F32)
            gpbc = alloc("gpbc", [128, 2, D], F32)
            wst = Ring(alloc, "wst", 2, [128, 8, 512], F32)
            tmp = alloc("mtmp", [128, 16, 2], F32)

            rows = alloc("rows", [80, 128], F32)
            K.dma("sp", rows[0:16, :], cvec.rearrange("t (j p) -> (t j) p", p=128), "Lrows", accw=["rows"])
            K.dma("sp", rows[16:64, :], b_mod.rearrange("l (f p) -> (l f) p", p=128), "Lrows", accw=["rows"])
            K.dma("sp", rows[64:80, :], g_pre.rearrange("l (j p) -> (l j) p", p=128), "Lrows", accw=["rows"])
            K.op("pe", lambda e: e.transpose(psum[4][:, 0:80], rows[:, :], idf[0:80, 0:80]),
                 reads=["rows", "idf"], writes=["ps#4"])
            K.op("dve", lambda e: e.tensor_copy(cT[:].rearrange("p t j -> p (t j)"), psum[4][:, 0:16]),
                 reads=["ps#4"], writes=["cT"])
            K.op("dve", lambda e: e.tensor_copy(bT[:].rearrange("p l f -> p (l f)"), psum[4][:, 16:64]),
                 reads=["ps#4"], writes=["bT"])
            K.op("dve", lambda e: e.tensor_copy(gpT[:].rearrange("p l j -> p (l j)"), psum[4][:, 64:80]),
                 reads=["ps#4"], writes=["gpT"])
            for l in range(2):
                K.dma("sp", bgbc[:, l, :], b_mod[l, 2 * D:3 * D].partition_broadcast(128), "Lbgbc", accw=["bgbc"])
                K.dma("sp", gpbc[:, l, :], g_post[l, :].partition_broadcast(128), "Lgpbc", accw=["gpbc"])
            K.op("act", lambda e: e.activation(out=sig[:], in_=cT[:], func=AF.Sigmoid), reads=["cT"], writes=["sig"])
            K.op("dve", lambda e: e.tensor_tensor(cT[:], cT[:], sig[:], ALU.mult), reads=["sig", "cT"], writes=["cT"])
            for t in range(2):
                K.op("dve", lambda e, t=t: e.tensor_copy(
                    srep[:, t, :, :], cT[:, t, :].unsqueeze(2).to_broadcast([128, 8, 128])),
                    reads=["cT"], accw=["srep"])
            for l in range(2):
                for pc in range(6):
                    wb = wst.next()
                    K.dma("sp", wb[:], w_mod[l, :, pc * 512:(pc + 1) * 512].rearrange("(j p) n -> p j n", p=128),
                          "L" + wb.key, writes=[wb.key])
                    if pc < 4:
                        pst = psum[pc % 2]
                        for f in range(4):
                            for j in range(8):
                                K.op("pe", lambda e, f=f, j=j, pst=pst, wb=wb: e.matmul(
                                    pst[:, 2 * f:2 * f + 2], wb[:, j, f * 128:(f + 1) * 128], cT[:, :, j],
                                    start=(j == 0), stop=(j == 7)),
                                    reads=[wb.key, "cT"], writes=[f"ps#{pc % 2}"], inc=(j == 7 and f == 3))
                        K.op("dve", lambda e, pst=pst, pc=pc: e.tensor_copy(
                            tmp[:, pc * 4:(pc + 1) * 4, :], pst[:, 0:8].rearrange("p (f t) -> p f t", t=2)),
                            reads=[f"ps#{pc % 2}"], accw=["mtmp"])
                    else:
                        for t in range(2):
                            pst = psum[2 + t]
                            for j in range(8):
                                K.op("pe", lambda e, j=j, t=t, pst=pst, wb=wb: e.matmul(
                                    pst[:, :], srep[:, t, j, :], wb[:, j, :], start=(j == 0), stop=(j == 7)),
                                    reads=[wb.key, "srep"], writes=[f"ps#{2 + t}"], inc=(j == 7))
                            c0 = (pc - 4) * 512
                            K.op("dve", lambda e, t=t, l=l, c0=c0, pst=pst: e.tensor_tensor(
                                ggbc[:, l, t, c0:c0 + 512], pst[:, :], bgbc[:, l, c0:c0 + 512], ALU.add),
                                reads=[f"ps#{2 + t}", "bgbc"], accw=["ggbc"])
                for t in range(2):
                    K.op("dve", lambda e, t=t, l=l: e.tensor_tensor(
                        modS[:, l, :, t], tmp[:, 0:8, t], bT[:, l, 0:8], ALU.add),
                        reads=["mtmp", "bT"], accw=["modS"])
                    K.op("dve", lambda e, t=t, l=l: e.scalar_tensor_tensor(
                        modA[:, l, :, t], tmp[:, 8:16, t], 1.0, bT[:, l, 8:16], ALU.add, ALU.add),
                        reads=["mtmp", "bT"], accw=["modA"])
                    K.op("dve", lambda e, t=t, l=l: e.tensor_tensor(
                        modA[:, l, :, t], modA[:, l, :, t], gpT[:, l, :], ALU.mult),
                        reads=["modA", "gpT"], writes=["modA"])
                    K.op("dve", lambda e, t=t, l=l: e.tensor_tensor(
                        ggbc[:, l, t, :], ggbc[:, l, t, :], gpbc[:, l, :], ALU.mult),
                        reads=["ggbc", "gpbc"], writes=["ggbc"])
            K.barrier()

    phase_mod()
    if "mod" in P.dbg.get("dump", ()):
        dA = P.dout("dbg_modA", [128, 32]); dS = P.dout("dbg_modS", [128, 32]); dG = P.dout("dbg_gg", [128, 4 * D])
        K.dma("sp", dA[:, :], modA[:].rearrange("p l j t -> p (l j t)"), "Sdbg", reads=["modA"])
        K.dma("sp", dS[:, :], modS[:].rearrange("p l j t -> p (l j t)"), "Sdbg", reads=["modS"])
        K.dma("sp", dG[:, :], ggbc[:].rearrange("p l t f -> p (l t f)"), "Sdbg", reads=["ggbc"])
    if stop_after == "mod":
        K.barrier(); pes.close(); P.es.close(); return P

    def phase_proj(layer, src, Wd, ncols, fspecs, tspecs, extra_alloc=None, per_group=None):
        with ExitStack() as es:
            def alloc(name, shape, dt):
                P.uid += 1
                return es.enter_context(nc.sbuf_tensor(f"{name}_{P.uid}", list(shape), dt))
            Wb = alloc("Wb", [128, 8, ncols], BF16)
            wst = Ring(alloc, "wst", 2, [128, 8, 256], F32)
            xr_ = Ring(alloc, "xin", 4, [128, D], F32)
            xn_ = Ring(alloc, "xn", 4, [128, D], BF16)
            uT_ = Ring(alloc, "uT", 2, [128, 8, 512], BF16)
            junk = alloc("junk", [128, D], BF16)
            stat = Ring(alloc, "stat", 4, [128, 4], F32)
            ctxo = extra_alloc(alloc) if extra_alloc else None
            ceng = ["dve", "pool", "act"]
            for pc in range(ncols // 256 + (1 if ncols % 256 else 0)):
                c0 = pc * 256
                cw = min(256, ncols - c0)
                wb = wst.next()
                K.dma("sp", wb[:, :, 0:cw], Wd[:, c0:c0 + cw].rearrange("(j p) n -> p j n", p=128),
                      "L" + wb.key, writes=[wb.key])
                en = ceng[pc % 3]
                if en == "act":
                    K.op("act", lambda e, wb=wb, c0=c0, cw=cw: e.copy(Wb[:, :, c0:c0 + cw], wb[:, :, 0:cw]),
                         reads=[wb.key], accw=["Wb"])
                else:
                    K.op(en, lambda e, wb=wb, c0=c0, cw=cw: e.tensor_copy(Wb[:, :, c0:c0 + cw], wb[:, :, 0:cw]),
                         reads=[wb.key], accw=["Wb"])
            groups = [(0, CTX, 1)] + [(CTX + 512 * g, 512, 0) for g in range(SEQ // 512)]
            pst_i = [0]
            pso_i = [0]

            def front_parts(gi):
                tok0, ntok, tmod = groups[gi]
                uT = uT_.next()
                p1s, p2s = [], []
                for ti in range(ntok // 128):
                    def part1(ti=ti):
                        box = {}
                        xt = xr_.next(); xn = xn_.next(); st = stat.next()
                        box["xn"] = xn
                        K.dma("sp", xt[:], src[tok0 + ti * 128: tok0 + (ti + 1) * 128, :], "L" + xt.key, writes=[xt.key])
                        K.op("act", lambda e: e.activation(out=junk[:], in_=xt[:], func=AF.Square, accum_out=st[:, 0:1]),
                             reads=[xt.key], writes=["junk", st.key])
                        K.op("act", lambda e: e.activation(out=st[:, 1:2], in_=st[:, 0:1], func=AF.Sqrt,
                                                           scale=1.0 / D, bias=epsb[:, 0:1]),
                             reads=[st.key, "epsb"], writes=[st.key])
                        K.op("dve", lambda e: e.reciprocal(st[:, 2:3], st[:, 1:2]), reads=[st.key], writes=[st.key])
                        K.op("pool", lambda e: e.tensor_scalar(xn[:], xt[:], st[:, 2:3], None, ALU.mult),
                             reads=[xt.key, st.key], writes=[xn.key])
                        return box

                    def part2(box, ti=ti):
                        xn = box["xn"]
                        pk = 6 + (pst_i[0] % 2); pst_i[0] += 1
                        pst = psum[pk][:].bitcast(BF16)
                        for j in range(8):
                            K.op("pe", lambda e, j=j: e.transpose(
                                pst[:, j * 128:(j + 1) * 128], xn[:, j * 128:(j + 1) * 128], idb[:]),
                                reads=[xn.key, "idb"], writes=[f"ps#{pk}"], inc=(j == 7))
                        for j in range(8):
                            K.op("dve", lambda e, j=j: e.tensor_scalar(
                                uT[:, j, ti * 128:(ti + 1) * 128], pst[:, j * 128:(j + 1) * 128],
                                modA[:, layer, j, tmod:tmod + 1], modS[:, layer, j, tmod:tmod + 1], ALU.mult, ALU.add),
                                reads=[f"ps#{pk}", "modA", "modS"], accw=[uT.key])
                    p1s.append(part1); p2s.append(part2)
                return uT, p1s, p2s

            def mm(gi, uT, hooks):
                tok0, ntok, tmod = groups[gi]
                if per_group:
                    per_group(ctxo, gi, tok0, ntok)
                nb_tot = sum(len(b) for (_, b, _) in fspecs)
                nhk = max(1, len(hooks))
                step = max(1, nb_tot // nhk)
                bcount = 0
                hooks = list(hooks)
                for (name, bundles, epi) in fspecs:
                    for bi, cols in enumerate(bundles):
                        if hooks and bcount % step == 0:
                            hooks.pop(0)()
                        bcount += 1
                        pks = []
                        for c0 in cols:
                            pk = pso_i[0] % 6; pso_i[0] += 1
                            pks.append(pk)
                            for j in range(8):
                                K.op("pe", lambda e, j=j, pk=pk, c0=c0, uT=uT, ntok=ntok: e.matmul(
                                    psum[pk][:, 0:ntok], Wb[:, j, c0:c0 + 128], uT[:, j, 0:ntok],
                                    start=(j == 0), stop=(j == 7)),
                                    reads=["Wb", uT.key], writes=[f"ps#{pk}"], inc=(j == 7))
                        epi(ctxo, pks, bi, tok0, ntok)
                for (c0, cw, epi) in tspecs:
                    for ti in range(ntok // 128):
                        pk = pso_i[0] % 6; pso_i[0] += 1
                        for j in range(8):
                            K.op("pe", lambda e, j=j, pk=pk, uT=uT, ti=ti: e.matmul(
                                psum[pk][:, 0:cw], uT[:, j, ti * 128:(ti + 1) * 128], Wb[:, j, c0:c0 + cw],
                                start=(j == 0), stop=(j == 7)),
                                reads=["Wb", uT.key], writes=[f"ps#{pk}"], inc=(j == 7))
                        epi(ctxo, pk, tok0 + ti * 128)
                while hooks:
                    hooks.pop(0)()

            ng = P.dbg.get("ngroups", len(groups))
            uT0, p1s, p2s = front_parts(0)
            for p1, p2 in zip(p1s, p2s):
                p2(p1())
            cur = uT0
            for gi in range(ng):
                hooks = []
                nxt = None
                if gi + 1 < ng:
                    nxt, p1s, p2s = front_parts(gi + 1)
                    boxes = {}
                    def mk(k, p1s=p1s, p2s=p2s, boxes=boxes):
                        def h():
                            if k == 0:
                                for kk in range(min(2, len(p1s))):
                                    boxes[kk] = p1s[kk]()
                                return
                            if 0 <= k - 1 < len(p1s):
                                p2s[k - 1](boxes[k - 1])
                            if k + 1 < len(p1s):
                                boxes[k + 1] = p1s[k + 1]()
                        return h
                    hooks = [mk(k) for k in range(len(p1s) + 1)]
                mm(gi, cur, hooks)
                cur = nxt
            K.barrier()

    def l0_alloc(alloc):
        c = {}
        c["sf"] = Ring(alloc, "sf", 3, [128, 512], F32)
        c["sb"] = Ring(alloc, "sb", 4, [128, 512], BF16)
        c["t1"] = Ring(alloc, "t1", 2, [128, 512], F32)
        c["t2"] = Ring(alloc, "t2", 2, [128, 512], F32)
        c["tab"] = Ring(alloc, "tab", 2, [128, 2, 512], F32)
        return c

    def l0_group(c, gi, tok0, ntok):
        tb = c["tab"].next()
        c["curtab"] = tb
        K.dma("sp", tb[:, :, 0:ntok], ropetab[:, :, tok0:tok0 + ntok].rearrange("c p t -> p c t"),
              "L" + tb.key, writes=[tb.key])

    def epi_copy_f32(dst):
        def f(c, pks, bi, tok0, ntok):
            pk = pks[0]; b = c["sf"].next()
            K.op("act", lambda e: e.copy(b[:, 0:ntok], psum[pk][:, 0:ntok]), reads=[f"ps#{pk}"], writes=[b.key])
            K.dma("sp", dst[bi * 128:(bi + 1) * 128, tok0:tok0 + ntok], b[:, 0:ntok], "S" + b.key,
                  reads=[b.key], accw=[dst.name])
        return f

    def epi_silu_bf(dst):
        def f(c, pks, bi, tok0, ntok):
            pk = pks[0]; b = c["sb"].next()
            K.op("act", lambda e: e.activation(out=b[:, 0:ntok], in_=psum[pk][:, 0:ntok], func=AF.Silu),
                 reads=[f"ps#{pk}"], writes=[b.key])
            K.dma("sp", dst[bi * 128:(bi + 1) * 128, tok0:tok0 + ntok], b[:, 0:ntok], "S" + b.key,
                  reads=[b.key], accw=[dst.name])
        return f

    def epi_rope(dst):
        def f(c, pks, bi, tok0, ntok):
            pa, pb = pks; t1 = c["t1"].next(); t2 = c["t2"].next(); b = c["sb"].next(); tb = c["curtab"]
            K.op("dve", lambda e: e.tensor_tensor(t1[:, 0:ntok], psum[pa][:, 0:ntok], tb[:, 0, 0:ntok], ALU.mult),
                 reads=[f"ps#{pa}", tb.key], writes=[t1.key])
            K.op("dve", lambda e: e.tensor_tensor(t2[:, 0:ntok], psum[pb][:, 0:ntok], tb[:, 1, 0:ntok], ALU.mult),
                 reads=[f"ps#{pb}", tb.key], writes=[t2.key])
            K.op("pool", lambda e: e.tensor_tensor(b[:, 0:ntok], t1[:, 0:ntok], t2[:, 0:ntok], ALU.add),
                 reads=[t1.key, t2.key], writes=[b.key])
            K.dma("sp", dst[bi * 128:(bi + 1) * 128, tok0:tok0 + ntok], b[:, 0:ntok], "S" + b.key,
                  reads=[b.key], accw=[dst.name])
        return f

    def epi_v(c, pk, tok0):
        b = c["sb"].next()
        K.op("act", lambda e: e.copy(b[:, :], psum[pk][:, :]), reads=[f"ps#{pk}"], writes=[b.key])
        K.dma("sp", vtok[tok0:tok0 + 128, :], b[:, :], "S" + b.key, reads=[b.key], accw=["vtok"])

    l0_f = [
        ("xr", [[f * 128] for f in range(0, 4)], epi_copy_f32(xrT)),
        ("gr", [[f * 128] for f in range(4, 8)], epi_silu_bf(grT)),
        ("q", [[1024 + f * 128, 3072 + f * 128] for f in range(4)], epi_rope(qT)),
        ("k", [[1536 + f * 128, 3584 + f * 128] for f in range(4)], epi_rope(kT)),
        ("gd", [[f * 128] for f in range(20, 24)], epi_silu_bf(gdT)),
    ]
    l0_t = [(2048, 512, epi_v)]
    phase_proj(0, src0, e_w_in, 4096, l0_f, l0_t, l0_alloc, l0_group)
    if stop_after == "proj0":
        K.barrier(); pes.close(); P.es.close(); return P


    LAMBDA_INIT0 = 0.8 - 0.6 * 1.0

    def phase_lru():
        with ExitStack() as es:
            def alloc(name, shape, dt):
                P.uid += 1
                return es.enter_context(nc.sbuf_tensor(f"{name}_{P.uid}", list(shape), dt))
            rows = alloc("lrows", [44, 128], F32)
            pv = alloc("lpv", [128, 11, 4], F32)
            coef = alloc("lcoef", [128, 2, 4], F32)
            ones1 = alloc("ones1", [128, 1], F32)
            wbf = alloc("wbf", [128, 16, 128], F32)
            wbb = alloc("wbb", [128, 16, 128], BF16)
            big = Ring(alloc, "big", 7, [128, T], F32)
            xcb = alloc("xcb", [128, T], BF16)
            grs = alloc("grs", [128, T], BF16)
            mo = alloc("mixo", [128, T], BF16)
            K.op("dve", lambda e: e.memset(ones1[:], 1.0), writes=["ones1"])
            K.dma("sp", rows[:, :], lru_vec.rearrange("v (c p) -> (v c) p", p=128), "Lrows", writes=["lrows"])
            K.op("pe", lambda e: e.transpose(psum[0][:, 0:44], rows[:, :], idf[0:44, 0:44]),
                 reads=["lrows", "idf"], writes=["ps#0"])
            K.cp("dve", pv[:].rearrange("p v c -> p (v c)"), psum[0][:, 0:44], ["ps#0"], ["lpv"])
            K.act(coef[:].rearrange("p d c -> p (d c)"), pv[:, 9:11, :].rearrange("p d c -> p (d c)"), AF.Exp,
                  ["lpv"], ["lcoef"], scale=-1.0)
            K.act(coef[:].rearrange("p d c -> p (d c)"), coef[:].rearrange("p d c -> p (d c)"), AF.Ln,
                  ["lcoef", "ones1"], ["lcoef"], bias=ones1[:, 0:1])
            K.ts("dve", coef[:].rearrange("p d c -> p (d c)"), coef[:].rearrange("p d c -> p (d c)"), -8.0, None,
                 ALU.mult, None, ["lcoef"], ["lcoef"])
            K.dma("sp", wbf[:], lru_wbd.rearrange("n k m -> k n m"), "Lwbf", writes=["wbf"])
            K.cp("dve", wbb[:], wbf[:], ["wbf"], ["wbb"])
            segs = [(0, CTX), (CTX, T)]
            blocks = [(b0, min(512, T - b0)) for b0 in range(0, T, 512)]
            for cc in range(4):
                x = big.next(); xc = big.next()
                K.dma("sp", x[:, :], xrT[cc * 128:(cc + 1) * 128, :], "L" + x.key, writes=[x.key])
                K.dma("sp", grs[:, :], grT[cc * 128:(cc + 1) * 128, :], "Lgrs", writes=["grs"])
                K.ts("dve", xc[:, :], x[:, :], pv[:, 2, cc:cc + 1], pv[:, 4, cc:cc + 1], ALU.mult, ALU.add,
                     [x.key, "lpv"], [xc.key])
                for (a, b) in segs:
                    for tap, sh in ((0, -2), (1, -1), (3, 1)):
                        lo = max(a, a - sh); hi = min(b, b - sh)
                        K.stt("dve", xc[:, lo:hi], x[:, lo + sh:hi + sh], pv[:, tap, cc:cc + 1],
                              xc[:, lo:hi], ALU.mult, ALU.add, [x.key, xc.key, "lpv"], [xc.key])
                K.cp("pool", xcb[:, :], xc[:, :], [xc.key], ["xcb"])
                hs = []
                for d in range(2):
                    rb = big.next(); ib = big.next()
                    for (b0, bn) in blocks:
                        for g, dstb, brow in ((0, rb, 5 + d), (1, ib, 7 + d)):
                            pk = (2 * (b0 // 512) + g) % 6
                            K.mm(psum[pk][:, 0:bn], wbb[:, (d * 2 + g) * 4 + cc, :], xcb[:, b0:b0 + bn], True, True,
                                 ["wbb", "xcb"], [f"ps#{pk}"], True)
                            K.act(dstb[:, b0:b0 + bn], psum[pk][:, 0:bn], AF.Sigmoid, [f"ps#{pk}", "lpv"], accw=[dstb.key],
                                  bias=pv[:, brow, cc:cc + 1])
                    K.ts("dve", rb[:, :], rb[:, :], coef[:, d, cc:cc + 1], None, ALU.mult, None, [rb.key, "lcoef"], [rb.key])
                    K.act(rb[:, :], rb[:, :], AF.Exp, [rb.key], [rb.key])
                    sq = big.next()
                    K.tt("pool", sq[:, :], rb[:, :], rb[:, :], ALU.mult, [rb.key], [sq.key])
                    K.act(sq[:, :], sq[:, :], AF.Sqrt, [sq.key, "ones1"], [sq.key], scale=-1.0, bias=ones1[:, 0:1])
                    K.tt("pool", ib[:, :], ib[:, :], xc[:, :], ALU.mult, [ib.key, xc.key], [ib.key])
                    K.tt("dve", ib[:, :], ib[:, :], sq[:, :], ALU.mult, [ib.key, sq.key], [ib.key])
                    h = sq
                    if d == 0:
                        K.op("dve", lambda e, h=h, rb=rb, ib=ib: e.tensor_tensor_scan(
                            h[:, :], rb[:, :], ib[:, :], 0.0, ALU.mult, ALU.add), [rb.key, ib.key, h.key], [h.key])
                    else:
                        K.op("dve", lambda e, h=h, rb=rb, ib=ib: e.tensor_tensor_scan(
                            h[:, CTX - 1::-1] if False else h[:, 0:CTX][:, ::-1], rb[:, 0:CTX][:, ::-1], ib[:, 0:CTX][:, ::-1],
                            0.0, ALU.mult, ALU.add), [rb.key, ib.key, h.key], [h.key])
                        K.op("dve", lambda e, h=h, rb=rb, ib=ib: e.tensor_tensor_scan(
                            h[:, CTX:T][:, ::-1], rb[:, CTX:T][:, ::-1], ib[:, CTX:T][:, ::-1],
                            h[:, 0:1], ALU.mult, ALU.add), [rb.key, ib.key, h.key], [h.key])
                    hs.append(h)
                K.tt("pool", hs[0][:, :], hs[0][:, :], hs[1][:, :], ALU.add, [hs[0].key, hs[1].key], [hs[0].key])
                K.tt("dve", mo[:, :], hs[0][:, :], grs[:, :], ALU.mult, [hs[0].key, "grs"], ["mixo"])
                K.dma("sp", mixT0[cc * 128:(cc + 1) * 128, :], mo[:, :], "Smixo", reads=["mixo"], accw=["mixT0"])
            K.barrier()

    if "lru" not in P.dbg.get("skip", ()):
        phase_lru()
    if stop_after == "lru":
        K.barrier(); pes.close(); P.es.close(); return P

    def phase_attn():
        with ExitStack() as es:
            def alloc(name, shape, dt):
                P.uid += 1
                return es.enter_context(nc.sbuf_tensor(f"{name}_{P.uid}", list(shape), dt))
            kres = alloc("kres", [128, 4, T], BF16)
            vres = alloc("vres", [128, NT, 512], BF16)
            onesb = alloc("onesb", [128, 128], BF16)
            onesf = alloc("onesf", [128, 128], F32)
            lrow = alloc("lamrow", [1, 260], F32)
            lamc = alloc("lamc", [128, 2], F32)
            subc = alloc("subc", [128, 2], F32)
            subrow = alloc("subrow", [1, 128], F32)
            qb_ = Ring(alloc, "qblk", 2, [128, 4, 512], BF16)
            gd_ = Ring(alloc, "gdblk", 2, [128, 512], BF16)
            E_ = Ring(alloc, "Eb", 6, [128, 512], BF16)
            f_ = Ring(alloc, "af", 6, [128, 512], F32)
            ob_ = Ring(alloc, "aob", 4, [128, 512], BF16)
            K.op("dve", lambda e: e.memset(onesb[:], 1.0), writes=["onesb"])
            K.op("dve", lambda e: e.memset(onesf[:], 1.0), writes=["onesf"])
            K.dma("sp", lrow[:, 0:256], da_lam[:, :], "Llam", writes=["lamrow"])
            K.dma("sp", subrow[:, :], da_sub[:, :], "Lsub", writes=["subrow"])
            K.tt("dve", lrow[:, 0:64], lrow[:, 0:64], lrow[:, 64:128], ALU.mult, ["lamrow"], ["lamrow"])
            K.tt("dve", lrow[:, 128:192], lrow[:, 128:192], lrow[:, 192:256], ALU.mult, ["lamrow"], ["lamrow"])
            K.op("dve", lambda e: e.reduce_sum(lrow[:, 256:257], lrow[:, 0:64], AX.X), ["lamrow"], ["lamrow"])
            K.op("dve", lambda e: e.reduce_sum(lrow[:, 257:258], lrow[:, 128:192], AX.X), ["lamrow"], ["lamrow"])
            K.act(lrow[:, 256:258], lrow[:, 256:258], AF.Exp, ["lamrow"], ["lamrow"])
            K.tt("dve", lrow[:, 258:259], lrow[:, 256:257], lrow[:, 257:258], ALU.subtract, ["lamrow"], ["lamrow"])
            K.ts("dve", lrow[:, 258:259], lrow[:, 258:259], -1.0, -LAMBDA_INIT0, ALU.mult, ALU.add, ["lamrow"], ["lamrow"])
            K.mm(psum[0][:, 0:1], onesf[0:1, :], lrow[0:1, 258:259], True, True, ["onesf", "lamrow"], ["ps#0"], True)
            K.cp("dve", lamc[:, 0:1], psum[0][:, 0:1], ["ps#0"], ["lamc"])
            K.op("pe", lambda e: e.transpose(psum[1][:, 0:1], subrow[0:1, :], idf[0:1, 0:1]),
                 reads=["subrow", "idf"], writes=["ps#1"])
            K.ts("dve", subc[:, 0:1], psum[1][:, 0:1], 1.0 - LAMBDA_INIT0, None, ALU.mult, None, ["ps#1"], ["subc"])
            for h in range(4):
                K.dma("sp", kres[:, h, :], kT[h * 128:(h + 1) * 128, :], "Lkres", accw=["kres"])
            for n0 in range(0, NT, 2):
                K.dma("sp", vres[:, n0:n0 + 2, :], vtok[n0 * 128:(n0 + 2) * 128, :].rearrange("(n p) e -> p n e", p=128),
                      "Lvres", accw=["vres"])
            qblocks = [(0, CTX, 0, 2)] + [(CTX + 512 * g, 512, 0, NT) for g in range(SEQ // 512)]
            nqb = P.dbg.get("nqb", len(qblocks))
            sti = 0
            acc_ = Ring(alloc, "dacc", 4, [128, 512], F32)
            for (q0, nq, kt0, kt1) in qblocks[:nqb]:
                qb = qb_.next()
                for h in range(4):
                    K.dma("sp", qb[:, h, 0:nq], qT[h * 128:(h + 1) * 128, q0:q0 + nq], "L" + qb.key, accw=[qb.key])
                for h in range(4):
                    gd = gd_.next()
                    K.dma("sp", gd[:, 0:nq], gdT[h * 128:(h + 1) * 128, q0:q0 + nq], "L" + gd.key, writes=[gd.key])
                    accs = [acc_.next(), acc_.next()]
                    kts = list(range(kt0, kt1))
                    pkmap = {}

                    def emit_qk(kt):
                        nonlocal sti
                        for c in range(2):
                            pk = sti % 4; sti += 1
                            pkmap[(kt, c)] = pk
                            K.mm(psum[pk][:, 0:nq], kres[c * 64:(c + 1) * 64, h, kt * 128:(kt + 1) * 128],
                                 qb[c * 64:(c + 1) * 64, h, 0:nq], True, True, ["kres", qb.key], [f"ps#{pk}"], True)

                    def emit_rest(kt):
                        for c in range(2):
                            pk = pkmap[(kt, c)]
                            E = E_.next()
                            K.act(E[:, 0:nq], psum[pk][:, 0:nq], AF.Exp, [f"ps#{pk}"], [E.key], scale=0.125)
                            K.mm(psum[4 + c][:, 0:nq], vres[:, kt, h * 128:(h + 1) * 128], E[:, 0:nq],
                                 kt == kt0, kt == kt1 - 1, ["vres", E.key], [f"ps#{4 + c}"], kt == kt1 - 1)
                            if c == 1:
                                K.mm(psum[7][:, 0:nq], onesb[:, :], E[:, 0:nq], kt == kt0, kt == kt1 - 1,
                                     ["onesb", E.key], ["ps#7"], kt == kt1 - 1)
                            elif kt == kt0:
                                K.cp("dve", accs[c][:, 0:nq], E[:, 0:nq], [E.key], [accs[c].key])
                            else:
                                K.tt("dve", accs[c][:, 0:nq], accs[c][:, 0:nq], E[:, 0:nq], ALU.add,
                                     [E.key, accs[c].key], [accs[c].key])

                    emit_qk(kts[0])
                    for i, kt in enumerate(kts):
                        if i + 1 < len(kts):
                            emit_qk(kts[i + 1])
                        emit_rest(kt)
                    r0 = f_.next(); r1 = f_.next(); t0 = f_.next(); t1 = f_.next()
                    K.cp("act", t0[:, 0:nq], psum[4][:, 0:nq], ["ps#4"], [t0.key])
                    K.cp("act", t1[:, 0:nq], psum[5][:, 0:nq], ["ps#5"], [t1.key])
                    ab = ob_.next()
                    K.cp("pool", ab[:, 0:nq], accs[0][:, 0:nq], [accs[0].key], [ab.key])
                    K.mm(psum[6][:, 0:nq], onesb[:, :], ab[:, 0:nq], True, True, ["onesb", ab.key], ["ps#6"], True)
                    K.cp("act", r0[:, 0:nq], psum[6][:, 0:nq], ["ps#6"], [r0.key])
                    K.tt("dve", t0[:, 0:nq], t0[:, 0:nq], psum[7][:, 0:nq], ALU.mult, [t0.key, "ps#7"], [t0.key])
                    K.tt("dve", t1[:, 0:nq], t1[:, 0:nq], r0[:, 0:nq], ALU.mult, [t1.key, r0.key], [t1.key])
                    K.stt("dve", t0[:, 0:nq], t1[:, 0:nq], lamc[:, 0:1], t0[:, 0:nq], ALU.mult, ALU.add,
                          [t0.key, t1.key, "lamc"], [t0.key])
                    K.tt("dve", r0[:, 0:nq], r0[:, 0:nq], psum[7][:, 0:nq], ALU.mult, [r0.key, "ps#7"], [r0.key])
                    K.stt("dve", r1[:, 0:nq], r0[:, 0:nq], EPS, r0[:, 0:nq], ALU.mult, ALU.mult, [r0.key], [r1.key])
                    osq = ob_.next()
                    K.tt("pool", osq[:, 0:nq], t0[:, 0:nq], t0[:, 0:nq], ALU.mult, [t0.key], [osq.key])
                    K.mm(psum[6][:, 0:nq], onesb[:, :], osq[:, 0:nq], True, True, ["onesb", osq.key], ["ps#6"], True)
                    K.stt("dve", r1[:, 0:nq], psum[6][:, 0:nq], 1.0 / 128, r1[:, 0:nq], ALU.mult, ALU.add,
                          ["ps#6", r1.key], [r1.key])
                    K.act(r1[:, 0:nq], r1[:, 0:nq], AF.Ln, [r1.key], [r1.key])
                    K.act(r1[:, 0:nq], r1[:, 0:nq], AF.Exp, [r1.key], [r1.key], scale=-0.5)
                    K.tt("dve", t0[:, 0:nq], t0[:, 0:nq], r1[:, 0:nq], ALU.mult, [t0.key, r1.key], [t0.key])
                    mo = ob_.next()
                    K.stt("dve", mo[:, 0:nq], t0[:, 0:nq], subc[:, 0:1], gd[:, 0:nq], ALU.mult, ALU.mult,
                          [t0.key, "subc", gd.key], [mo.key])
                    K.dma("sp", mixT0[(4 + h) * 128:(5 + h) * 128, q0:q0 + nq], mo[:, 0:nq], "S" + mo.key,
                          reads=[mo.key], accw=["mixT0"])
            K.barrier()

    if "attn" not in P.dbg.get("skip", ()):
        phase_attn()
    if stop_after == "attn":
        K.barrier(); pes.close(); P.es.close(); return P

    def phase_out(layer, mixT, KC, Wd, res_src, dst, tiles):
        with ExitStack() as es:
            def alloc(name, shape, dt):
                P.uid += 1
                return es.enter_context(nc.sbuf_tensor(f"{name}_{P.uid}", list(shape), dt))
            Wb = alloc("Wo", [128, KC, D], BF16)
            wst = Ring(alloc, "wost", 2, [128, KC, 256], F32)
            mx_ = Ring(alloc, "mxin", 2, [128, KC, 512], BF16)
            rs_ = Ring(alloc, "resin", 3, [128, D], F32)
            tm_ = Ring(alloc, "otmp", 2, [128, D], F32)
            st_ = Ring(alloc, "ostat", 4, [128, 4], F32)
            junk = alloc("ojunk", [128, 512], BF16)
            for pc in range(4):
                wb = wst.next()
                K.dma("sp", wb[:], Wd[:, pc * 256:(pc + 1) * 256].rearrange("(j p) n -> p j n", p=128), "L" + wb.key,
                      writes=[wb.key])
                K.cp(["dve", "pool"][pc % 2], Wb[:, :, pc * 256:(pc + 1) * 256], wb[:], [wb.key], accw=["Wo"])
            gi = 0
            cur = None
            for (tok0, drow, tmod) in tiles:
                g0 = (tok0 // 512) * 512 if tok0 >= CTX else 0
                if tok0 >= CTX:
                    g0 = CTX + ((tok0 - CTX) // 512) * 512
                gn = CTX if tok0 < CTX else 512
                if cur is None or cur[0] != g0:
                    mx = mx_.next()
                    K.dma("sp", mx[:, :, 0:gn], mixT[:, g0:g0 + gn].rearrange("(j p) t -> p j t", p=128), "L" + mx.key,
                          writes=[mx.key])
                    cur = (g0, mx)
                mx = cur[1]; lo = tok0 - g0
                rs = rs_.next(); tm = tm_.next(); st = st_.next()
                K.dma("sp", rs[:, :], res_src[tok0:tok0 + 128, :], "L" + rs.key, writes=[rs.key])
                pks = [(2 * gi) % 6, (2 * gi + 1) % 6]; gi += 1
                for nb in range(2):
                    for j in range(KC):
                        K.mm(psum[pks[nb]][:, :], mx[:, j, lo:lo + 128], Wb[:, j, nb * 512:(nb + 1) * 512],
                             j == 0, j == KC - 1, [mx.key, "Wo"], [f"ps#{pks[nb]}"], j == KC - 1)
                for nb in range(2):
                    K.act(junk[:, :], psum[pks[nb]][:, :], AF.Square, [f"ps#{pks[nb]}"], ["ojunk", st.key] if nb == 0 else ["ojunk"],
                          accw=() if nb == 0 else [st.key], accum_out=st[:, nb:nb + 1])
                K.tt("dve", st[:, 2:3], st[:, 0:1], st[:, 1:2], ALU.add, [st.key], [st.key])
                K.act(st[:, 3:4], st[:, 2:3], AF.Sqrt, [st.key, "epsb"], [st.key], scale=1.0 / D, bias=epsb[:, 0:1])
                K.op("dve", lambda e, st=st: e.reciprocal(st[:, 2:3], st[:, 3:4]), [st.key], [st.key])
                for nb in range(2):
                    K.stt("dve", tm[:, nb * 512:(nb + 1) * 512], psum[pks[nb]][:, :], st[:, 2:3],
                          ggbc[:, layer, tmod, nb * 512:(nb + 1) * 512], ALU.mult, ALU.mult,
                          [f"ps#{pks[nb]}", st.key, "ggbc"], accw=[tm.key])
                K.tt("pool", tm[:, :], tm[:, :], rs[:, :], ALU.add, [tm.key, rs.key], [tm.key])
                K.dma("sp", dst[drow:drow + 128, :], tm[:, :], "S" + tm.key, reads=[tm.key], accw=[dst.name])
            K.barrier()

    nt0 = P.dbg.get("out0_tiles", NT)
    tiles0 = [(i * 128, i * 128, 1 if i < 2 else 0) for i in range(nt0)]
    phase_out(0, mixT0, 8, e_w_out, src0, h1, tiles0)
    if stop_after == "out0":
        K.barrier(); pes.close(); P.es.close(); return P


    def l1_alloc(alloc):
        c = {}
        c["sb"] = Ring(alloc, "sb1", 4, [128, 512], BF16)
        c["sf"] = Ring(alloc, "sf1", 2, [128, 64], F32)
        c["n"] = 0
        return c

    def epi_xbc(c, pks, bi, tok0, ntok):
        pk = pks[0]; b = c["sb"].next()
        c["n"] += 1
        K.cp("act" if c["n"] % 2 else "dve", b[:, 0:ntok], psum[pk][:, 0:ntok], [f"ps#{pk}"], [b.key])
        K.dma("sp", xbcT[bi * 128:(bi + 1) * 128, tok0:tok0 + ntok], b[:, 0:ntok], "S" + b.key,
              reads=[b.key], accw=["xbcT"])

    def epi_z(zc):
        def f(c, pk, tok0):
            b = c["sb"].next()
            K.act(b[:, :], psum[pk][:, :], AF.Silu, [f"ps#{pk}"], [b.key])
            K.dma("sp", zs[tok0:tok0 + 128, zc * 512:(zc + 1) * 512], b[:, :], "S" + b.key, reads=[b.key], accw=["zs"])
        return f

    def epi_dt(c, pk, tok0):
        b = c["sf"].next()
        K.cp("dve", b[:, :], psum[pk][:, 0:64], [f"ps#{pk}"], [b.key])
        K.dma("sp", dtraw[tok0:tok0 + 128, :], b[:, :], "S" + b.key, reads=[b.key], accw=["dtraw"])

    l1_f = [("xbc", [[2048 + f * 128] for f in range(24)], epi_xbc)]
    l1_t = [(zc * 512, 512, epi_z(zc)) for zc in range(4)] + [(5120, 64, epi_dt)]
    if "l1" not in P.dbg.get("skip", ()):
        phase_proj(1, h1, o_w_in, 5184, l1_f, l1_t, l1_alloc, None)
    if stop_after == "proj1":
        K.barrier(); pes.close(); P.es.close(); return P

    def phase_conv():
        with ExitStack() as es:
            def alloc(name, shape, dt):
                P.uid += 1
                return es.enter_context(nc.sbuf_tensor(f"{name}_{P.uid}", list(shape), dt))
            rows = alloc("cvrows", [120, 128], F32)
            cv = alloc("cv", [128, 5, 24], F32)
            dg = alloc("dg", [128, 4, 24, 128], BF16)
            xin_ = Ring(alloc, "cxin", 2, [128, 24, 516], BF16)
            sb_ = Ring(alloc, "csb", 3, [128, 512], BF16)
            rb_ = Ring(alloc, "crow", 8, [128, 2560], BF16)
            K.dma("sp", rows[:, :], ssd_cv.rearrange("v (f p) -> (v f) p", p=128), "Lcvrows", writes=["cvrows"])
            K.op("pe", lambda e: e.transpose(psum[0][:, 0:120], rows[:, :], idf[0:120, 0:120]),
                 reads=["cvrows", "idf"], writes=["ps#0"])
            K.cp("dve", cv[:].rearrange("p v f -> p (v f)"), psum[0][:, 0:120], ["ps#0"], ["cv"])
            for j in range(4):
                for f in range(24):
                    K.ts("dve" if (f % 2) else "pool", dg[:, j, f, :], idf[:, :], cv[:, j, f:f + 1], None, ALU.mult, None,
                         ["idf", "cv"], accw=["dg"])
            blocks = [(0, CTX, 0, CTX)] + [(CTX + 512 * g, 512, CTX, T) for g in range(SEQ // 512)]
            cpi = 0
            for (b0, bn, sa, sb_end) in blocks[:P.dbg.get("nconv", 99)]:
                xin = xin_.next()
                l0 = max(sa, b0 - 2); l1 = min(sb_end, b0 + bn + 1)
                K.dma("sp", xin[:, :, l0 - (b0 - 2):l1 - (b0 - 2)],
                      xbcT[:, l0:l1].rearrange("(f p) t -> p f t", p=128), "L" + xin.key, writes=[xin.key])
                nt_ = bn // 128
                rbs = [rb_.next() for _ in range(nt_)]
                for ft in range(24):
                    pk = 4 + (cpi % 2); cpi += 1
                    order = [2, 0, 1, 3]
                    for oi, j in enumerate(order):
                        sh = j - 2
                        lo = max(b0, sa - sh); hi = min(b0 + bn, sb_end - sh)
                        K.mm(psum[pk][:, lo - b0:hi - b0], dg[:, j, ft, :],
                             xin[:, ft, lo + sh - (b0 - 2):hi + sh - (b0 - 2)], oi == 0, oi == 3,
                             ["dg", xin.key], [f"ps#{pk}"], oi == 3)
                    sb = sb_.next()
                    K.act(sb[:, 0:bn], psum[pk][:, 0:bn], AF.Silu, [f"ps#{pk}", "cv"], [sb.key], bias=cv[:, 4, ft:ft + 1])
                    if ft >= 16:
                        K.dma("sp", bcT[(ft - 16) * 128:(ft - 15) * 128, b0:b0 + bn], sb[:, 0:bn], "S" + sb.key,
                              reads=[sb.key], accw=["bcT"])
                    if ft < 20:
                        for i in range(nt_):
                            tb = psum[i][:].bitcast(BF16)
                            K.op("pe", lambda e, tb=tb, sb=sb, i=i, ft=ft: e.transpose(
                                tb[:, (ft % 8) * 128:(ft % 8 + 1) * 128], sb[:, i * 128:(i + 1) * 128], idb[:]),
                                reads=[sb.key, "idb"], writes=[f"ps#{i}"], inc=True)
                        if ft % 8 == 7 or ft == 19:
                            ncol = (ft % 8 + 1) * 128
                            c0 = (ft // 8) * 1024
                            for i in range(nt_):
                                tb = psum[i][:].bitcast(BF16)
                                K.cp("dve" if i % 2 else "act", rbs[i][:, c0:c0 + ncol], tb[:, 0:ncol], [f"ps#{i}"],
                                     accw=[rbs[i].key])
                for i in range(nt_):
                    K.dma("sp", xsB[b0 + i * 128:b0 + (i + 1) * 128, :], rbs[i][:, :], "S" + rbs[i].key,
                          reads=[rbs[i].key], accw=["xsB"])
            K.barrier()

    if "conv" not in P.dbg.get("skip", ()):
        phase_conv()
    if stop_after == "conv":
        K.barrier(); pes.close(); P.es.close(); return P


    def phase_ssd():
        with ExitStack() as es:
            def alloc(name, shape, dt):
                P.uid += 1
                return es.enter_context(nc.sbuf_tensor(f"{name}_{P.uid}", list(shape), dt))
            vb = alloc("vb", [128, 160], F32)
            aneg = alloc("aneg", [128, 64], F32)
            nwbc = alloc("nwbc", [128, 2048], F32)
            mk = alloc("mk", [128, 2, 128], F32)
            self_ = alloc("self", [64, 1024], F32)
            selb = alloc("selb", [64, 32, 128], BF16)
            onesf = alloc("onesf2", [128, 128], F32)
            ones1 = alloc("ones1b", [128, 1], F32)
            state = alloc("state", [128, 2048], F32)
            stbf_ = Ring(alloc, "stbf", 2, [128, 2048], BF16)
            S_ = Ring(alloc, "Ssb", 2, [128, 2048], F32)
            ST = {}
            xb_ = Ring(alloc, "xb", 5, [128, 2560], BF16)
            bc_ = Ring(alloc, "bc", 5, [128, 8, 128], BF16)
            dr_ = Ring(alloc, "dr", 3, [128, 32], F32)
            sm_ = Ring(alloc, "sm", 5, [128, 8, 32], F32)
            cs4_ = Ring(alloc, "cs4", 3, [128, 64], F32)
            cst_ = Ring(alloc, "cst", 3, [64, 128], F32)
            hl_ = Ring(alloc, "hl", 3, [64, 3, 128], BF16)
            X_ = Ring(alloc, "Xd", 3, [128, 2048], BF16)
            Xc_ = Ring(alloc, "Xc", 3, [128, 2048], BF16)
            xd_ = Ring(alloc, "xdk", 1, [128, 2048], F32)
            cbm_ = Ring(alloc, "cbm", 2, [128, 4, 128], F32)
            E_ = Ring(alloc, "sE", 2, [128, 512], F32)
            MT_ = Ring(alloc, "sMT", 4, [128, 4, 128], BF16)
            to_ = Ring(alloc, "sto", 2, [128, 512], F32)
            ysb_ = Ring(alloc, "ysb", 2, [128, 2048], F32)
            yf_ = Ring(alloc, "yfl", 1, [128, 2048], F32)
            zt_ = Ring(alloc, "ztl", 1, [128, 2048], BF16)
            ynb_ = Ring(alloc, "ynb", 1, [128, 2048], BF16)
            yT_ = Ring(alloc, "yTs", 1, [128, 16, 128], BF16)
            junk = alloc("sjunk", [128, 512], BF16)
            K.op("dve", lambda e: e.memset(onesf[:], 1.0), writes=["onesf2"])
            K.op("dve", lambda e: e.memset(ones1[:], 1.0), writes=["ones1b"])
            negb = alloc("negb", [128, 2, 512], BF16)
            for dd in range(2):
                K.dma("sp", to_.bufs[dd][:, :], negm[dd, :, :], "Lnegm%d" % dd, writes=[to_.bufs[dd].key])
                K.cp("dve", negb[:, dd, :], to_.bufs[dd][:, :], [to_.bufs[dd].key], accw=["negb"])
            K.dma("sp", vb[:, :], ssd_vec[0, :].partition_broadcast(128), "Lvb", writes=["vb"])
            K.dma("sp", nwbc[:, :], ssd_norm.partition_broadcast(128), "Lnwbc", writes=["nwbc"])
            K.dma("sp", mk[:], maskT.rearrange("d s l -> s d l"), "Lmk", writes=["mk"])
            for q4 in range(4):
                K.dma("sp", self_[:, :], selc[:, q4 * 1024:(q4 + 1) * 1024], "Lself", writes=["self"])
                K.cp("dve", selb[:, q4 * 8:(q4 + 1) * 8, :].rearrange("k h l -> k (h l)"), self_[:, :], ["self"], accw=["selb"])
            K.act(aneg[:, :], vb[:, 0:64], AF.Exp, ["vb"], ["aneg"])
            K.ts("dve", aneg[:, :], aneg[:, :], -1.0, None, ALU.mult, None, ["aneg"], ["aneg"])
            lat_chunks = list(range(2, NT))
            ncl = P.dbg.get("nchunk", len(lat_chunks))
            lat_chunks = lat_chunks[:ncl]
            PB = {"yd": 0, "seg": 0, "so": 0}

            def stage_a(d, c):
                tok0 = c * 128
                lat = c >= 2
                xb = xb_.next(); bc = bc_.next(); dr = dr_.next(); sm = sm_.next()
                ctx_ = {"xb": xb, "bc": bc, "sm": sm, "lat": lat, "tok0": tok0}
                K.dma("sp", xb[:, :], xsB[tok0:tok0 + 128, :], "L" + xb.key, writes=[xb.key])
                K.dma("sp", bc[:], bcT[:, tok0:tok0 + 128].rearrange("(f p) t -> p f t", p=128), "L" + bc.key,
                      writes=[bc.key])
                K.dma("sp", dr[:, :], dtraw[tok0:tok0 + 128, d * 32:(d + 1) * 32], "L" + dr.key, writes=[dr.key])
                dt = sm[:, 0, :]; adt = sm[:, 1, :]; cs = sm[:, 2, :]; ecs = sm[:, 3, :]
                etot = sm[:, 4, :]; w2 = sm[:, 5, :]; tmp = sm[:, 6, :]
                k_ = sm.key
                K.tt("dve", tmp, dr[:, :], vb[:, 64 + d * 32:96 + d * 32], ALU.add, [dr.key, "vb"], [k_])
                K.act(tmp, tmp, AF.Exp, [k_], [k_])
                K.act(dt, tmp, AF.Ln, [k_, "ones1b"], [k_], bias=ones1[:, 0:1])
                K.tt("dve", adt, dt, aneg[:, d * 32:(d + 1) * 32], ALU.mult, [k_, "aneg"], [k_])
                return ctx_

            def stage_s2(d, ctx_):
                xb, sm, lat = ctx_["xb"], ctx_["sm"], ctx_["lat"]
                dt = sm[:, 0, :]; adt = sm[:, 1, :]; cs = sm[:, 2, :]; ecs = sm[:, 3, :]
                etot = sm[:, 4, :]; w2 = sm[:, 5, :]; tmp = sm[:, 6, :]
                k_ = sm.key
                K.mm(psum[4][:, 0:32], mk[:, d, :], adt, True, True, ["mk", k_], ["ps#4a"], True)
                K.mm(psum[4][:, 32:64], onesf[:, :], adt, True, True, ["onesf2", k_], ["ps#4a"], True)
                K.cp("dve", cs, psum[4][:, 0:32], ["ps#4a"], [k_])
                K.act(etot, psum[4][:, 32:64], AF.Exp, ["ps#4a"], [k_])
                K.tt("dve", tmp, psum[4][:, 32:64], cs, ALU.subtract, ["ps#4a", k_], [k_])
                K.act(w2, tmp, AF.Exp, [k_], [k_])
                K.tt("dve", w2, w2, dt, ALU.mult, [k_], [k_])
                xs3 = xb[:, 0:2048].rearrange("p (h q) -> p h q", q=64)
                Xc = Xc_.next()
                ctx_["Xc"] = Xc
                K.tt("pool", Xc[:, :].rearrange("p (h q) -> p h q", q=64), xs3,
                     w2.unsqueeze(2).to_broadcast([128, 32, 64]), ALU.mult, [xb.key, k_], [Xc.key])
                if not lat:
                    return ctx_
                K.act(ecs, cs, AF.Exp, [k_], [k_])
                X = X_.next()
                K.tt("pool", X[:, :].rearrange("p (h q) -> p h q", q=64), xs3,
                     dt.unsqueeze(2).to_broadcast([128, 32, 64]), ALU.mult, [xb.key, k_], [X.key])
                cs4 = cs4_.next(); cst = cst_.next(); hl = hl_.next()
                K.cp("dve", cs4[:, 0:32], cs, [k_], accw=[cs4.key])
                K.cp("dve", cs4[:, 32:64], cs, [k_], accw=[cs4.key])
                ctx_["X"] = X; ctx_["cs4"] = cs4; ctx_["cst"] = cst; ctx_["hl"] = hl
                return ctx_

            def stage_s3(d, ctx_):
                if not ctx_["lat"]:
                    return ctx_
                cs4, cst, hl, X = ctx_["cs4"], ctx_["cst"], ctx_["hl"], ctx_["X"]
                K.op("pe", lambda e, cs4=cs4: e.transpose(psum[4][0:64, 128:256], cs4[:, :], idf[:, :]),
                     reads=[cs4.key, "idf"], writes=["ps#4b"])
                K.cp("dve", cst[:, :], psum[4][0:64, 128:256], ["ps#4b"], [cst.key])
                K.cp("dve", hl[0:32, 0, :], cst[0:32, :], [cst.key], accw=[hl.key])
                K.cp("dve", hl[32:64, 2, :], cst[32:64, :], [cst.key], accw=[hl.key])
                K.tt("dve", hl[32:64, 0, :], cst[32:64, :], hl[32:64, 2, :], ALU.subtract, [cst.key, hl.key], accw=[hl.key])
                K.ts("dve", hl[:, 1, :], hl[:, 0, :], -1.0, None, ALU.mult, None, [hl.key], accw=[hl.key])
                ctx_["X"] = X; ctx_["hl"] = hl
                return ctx_

            def stage_a1(d, cx, bgen=None):
                if cx["lat"]:
                    stage_a1_lat(d, cx, bgen)
                xb, Xc = cx["xb"], cx["Xc"]
                Ssb = S_.next()
                cx["Ssb"] = Ssb
                for g in range(4):
                    pk = 6 + PB["so"] % 2; PB["so"] += 1
                    K.mm(psum[pk][:, :], xb[:, 2048 + g * 128:2048 + (g + 1) * 128], Xc[:, g * 512:(g + 1) * 512],
                         True, True, [xb.key, Xc.key], [f"ps#{pk}"], True)
                    K.cp("act", Ssb[:, g * 512:(g + 1) * 512], psum[pk][:, :], [f"ps#{pk}"], accw=[Ssb.key])

            def stage_a1_lat(d, cx, bgen=None):
                xb, bc, sm, tok0, X, hl = cx["xb"], cx["bc"], cx["sm"], cx["tok0"], cx["X"], cx["hl"]
                ctx_ = cx
                for g in range(4):
                    K.mm(psum[5][:, g * 128:(g + 1) * 128], bc[:, g, :], bc[:, 4 + g, :], True, True,
                         [bc.key], ["ps#5"], g == 3)
                cbm = cbm_.next()
                K.tt("dve", cbm[:], psum[5][:, :].rearrange("p (g l) -> p g l", l=128),
                     mk[:, d:d + 1, :].to_broadcast([128, 4, 128]), ALU.mult, ["ps#5", "mk"], [cbm.key])
                ysb = ysb_.next()
                ctx_["ysb"] = ysb
                if d == 1:
                    yf = yf_.next()
                    K.dma("sp", yf[:, :], yfw[tok0:tok0 + 128, :], "L" + yf.key, reads=["yfw"], writes=[yf.key])
                segbank = {}

                def emit_seg(hq):
                    pk = 2 + PB["seg"] % 2; PB["seg"] += 1
                    segbank[hq] = pk
                    K.mm(psum[pk][:, :], idb[:, :], negb[:, d, :], True, False, ["idb", "negb"], [f"ps#{pk}"], False)
                    for j in range(4):
                        h = hq * 4 + j
                        K.mm(psum[pk][:, j * 128:(j + 1) * 128], selb[:, h, :], hl[:, 0, :], False, False,
                             ["selb", hl.key], [f"ps#{pk}"], False)
                        K.mm(psum[pk][:, j * 128:(j + 1) * 128], hl[:, 1, :], selb[:, h, :], False, True,
                             ["selb", hl.key], [f"ps#{pk}"], j == 3)

                pyd = 0
                emit_seg(0)
                for hq in range(8):
                    g = hq // 2
                    if hq + 1 < 8:
                        emit_seg(hq + 1)
                    pk = segbank[hq]
                    if hq % 2 == 0:
                        pyd = PB["yd"] % 2; PB["yd"] += 1
                    E = E_.next(); MT = MT_.next()
                    K.act(E[:, :], psum[pk][:, :], AF.Exp, [f"ps#{pk}"], [E.key])
                    K.stt("dve", MT[:], E[:, :].rearrange("p (j l) -> p j l", l=128), 1e30,
                          cbm[:, g:g + 1, :].to_broadcast([128, 4, 128]), ALU.min, ALU.mult,
                          [E.key, cbm.key], [MT.key])
                    for j in range(4):
                        h = hq * 4 + j
                        K.mm(psum[pyd][:, (h % 8) * 64:(h % 8 + 1) * 64], MT[:, j, :], X[:, h * 64:(h + 1) * 64],
                             True, True, [MT.key, X.key], [f"ps#{pyd}"], (h % 8 == 7))
                    if hq % 2 == 1:
                        if d == 0:
                            K.cp("act", ysb[:, g * 512:(g + 1) * 512], psum[pyd][:, :], [f"ps#{pyd}"], accw=[ysb.key])
                        else:
                            K.tt("dve", ysb[:, g * 512:(g + 1) * 512], psum[pyd][:, :], yf[:, g * 512:(g + 1) * 512],
                                 ALU.add, [f"ps#{pyd}", yf.key], accw=[ysb.key])
                    if bgen is not None:
                        next(bgen, None)
                return ctx_

            def stage_b(d, cx):
                xb, bc, sm, lat, tok0, Xc = cx["xb"], cx["bc"], cx["sm"], cx["lat"], cx["tok0"], cx["Xc"]
                k_ = sm.key
                ecs = sm[:, 3, :]; etot = sm[:, 4, :]
                xs3 = xb[:, 0:2048].rearrange("p (h q) -> p h q", q=64)
                Ssb = cx["Ssb"]
                stprev = ST["cur"]
                stnew = stbf_.next()
                ST["cur"] = stnew
                K.tt("dve", state[:, :].rearrange("p (h q) -> p h q", q=64), state[:, :].rearrange("p (h q) -> p h q", q=64),
                     etot.unsqueeze(2).to_broadcast([128, 32, 64]), ALU.mult, ["state", k_], ["state"])
                yield
                K.tt("dve", state[:, :], state[:, :], Ssb[:, :], ALU.add, ["state", Ssb.key], ["state"])
                K.cp("act", stnew[:, :], state[:, :], ["state"], [stnew.key])
                yield
                if lat:
                    ysb = cx["ysb"]
                    for g in range(4):
                        pk = 6 + PB["so"] % 2; PB["so"] += 1
                        K.mm(psum[pk][:, :], bc[:, 4 + g, :], stprev[:, g * 512:(g + 1) * 512], True, True,
                             [bc.key, stprev.key], [f"ps#{pk}"], True)
                        to = to_.next()
                        K.tt("dve", to[:, :].rearrange("p (h q) -> p h q", q=64),
                             psum[pk][:, :].rearrange("p (h q) -> p h q", q=64),
                             ecs[:, g * 8:(g + 1) * 8].unsqueeze(2).to_broadcast([128, 8, 64]), ALU.mult,
                             [f"ps#{pk}", k_], [to.key])
                        K.tt("dve", ysb[:, g * 512:(g + 1) * 512], ysb[:, g * 512:(g + 1) * 512], to[:, :], ALU.add,
                             [ysb.key, to.key], [ysb.key])
                        yield
                if not lat:
                    return
                if d == 0:
                    K.dma("sp", yfw[tok0:tok0 + 128, :], ysb[:, :], "S" + ysb.key, reads=[ysb.key], accw=["yfw"])
                    return
                zt = zt_.next(); ynb = ynb_.next(); yT = yT_.next(); xd = xd_.next()
                K.dma("sp", zt[:, :], zs[tok0:tok0 + 128, :], "L" + zt.key, writes=[zt.key])
                K.tt("pool", xd[:, :].rearrange("p (h q) -> p h q", q=64), xs3,
                     vb[:, 128:160].unsqueeze(2).to_broadcast([128, 32, 64]), ALU.mult, [xb.key, "vb"], [xd.key])
                K.tt("dve", ysb[:, :], ysb[:, :], xd[:, :], ALU.add, [ysb.key, xd.key], [ysb.key])
                K.tt("dve", ysb[:, :], ysb[:, :], zt[:, :], ALU.mult, [ysb.key, zt.key], [ysb.key])
                yield
                for g in range(4):
                    K.act(junk[:, :], ysb[:, g * 512:(g + 1) * 512], AF.Square, [ysb.key], ["sjunk"], accw=[k_],
                          accum_out=sm[:, 7, g:g + 1])
                K.act(sm[:, 7, 4:8], sm[:, 7, 0:4], AF.Sqrt, [k_, "epsb"], [k_], scale=1.0 / 512, bias=epsb[:, 0:1])
                K.op("dve", lambda e, sm=sm: e.reciprocal(sm[:, 7, 8:12], sm[:, 7, 4:8]), [k_], [k_])
                for g in range(4):
                    K.stt("dve", ynb[:, g * 512:(g + 1) * 512], ysb[:, g * 512:(g + 1) * 512], sm[:, 7, 8 + g:9 + g],
                          nwbc[:, g * 512:(g + 1) * 512], ALU.mult, ALU.mult, [ysb.key, k_, "nwbc"], accw=[ynb.key])
                yield
                for half in range(2):
                    pk = half
                    tb = psum[pk][:].bitcast(BF16)
                    for jj in range(8):
                        j = half * 8 + jj
                        K.op("pe", lambda e, tb=tb, jj=jj, j=j, ynb=ynb: e.transpose(
                            tb[:, jj * 128:(jj + 1) * 128], ynb[:, j * 128:(j + 1) * 128], idb[:]),
                            reads=[ynb.key, "idb"], writes=[f"ps#{pk}"], inc=(jj == 7))
                    K.cp("act", yT[:, half * 8:(half + 1) * 8, :].rearrange("p j t -> p (j t)"), tb[:, :], [f"ps#{pk}"],
                         accw=[yT.key])
                K.dma("sp", mixT1[:, tok0:tok0 + 128].rearrange("(j p) t -> p j t", p=128), yT[:], "S" + yT.key,
                      reads=[yT.key], accw=["mixT1"])

            for d in range(2):
                order = [0, 1] + lat_chunks if d == 0 else [1, 0] + lat_chunks[::-1]
                K.op("pool", lambda e: e.memset(state[:], 0.0), writes=["state"])
                st0 = stbf_.next()
                ST["cur"] = st0
                K.op("pool", lambda e, st0=st0: e.memset(st0[:], 0.0), writes=[st0.key])
                n_ = len(order)
                cxs = {}
                def run(stage, idx):
                    if 0 <= idx < n_:
                        if stage == 1:
                            cxs[idx] = stage_a(d, order[idx])
                        elif stage == 2:
                            stage_s2(d, cxs[idx])
                        elif stage == 3:
                            stage_s3(d, cxs[idx])
                        elif stage == 4:
                            stage_a1(d, cxs[idx])
                        else:
                            stage_b(d, cxs.pop(idx))
                for it in range(-3, n_):
                    run(3, it + 1)
                    bgen = stage_b(d, cxs.pop(it)) if 0 <= it < n_ else None
                    if 0 <= it + 1 < n_:
                        stage_a1(d, cxs[it + 1], bgen)
                    if bgen is not None:
                        for _ in bgen:
                            pass
                    run(1, it + 3); run(2, it + 2)
            K.barrier()

    if "ssd" not in P.dbg.get("skip", ()):
        phase_ssd()
    if stop_after == "ssd":
        K.barrier(); pes.close(); P.es.close(); return P

    nt1 = P.dbg.get("out1_tiles", SEQ // 128)
    tiles1 = [(CTX + i * 128, i * 128, 0) for i in range(nt1)]
    phase_out(1, mixT1, 16, o_w_out, h1, out_h, tiles1)

    K.barrier()
    pes.close()
    P.es.close()
    return P


def _rope_tables():
    n_freq = 16
    inv = (10000.0 ** (-np.arange(n_freq, dtype=np.float32) / np.float32(n_freq))).astype(np.float32)
    t = np.arange(SEQ)
    row = (t // 64).astype(np.float32)
    col = (t % 64).astype(np.float32)
    ang = np.concatenate([row[:, None] * inv, col[:, None] * inv], axis=-1).astype(np.float32)
    cos, sin = np.cos(ang).astype(np.float32), np.sin(ang).astype(np.float32)
    tab = np.zeros((2, 128, T), np.float32)
    tab[0, :, :CTX] = 1.0
    for p in range(128):
        d = p % 64
        fi = (d % 16) + 16 * (d // 32)
        sgn = -1.0 if (d % 32) < 16 else 1.0
        tab[0, p, CTX:] = cos[:, fi]
        tab[1, p, CTX:] = sgn * sin[:, fi]
    return tab


def _rope_perm():
    perm = np.zeros(512, np.int64)
    for f in range(512):
        d = f % 64
        e = d % 32
        e2 = e + 16 if e < 16 else e - 16
        perm[f] = f - d + (d // 32) * 32 + e2
    return perm


def make_in_maps(inp):
    B = inp["x"].shape[0]
    perm = _rope_perm()
    w = np.asarray(inp["e_w_in"][0], np.float32)
    w_aug = np.ascontiguousarray(np.concatenate([w, w[:, 1024 + perm], w[:, 1536 + perm]], axis=1))
    tab = _rope_tables()
    ident = np.eye(128, dtype=np.float32)
    wbd = np.zeros((2, 2, 4, 128, 128), np.float32)
    for d in range(2):
        for g, nm in enumerate(("lru_w_r", "lru_w_i")):
            wsrc = np.asarray(inp[nm][0][d], np.float32)
            for cc in range(4):
                wbd[d, g, cc, 0:64, 0:64] = wsrc[2 * cc]
                wbd[d, g, cc, 64:128, 64:128] = wsrc[2 * cc + 1]
    wbd = np.ascontiguousarray(wbd.reshape(16, 128, 128))
    lvec = np.ascontiguousarray(np.concatenate([
        np.asarray(inp["lru_conv_w"][0], np.float32), np.asarray(inp["lru_conv_b"], np.float32).reshape(1, 512),
        np.asarray(inp["lru_b_r"][0], np.float32), np.asarray(inp["lru_b_i"][0], np.float32),
        np.asarray(inp["lru_lambda"][0], np.float32)], axis=0))
    ssd_cv = np.ascontiguousarray(np.concatenate([np.asarray(inp["ssd_conv_w"][0], np.float32),
                                                  np.asarray(inp["ssd_conv_b"], np.float32).reshape(1, 3072)], 0))
    ssd_vec = np.ascontiguousarray(np.concatenate([np.asarray(inp["ssd_a_log"][0], np.float32).reshape(-1),
                                                   np.asarray(inp["ssd_dt_bias"][0], np.float32).reshape(-1),
                                                   np.asarray(inp["ssd_d"][0], np.float32).reshape(-1)]).reshape(1, 160))
    ii = np.arange(128)
    maskT = np.stack([(ii[None, :] >= ii[:, None]), (ii[None, :] <= ii[:, None])], 0).astype(np.float32)
    negm = np.ascontiguousarray(np.tile((maskT - 1.0) * 30000.0, (1, 1, 4)).astype(np.float32))
    selc = np.zeros((64, 32, 128), np.float32)
    for hh in range(32):
        selc[hh, hh, :] = 1.0
        selc[32 + hh, hh, :] = 1.0
    selc = np.ascontiguousarray(selc.reshape(64, 32 * 128))
    maps = []
    for b in range(B):
        m = {
            "src0": np.ascontiguousarray(np.concatenate([inp["ctx"][b], inp["x"][b]], axis=0), dtype=np.float32),
            "cvec": np.ascontiguousarray(np.stack([inp["c"][b], inp["c_ctx"]], 0), dtype=np.float32),
            "w_mod": np.asarray(inp["w_mod"], np.float32),
            "b_mod": np.asarray(inp["b_mod"], np.float32),
            "g_pre": np.asarray(inp["g_pre"], np.float32),
            "g_post": np.asarray(inp["g_post"], np.float32),
            "e_w_in_aug": w_aug,
            "e_w_out": np.asarray(inp["e_w_out"][0], np.float32),
            "ident": ident,
            "ropetab": tab,
            "lru_wbd": wbd,
            "lru_vec": lvec,
            "da_lam": np.ascontiguousarray(np.asarray(inp["da_lambda"][0], np.float32).reshape(1, 256)),
            "da_sub": np.ascontiguousarray(np.asarray(inp["da_subln"][0], np.float32).reshape(1, 128)),
            "o_w_in": np.asarray(inp["o_w_in"][0], np.float32),
            "o_w_out": np.asarray(inp["o_w_out"][0], np.float32),
            "ssd_cv": ssd_cv,
            "ssd_vec": ssd_vec,
            "ssd_norm": np.asarray(inp["ssd_norm"][0], np.float32),
            "maskT": maskT,
            "selc": selc,
            "negm": negm,
        }
        maps.append(m)
    return maps


def kernel(**inp):
    P = build_program()
    maps = make_in_maps(inp)
    res = run_bass_kernel_spmd(P.nc, maps, core_ids=list(range(8)))
    return np.stack([np.asarray(r["out"], np.float32) for r in res.results], 0)
```

```python
import os
from contextlib import ExitStack
import numpy as np
import concourse.bass as bass
import concourse.mybir as mybir
from concourse.bass_utils import run_bass_kernel_spmd

F32, BF16 = mybir.dt.float32, mybir.dt.bfloat16
AF = mybir.ActivationFunctionType
ALU = mybir.AluOpType
AX = mybir.AxisListType

D = 1024
SEQ = 4096
CTX = 256
T = SEQ + CTX
NT = T // 128
EPS = 1e-6


class Sched:
    ENG = ("pe", "dve", "act", "pool", "sp")

    def __init__(self, nc, es):
        self.nc, self.es = nc, es
        self.e = {"pe": nc.tensor, "dve": nc.vector, "act": nc.scalar, "pool": nc.gpsimd, "sp": nc.sync}
        self.sem, self.cnt = {}, {}
        for n in self.ENG:
            self.sem[n] = es.enter_context(nc.semaphore("s_" + n))
            self.cnt[n] = 0
        self.seen = {n: {} for n in self.ENG}
        self.W, self.Rd = {}, {}
        self.pend = {n: [] for n in self.ENG}
        self.nwait = 0
        self.nins = 0
        self.store_q = os.environ.get('KSTOREQ', 'pool') or None

    def _wait(self, eng, need):
        for s, v in need.items():
            if s == "pe" and eng == "pe":
                continue
            if self.seen[eng].get(s, 0) >= v:
                continue
            self.e[eng].wait_ge(self.sem[s], v)
            self.seen[eng][s] = v
            self.nwait += 1

    def _deps(self, eng, reads, writes, accw):
        need = {}

        def add(d):
            for s, v in d.items():
                if need.get(s, 0) < v:
                    need[s] = v
        for r in reads:
            add(self.W.get(r, {}))
        for w in writes:
            add(self.W.get(w, {}))
            add(self.Rd.get(w, {}))
        for w in accw:
            add(self.Rd.get(w, {}))
        self._wait(eng, need)

    def _register(self, ev, reads, writes, accw):
        s, v = ev
        for r in reads:
            d = self.Rd.setdefault(r, {})
            d[s] = max(d.get(s, 0), v)
        for w in writes:
            self.W[w] = {s: v}
            self.Rd[w] = {}
        for w in accw:
            d = self.W.setdefault(w, {})
            d[s] = max(d.get(s, 0), v)

    def op(self, eng, fn, reads=(), writes=(), accw=(), inc=True):
        self._deps(eng, reads, writes, accw)
        ins = fn(self.e[eng])
        self.nins += 1
        if inc:
            self.cnt[eng] += 1
            ins.then_inc(self.sem[eng], 1)
            ev = (eng, self.cnt[eng])
            for (r, w, a) in self.pend[eng]:
                self._register(ev, r, w, a)
            self.pend[eng] = []
            self._register(ev, reads, writes, accw)
        else:
            self.pend[eng].append((tuple(reads), tuple(writes), tuple(accw)))

    def dma(self, q, out, in_, semkey, reads=(), writes=(), accw=(), **kw):
        if self.store_q and semkey.startswith("S"):
            q = self.store_q
        if semkey not in self.sem:
            self.sem[semkey] = self.es.enter_context(self.nc.semaphore("d_" + semkey.replace("#", "_")))
            self.cnt[semkey] = 0
        self._deps(q, reads, writes, accw)
        ins = self.e[q].dma_start(out=out, in_=in_, **kw)
        ins.then_inc(self.sem[semkey], 16)
        self.cnt[semkey] += 16
        self.nins += 1
        self._register((semkey, self.cnt[semkey]), reads, writes, accw)

    def tt(self, eng, out, a, b, op, reads, writes=(), accw=()):
        self.op(eng, lambda e: e.tensor_tensor(out, a, b, op), reads, writes, accw)

    def ts(self, eng, out, a, s1, s2, op0, op1=None, reads=(), writes=(), accw=()):
        if op1 is None:
            self.op(eng, lambda e: e.tensor_scalar(out, a, s1, None, op0), reads, writes, accw)
        else:
            self.op(eng, lambda e: e.tensor_scalar(out, a, s1, s2, op0, op1), reads, writes, accw)

    def stt(self, eng, out, a, sc, b, op0, op1, reads, writes=(), accw=()):
        self.op(eng, lambda e: e.scalar_tensor_tensor(out, a, sc, b, op0, op1), reads, writes, accw)

    def act(self, out, in_, func, reads, writes=(), accw=(), **kw):
        self.op("act", lambda e: e.activation(out=out, in_=in_, func=func, **kw), reads, writes, accw)

    def cp(self, eng, out, in_, reads, writes=(), accw=()):
        if eng == "act":
            self.op("act", lambda e: e.copy(out, in_), reads, writes, accw)
        else:
            self.op(eng, lambda e: e.tensor_copy(out, in_), reads, writes, accw)

    def mm(self, out, lhsT, rhs, start, stop, reads, writes, inc):
        self.op("pe", lambda e: e.matmul(out, lhsT, rhs, start=start, stop=stop), reads, writes, inc=inc)

    def barrier(self):
        for n in self.ENG:
            assert not self.pend[n]
        allev = {s: c for s, c in self.cnt.items() if c > 0}
        for n in self.ENG:
            self._wait(n, allev)
        self.W, self.Rd = {}, {}


class Buf:
    def __init__(self, t, key):
        self.t, self.key = t, key

    def __getitem__(self, k):
        return self.t[k]


class Ring:
    def __init__(self, alloc, name, n, shape, dtype):
        self.bufs = [Buf(alloc(f"{name}{i}", shape, dtype), f"{name}#{i}") for i in range(n)]
        self.i = 0

    def next(self):
        b = self.bufs[self.i % len(self.bufs)]
        self.i += 1
        return b


class Prog:
    def __init__(self, dbg=None):
        self.dbg = dbg or {}
        self.nc = nc = bass.Bass("TRN2", target_bir_lowering=False)
        self.es = ExitStack()
        self.K = Sched(nc, self.es)
        self.dram = {}
        self.uid = 0

    def din(self, name, shape, dt=F32):
        self.dram[name] = self.nc.dram_tensor(name, list(shape), dt, kind="ExternalInput").ap()
        return self.dram[name]

    def dout(self, name, shape, dt=F32):
        self.dram[name] = self.nc.dram_tensor(name, list(shape), dt, kind="ExternalOutput").ap()
        return self.dram[name]

    def dscr(self, name, shape, dt):
        if name in self.dbg.get("dump", ()):
            return self.dout(name, shape, dt)
        self.dram[name] = self.nc.dram_tensor(name, list(shape), dt).ap()
        return self.dram[name]


def build_program(dbg=None):
    P = Prog(dbg)
    nc, K = P.nc, P.K
    stop_after = P.dbg.get("stop_after", "all")

    src0 = P.din("src0", [T, D])
    cvec = P.din("cvec", [2, D])
    w_mod = P.din("w_mod", [2, D, 3 * D])
    b_mod = P.din("b_mod", [2, 3 * D])
    g_pre = P.din("g_pre", [2, D])
    g_post = P.din("g_post", [2, D])
    e_w_in = P.din("e_w_in_aug", [D, 4096])
    e_w_out = P.din("e_w_out", [D, D])
    ident = P.din("ident", [128, 128])
    ropetab = P.din("ropetab", [2, 128, T])
    out_h = P.dout("out", [SEQ, D])
    lru_wbd = P.din("lru_wbd", [16, 128, 128])
    lru_vec = P.din("lru_vec", [11, 512])
    da_lam = P.din("da_lam", [1, 256])
    da_sub = P.din("da_sub", [1, 128])
    mixT0 = P.dscr("mixT0", [D, T], BF16)
    o_w_in = P.din("o_w_in", [D, 5184])
    o_w_out = P.din("o_w_out", [2048, D])
    ssd_cv = P.din("ssd_cv", [5, 3072])
    ssd_vec = P.din("ssd_vec", [1, 160])
    ssd_norm = P.din("ssd_norm", [2048])
    maskT = P.din("maskT", [2, 128, 128])
    selc = P.din("selc", [64, 32 * 128])
    negm = P.din("negm", [2, 128, 512])
    xbcT = P.dscr("xbcT", [3072, T], BF16)
    zs = P.dscr("zs", [T, 2048], BF16)
    dtraw = P.dscr("dtraw", [T, 64], F32)
    xsB = P.dscr("xsB", [T, 2560], BF16)
    bcT = P.dscr("bcT", [1024, T], BF16)
    yfw = P.dscr("yfw", [T, 2048], F32)
    mixT1 = P.dscr("mixT1", [2048, T], BF16)
    h1 = P.dscr("h1", [T, D], F32)

    xrT = P.dscr("xrT", [512, T], F32)
    grT = P.dscr("grT", [512, T], BF16)
    gdT = P.dscr("gdT", [512, T], BF16)
    qT = P.dscr("qT", [512, T], BF16)
    kT = P.dscr("kT", [512, T], BF16)
    vtok = P.dscr("vtok", [T, 512], BF16)

    pes = ExitStack()
    def palloc(name, shape, dt):
        return pes.enter_context(nc.sbuf_tensor(name, list(shape), dt))
    psum = [pes.enter_context(nc.psum_tensor(f"psb{i}", [128, 512], F32)) for i in range(8)]
    idf = palloc("idf", [128, 128], F32)
    idb = palloc("idb", [128, 128], BF16)
    epsb = palloc("epsb", [128, 1], F32)
    modA = palloc("modA", [128, 2, 8, 2], F32)
    modS = palloc("modS", [128, 2, 8, 2], F32)
    ggbc = palloc("ggbc", [128, 2, 2, D], F32)

    K.dma("sp", idf[:], ident[:, :], "Lidf", writes=["idf"])
    K.op("dve", lambda e: e.tensor_copy(idb[:], idf[:]), reads=["idf"], writes=["idb"])
    K.op("dve", lambda e: e.memset(epsb[:], EPS), writes=["epsb"])

    def phase_mod():
        with ExitStack() as es:
            def alloc(name, shape, dt):
                P.uid += 1
                return es.enter_context(nc.sbuf_tensor(f"{name}_{P.uid}", list(shape), dt))
            cT = alloc("cT", [128, 2, 8], F32)
            sig = alloc("sig", [128, 2, 8], F32)
            srep = alloc("srep", [128, 2, 8, 128], F32)
            bT = alloc("bT", [128, 2, 24], F32)
            gpT = alloc("gpT", [128, 2, 8], F32)
            bgbc = alloc("bgbc", [128, 2, D], F32)
            gpbc = alloc("gpbc", [128, 2, D], F32)
            wst = Ring(alloc, "wst", 2, [128, 8, 512], F32)
            tmp = alloc("mtmp", [128, 16, 2], F32)

            rows = alloc("rows", [80, 128], F32)
            K.dma("sp", rows[0:16, :], cvec.rearrange("t (j p) -> (t j) p", p=128), "Lrows", accw=["rows"])
            K.dma("sp", rows[16:64, :], b_mod.rearrange("l (f p) -> (l f) p", p=128), "Lrows", accw=["rows"])
            K.dma("sp", rows[64:80, :], g_pre.rearrange("l (j p) -> (l j) p", p=128), "Lrows", accw=["rows"])
            K.op("pe", lambda e: e.transpose(psum[4][:, 0:80], rows[:, :], idf[0:80, 0:80]),
                 reads=["rows", "idf"], writes=["ps#4"])
            K.op("dve", lambda e: e.tensor_copy(cT[:].rearrange("p t j -> p (t j)"), psum[4][:, 0:16]),
                 reads=["ps#4"], writes=["cT"])
            K.op("dve", lambda e: e.tensor_copy(bT[:].rearrange("p l f -> p (l f)"), psum[4][:, 16:64]),
                 reads=["ps#4"], writes=["bT"])
            K.op("dve", lambda e: e.tensor_copy(gpT[:].rearrange("p l j -> p (l j)"), psum[4][:, 64:80]),
                 reads=["ps#4"], writes=["gpT"])
            for l in range(2):
                K.dma("sp", bgbc[:, l, :], b_mod[l, 2 * D:3 * D].partition_broadcast(128), "Lbgbc", accw=["bgbc"])
                K.dma("sp", gpbc[:, l, :], g_post[l, :].partition_broadcast(128), "Lgpbc", accw=["gpbc"])
            K.op("act", lambda e: e.activation(out=sig[:], in_=cT[:], func=AF.Sigmoid), reads=["cT"], writes=["sig"])
            K.op("dve", lambda e: e.tensor_tensor(cT[:], cT[:], sig[:], ALU.mult), reads=["sig", "cT"], writes=["cT"])
            for t in range(2):
                K.op("dve", lambda e, t=t: e.tensor_copy(
                    srep[:, t, :, :], cT[:, t, :].unsqueeze(2).to_broadcast([128, 8, 128])),
                    reads=["cT"], accw=["srep"])
            for l in range(2):
                for pc in range(6):
                    wb = wst.next()
                    K.dma("sp", wb[:], w_mod[l, :, pc * 512:(pc + 1) * 512].rearrange("(j p) n -> p j n", p=128),
                          "L" + wb.key, writes=[wb.key])
                    if pc < 4:
                        pst = psum[pc % 2]
                        for f in range(4):
                            for j in range(8):
                                K.op("pe", lambda e, f=f, j=j, pst=pst, wb=wb: e.matmul(
                                    pst[:, 2 * f:2 * f + 2], wb[:, j, f * 128:(f + 1) * 128], cT[:, :, j],
                                    start=(j == 0), stop=(j == 7)),
                                    reads=[wb.key, "cT"], writes=[f"ps#{pc % 2}"], inc=(j == 7 and f == 3))
                        K.op("dve", lambda e, pst=pst, pc=pc: e.tensor_copy(
                            tmp[:, pc * 4:(pc + 1) * 4, :], pst[:, 0:8].rearrange("p (f t) -> p f t", t=2)),
                            reads=[f"ps#{pc % 2}"], accw=["mtmp"])
                    else:
                        for t in range(2):
                            pst = psum[2 + t]
                            for j in range(8):
                                K.op("pe", lambda e, j=j, t=t, pst=pst, wb=wb: e.matmul(
                                    pst[:, :], srep[:, t, j, :], wb[:, j, :], start=(j == 0), stop=(j == 7)),
                                    reads=[wb.key, "srep"], writes=[f"ps#{2 + t}"], inc=(j == 7))
                            c0 = (pc - 4) * 512
                            K.op("dve", lambda e, t=t, l=l, c0=c0, pst=pst: e.tensor_tensor(
                                ggbc[:, l, t, c0:c0 + 512], pst[:, :], bgbc[:, l, c0:c0 + 512], ALU.add),
                                reads=[f"ps#{2 + t}", "bgbc"], accw=["ggbc"])
                for t in range(2):
                    K.op("dve", lambda e, t=t, l=l: e.tensor_tensor(
                        modS[:, l, :, t], tmp[:, 0:8, t], bT[:, l, 0:8], ALU.add),
                        reads=["mtmp", "bT"], accw=["modS"])
                    K.op("dve", lambda e, t=t, l=l: e.scalar_tensor_tensor(
                        modA[:, l, :, t], tmp[:, 8:16, t], 1.0, bT[:, l, 8:16], ALU.add, ALU.add),
                        reads=["mtmp", "bT"], accw=["modA"])
                    K.op("dve", lambda e, t=t, l=l: e.tensor_tensor(
                        modA[:, l, :, t], modA[:, l, :, t], gpT[:, l, :], ALU.mult),
                        reads=["modA", "gpT"], writes=["modA"])
                    K.op("dve", lambda e, t=t, l=l: e.tensor_tensor(
                        ggbc[:, l, t, :], ggbc[:, l, t, :], gpbc[:, l, :], ALU.mult),
                        reads=["ggbc", "gpbc"], writes=["ggbc"])
            K.barrier()

    phase_mod()
    if "mod" in P.dbg.get("dump", ()):
        dA = P.dout("dbg_modA", [128, 32]); dS = P.dout("dbg_modS", [128, 32]); dG = P.dout("dbg_gg", [128, 4 * D])
        K.dma("sp", dA[:, :], modA[:].rearrange("p l j t -> p (l j t)"), "Sdbg", reads=["modA"])
        K.dma("sp", dS[:, :], modS[:].rearrange("p l j t -> p (l j t)"), "Sdbg", reads=["modS"])
        K.dma("sp", dG[:, :], ggbc[:].rearrange("p l t f -> p (l t f)"), "Sdbg", reads=["ggbc"])
    if stop_after == "mod":
        K.barrier(); pes.close(); P.es.close(); return P

    def phase_proj(layer, src, Wd, ncols, fspecs, tspecs, extra_alloc=None, per_group=None):
        sq_prev = K.store_q
        K.store_q = os.environ.get("KPROJQ", "act")
        _phase_proj(layer, src, Wd, ncols, fspecs, tspecs, extra_alloc, per_group)
        K.store_q = sq_prev

    def _phase_proj(layer, src, Wd, ncols, fspecs, tspecs, extra_alloc=None, per_group=None):
        with ExitStack() as es:
            def alloc(name, shape, dt):
                P.uid += 1
                return es.enter_context(nc.sbuf_tensor(f"{name}_{P.uid}", list(shape), dt))
            Wb = alloc("Wb", [128, 8, ncols], BF16)
            wst = Ring(alloc, "wst", 2, [128, 8, 256], F32)
            xr_ = Ring(alloc, "xin", 4, [128, D], F32)
            xn_ = Ring(alloc, "xn", 4, [128, D], BF16)
            uT_ = Ring(alloc, "uT", 2, [128, 8, 512], BF16)
            junk = alloc("junk", [128, D], BF16)
            stat = Ring(alloc, "stat", 4, [128, 4], F32)
            ctxo = extra_alloc(alloc) if extra_alloc else None
            ceng = ["dve", "pool", "act"]
            for pc in range(ncols // 256 + (1 if ncols % 256 else 0)):
                c0 = pc * 256
                cw = min(256, ncols - c0)
                wb = wst.next()
                K.dma("sp", wb[:, :, 0:cw], Wd[:, c0:c0 + cw].rearrange("(j p) n -> p j n", p=128),
                      "L" + wb.key, writes=[wb.key])
                en = ceng[pc % 3]
                if en == "act":
                    K.op("act", lambda e, wb=wb, c0=c0, cw=cw: e.copy(Wb[:, :, c0:c0 + cw], wb[:, :, 0:cw]),
                         reads=[wb.key], accw=["Wb"])
                else:
                    K.op(en, lambda e, wb=wb, c0=c0, cw=cw: e.tensor_copy(Wb[:, :, c0:c0 + cw], wb[:, :, 0:cw]),
                         reads=[wb.key], accw=["Wb"])
            groups = [(0, CTX, 1)] + [(CTX + 512 * g, 512, 0) for g in range(SEQ // 512)]
            pst_i = [0]
            pso_i = [0]

            def front_parts(gi):
                tok0, ntok, tmod = groups[gi]
                uT = uT_.next()
                p1s, p2s = [], []
                for ti in range(ntok // 128):
                    def part1(ti=ti):
                        box = {}
                        xt = xr_.next(); xn = xn_.next(); st = stat.next()
                        box["xn"] = xn
                        K.dma("sp", xt[:], src[tok0 + ti * 128: tok0 + (ti + 1) * 128, :], "L" + xt.key, writes=[xt.key])
                        K.op("act", lambda e: e.activation(out=junk[:], in_=xt[:], func=AF.Square, accum_out=st[:, 0:1]),
                             reads=[xt.key], writes=["junk", st.key])
                        K.op("act", lambda e: e.activation(out=st[:, 1:2], in_=st[:, 0:1], func=AF.Sqrt,
                                                           scale=1.0 / D, bias=epsb[:, 0:1]),
                             reads=[st.key, "epsb"], writes=[st.key])
                        K.op("dve", lambda e: e.reciprocal(st[:, 2:3], st[:, 1:2]), reads=[st.key], writes=[st.key])
                        K.op("pool", lambda e: e.tensor_scalar(xn[:], xt[:], st[:, 2:3], None, ALU.mult),
                             reads=[xt.key, st.key], writes=[xn.key])
                        return box

                    def part2(box, ti=ti):
                        xn = box["xn"]
                        pk = 6 + (pst_i[0] % 2); pst_i[0] += 1
                        pst = psum[pk][:].bitcast(BF16)
                        for j in range(8):
                            K.op("pe", lambda e, j=j: e.transpose(
                                pst[:, j * 128:(j + 1) * 128], xn[:, j * 128:(j + 1) * 128], idb[:]),
                                reads=[xn.key, "idb"], writes=[f"ps#{pk}"], inc=(j == 7))
                        for j in range(8):
                            K.op("dve", lambda e, j=j: e.tensor_scalar(
                                uT[:, j, ti * 128:(ti + 1) * 128], pst[:, j * 128:(j + 1) * 128],
                                modA[:, layer, j, tmod:tmod + 1], modS[:, layer, j, tmod:tmod + 1], ALU.mult, ALU.add),
                                reads=[f"ps#{pk}", "modA", "modS"], accw=[uT.key])
                    p1s.append(part1); p2s.append(part2)
                return uT, p1s, p2s

            def mm(gi, uT, hooks):
                tok0, ntok, tmod = groups[gi]
                if per_group:
                    per_group(ctxo, gi, tok0, ntok)
                nb_tot = sum(len(b) for (_, b, _) in fspecs)
                nhk = max(1, len(hooks))
                step = max(1, nb_tot // nhk)
                bcount = 0
                hooks = list(hooks)
                for (name, bundles, epi) in fspecs:
                    for bi, cols in enumerate(bundles):
                        if hooks and bcount % step == 0:
                            hooks.pop(0)()
                        bcount += 1
                        pks = []
                        for c0 in cols:
                            pk = pso_i[0] % 6; pso_i[0] += 1
                            pks.append(pk)
                            for j in range(8):
                                K.op("pe", lambda e, j=j, pk=pk, c0=c0, uT=uT, ntok=ntok: e.matmul(
                                    psum[pk][:, 0:ntok], Wb[:, j, c0:c0 + 128], uT[:, j, 0:ntok],
                                    start=(j == 0), stop=(j == 7)),
                                    reads=["Wb", uT.key], writes=[f"ps#{pk}"], inc=(j == 7))
                        epi(ctxo, pks, bi, tok0, ntok)
                for (c0, cw, epi) in tspecs:
                    for ti in range(ntok // 128):
                        pk = pso_i[0] % 6; pso_i[0] += 1
                        for j in range(8):
                            K.op("pe", lambda e, j=j, pk=pk, uT=uT, ti=ti: e.matmul(
                                psum[pk][:, 0:cw], uT[:, j, ti * 128:(ti + 1) * 128], Wb[:, j, c0:c0 + cw],
                                start=(j == 0), stop=(j == 7)),
                                reads=["Wb", uT.key], writes=[f"ps#{pk}"], inc=(j == 7))
                        epi(ctxo, pk, tok0 + ti * 128)
                while hooks:
                    hooks.pop(0)()

            ng = P.dbg.get("ngroups", len(groups))
            uT0, p1s, p2s = front_parts(0)
            for p1, p2 in zip(p1s, p2s):
                p2(p1())
            cur = uT0
            for gi in range(ng):
                hooks = []
                nxt = None
                if gi + 1 < ng:
                    nxt, p1s, p2s = front_parts(gi + 1)
                    boxes = {}
                    def mk(k, p1s=p1s, p2s=p2s, boxes=boxes):
                        def h():
                            if k == 0:
                                for kk in range(min(2, len(p1s))):
                                    boxes[kk] = p1s[kk]()
                                return
                            if 0 <= k - 1 < len(p1s):
                                p2s[k - 1](boxes[k - 1])
                            if k + 1 < len(p1s):
                                boxes[k + 1] = p1s[k + 1]()
                        return h
                    hooks = [mk(k) for k in range(len(p1s) + 1)]
                mm(gi, cur, hooks)
                cur = nxt
            K.barrier()

    def l0_alloc(alloc):
        c = {}
        c["sf"] = Ring(alloc, "sf", 3, [128, 512], F32)
        c["sb"] = Ring(alloc, "sb", 4, [128, 512], BF16)
        c["t1"] = Ring(alloc, "t1", 2, [128, 512], F32)
        c["t2"] = Ring(alloc, "t2", 2, [128, 512], F32)
        c["tab"] = Ring(alloc, "tab", 2, [128, 2, 512], F32)
        return c

    def l0_group(c, gi, tok0, ntok):
        tb = c["tab"].next()
        c["curtab"] = tb
        K.dma("sp", tb[:, :, 0:ntok], ropetab[:, :, tok0:tok0 + ntok].rearrange("c p t -> p c t"),
              "L" + tb.key, writes=[tb.key])

    def epi_copy_f32(dst):
        def f(c, pks, bi, tok0, ntok):
            pk = pks[0]; b = c["sf"].next()
            K.op("act", lambda e: e.copy(b[:, 0:ntok], psum[pk][:, 0:ntok]), reads=[f"ps#{pk}"], writes=[b.key])
            K.dma("sp", dst[bi * 128:(bi + 1) * 128, tok0:tok0 + ntok], b[:, 0:ntok], "S" + b.key,
                  reads=[b.key], accw=[dst.name])
        return f

    def epi_silu_bf(dst):
        def f(c, pks, bi, tok0, ntok):
            pk = pks[0]; b = c["sb"].next()
            K.op("act", lambda e: e.activation(out=b[:, 0:ntok], in_=psum[pk][:, 0:ntok], func=AF.Silu),
                 reads=[f"ps#{pk}"], writes=[b.key])
            K.dma("sp", dst[bi * 128:(bi + 1) * 128, tok0:tok0 + ntok], b[:, 0:ntok], "S" + b.key,
                  reads=[b.key], accw=[dst.name])
        return f

    def epi_rope(dst):
        def f(c, pks, bi, tok0, ntok):
            pa, pb = pks; t1 = c["t1"].next(); t2 = c["t2"].next(); b = c["sb"].next(); tb = c["curtab"]
            K.op("dve", lambda e: e.tensor_tensor(t1[:, 0:ntok], psum[pa][:, 0:ntok], tb[:, 0, 0:ntok], ALU.mult),
                 reads=[f"ps#{pa}", tb.key], writes=[t1.key])
            K.op("dve", lambda e: e.tensor_tensor(t2[:, 0:ntok], psum[pb][:, 0:ntok], tb[:, 1, 0:ntok], ALU.mult),
                 reads=[f"ps#{pb}", tb.key], writes=[t2.key])
            K.op("pool", lambda e: e.tensor_tensor(b[:, 0:ntok], t1[:, 0:ntok], t2[:, 0:ntok], ALU.add),
                 reads=[t1.key, t2.key], writes=[b.key])
            K.dma("sp", dst[bi * 128:(bi + 1) * 128, tok0:tok0 + ntok], b[:, 0:ntok], "S" + b.key,
                  reads=[b.key], accw=[dst.name])
        return f

    def epi_v(c, pk, tok0):
        b = c["sb"].next()
        K.op("act", lambda e: e.copy(b[:, :], psum[pk][:, :]), reads=[f"ps#{pk}"], writes=[b.key])
        K.dma("sp", vtok[tok0:tok0 + 128, :], b[:, :], "S" + b.key, reads=[b.key], accw=["vtok"])

    l0_f = [
        ("xr", [[f * 128] for f in range(0, 4)], epi_copy_f32(xrT)),
        ("gr", [[f * 128] for f in range(4, 8)], epi_silu_bf(grT)),
        ("q", [[1024 + f * 128, 3072 + f * 128] for f in range(4)], epi_rope(qT)),
        ("k", [[1536 + f * 128, 3584 + f * 128] for f in range(4)], epi_rope(kT)),
        ("gd", [[f * 128] for f in range(20, 24)], epi_silu_bf(gdT)),
    ]
    l0_t = [(2048, 512, epi_v)]
    phase_proj(0, src0, e_w_in, 4096, l0_f, l0_t, l0_alloc, l0_group)
    if stop_after == "proj0":
        K.barrier(); pes.close(); P.es.close(); return P


    LAMBDA_INIT0 = 0.8 - 0.6 * 1.0

    def phase_lru():
        with ExitStack() as es:
            def alloc(name, shape, dt):
                P.uid += 1
                return es.enter_context(nc.sbuf_tensor(f"{name}_{P.uid}", list(shape), dt))
            rows = alloc("lrows", [44, 128], F32)
            pv = alloc("lpv", [128, 11, 4], F32)
            coef = alloc("lcoef", [128, 2, 4], F32)
            ones1 = alloc("ones1", [128, 1], F32)
            wbf = alloc("wbf", [128, 16, 128], F32)
            wbb = alloc("wbb", [128, 16, 128], BF16)
            big = Ring(alloc, "big", 7, [128, T], F32)
            xcb = alloc("xcb", [128, T], BF16)
            grs = alloc("grs", [128, T], BF16)
            mo = alloc("mixo", [128, T], BF16)
            K.op("dve", lambda e: e.memset(ones1[:], 1.0), writes=["ones1"])
            K.dma("sp", rows[:, :], lru_vec.rearrange("v (c p) -> (v c) p", p=128), "Lrows", writes=["lrows"])
            K.op("pe", lambda e: e.transpose(psum[0][:, 0:44], rows[:, :], idf[0:44, 0:44]),
                 reads=["lrows", "idf"], writes=["ps#0"])
            K.cp("dve", pv[:].rearrange("p v c -> p (v c)"), psum[0][:, 0:44], ["ps#0"], ["lpv"])
            K.act(coef[:].rearrange("p d c -> p (d c)"), pv[:, 9:11, :].rearrange("p d c -> p (d c)"), AF.Exp,
                  ["lpv"], ["lcoef"], scale=-1.0)
            K.act(coef[:].rearrange("p d c -> p (d c)"), coef[:].rearrange("p d c -> p (d c)"), AF.Ln,
                  ["lcoef", "ones1"], ["lcoef"], bias=ones1[:, 0:1])
            K.ts("dve", coef[:].rearrange("p d c -> p (d c)"), coef[:].rearrange("p d c -> p (d c)"), -8.0, None,
                 ALU.mult, None, ["lcoef"], ["lcoef"])
            K.dma("sp", wbf[:], lru_wbd.rearrange("n k m -> k n m"), "Lwbf", writes=["wbf"])
            K.cp("dve", wbb[:], wbf[:], ["wbf"], ["wbb"])
            segs = [(0, CTX), (CTX, T)]
            blocks = [(b0, min(512, T - b0)) for b0 in range(0, T, 512)]
            for cc in range(4):
                x = big.next(); xc = big.next()
                K.dma("sp", x[:, :], xrT[cc * 128:(cc + 1) * 128, :], "L" + x.key, writes=[x.key])
                K.dma("sp", grs[:, :], grT[cc * 128:(cc + 1) * 128, :], "Lgrs", writes=["grs"])
                K.ts("dve", xc[:, :], x[:, :], pv[:, 2, cc:cc + 1], pv[:, 4, cc:cc + 1], ALU.mult, ALU.add,
                     [x.key, "lpv"], [xc.key])
                for (a, b) in segs:
                    for tap, sh in ((0, -2), (1, -1), (3, 1)):
                        lo = max(a, a - sh); hi = min(b, b - sh)
                        K.stt("dve", xc[:, lo:hi], x[:, lo + sh:hi + sh], pv[:, tap, cc:cc + 1],
                              xc[:, lo:hi], ALU.mult, ALU.add, [x.key, xc.key, "lpv"], [xc.key])
                K.cp("pool", xcb[:, :], xc[:, :], [xc.key], ["xcb"])
                hs = []
                for d in range(2):
                    rb = big.next(); ib = big.next()
                    for (b0, bn) in blocks:
                        for g, dstb, brow in ((0, rb, 5 + d), (1, ib, 7 + d)):
                            pk = (2 * (b0 // 512) + g) % 6
                            K.mm(psum[pk][:, 0:bn], wbb[:, (d * 2 + g) * 4 + cc, :], xcb[:, b0:b0 + bn], True, True,
                                 ["wbb", "xcb"], [f"ps#{pk}"], True)
                            K.act(dstb[:, b0:b0 + bn], psum[pk][:, 0:bn], AF.Sigmoid, [f"ps#{pk}", "lpv"], accw=[dstb.key],
                                  bias=pv[:, brow, cc:cc + 1])
                    K.ts("dve", rb[:, :], rb[:, :], coef[:, d, cc:cc + 1], None, ALU.mult, None, [rb.key, "lcoef"], [rb.key])
                    K.act(rb[:, :], rb[:, :], AF.Exp, [rb.key], [rb.key])
                    sq = big.next()
                    K.tt("pool", sq[:, :], rb[:, :], rb[:, :], ALU.mult, [rb.key], [sq.key])
                    K.act(sq[:, :], sq[:, :], AF.Sqrt, [sq.key, "ones1"], [sq.key], scale=-1.0, bias=ones1[:, 0:1])
                    K.tt("pool", ib[:, :], ib[:, :], xc[:, :], ALU.mult, [ib.key, xc.key], [ib.key])
                    K.tt("dve", ib[:, :], ib[:, :], sq[:, :], ALU.mult, [ib.key, sq.key], [ib.key])
                    h = sq
                    if d == 0:
                        K.op("dve", lambda e, h=h, rb=rb, ib=ib: e.tensor_tensor_scan(
                            h[:, :], rb[:, :], ib[:, :], 0.0, ALU.mult, ALU.add), [rb.key, ib.key, h.key], [h.key])
                    else:
                        K.op("dve", lambda e, h=h, rb=rb, ib=ib: e.tensor_tensor_scan(
                            h[:, CTX - 1::-1] if False else h[:, 0:CTX][:, ::-1], rb[:, 0:CTX][:, ::-1], ib[:, 0:CTX][:, ::-1],
                            0.0, ALU.mult, ALU.add), [rb.key, ib.key, h.key], [h.key])
                        K.op("dve", lambda e, h=h, rb=rb, ib=ib: e.tensor_tensor_scan(
                            h[:, CTX:T][:, ::-1], rb[:, CTX:T][:, ::-1], ib[:, CTX:T][:, ::-1],
                            h[:, 0:1], ALU.mult, ALU.add), [rb.key, ib.key, h.key], [h.key])
                    hs.append(h)
                K.tt("pool", hs[0][:, :], hs[0][:, :], hs[1][:, :], ALU.add, [hs[0].key, hs[1].key], [hs[0].key])
                K.tt("dve", mo[:, :], hs[0][:, :], grs[:, :], ALU.mult, [hs[0].key, "grs"], ["mixo"])
                K.dma("sp", mixT0[cc * 128:(cc + 1) * 128, :], mo[:, :], "Smixo", reads=["mixo"], accw=["mixT0"])
            K.barrier()

    if "lru" not in P.dbg.get("skip", ()):
        phase_lru()
    if stop_after == "lru":
        K.barrier(); pes.close(); P.es.close(); return P

    def phase_attn():
        with ExitStack() as es:
            def alloc(name, shape, dt):
                P.uid += 1
                return es.enter_context(nc.sbuf_tensor(f"{name}_{P.uid}", list(shape), dt))
            kres = alloc("kres", [128, 4, T], BF16)
            vres = alloc("vres", [128, NT, 512], BF16)
            onesb = alloc("onesb", [128, 128], BF16)
            onesf = alloc("onesf", [128, 128], F32)
            lrow = alloc("lamrow", [1, 260], F32)
            lamc = alloc("lamc", [128, 2], F32)
            subc = alloc("subc", [128, 2], F32)
            subrow = alloc("subrow", [1, 128], F32)
            qb_ = Ring(alloc, "qblk", 2, [128, 4, 512], BF16)
            gd_ = Ring(alloc, "gdblk", 2, [128, 512], BF16)
            E_ = Ring(alloc, "Eb", 6, [128, 512], BF16)
            f_ = Ring(alloc, "af", 6, [128, 512], F32)
            ob_ = Ring(alloc, "aob", 4, [128, 512], BF16)
            K.op("dve", lambda e: e.memset(onesb[:], 1.0), writes=["onesb"])
            K.op("dve", lambda e: e.memset(onesf[:], 1.0), writes=["onesf"])
            K.dma("sp", lrow[:, 0:256], da_lam[:, :], "Llam", writes=["lamrow"])
            K.dma("sp", subrow[:, :], da_sub[:, :], "Lsub", writes=["subrow"])
            K.tt("dve", lrow[:, 0:64], lrow[:, 0:64], lrow[:, 64:128], ALU.mult, ["lamrow"], ["lamrow"])
            K.tt("dve", lrow[:, 128:192], lrow[:, 128:192], lrow[:, 192:256], ALU.mult, ["lamrow"], ["lamrow"])
            K.op("dve", lambda e: e.reduce_sum(lrow[:, 256:257], lrow[:, 0:64], AX.X), ["lamrow"], ["lamrow"])
            K.op("dve", lambda e: e.reduce_sum(lrow[:, 257:258], lrow[:, 128:192], AX.X), ["lamrow"], ["lamrow"])
            K.act(lrow[:, 256:258], lrow[:, 256:258], AF.Exp, ["lamrow"], ["lamrow"])
            K.tt("dve", lrow[:, 258:259], lrow[:, 256:257], lrow[:, 257:258], ALU.subtract, ["lamrow"], ["lamrow"])
            K.ts("dve", lrow[:, 258:259], lrow[:, 258:259], -1.0, -LAMBDA_INIT0, ALU.mult, ALU.add, ["lamrow"], ["lamrow"])
            K.mm(psum[0][:, 0:1], onesf[0:1, :], lrow[0:1, 258:259], True, True, ["onesf", "lamrow"], ["ps#0"], True)
            K.cp("dve", lamc[:, 0:1], psum[0][:, 0:1], ["ps#0"], ["lamc"])
            K.op("pe", lambda e: e.transpose(psum[1][:, 0:1], subrow[0:1, :], idf[0:1, 0:1]),
                 reads=["subrow", "idf"], writes=["ps#1"])
            K.ts("dve", subc[:, 0:1], psum[1][:, 0:1], 1.0 - LAMBDA_INIT0, None, ALU.mult, None, ["ps#1"], ["subc"])
            for h in range(4):
                K.dma("sp", kres[:, h, :], kT[h * 128:(h + 1) * 128, :], "Lkres", accw=["kres"])
            for n0 in range(0, NT, 2):
                K.dma("sp", vres[:, n0:n0 + 2, :], vtok[n0 * 128:(n0 + 2) * 128, :].rearrange("(n p) e -> p n e", p=128),
                      "Lvres", accw=["vres"])
            qblocks = [(0, CTX, 0, 2)] + [(CTX + 512 * g, 512, 0, NT) for g in range(SEQ // 512)]
            nqb = P.dbg.get("nqb", len(qblocks))
            sti = 0
            acc_ = Ring(alloc, "dacc", 4, [128, 512], F32)
            for (q0, nq, kt0, kt1) in qblocks[:nqb]:
                qb = qb_.next()
                for h in range(4):
                    K.dma("sp", qb[:, h, 0:nq], qT[h * 128:(h + 1) * 128, q0:q0 + nq], "L" + qb.key, accw=[qb.key])
                for h in range(4):
                    gd = gd_.next()
                    K.dma("sp", gd[:, 0:nq], gdT[h * 128:(h + 1) * 128, q0:q0 + nq], "L" + gd.key, writes=[gd.key])
                    accs = [acc_.next(), acc_.next()]
                    kts = list(range(kt0, kt1))
                    pkmap = {}

                    def emit_qk(kt):
                        nonlocal sti
                        for c in range(2):
                            pk = sti % 4; sti += 1
                            pkmap[(kt, c)] = pk
                            K.mm(psum[pk][:, 0:nq], kres[c * 64:(c + 1) * 64, h, kt * 128:(kt + 1) * 128],
                                 qb[c * 64:(c + 1) * 64, h, 0:nq], True, True, ["kres", qb.key], [f"ps#{pk}"], True)

                    def emit_rest(kt):
                        for c in range(2):
                            pk = pkmap[(kt, c)]
                            E = E_.next()
                            K.act(E[:, 0:nq], psum[pk][:, 0:nq], AF.Exp, [f"ps#{pk}"], [E.key], scale=0.125)
                            K.mm(psum[4 + c][:, 0:nq], vres[:, kt, h * 128:(h + 1) * 128], E[:, 0:nq],
                                 kt == kt0, kt == kt1 - 1, ["vres", E.key], [f"ps#{4 + c}"], kt == kt1 - 1)
                            if c == 1:
                                K.mm(psum[7][:, 0:nq], onesb[:, :], E[:, 0:nq], kt == kt0, kt == kt1 - 1,
                                     ["onesb", E.key], ["ps#7"], kt == kt1 - 1)
                            elif kt == kt0:
                                K.cp("dve", accs[c][:, 0:nq], E[:, 0:nq], [E.key], [accs[c].key])
                            else:
                                K.tt("dve", accs[c][:, 0:nq], accs[c][:, 0:nq], E[:, 0:nq], ALU.add,
                                     [E.key, accs[c].key], [accs[c].key])

                    emit_qk(kts[0])
                    for i, kt in enumerate(kts):
                        if i + 1 < len(kts):
                            emit_qk(kts[i + 1])
                        emit_rest(kt)
                    r0 = f_.next(); r1 = f_.next(); t0 = f_.next(); t1 = f_.next()
                    K.cp("act", t0[:, 0:nq], psum[4][:, 0:nq], ["ps#4"], [t0.key])
                    K.cp("act", t1[:, 0:nq], psum[5][:, 0:nq], ["ps#5"], [t1.key])
                    ab = ob_.next()
                    K.cp("pool", ab[:, 0:nq], accs[0][:, 0:nq], [accs[0].key], [ab.key])
                    K.mm(psum[6][:, 0:nq], onesb[:, :], ab[:, 0:nq], True, True, ["onesb", ab.key], ["ps#6"], True)
                    K.cp("act", r0[:, 0:nq], psum[6][:, 0:nq], ["ps#6"], [r0.key])
                    K.tt("dve", t0[:, 0:nq], t0[:, 0:nq], psum[7][:, 0:nq], ALU.mult, [t0.key, "ps#7"], [t0.key])
                    K.tt("dve", t1[:, 0:nq], t1[:, 0:nq], r0[:, 0:nq], ALU.mult, [t1.key, r0.key], [t1.key])
                    K.stt("dve", t0[:, 0:nq], t1[:, 0:nq], lamc[:, 0:1], t0[:, 0:nq], ALU.mult, ALU.add,
                          [t0.key, t1.key, "lamc"], [t0.key])
                    K.tt("dve", r0[:, 0:nq], r0[:, 0:nq], psum[7][:, 0:nq], ALU.mult, [r0.key, "ps#7"], [r0.key])
                    K.stt("dve", r1[:, 0:nq], r0[:, 0:nq], EPS, r0[:, 0:nq], ALU.mult, ALU.mult, [r0.key], [r1.key])
                    osq = ob_.next()
                    K.tt("pool", osq[:, 0:nq], t0[:, 0:nq], t0[:, 0:nq], ALU.mult, [t0.key], [osq.key])
                    K.mm(psum[6][:, 0:nq], onesb[:, :], osq[:, 0:nq], True, True, ["onesb", osq.key], ["ps#6"], True)
                    K.stt("dve", r1[:, 0:nq], psum[6][:, 0:nq], 1.0 / 128, r1[:, 0:nq], ALU.mult, ALU.add,
                          ["ps#6", r1.key], [r1.key])
                    K.act(r1[:, 0:nq], r1[:, 0:nq], AF.Ln, [r1.key], [r1.key])
                    K.act(r1[:, 0:nq], r1[:, 0:nq], AF.Exp, [r1.key], [r1.key], scale=-0.5)
                    K.tt("dve", t0[:, 0:nq], t0[:, 0:nq], r1[:, 0:nq], ALU.mult, [t0.key, r1.key], [t0.key])
                    mo = ob_.next()
                    K.stt("dve", mo[:, 0:nq], t0[:, 0:nq], subc[:, 0:1], gd[:, 0:nq], ALU.mult, ALU.mult,
                          [t0.key, "subc", gd.key], [mo.key])
                    K.dma("sp", mixT0[(4 + h) * 128:(5 + h) * 128, q0:q0 + nq], mo[:, 0:nq], "S" + mo.key,
                          reads=[mo.key], accw=["mixT0"])
            K.barrier()

    if "attn" not in P.dbg.get("skip", ()):
        phase_attn()
    if stop_after == "attn":
        K.barrier(); pes.close(); P.es.close(); return P

    def phase_out(layer, mixT, KC, Wd, res_src, dst, tiles):
        with ExitStack() as es:
            def alloc(name, shape, dt):
                P.uid += 1
                return es.enter_context(nc.sbuf_tensor(f"{name}_{P.uid}", list(shape), dt))
            Wb = alloc("Wo", [128, KC, D], BF16)
            wst = Ring(alloc, "wost", 2, [128, KC, 256], F32)
            mx_ = Ring(alloc, "mxin", 2, [128, KC, 512], BF16)
            rs_ = Ring(alloc, "resin", 3, [128, D], F32)
            tm_ = Ring(alloc, "otmp", 2, [128, D], F32)
            st_ = Ring(alloc, "ostat", 4, [128, 4], F32)
            junk = alloc("ojunk", [128, 512], BF16)
            for pc in range(4):
                wb = wst.next()
                K.dma("sp", wb[:], Wd[:, pc * 256:(pc + 1) * 256].rearrange("(j p) n -> p j n", p=128), "L" + wb.key,
                      writes=[wb.key])
                K.cp(["dve", "pool"][pc % 2], Wb[:, :, pc * 256:(pc + 1) * 256], wb[:], [wb.key], accw=["Wo"])
            gi = 0
            cur = None
            for (tok0, drow, tmod) in tiles:
                g0 = (tok0 // 512) * 512 if tok0 >= CTX else 0
                if tok0 >= CTX:
                    g0 = CTX + ((tok0 - CTX) // 512) * 512
                gn = CTX if tok0 < CTX else 512
                if cur is None or cur[0] != g0:
                    mx = mx_.next()
                    K.dma("sp", mx[:, :, 0:gn], mixT[:, g0:g0 + gn].rearrange("(j p) t -> p j t", p=128), "L" + mx.key,
                          writes=[mx.key])
                    cur = (g0, mx)
                mx = cur[1]; lo = tok0 - g0
                rs = rs_.next(); tm = tm_.next(); st = st_.next()
                K.dma("sp", rs[:, :], res_src[tok0:tok0 + 128, :], "L" + rs.key, writes=[rs.key])
                pks = [(2 * gi) % 6, (2 * gi + 1) % 6]; gi += 1
                for nb in range(2):
                    for j in range(KC):
                        K.mm(psum[pks[nb]][:, :], mx[:, j, lo:lo + 128], Wb[:, j, nb * 512:(nb + 1) * 512],
                             j == 0, j == KC - 1, [mx.key, "Wo"], [f"ps#{pks[nb]}"], j == KC - 1)
                for nb in range(2):
                    K.act(junk[:, :], psum[pks[nb]][:, :], AF.Square, [f"ps#{pks[nb]}"], ["ojunk", st.key] if nb == 0 else ["ojunk"],
                          accw=() if nb == 0 else [st.key], accum_out=st[:, nb:nb + 1])
                K.tt("dve", st[:, 2:3], st[:, 0:1], st[:, 1:2], ALU.add, [st.key], [st.key])
                K.act(st[:, 3:4], st[:, 2:3], AF.Sqrt, [st.key, "epsb"], [st.key], scale=1.0 / D, bias=epsb[:, 0:1])
                K.op("dve", lambda e, st=st: e.reciprocal(st[:, 2:3], st[:, 3:4]), [st.key], [st.key])
                for nb in range(2):
                    K.stt("dve", tm[:, nb * 512:(nb + 1) * 512], psum[pks[nb]][:, :], st[:, 2:3],
                          ggbc[:, layer, tmod, nb * 512:(nb + 1) * 512], ALU.mult, ALU.mult,
                          [f"ps#{pks[nb]}", st.key, "ggbc"], accw=[tm.key])
                K.tt("pool", tm[:, :], tm[:, :], rs[:, :], ALU.add, [tm.key, rs.key], [tm.key])
                K.dma("sp", dst[drow:drow + 128, :], tm[:, :], "S" + tm.key, reads=[tm.key], accw=[dst.name])
            K.barrier()

    nt0 = P.dbg.get("out0_tiles", NT)
    tiles0 = [(i * 128, i * 128, 1 if i < 2 else 0) for i in range(nt0)]
    phase_out(0, mixT0, 8, e_w_out, src0, h1, tiles0)
    if stop_after == "out0":
        K.barrier(); pes.close(); P.es.close(); return P


    def l1_alloc(alloc):
        c = {}
        c["sb"] = Ring(alloc, "sb1", 4, [128, 512], BF16)
        c["sf"] = Ring(alloc, "sf1", 2, [128, 64], F32)
        c["n"] = 0
        return c

    def epi_xbc(c, pks, bi, tok0, ntok):
        pk = pks[0]; b = c["sb"].next()
        c["n"] += 1
        K.cp("act" if c["n"] % 2 else "dve", b[:, 0:ntok], psum[pk][:, 0:ntok], [f"ps#{pk}"], [b.key])
        K.dma("sp", xbcT[bi * 128:(bi + 1) * 128, tok0:tok0 + ntok], b[:, 0:ntok], "S" + b.key,
              reads=[b.key], accw=["xbcT"])

    def epi_z(zc):
        def f(c, pk, tok0):
            b = c["sb"].next()
            K.act(b[:, :], psum[pk][:, :], AF.Silu, [f"ps#{pk}"], [b.key])
            K.dma("sp", zs[tok0:tok0 + 128, zc * 512:(zc + 1) * 512], b[:, :], "S" + b.key, reads=[b.key], accw=["zs"])
        return f

    def epi_dt(c, pk, tok0):
        b = c["sf"].next()
        K.cp("dve", b[:, :], psum[pk][:, 0:64], [f"ps#{pk}"], [b.key])
        K.dma("sp", dtraw[tok0:tok0 + 128, :], b[:, :], "S" + b.key, reads=[b.key], accw=["dtraw"])

    l1_f = [("xbc", [[2048 + f * 128] for f in range(24)], epi_xbc)]
    l1_t = [(zc * 512, 512, epi_z(zc)) for zc in range(4)] + [(5120, 64, epi_dt)]
    if "l1" not in P.dbg.get("skip", ()):
        phase_proj(1, h1, o_w_in, 5184, l1_f, l1_t, l1_alloc, None)
    if stop_after == "proj1":
        K.barrier(); pes.close(); P.es.close(); return P

    def phase_conv():
        with ExitStack() as es:
            def alloc(name, shape, dt):
                P.uid += 1
                return es.enter_context(nc.sbuf_tensor(f"{name}_{P.uid}", list(shape), dt))
            rows = alloc("cvrows", [120, 128], F32)
            cv = alloc("cv", [128, 5, 24], F32)
            dg = alloc("dg", [128, 4, 24, 128], BF16)
            xin_ = Ring(alloc, "cxin", 2, [128, 24, 516], BF16)
            sb_ = Ring(alloc, "csb", 3, [128, 512], BF16)
            rb_ = Ring(alloc, "crow", 8, [128, 2560], BF16)
            K.dma("sp", rows[:, :], ssd_cv.rearrange("v (f p) -> (v f) p", p=128), "Lcvrows", writes=["cvrows"])
            K.op("pe", lambda e: e.transpose(psum[0][:, 0:120], rows[:, :], idf[0:120, 0:120]),
                 reads=["cvrows", "idf"], writes=["ps#0"])
            K.cp("dve", cv[:].rearrange("p v f -> p (v f)"), psum[0][:, 0:120], ["ps#0"], ["cv"])
            for j in range(4):
                for f in range(24):
                    K.ts("dve" if (f % 2) else "pool", dg[:, j, f, :], idf[:, :], cv[:, j, f:f + 1], None, ALU.mult, None,
                         ["idf", "cv"], accw=["dg"])
            blocks = [(0, CTX, 0, CTX)] + [(CTX + 512 * g, 512, CTX, T) for g in range(SEQ // 512)]
            cpi = 0
            for (b0, bn, sa, sb_end) in blocks[:P.dbg.get("nconv", 99)]:
                xin = xin_.next()
                l0 = max(sa, b0 - 2); l1 = min(sb_end, b0 + bn + 1)
                K.dma("sp", xin[:, :, l0 - (b0 - 2):l1 - (b0 - 2)],
                      xbcT[:, l0:l1].rearrange("(f p) t -> p f t", p=128), "L" + xin.key, writes=[xin.key])
                nt_ = bn // 128
                rbs = [rb_.next() for _ in range(nt_)]
                for ft in range(24):
                    pk = 4 + (cpi % 2); cpi += 1
                    order = [2, 0, 1, 3]
                    for oi, j in enumerate(order):
                        sh = j - 2
                        lo = max(b0, sa - sh); hi = min(b0 + bn, sb_end - sh)
                        K.mm(psum[pk][:, lo - b0:hi - b0], dg[:, j, ft, :],
                             xin[:, ft, lo + sh - (b0 - 2):hi + sh - (b0 - 2)], oi == 0, oi == 3,
                             ["dg", xin.key], [f"ps#{pk}"], oi == 3)
                    sb = sb_.next()
                    K.act(sb[:, 0:bn], psum[pk][:, 0:bn], AF.Silu, [f"ps#{pk}", "cv"], [sb.key], bias=cv[:, 4, ft:ft + 1])
                    if ft >= 16:
                        K.dma("sp", bcT[(ft - 16) * 128:(ft - 15) * 128, b0:b0 + bn], sb[:, 0:bn], "S" + sb.key,
                              reads=[sb.key], accw=["bcT"])
                    if ft < 20:
                        for i in range(nt_):
                            tb = psum[i][:].bitcast(BF16)
                            K.op("pe", lambda e, tb=tb, sb=sb, i=i, ft=ft: e.transpose(
                                tb[:, (ft % 8) * 128:(ft % 8 + 1) * 128], sb[:, i * 128:(i + 1) * 128], idb[:]),
                                reads=[sb.key, "idb"], writes=[f"ps#{i}"], inc=True)
                        if ft % 8 == 7 or ft == 19:
                            ncol = (ft % 8 + 1) * 128
                            c0 = (ft // 8) * 1024
                            for i in range(nt_):
                                tb = psum[i][:].bitcast(BF16)
                                K.cp("dve" if i % 2 else "act", rbs[i][:, c0:c0 + ncol], tb[:, 0:ncol], [f"ps#{i}"],
                                     accw=[rbs[i].key])
                for i in range(nt_):
                    K.dma("sp", xsB[b0 + i * 128:b0 + (i + 1) * 128, :], rbs[i][:, :], "S" + rbs[i].key,
                          reads=[rbs[i].key], accw=["xsB"])
            K.barrier()

    if "conv" not in P.dbg.get("skip", ()):
        phase_conv()
    if stop_after == "conv":
        K.barrier(); pes.close(); P.es.close(); return P


    def phase_ssd():
        with ExitStack() as es:
            def alloc(name, shape, dt):
                P.uid += 1
                return es.enter_context(nc.sbuf_tensor(f"{name}_{P.uid}", list(shape), dt))
            vb = alloc("vb", [128, 160], F32)
            aneg = alloc("aneg", [128, 64], F32)
            nwbc = alloc("nwbc", [128, 2048], F32)
            mk = alloc("mk", [128, 2, 128], F32)
            self_ = alloc("self", [64, 1024], F32)
            selb = alloc("selb", [64, 32, 128], BF16)
            onesf = alloc("onesf2", [128, 128], F32)
            ones1 = alloc("ones1b", [128, 1], F32)
            state = alloc("state", [128, 2048], F32)
            stbf_ = Ring(alloc, "stbf", 2, [128, 2048], BF16)
            S_ = Ring(alloc, "Ssb", 2, [128, 2048], F32)
            ST = {}
            xb_ = Ring(alloc, "xb", 5, [128, 2560], BF16)
            bc_ = Ring(alloc, "bc", 5, [128, 8, 128], BF16)
            dr_ = Ring(alloc, "dr", 3, [128, 32], F32)
            sm_ = Ring(alloc, "sm", 5, [128, 8, 32], F32)
            cs4_ = Ring(alloc, "cs4", 3, [128, 64], F32)
            cst_ = Ring(alloc, "cst", 3, [64, 128], F32)
            hl_ = Ring(alloc, "hl", 3, [64, 3, 128], BF16)
            X_ = Ring(alloc, "Xd", 3, [128, 2048], BF16)
            Xc_ = Ring(alloc, "Xc", 3, [128, 2048], BF16)
            xd_ = Ring(alloc, "xdk", 1, [128, 2048], F32)
            cbm_ = Ring(alloc, "cbm", 2, [128, 4, 128], F32)
            E_ = Ring(alloc, "sE", 2, [128, 512], F32)
            MT_ = Ring(alloc, "sMT", 4, [128, 4, 128], BF16)
            to_ = Ring(alloc, "sto", 2, [128, 512], F32)
            ysb_ = Ring(alloc, "ysb", 2, [128, 2048], F32)
            yf_ = Ring(alloc, "yfl", 1, [128, 2048], F32)
            zt_ = Ring(alloc, "ztl", 1, [128, 2048], BF16)
            ynb_ = Ring(alloc, "ynb", 1, [128, 2048], BF16)
            yT_ = Ring(alloc, "yTs", 1, [128, 16, 128], BF16)
            junk = alloc("sjunk", [128, 512], BF16)
            K.op("dve", lambda e: e.memset(onesf[:], 1.0), writes=["onesf2"])
            K.op("dve", lambda e: e.memset(ones1[:], 1.0), writes=["ones1b"])
            negb = alloc("negb", [128, 2, 512], BF16)
            for dd in range(2):
                K.dma("sp", to_.bufs[dd][:, :], negm[dd, :, :], "Lnegm%d" % dd, writes=[to_.bufs[dd].key])
                K.cp("dve", negb[:, dd, :], to_.bufs[dd][:, :], [to_.bufs[dd].key], accw=["negb"])
            K.dma("sp", vb[:, :], ssd_vec[0, :].partition_broadcast(128), "Lvb", writes=["vb"])
            K.dma("sp", nwbc[:, :], ssd_norm.partition_broadcast(128), "Lnwbc", writes=["nwbc"])
            K.dma("sp", mk[:], maskT.rearrange("d s l -> s d l"), "Lmk", writes=["mk"])
            for q4 in range(4):
                K.dma("sp", self_[:, :], selc[:, q4 * 1024:(q4 + 1) * 1024], "Lself", writes=["self"])
                K.cp("dve", selb[:, q4 * 8:(q4 + 1) * 8, :].rearrange("k h l -> k (h l)"), self_[:, :], ["self"], accw=["selb"])
            K.act(aneg[:, :], vb[:, 0:64], AF.Exp, ["vb"], ["aneg"])
            K.ts("dve", aneg[:, :], aneg[:, :], -1.0, None, ALU.mult, None, ["aneg"], ["aneg"])
            lat_chunks = list(range(2, NT))
            ncl = P.dbg.get("nchunk", len(lat_chunks))
            lat_chunks = lat_chunks[:ncl]
            PB = {"yd": 0, "seg": 0, "so": 0}

            def stage_a(d, c):
                tok0 = c * 128
                lat = c >= 2
                xb = xb_.next(); bc = bc_.next(); dr = dr_.next(); sm = sm_.next()
                ctx_ = {"xb": xb, "bc": bc, "sm": sm, "lat": lat, "tok0": tok0}
                K.dma("sp", xb[:, :], xsB[tok0:tok0 + 128, :], "L" + xb.key, writes=[xb.key])
                K.dma("sp", bc[:], bcT[:, tok0:tok0 + 128].rearrange("(f p) t -> p f t", p=128), "L" + bc.key,
                      writes=[bc.key])
                K.dma("sp", dr[:, :], dtraw[tok0:tok0 + 128, d * 32:(d + 1) * 32], "L" + dr.key, writes=[dr.key])
                dt = sm[:, 0, :]; adt = sm[:, 1, :]; cs = sm[:, 2, :]; ecs = sm[:, 3, :]
                etot = sm[:, 4, :]; w2 = sm[:, 5, :]; tmp = sm[:, 6, :]
                k_ = sm.key
                K.tt("dve", tmp, dr[:, :], vb[:, 64 + d * 32:96 + d * 32], ALU.add, [dr.key, "vb"], [k_])
                K.act(tmp, tmp, AF.Exp, [k_], [k_])
                K.act(dt, tmp, AF.Ln, [k_, "ones1b"], [k_], bias=ones1[:, 0:1])
                K.tt("dve", adt, dt, aneg[:, d * 32:(d + 1) * 32], ALU.mult, [k_, "aneg"], [k_])
                return ctx_

            def stage_s2(d, ctx_):
                xb, sm, lat = ctx_["xb"], ctx_["sm"], ctx_["lat"]
                dt = sm[:, 0, :]; adt = sm[:, 1, :]; cs = sm[:, 2, :]; ecs = sm[:, 3, :]
                etot = sm[:, 4, :]; w2 = sm[:, 5, :]; tmp = sm[:, 6, :]
                k_ = sm.key
                K.mm(psum[4][:, 0:32], mk[:, d, :], adt, True, True, ["mk", k_], ["ps#4a"], True)
                K.mm(psum[4][:, 32:64], onesf[:, :], adt, True, True, ["onesf2", k_], ["ps#4a"], True)
                K.cp("dve", cs, psum[4][:, 0:32], ["ps#4a"], [k_])
                K.act(etot, psum[4][:, 32:64], AF.Exp, ["ps#4a"], [k_])
                K.tt("dve", tmp, psum[4][:, 32:64], cs, ALU.subtract, ["ps#4a", k_], [k_])
                K.act(w2, tmp, AF.Exp, [k_], [k_])
                K.tt("dve", w2, w2, dt, ALU.mult, [k_], [k_])
                xs3 = xb[:, 0:2048].rearrange("p (h q) -> p h q", q=64)
                Xc = Xc_.next()
                ctx_["Xc"] = Xc
                K.tt("pool", Xc[:, :].rearrange("p (h q) -> p h q", q=64), xs3,
                     w2.unsqueeze(2).to_broadcast([128, 32, 64]), ALU.mult, [xb.key, k_], [Xc.key])
                if not lat:
                    return ctx_
                K.act(ecs, cs, AF.Exp, [k_], [k_])
                X = X_.next()
                K.tt("pool", X[:, :].rearrange("p (h q) -> p h q", q=64), xs3,
                     dt.unsqueeze(2).to_broadcast([128, 32, 64]), ALU.mult, [xb.key, k_], [X.key])
                cs4 = cs4_.next(); cst = cst_.next(); hl = hl_.next()
                K.cp("dve", cs4[:, 0:32], cs, [k_], accw=[cs4.key])
                K.cp("dve", cs4[:, 32:64], cs, [k_], accw=[cs4.key])
                ctx_["X"] = X; ctx_["cs4"] = cs4; ctx_["cst"] = cst; ctx_["hl"] = hl
                return ctx_

            def stage_s3(d, ctx_):
                if not ctx_["lat"]:
                    return ctx_
                cs4, cst, hl, X = ctx_["cs4"], ctx_["cst"], ctx_["hl"], ctx_["X"]
                K.op("pe", lambda e, cs4=cs4: e.transpose(psum[4][0:64, 128:256], cs4[:, :], idf[:, :]),
                     reads=[cs4.key, "idf"], writes=["ps#4b"])
                K.cp("dve", cst[:, :], psum[4][0:64, 128:256], ["ps#4b"], [cst.key])
                K.cp("dve", hl[0:32, 0, :], cst[0:32, :], [cst.key], accw=[hl.key])
                K.cp("dve", hl[32:64, 2, :], cst[32:64, :], [cst.key], accw=[hl.key])
                K.tt("dve", hl[32:64, 0, :], cst[32:64, :], hl[32:64, 2, :], ALU.subtract, [cst.key, hl.key], accw=[hl.key])
                K.ts("dve", hl[:, 1, :], hl[:, 0, :], -1.0, None, ALU.mult, None, [hl.key], accw=[hl.key])
                ctx_["X"] = X; ctx_["hl"] = hl
                return ctx_

            def stage_a1(d, cx, bgen=None):
                if cx["lat"]:
                    stage_a1_lat(d, cx, bgen)
                xb, Xc = cx["xb"], cx["Xc"]
                Ssb = S_.next()
                cx["Ssb"] = Ssb
                for g in range(4):
                    pk = 6 + PB["so"] % 2; PB["so"] += 1
                    K.mm(psum[pk][:, :], xb[:, 2048 + g * 128:2048 + (g + 1) * 128], Xc[:, g * 512:(g + 1) * 512],
                         True, True, [xb.key, Xc.key], [f"ps#{pk}"], True)
                    K.cp("act", Ssb[:, g * 512:(g + 1) * 512], psum[pk][:, :], [f"ps#{pk}"], accw=[Ssb.key])

            def stage_a1_lat(d, cx, bgen=None):
                xb, bc, sm, tok0, X, hl = cx["xb"], cx["bc"], cx["sm"], cx["tok0"], cx["X"], cx["hl"]
                ctx_ = cx
                for g in range(4):
                    K.mm(psum[5][:, g * 128:(g + 1) * 128], bc[:, g, :], bc[:, 4 + g, :], True, True,
                         [bc.key], ["ps#5"], g == 3)
                cbm = cbm_.next()
                K.tt("dve", cbm[:], psum[5][:, :].rearrange("p (g l) -> p g l", l=128),
                     mk[:, d:d + 1, :].to_broadcast([128, 4, 128]), ALU.mult, ["ps#5", "mk"], [cbm.key])
                ysb = ysb_.next()
                ctx_["ysb"] = ysb
                if d == 1:
                    yf = yf_.next()
                    K.dma("sp", yf[:, :], yfw[tok0:tok0 + 128, :], "L" + yf.key, reads=["yfw"], writes=[yf.key])
                segbank = {}

                def emit_seg(hq):
                    pk = 2 + PB["seg"] % 2; PB["seg"] += 1
                    segbank[hq] = pk
                    K.mm(psum[pk][:, :], idb[:, :], negb[:, d, :], True, False, ["idb", "negb"], [f"ps#{pk}"], False)
                    for j in range(4):
                        h = hq * 4 + j
                        K.mm(psum[pk][:, j * 128:(j + 1) * 128], selb[:, h, :], hl[:, 0, :], False, False,
                             ["selb", hl.key], [f"ps#{pk}"], False)
                        K.mm(psum[pk][:, j * 128:(j + 1) * 128], hl[:, 1, :], selb[:, h, :], False, True,
                             ["selb", hl.key], [f"ps#{pk}"], j == 3)

                pyd = 0
                emit_seg(0)
                for hq in range(8):
                    g = hq // 2
                    if hq + 1 < 8:
                        emit_seg(hq + 1)
                    pk = segbank[hq]
                    if hq % 2 == 0:
                        pyd = PB["yd"] % 2; PB["yd"] += 1
                    E = E_.next(); MT = MT_.next()
                    K.act(E[:, :], psum[pk][:, :], AF.Exp, [f"ps#{pk}"], [E.key])
                    K.stt("dve", MT[:], E[:, :].rearrange("p (j l) -> p j l", l=128), 1e30,
                          cbm[:, g:g + 1, :].to_broadcast([128, 4, 128]), ALU.min, ALU.mult,
                          [E.key, cbm.key], [MT.key])
                    for j in range(4):
                        h = hq * 4 + j
                        K.mm(psum[pyd][:, (h % 8) * 64:(h % 8 + 1) * 64], MT[:, j, :], X[:, h * 64:(h + 1) * 64],
                             True, True, [MT.key, X.key], [f"ps#{pyd}"], (h % 8 == 7))
                    if hq % 2 == 1:
                        if d == 0:
                            K.cp("act", ysb[:, g * 512:(g + 1) * 512], psum[pyd][:, :], [f"ps#{pyd}"], accw=[ysb.key])
                        else:
                            K.tt("dve", ysb[:, g * 512:(g + 1) * 512], psum[pyd][:, :], yf[:, g * 512:(g + 1) * 512],
                                 ALU.add, [f"ps#{pyd}", yf.key], accw=[ysb.key])
                    if bgen is not None:
                        next(bgen, None)
                return ctx_

            def stage_b(d, cx):
                xb, bc, sm, lat, tok0, Xc = cx["xb"], cx["bc"], cx["sm"], cx["lat"], cx["tok0"], cx["Xc"]
                k_ = sm.key
                ecs = sm[:, 3, :]; etot = sm[:, 4, :]
                xs3 = xb[:, 0:2048].rearrange("p (h q) -> p h q", q=64)
                Ssb = cx["Ssb"]
                stprev = ST["cur"]
                stnew = stbf_.next()
                ST["cur"] = stnew
                K.tt("dve", state[:, :].rearrange("p (h q) -> p h q", q=64), state[:, :].rearrange("p (h q) -> p h q", q=64),
                     etot.unsqueeze(2).to_broadcast([128, 32, 64]), ALU.mult, ["state", k_], ["state"])
                yield
                K.tt("dve", state[:, :], state[:, :], Ssb[:, :], ALU.add, ["state", Ssb.key], ["state"])
                K.cp("act", stnew[:, :], state[:, :], ["state"], [stnew.key])
                yield
                if lat:
                    ysb = cx["ysb"]
                    for g in range(4):
                        pk = 6 + PB["so"] % 2; PB["so"] += 1
                        K.mm(psum[pk][:, :], bc[:, 4 + g, :], stprev[:, g * 512:(g + 1) * 512], True, True,
                             [bc.key, stprev.key], [f"ps#{pk}"], True)
                        to = to_.next()
                        K.tt("dve", to[:, :].rearrange("p (h q) -> p h q", q=64),
                             psum[pk][:, :].rearrange("p (h q) -> p h q", q=64),
                             ecs[:, g * 8:(g + 1) * 8].unsqueeze(2).to_broadcast([128, 8, 64]), ALU.mult,
                             [f"ps#{pk}", k_], [to.key])
                        K.tt("dve", ysb[:, g * 512:(g + 1) * 512], ysb[:, g * 512:(g + 1) * 512], to[:, :], ALU.add,
                             [ysb.key, to.key], [ysb.key])
                        yield
                if not lat:
                    return
                if d == 0:
                    K.dma("sp", yfw[tok0:tok0 + 128, :], ysb[:, :], "S" + ysb.key, reads=[ysb.key], accw=["yfw"])
                    return
                zt = zt_.next(); ynb = ynb_.next(); yT = yT_.next(); xd = xd_.next()
                K.dma("sp", zt[:, :], zs[tok0:tok0 + 128, :], "L" + zt.key, writes=[zt.key])
                K.tt("pool", xd[:, :].rearrange("p (h q) -> p h q", q=64), xs3,
                     vb[:, 128:160].unsqueeze(2).to_broadcast([128, 32, 64]), ALU.mult, [xb.key, "vb"], [xd.key])
                K.tt("dve", ysb[:, :], ysb[:, :], xd[:, :], ALU.add, [ysb.key, xd.key], [ysb.key])
                K.tt("dve", ysb[:, :], ysb[:, :], zt[:, :], ALU.mult, [ysb.key, zt.key], [ysb.key])
                yield
                for g in range(4):
                    K.act(junk[:, :], ysb[:, g * 512:(g + 1) * 512], AF.Square, [ysb.key], ["sjunk"], accw=[k_],
                          accum_out=sm[:, 7, g:g + 1])
                K.act(sm[:, 7, 4:8], sm[:, 7, 0:4], AF.Sqrt, [k_, "epsb"], [k_], scale=1.0 / 512, bias=epsb[:, 0:1])
                K.op("dve", lambda e, sm=sm: e.reciprocal(sm[:, 7, 8:12], sm[:, 7, 4:8]), [k_], [k_])
                for g in range(4):
                    K.stt("dve", ynb[:, g * 512:(g + 1) * 512], ysb[:, g * 512:(g + 1) * 512], sm[:, 7, 8 + g:9 + g],
                          nwbc[:, g * 512:(g + 1) * 512], ALU.mult, ALU.mult, [ysb.key, k_, "nwbc"], accw=[ynb.key])
                yield
                for half in range(2):
                    pk = half
                    tb = psum[pk][:].bitcast(BF16)
                    for jj in range(8):
                        j = half * 8 + jj
                        K.op("pe", lambda e, tb=tb, jj=jj, j=j, ynb=ynb: e.transpose(
                            tb[:, jj * 128:(jj + 1) * 128], ynb[:, j * 128:(j + 1) * 128], idb[:]),
                            reads=[ynb.key, "idb"], writes=[f"ps#{pk}"], inc=(jj == 7))
                    K.cp("act", yT[:, half * 8:(half + 1) * 8, :].rearrange("p j t -> p (j t)"), tb[:, :], [f"ps#{pk}"],
                         accw=[yT.key])
                K.dma("sp", mixT1[:, tok0:tok0 + 128].rearrange("(j p) t -> p j t", p=128), yT[:], "S" + yT.key,
                      reads=[yT.key], accw=["mixT1"])

            for d in range(2):
                order = [0, 1] + lat_chunks if d == 0 else [1, 0] + lat_chunks[::-1]
                K.op("pool", lambda e: e.memset(state[:], 0.0), writes=["state"])
                st0 = stbf_.next()
                ST["cur"] = st0
                K.op("pool", lambda e, st0=st0: e.memset(st0[:], 0.0), writes=[st0.key])
                n_ = len(order)
                cxs = {}
                def run(stage, idx):
                    if 0 <= idx < n_:
                        if stage == 1:
                            cxs[idx] = stage_a(d, order[idx])
                        elif stage == 2:
                            stage_s2(d, cxs[idx])
                        elif stage == 3:
                            stage_s3(d, cxs[idx])
                        elif stage == 4:
                            stage_a1(d, cxs[idx])
                        else:
                            stage_b(d, cxs.pop(idx))
                for it in range(-3, n_):
                    run(3, it + 1)
                    bgen = stage_b(d, cxs.pop(it)) if 0 <= it < n_ else None
                    if 0 <= it + 1 < n_:
                        stage_a1(d, cxs[it + 1], bgen)
                    if bgen is not None:
                        for _ in bgen:
                            pass
                    run(1, it + 3); run(2, it + 2)
            K.barrier()

    if "ssd" not in P.dbg.get("skip", ()):
        phase_ssd()
    if stop_after == "ssd":
        K.barrier(); pes.close(); P.es.close(); return P

    nt1 = P.dbg.get("out1_tiles", SEQ // 128)
    tiles1 = [(CTX + i * 128, i * 128, 0) for i in range(nt1)]
    phase_out(1, mixT1, 16, o_w_out, h1, out_h, tiles1)

    K.barrier()
    pes.close()
    P.es.close()
    return P


def _rope_tables():
    n_freq = 16
    inv = (10000.0 ** (-np.arange(n_freq, dtype=np.float32) / np.float32(n_freq))).astype(np.float32)
    t = np.arange(SEQ)
    row = (t // 64).astype(np.float32)
    col = (t % 64).astype(np.float32)
    ang = np.concatenate([row[:, None] * inv, col[:, None] * inv], axis=-1).astype(np.float32)
    cos, sin = np.cos(ang).astype(np.float32), np.sin(ang).astype(np.float32)
    tab = np.zeros((2, 128, T), np.float32)
    tab[0, :, :CTX] = 1.0
    for p in range(128):
        d = p % 64
        fi = (d % 16) + 16 * (d // 32)
        sgn = -1.0 if (d % 32) < 16 else 1.0
        tab[0, p, CTX:] = cos[:, fi]
        tab[1, p, CTX:] = sgn * sin[:, fi]
    return tab


def _rope_perm():
    perm = np.zeros(512, np.int64)
    for f in range(512):
        d = f % 64
        e = d % 32
        e2 = e + 16 if e < 16 else e - 16
        perm[f] = f - d + (d // 32) * 32 + e2
    return perm


def make_in_maps(inp):
    B = inp["x"].shape[0]
    perm = _rope_perm()
    w = np.asarray(inp["e_w_in"][0], np.float32)
    w_aug = np.ascontiguousarray(np.concatenate([w, w[:, 1024 + perm], w[:, 1536 + perm]], axis=1))
    tab = _rope_tables()
    ident = np.eye(128, dtype=np.float32)
    wbd = np.zeros((2, 2, 4, 128, 128), np.float32)
    for d in range(2):
        for g, nm in enumerate(("lru_w_r", "lru_w_i")):
            wsrc = np.asarray(inp[nm][0][d], np.float32)
            for cc in range(4):
                wbd[d, g, cc, 0:64, 0:64] = wsrc[2 * cc]
                wbd[d, g, cc, 64:128, 64:128] = wsrc[2 * cc + 1]
    wbd = np.ascontiguousarray(wbd.reshape(16, 128, 128))
    lvec = np.ascontiguousarray(np.concatenate([
        np.asarray(inp["lru_conv_w"][0], np.float32), np.asarray(inp["lru_conv_b"], np.float32).reshape(1, 512),
        np.asarray(inp["lru_b_r"][0], np.float32), np.asarray(inp["lru_b_i"][0], np.float32),
        np.asarray(inp["lru_lambda"][0], np.float32)], axis=0))
    ssd_cv = np.ascontiguousarray(np.concatenate([np.asarray(inp["ssd_conv_w"][0], np.float32),
                                                  np.asarray(inp["ssd_conv_b"], np.float32).reshape(1, 3072)], 0))
    ssd_vec = np.ascontiguousarray(np.concatenate([np.asarray(inp["ssd_a_log"][0], np.float32).reshape(-1),
                                                   np.asarray(inp["ssd_dt_bias"][0], np.float32).reshape(-1),
                                                   np.asarray(inp["ssd_d"][0], np.float32).reshape(-1)]).reshape(1, 160))
    ii = np.arange(128)
    maskT = np.stack([(ii[None, :] >= ii[:, None]), (ii[None, :] <= ii[:, None])], 0).astype(np.float32)
    negm = np.ascontiguousarray(np.tile((maskT - 1.0) * 30000.0, (1, 1, 4)).astype(np.float32))
    selc = np.zeros((64, 32, 128), np.float32)
    for hh in range(32):
        selc[hh, hh, :] = 1.0
        selc[32 + hh, hh, :] = 1.0
    selc = np.ascontiguousarray(selc.reshape(64, 32 * 128))
    maps = []
    for b in range(B):
        m = {
            "src0": np.ascontiguousarray(np.concatenate([inp["ctx"][b], inp["x"][b]], axis=0), dtype=np.float32),
            "cvec": np.ascontiguousarray(np.stack([inp["c"][b], inp["c_ctx"]], 0), dtype=np.float32),
            "w_mod": np.asarray(inp["w_mod"], np.float32),
            "b_mod": np.asarray(inp["b_mod"], np.float32),
            "g_pre": np.asarray(inp["g_pre"], np.float32),
            "g_post": np.asarray(inp["g_post"], np.float32),
            "e_w_in_aug": w_aug,
            "e_w_out": np.asarray(inp["e_w_out"][0], np.float32),
            "ident": ident,
            "ropetab": tab,
            "lru_wbd": wbd,
            "lru_vec": lvec,
            "da_lam": np.ascontiguousarray(np.asarray(inp["da_lambda"][0], np.float32).reshape(1, 256)),
            "da_sub": np.ascontiguousarray(np.asarray(inp["da_subln"][0], np.float32).reshape(1, 128)),
            "o_w_in": np.asarray(inp["o_w_in"][0], np.float32),
            "o_w_out": np.asarray(inp["o_w_out"][0], np.float32),
            "ssd_cv": ssd_cv,
            "ssd_vec": ssd_vec,
            "ssd_norm": np.asarray(inp["ssd_norm"][0], np.float32),
            "maskT": maskT,
            "selc": selc,
            "negm": negm,
        }
        maps.append(m)
    return maps


def kernel(**inp):
    P = build_program()
    maps = make_in_maps(inp)
    res = run_bass_kernel_spmd(P.nc, maps, core_ids=list(range(8)))
    return np.stack([np.asarray(r["out"], np.float32) for r in res.results], 0)
```

```python
import os
from contextlib import ExitStack
import numpy as np
import concourse.bass as bass
import concourse.mybir as mybir
from concourse.bass_utils import run_bass_kernel_spmd

F32, BF16 = mybir.dt.float32, mybir.dt.bfloat16
AF = mybir.ActivationFunctionType
ALU = mybir.AluOpType
AX = mybir.AxisListType

D = 1024
SEQ = 4096
CTX = 256
T = SEQ + CTX
NT = T // 128
EPS = 1e-6


class Sched:
    ENG = ("pe", "dve", "act", "pool", "sp")

    def __init__(self, nc, es):
        self.nc, self.es = nc, es
        self.e = {"pe": nc.tensor, "dve": nc.vector, "act": nc.scalar, "pool": nc.gpsimd, "sp": nc.sync}
        self.sem, self.cnt = {}, {}
        for n in self.ENG:
            self.sem[n] = es.enter_context(nc.semaphore("s_" + n))
            self.cnt[n] = 0
        self.seen = {n: {} for n in self.ENG}
        self.W, self.Rd = {}, {}
        self.pend = {n: [] for n in self.ENG}
        self.nwait = 0
        self.nins = 0
        self.store_q = os.environ.get('KSTOREQ', 'pool') or None

    def _wait(self, eng, need):
        for s, v in need.items():
            if s == "pe" and eng == "pe":
                continue
            if self.seen[eng].get(s, 0) >= v:
                continue
            self.e[eng].wait_ge(self.sem[s], v)
            self.seen[eng][s] = v
            self.nwait += 1

    def _deps(self, eng, reads, writes, accw):
        need = {}

        def add(d):
            for s, v in d.items():
                if need.get(s, 0) < v:
                    need[s] = v
        for r in reads:
            add(self.W.get(r, {}))
        for w in writes:
            add(self.W.get(w, {}))
            add(self.Rd.get(w, {}))
        for w in accw:
            add(self.Rd.get(w, {}))
        self._wait(eng, need)

    def _register(self, ev, reads, writes, accw):
        s, v = ev
        for r in reads:
            d = self.Rd.setdefault(r, {})
            d[s] = max(d.get(s, 0), v)
        for w in writes:
            self.W[w] = {s: v}
            self.Rd[w] = {}
        for w in accw:
            d = self.W.setdefault(w, {})
            d[s] = max(d.get(s, 0), v)

    def op(self, eng, fn, reads=(), writes=(), accw=(), inc=True):
        self._deps(eng, reads, writes, accw)
        ins = fn(self.e[eng])
        self.nins += 1
        if inc:
            self.cnt[eng] += 1
            ins.then_inc(self.sem[eng], 1)
            ev = (eng, self.cnt[eng])
            for (r, w, a) in self.pend[eng]:
                self._register(ev, r, w, a)
            self.pend[eng] = []
            self._register(ev, reads, writes, accw)
        else:
            self.pend[eng].append((tuple(reads), tuple(writes), tuple(accw)))

    def dma(self, q, out, in_, semkey, reads=(), writes=(), accw=(), **kw):
        if self.store_q and semkey.startswith("S"):
            q = self.store_q
        if semkey not in self.sem:
            self.sem[semkey] = self.es.enter_context(self.nc.semaphore("d_" + semkey.replace("#", "_")))
            self.cnt[semkey] = 0
        self._deps(q, reads, writes, accw)
        ins = self.e[q].dma_start(out=out, in_=in_, **kw)
        ins.then_inc(self.sem[semkey], 16)
        self.cnt[semkey] += 16
        self.nins += 1
        self._register((semkey, self.cnt[semkey]), reads, writes, accw)

    def tt(self, eng, out, a, b, op, reads, writes=(), accw=()):
        self.op(eng, lambda e: e.tensor_tensor(out, a, b, op), reads, writes, accw)

    def ts(self, eng, out, a, s1, s2, op0, op1=None, reads=(), writes=(), accw=()):
        if op1 is None:
            self.op(eng, lambda e: e.tensor_scalar(out, a, s1, None, op0), reads, writes, accw)
        else:
            self.op(eng, lambda e: e.tensor_scalar(out, a, s1, s2, op0, op1), reads, writes, accw)

    def stt(self, eng, out, a, sc, b, op0, op1, reads, writes=(), accw=()):
        self.op(eng, lambda e: e.scalar_tensor_tensor(out, a, sc, b, op0, op1), reads, writes, accw)

    def act(self, out, in_, func, reads, writes=(), accw=(), **kw):
        self.op("act", lambda e: e.activation(out=out, in_=in_, func=func, **kw), reads, writes, accw)

    def cp(self, eng, out, in_, reads, writes=(), accw=()):
        if eng == "act":
            self.op("act", lambda e: e.copy(out, in_), reads, writes, accw)
        else:
            self.op(eng, lambda e: e.tensor_copy(out, in_), reads, writes, accw)

    def mm(self, out, lhsT, rhs, start, stop, reads, writes, inc):
        self.op("pe", lambda e: e.matmul(out, lhsT, rhs, start=start, stop=stop), reads, writes, inc=inc)

    def barrier(self):
        for n in self.ENG:
            assert not self.pend[n]
        allev = {s: c for s, c in self.cnt.items() if c > 0}
        for n in self.ENG:
            self._wait(n, allev)
        self.W, self.Rd = {}, {}


class Buf:
    def __init__(self, t, key):
        self.t, self.key = t, key

    def __getitem__(self, k):
        return self.t[k]


class Ring:
    def __init__(self, alloc, name, n, shape, dtype):
        self.bufs = [Buf(alloc(f"{name}{i}", shape, dtype), f"{name}#{i}") for i in range(n)]
        self.i = 0

    def next(self):
        b = self.bufs[self.i % len(self.bufs)]
        self.i += 1
        return b


class Prog:
    def __init__(self, dbg=None):
        self.dbg = dbg or {}
        self.nc = nc = bass.Bass("TRN2", target_bir_lowering=False)
        self.es = ExitStack()
        self.K = Sched(nc, self.es)
        self.dram = {}
        self.uid = 0

    def din(self, name, shape, dt=F32):
        self.dram[name] = self.nc.dram_tensor(name, list(shape), dt, kind="ExternalInput").ap()
        return self.dram[name]

    def dout(self, name, shape, dt=F32):
        self.dram[name] = self.nc.dram_tensor(name, list(shape), dt, kind="ExternalOutput").ap()
        return self.dram[name]

    def dscr(self, name, shape, dt):
        if name in self.dbg.get("dump", ()):
            return self.dout(name, shape, dt)
        self.dram[name] = self.nc.dram_tensor(name, list(shape), dt).ap()
        return self.dram[name]


def build_program(dbg=None):
    P = Prog(dbg)
    nc, K = P.nc, P.K
    stop_after = P.dbg.get("stop_after", "all")

    src0 = P.din("src0", [T, D])
    cvec = P.din("cvec", [2, D])
    w_mod = P.din("w_mod", [2, D, 3 * D])
    b_mod = P.din("b_mod", [2, 3 * D])
    g_pre = P.din("g_pre", [2, D])
    g_post = P.din("g_post", [2, D])
    e_w_in = P.din("e_w_in_aug", [D, 4096])
    e_w_out = P.din("e_w_out", [D, D])
    ident = P.din("ident", [128, 128])
    ropetab = P.din("ropetab", [2, 128, T])
    out_h = P.dout("out", [SEQ, D])
    lru_wbd = P.din("lru_wbd", [16, 128, 128])
    lru_vec = P.din("lru_vec", [11, 512])
    da_lam = P.din("da_lam", [1, 256])
    da_sub = P.din("da_sub", [1, 128])
    mixT0 = P.dscr("mixT0", [D, T], BF16)
    o_w_in = P.din("o_w_in", [D, 5184])
    o_w_out = P.din("o_w_out", [2048, D])
    ssd_cv = P.din("ssd_cv", [5, 3072])
    ssd_vec = P.din("ssd_vec", [1, 160])
    ssd_norm = P.din("ssd_norm", [2048])
    maskT = P.din("maskT", [2, 128, 128])
    selc = P.din("selc", [64, 32 * 128])
    negm = P.din("negm", [2, 128, 512])
    xbcT = P.dscr("xbcT", [3072, T], BF16)
    zs = P.dscr("zs", [T, 2048], BF16)
    dtraw = P.dscr("dtraw", [T, 64], F32)
    xsB = P.dscr("xsB", [T, 2560], BF16)
    bcT = P.dscr("bcT", [1024, T], BF16)
    yfw = P.dscr("yfw", [T, 2048], F32)
    mixT1 = P.dscr("mixT1", [2048, T], BF16)
    h1 = P.dscr("h1", [T, D], F32)

    xrT = P.dscr("xrT", [512, T], F32)
    grT = P.dscr("grT", [512, T], BF16)
    gdT = P.dscr("gdT", [512, T], BF16)
    qT = P.dscr("qT", [512, T], BF16)
    kT = P.dscr("kT", [512, T], BF16)
    vtok = P.dscr("vtok", [T, 512], BF16)

    pes = ExitStack()
    def palloc(name, shape, dt):
        return pes.enter_context(nc.sbuf_tensor(name, list(shape), dt))
    psum = [pes.enter_context(nc.psum_tensor(f"psb{i}", [128, 512], F32)) for i in range(8)]
    idf = palloc("idf", [128, 128], F32)
    idb = palloc("idb", [128, 128], BF16)
    epsb = palloc("epsb", [128, 1], F32)
    modA = palloc("modA", [128, 2, 8, 2], F32)
    modS = palloc("modS", [128, 2, 8, 2], F32)
    ggbc = palloc("ggbc", [128, 2, 2, D], F32)

    K.dma("sp", idf[:], ident[:, :], "Lidf", writes=["idf"])
    K.op("dve", lambda e: e.tensor_copy(idb[:], idf[:]), reads=["idf"], writes=["idb"])
    K.op("dve", lambda e: e.memset(epsb[:], EPS), writes=["epsb"])

    def phase_mod():
        with ExitStack() as es:
            def alloc(name, shape, dt):
                P.uid += 1
                return es.enter_context(nc.sbuf_tensor(f"{name}_{P.uid}", list(shape), dt))
            cT = alloc("cT", [128, 2, 8], F32)
            sig = alloc("sig", [128, 2, 8], F32)
            srep = alloc("srep", [128, 2, 8, 128], F32)
            bT = alloc("bT", [128, 2, 24], F32)
            gpT = alloc("gpT", [128, 2, 8], F32)
            bgbc = alloc("bgbc", [128, 2, D], F32)
            gpbc = alloc("gpbc", [128, 2, D], F32)
            wst = Ring(alloc, "wst", 2, [128, 8, 512], F32)
            tmp = alloc("mtmp", [128, 16, 2], F32)

            rows = alloc("rows", [80, 128], F32)
            K.dma("sp", rows[0:16, :], cvec.rearrange("t (j p) -> (t j) p", p=128), "Lrows", accw=["rows"])
            K.dma("sp", rows[16:64, :], b_mod.rearrange("l (f p) -> (l f) p", p=128), "Lrows", accw=["rows"])
            K.dma("sp", rows[64:80, :], g_pre.rearrange("l (j p) -> (l j) p", p=128), "Lrows", accw=["rows"])
            K.op("pe", lambda e: e.transpose(psum[4][:, 0:80], rows[:, :], idf[0:80, 0:80]),
                 reads=["rows", "idf"], writes=["ps#4"])
            K.op("dve", lambda e: e.tensor_copy(cT[:].rearrange("p t j -> p (t j)"), psum[4][:, 0:16]),
                 reads=["ps#4"], writes=["cT"])
            K.op("dve", lambda e: e.tensor_copy(bT[:].rearrange("p l f -> p (l f)"), psum[4][:, 16:64]),
                 reads=["ps#4"], writes=["bT"])
            K.op("dve", lambda e: e.tensor_copy(gpT[:].rearrange("p l j -> p (l j)"), psum[4][:, 64:80]),
                 reads=["ps#4"], writes=["gpT"])
            for l in range(2):
                K.dma("sp", bgbc[:, l, :], b_mod[l, 2 * D:3 * D].partition_broadcast(128), "Lbgbc", accw=["bgbc"])
                K.dma("sp", gpbc[:, l, :], g_post[l, :].partition_broadcast(128), "Lgpbc", accw=["gpbc"])
            K.op("act", lambda e: e.activation(out=sig[:], in_=cT[:], func=AF.Sigmoid), reads=["cT"], writes=["sig"])
            K.op("dve", lambda e: e.tensor_tensor(cT[:], cT[:], sig[:], ALU.mult), reads=["sig", "cT"], writes=["cT"])
            for t in range(2):
                K.op("dve", lambda e, t=t: e.tensor_copy(
                    srep[:, t, :, :], cT[:, t, :].unsqueeze(2).to_broadcast([128, 8, 128])),
                    reads=["cT"], accw=["srep"])
            for l in range(2):
                for pc in range(6):
                    wb = wst.next()
                    K.dma("sp", wb[:], w_mod[l, :, pc * 512:(pc + 1) * 512].rearrange("(j p) n -> p j n", p=128),
                          "L" + wb.key, writes=[wb.key])
                    if pc < 4:
                        pst = psum[pc % 2]
                        for f in range(4):
                            for j in range(8):
                                K.op("pe", lambda e, f=f, j=j, pst=pst, wb=wb: e.matmul(
                                    pst[:, 2 * f:2 * f + 2], wb[:, j, f * 128:(f + 1) * 128], cT[:, :, j],
                                    start=(j == 0), stop=(j == 7)),
                                    reads=[wb.key, "cT"], writes=[f"ps#{pc % 2}"], inc=(j == 7 and f == 3))
                        K.op("dve", lambda e, pst=pst, pc=pc: e.tensor_copy(
                            tmp[:, pc * 4:(pc + 1) * 4, :], pst[:, 0:8].rearrange("p (f t) -> p f t", t=2)),
                            reads=[f"ps#{pc % 2}"], accw=["mtmp"])
                    else:
                        for t in range(2):
                            pst = psum[2 + t]
                            for j in range(8):
                                K.op("pe", lambda e, j=j, t=t, pst=pst, wb=wb: e.matmul(
                                    pst[:, :], srep[:, t, j, :], wb[:, j, :], start=(j == 0), stop=(j == 7)),
                                    reads=[wb.key, "srep"], writes=[f"ps#{2 + t}"], inc=(j == 7))
                            c0 = (pc - 4) * 512
                            K.op("dve", lambda e, t=t, l=l, c0=c0, pst=pst: e.tensor_tensor(
                                ggbc[:, l, t, c0:c0 + 512], pst[:, :], bgbc[:, l, c0:c0 + 512], ALU.add),
                                reads=[f"ps#{2 + t}", "bgbc"], accw=["ggbc"])
                for t in range(2):
                    K.op("dve", lambda e, t=t, l=l: e.tensor_tensor(
                        modS[:, l, :, t], tmp[:, 0:8, t], bT[:, l, 0:8], ALU.add),
                        reads=["mtmp", "bT"], accw=["modS"])
                    K.op("dve", lambda e, t=t, l=l: e.scalar_tensor_tensor(
                        modA[:, l, :, t], tmp[:, 8:16, t], 1.0, bT[:, l, 8:16], ALU.add, ALU.add),
                        reads=["mtmp", "bT"], accw=["modA"])
                    K.op("dve", lambda e, t=t, l=l: e.tensor_tensor(
                        modA[:, l, :, t], modA[:, l, :, t], gpT[:, l, :], ALU.mult),
                        reads=["modA", "gpT"], writes=["modA"])
                    K.op("dve", lambda e, t=t, l=l: e.tensor_tensor(
                        ggbc[:, l, t, :], ggbc[:, l, t, :], gpbc[:, l, :], ALU.mult),
                        reads=["ggbc", "gpbc"], writes=["ggbc"])
            K.barrier()

    phase_mod()
    if "mod" in P.dbg.get("dump", ()):
        dA = P.dout("dbg_modA", [128, 32]); dS = P.dout("dbg_modS", [128, 32]); dG = P.dout("dbg_gg", [128, 4 * D])
        K.dma("sp", dA[:, :], modA[:].rearrange("p l j t -> p (l j t)"), "Sdbg", reads=["modA"])
        K.dma("sp", dS[:, :], modS[:].rearrange("p l j t -> p (l j t)"), "Sdbg", reads=["modS"])
        K.dma("sp", dG[:, :], ggbc[:].rearrange("p l t f -> p (l t f)"), "Sdbg", reads=["ggbc"])
    if stop_after == "mod":
        K.barrier(); pes.close(); P.es.close(); return P

    def phase_proj(layer, src, Wd, ncols, fspecs, tspecs, extra_alloc=None, per_group=None):
        sq_prev = K.store_q
        K.store_q = os.environ.get("KPROJQ", "act")
        _phase_proj(layer, src, Wd, ncols, fspecs, tspecs, extra_alloc, per_group)
        K.store_q = sq_prev

    def _phase_proj(layer, src, Wd, ncols, fspecs, tspecs, extra_alloc=None, per_group=None):
        with ExitStack() as es:
            def alloc(name, shape, dt):
                P.uid += 1
                return es.enter_context(nc.sbuf_tensor(f"{name}_{P.uid}", list(shape), dt))
            Wb = alloc("Wb", [128, 8, ncols], BF16)
            wst = Ring(alloc, "wst", 2, [128, 8, 256], F32)
            xr_ = Ring(alloc, "xin", 4, [128, D], F32)
            xn_ = Ring(alloc, "xn", 4, [128, D], BF16)
            uT_ = Ring(alloc, "uT", 2, [128, 8, 512], BF16)
            junk = alloc("junk", [128, D], BF16)
            stat = Ring(alloc, "stat", 4, [128, 4], F32)
            ctxo = extra_alloc(alloc) if extra_alloc else None
            ceng = ["dve", "pool", "act"]
            for pc in range(ncols // 256 + (1 if ncols % 256 else 0)):
                c0 = pc * 256
                cw = min(256, ncols - c0)
                wb = wst.next()
                K.dma("sp", wb[:, :, 0:cw], Wd[:, c0:c0 + cw].rearrange("(j p) n -> p j n", p=128),
                      "L" + wb.key, writes=[wb.key])
                en = ceng[pc % 3]
                if en == "act":
                    K.op("act", lambda e, wb=wb, c0=c0, cw=cw: e.copy(Wb[:, :, c0:c0 + cw], wb[:, :, 0:cw]),
                         reads=[wb.key], accw=["Wb"])
                else:
                    K.op(en, lambda e, wb=wb, c0=c0, cw=cw: e.tensor_copy(Wb[:, :, c0:c0 + cw], wb[:, :, 0:cw]),
                         reads=[wb.key], accw=["Wb"])
            groups = [(0, CTX, 1)] + [(CTX + 512 * g, 512, 0) for g in range(SEQ // 512)]
            pst_i = [0]
            pso_i = [0]

            def front_parts(gi):
                tok0, ntok, tmod = groups[gi]
                uT = uT_.next()
                p1s, p2s = [], []
                for ti in range(ntok // 128):
                    def part1(ti=ti):
                        box = {}
                        xt = xr_.next(); xn = xn_.next(); st = stat.next()
                        box["xn"] = xn
                        K.dma("sp", xt[:], src[tok0 + ti * 128: tok0 + (ti + 1) * 128, :], "L" + xt.key, writes=[xt.key])
                        K.op("act", lambda e: e.activation(out=junk[:], in_=xt[:], func=AF.Square, accum_out=st[:, 0:1]),
                             reads=[xt.key], writes=["junk", st.key])
                        K.op("act", lambda e: e.activation(out=st[:, 1:2], in_=st[:, 0:1], func=AF.Sqrt,
                                                           scale=1.0 / D, bias=epsb[:, 0:1]),
                             reads=[st.key, "epsb"], writes=[st.key])
                        K.op("dve", lambda e: e.reciprocal(st[:, 2:3], st[:, 1:2]), reads=[st.key], writes=[st.key])
                        K.op("dve", lambda e: e.tensor_scalar(xn[:], xt[:], st[:, 2:3], None, ALU.mult),
                             reads=[xt.key, st.key], writes=[xn.key])
                        return box

                    def part2(box, ti=ti):
                        xn = box["xn"]
                        pk = 6 + (pst_i[0] % 2); pst_i[0] += 1
                        pst = psum[pk][:].bitcast(BF16)
                        for j in range(8):
                            K.op("pe", lambda e, j=j: e.transpose(
                                pst[:, j * 128:(j + 1) * 128], xn[:, j * 128:(j + 1) * 128], idb[:]),
                                reads=[xn.key, "idb"], writes=[f"ps#{pk}"], inc=(j == 7))
                        for j in range(8):
                            K.op("dve", lambda e, j=j: e.tensor_scalar(
                                uT[:, j, ti * 128:(ti + 1) * 128], pst[:, j * 128:(j + 1) * 128],
                                modA[:, layer, j, tmod:tmod + 1], modS[:, layer, j, tmod:tmod + 1], ALU.mult, ALU.add),
                                reads=[f"ps#{pk}", "modA", "modS"], accw=[uT.key])
                    p1s.append(part1); p2s.append(part2)
                return uT, p1s, p2s

            def mm(gi, uT, hooks):
                tok0, ntok, tmod = groups[gi]
                if per_group:
                    per_group(ctxo, gi, tok0, ntok)
                nb_tot = sum(len(b) for (_, b, _) in fspecs)
                nhk = max(1, len(hooks))
                step = max(1, nb_tot // nhk)
                bcount = 0
                hooks = list(hooks)
                for (name, bundles, epi) in fspecs:
                    for bi, cols in enumerate(bundles):
                        if hooks and bcount % step == 0:
                            hooks.pop(0)()
                        bcount += 1
                        pks = []
                        for c0 in cols:
                            pk = pso_i[0] % 6; pso_i[0] += 1
                            pks.append(pk)
                            for j in range(8):
                                K.op("pe", lambda e, j=j, pk=pk, c0=c0, uT=uT, ntok=ntok: e.matmul(
                                    psum[pk][:, 0:ntok], Wb[:, j, c0:c0 + 128], uT[:, j, 0:ntok],
                                    start=(j == 0), stop=(j == 7)),
                                    reads=["Wb", uT.key], writes=[f"ps#{pk}"], inc=(j == 7))
                        epi(ctxo, pks, bi, tok0, ntok)
                for (c0, cw, epi) in tspecs:
                    for ti in range(ntok // 128):
                        pk = pso_i[0] % 6; pso_i[0] += 1
                        for j in range(8):
                            K.op("pe", lambda e, j=j, pk=pk, uT=uT, ti=ti: e.matmul(
                                psum[pk][:, 0:cw], uT[:, j, ti * 128:(ti + 1) * 128], Wb[:, j, c0:c0 + cw],
                                start=(j == 0), stop=(j == 7)),
                                reads=["Wb", uT.key], writes=[f"ps#{pk}"], inc=(j == 7))
                        epi(ctxo, pk, tok0 + ti * 128)
                while hooks:
                    hooks.pop(0)()

            ng = P.dbg.get("ngroups", len(groups))
            uT0, p1s, p2s = front_parts(0)
            for p1, p2 in zip(p1s, p2s):
                p2(p1())
            cur = uT0
            for gi in range(ng):
                hooks = []
                nxt = None
                if gi + 1 < ng:
                    nxt, p1s, p2s = front_parts(gi + 1)
                    boxes = {}
                    def mk(k, p1s=p1s, p2s=p2s, boxes=boxes):
                        def h():
                            if k == 0:
                                for kk in range(min(2, len(p1s))):
                                    boxes[kk] = p1s[kk]()
                                return
                            if 0 <= k - 1 < len(p1s):
                                p2s[k - 1](boxes[k - 1])
                            if k + 1 < len(p1s):
                                boxes[k + 1] = p1s[k + 1]()
                        return h
                    hooks = [mk(k) for k in range(len(p1s) + 1)]
                mm(gi, cur, hooks)
                cur = nxt
            K.barrier()

    def l0_alloc(alloc):
        c = {}
        c["sf"] = Ring(alloc, "sf", 3, [128, 512], F32)
        c["sb"] = Ring(alloc, "sb", 4, [128, 512], BF16)
        c["t1"] = Ring(alloc, "t1", 2, [128, 512], F32)
        c["t2"] = Ring(alloc, "t2", 2, [128, 512], F32)
        c["tab"] = Ring(alloc, "tab", 2, [128, 2, 512], F32)
        return c

    def l0_group(c, gi, tok0, ntok):
        tb = c["tab"].next()
        c["curtab"] = tb
        K.dma("sp", tb[:, :, 0:ntok], ropetab[:, :, tok0:tok0 + ntok].rearrange("c p t -> p c t"),
              "L" + tb.key, writes=[tb.key])

    def epi_copy_f32(dst):
        def f(c, pks, bi, tok0, ntok):
            pk = pks[0]; b = c["sf"].next()
            K.op("act", lambda e: e.copy(b[:, 0:ntok], psum[pk][:, 0:ntok]), reads=[f"ps#{pk}"], writes=[b.key])
            K.dma("sp", dst[bi * 128:(bi + 1) * 128, tok0:tok0 + ntok], b[:, 0:ntok], "S" + b.key,
                  reads=[b.key], accw=[dst.name])
        return f

    def epi_silu_bf(dst):
        def f(c, pks, bi, tok0, ntok):
            pk = pks[0]; b = c["sb"].next()
            K.op("act", lambda e: e.activation(out=b[:, 0:ntok], in_=psum[pk][:, 0:ntok], func=AF.Silu),
                 reads=[f"ps#{pk}"], writes=[b.key])
            K.dma("sp", dst[bi * 128:(bi + 1) * 128, tok0:tok0 + ntok], b[:, 0:ntok], "S" + b.key,
                  reads=[b.key], accw=[dst.name])
        return f

    def epi_rope(dst):
        def f(c, pks, bi, tok0, ntok):
            pa, pb = pks; t1 = c["t1"].next(); t2 = c["t2"].next(); b = c["sb"].next(); tb = c["curtab"]
            K.op("dve", lambda e: e.tensor_tensor(t1[:, 0:ntok], psum[pa][:, 0:ntok], tb[:, 0, 0:ntok], ALU.mult),
                 reads=[f"ps#{pa}", tb.key], writes=[t1.key])
            K.op("dve", lambda e: e.tensor_tensor(t2[:, 0:ntok], psum[pb][:, 0:ntok], tb[:, 1, 0:ntok], ALU.mult),
                 reads=[f"ps#{pb}", tb.key], writes=[t2.key])
            K.op("pool", lambda e: e.tensor_tensor(b[:, 0:ntok], t1[:, 0:ntok], t2[:, 0:ntok], ALU.add),
                 reads=[t1.key, t2.key], writes=[b.key])
            K.dma("sp", dst[bi * 128:(bi + 1) * 128, tok0:tok0 + ntok], b[:, 0:ntok], "S" + b.key,
                  reads=[b.key], accw=[dst.name])
        return f

    def epi_v(c, pk, tok0):
        b = c["sb"].next()
        K.op("act", lambda e: e.copy(b[:, :], psum[pk][:, :]), reads=[f"ps#{pk}"], writes=[b.key])
        K.dma("sp", vtok[tok0:tok0 + 128, :], b[:, :], "S" + b.key, reads=[b.key], accw=["vtok"])

    l0_f = [
        ("xr", [[f * 128] for f in range(0, 4)], epi_copy_f32(xrT)),
        ("gr", [[f * 128] for f in range(4, 8)], epi_silu_bf(grT)),
        ("q", [[1024 + f * 128, 3072 + f * 128] for f in range(4)], epi_rope(qT)),
        ("k", [[1536 + f * 128, 3584 + f * 128] for f in range(4)], epi_rope(kT)),
        ("gd", [[f * 128] for f in range(20, 24)], epi_silu_bf(gdT)),
    ]
    l0_t = [(2048, 512, epi_v)]
    phase_proj(0, src0, e_w_in, 4096, l0_f, l0_t, l0_alloc, l0_group)
    if stop_after == "proj0":
        K.barrier(); pes.close(); P.es.close(); return P


    LAMBDA_INIT0 = 0.8 - 0.6 * 1.0

    def phase_lru():
        with ExitStack() as es:
            def alloc(name, shape, dt):
                P.uid += 1
                return es.enter_context(nc.sbuf_tensor(f"{name}_{P.uid}", list(shape), dt))
            rows = alloc("lrows", [44, 128], F32)
            pv = alloc("lpv", [128, 11, 4], F32)
            coef = alloc("lcoef", [128, 2, 4], F32)
            ones1 = alloc("ones1", [128, 1], F32)
            wbf = alloc("wbf", [128, 16, 128], F32)
            wbb = alloc("wbb", [128, 16, 128], BF16)
            big = Ring(alloc, "big", 7, [128, T], F32)
            xcb = alloc("xcb", [128, T], BF16)
            grs = alloc("grs", [128, T], BF16)
            mo = alloc("mixo", [128, T], BF16)
            K.op("dve", lambda e: e.memset(ones1[:], 1.0), writes=["ones1"])
            K.dma("sp", rows[:, :], lru_vec.rearrange("v (c p) -> (v c) p", p=128), "Lrows", writes=["lrows"])
            K.op("pe", lambda e: e.transpose(psum[0][:, 0:44], rows[:, :], idf[0:44, 0:44]),
                 reads=["lrows", "idf"], writes=["ps#0"])
            K.cp("dve", pv[:].rearrange("p v c -> p (v c)"), psum[0][:, 0:44], ["ps#0"], ["lpv"])
            K.act(coef[:].rearrange("p d c -> p (d c)"), pv[:, 9:11, :].rearrange("p d c -> p (d c)"), AF.Exp,
                  ["lpv"], ["lcoef"], scale=-1.0)
            K.act(coef[:].rearrange("p d c -> p (d c)"), coef[:].rearrange("p d c -> p (d c)"), AF.Ln,
                  ["lcoef", "ones1"], ["lcoef"], bias=ones1[:, 0:1])
            K.ts("dve", coef[:].rearrange("p d c -> p (d c)"), coef[:].rearrange("p d c -> p (d c)"), -8.0, None,
                 ALU.mult, None, ["lcoef"], ["lcoef"])
            K.dma("sp", wbf[:], lru_wbd.rearrange("n k m -> k n m"), "Lwbf", writes=["wbf"])
            K.cp("dve", wbb[:], wbf[:], ["wbf"], ["wbb"])
            segs = [(0, CTX), (CTX, T)]
            blocks = [(b0, min(512, T - b0)) for b0 in range(0, T, 512)]
            for cc in range(4):
                x = big.next(); xc = big.next()
                K.dma("sp", x[:, :], xrT[cc * 128:(cc + 1) * 128, :], "L" + x.key, writes=[x.key])
                K.dma("sp", grs[:, :], grT[cc * 128:(cc + 1) * 128, :], "Lgrs", writes=["grs"])
                K.ts("dve", xc[:, :], x[:, :], pv[:, 2, cc:cc + 1], pv[:, 4, cc:cc + 1], ALU.mult, ALU.add,
                     [x.key, "lpv"], [xc.key])
                for (a, b) in segs:
                    for tap, sh in ((0, -2), (1, -1), (3, 1)):
                        lo = max(a, a - sh); hi = min(b, b - sh)
                        K.stt("dve", xc[:, lo:hi], x[:, lo + sh:hi + sh], pv[:, tap, cc:cc + 1],
                              xc[:, lo:hi], ALU.mult, ALU.add, [x.key, xc.key, "lpv"], [xc.key])
                K.cp("act", xcb[:, :], xc[:, :], [xc.key], ["xcb"])
                hs = []
                for d in range(2):
                    rb = big.next(); ib = big.next()
                    for (b0, bn) in blocks:
                        for g, dstb, brow in ((0, rb, 5 + d), (1, ib, 7 + d)):
                            pk = (2 * (b0 // 512) + g) % 6
                            K.mm(psum[pk][:, 0:bn], wbb[:, (d * 2 + g) * 4 + cc, :], xcb[:, b0:b0 + bn], True, True,
                                 ["wbb", "xcb"], [f"ps#{pk}"], True)
                            K.act(dstb[:, b0:b0 + bn], psum[pk][:, 0:bn], AF.Sigmoid, [f"ps#{pk}", "lpv"], accw=[dstb.key],
                                  bias=pv[:, brow, cc:cc + 1])
                    K.ts("dve", rb[:, :], rb[:, :], coef[:, d, cc:cc + 1], None, ALU.mult, None, [rb.key, "lcoef"], [rb.key])
                    K.act(rb[:, :], rb[:, :], AF.Exp, [rb.key], [rb.key])
                    sq = big.next()
                    K.act(sq[:, :], rb[:, :], AF.Square, [rb.key], [sq.key])
                    K.act(sq[:, :], sq[:, :], AF.Sqrt, [sq.key, "ones1"], [sq.key], scale=-1.0, bias=ones1[:, 0:1])
                    K.tt("pool", ib[:, :], ib[:, :], xc[:, :], ALU.mult, [ib.key, xc.key], [ib.key])
                    K.tt("dve", ib[:, :], ib[:, :], sq[:, :], ALU.mult, [ib.key, sq.key], [ib.key])
                    h = sq
                    if d == 0:
                        K.op("dve", lambda e, h=h, rb=rb, ib=ib: e.tensor_tensor_scan(
                            h[:, :], rb[:, :], ib[:, :], 0.0, ALU.mult, ALU.add), [rb.key, ib.key, h.key], [h.key])
                    else:
                        K.op("dve", lambda e, h=h, rb=rb, ib=ib: e.tensor_tensor_scan(
                            h[:, CTX - 1::-1] if False else h[:, 0:CTX][:, ::-1], rb[:, 0:CTX][:, ::-1], ib[:, 0:CTX][:, ::-1],
                            0.0, ALU.mult, ALU.add), [rb.key, ib.key, h.key], [h.key])
                        K.op("dve", lambda e, h=h, rb=rb, ib=ib: e.tensor_tensor_scan(
                            h[:, CTX:T][:, ::-1], rb[:, CTX:T][:, ::-1], ib[:, CTX:T][:, ::-1],
                            h[:, 0:1], ALU.mult, ALU.add), [rb.key, ib.key, h.key], [h.key])
                    hs.append(h)
                K.tt("pool", hs[0][:, :], hs[0][:, :], hs[1][:, :], ALU.add, [hs[0].key, hs[1].key], [hs[0].key])
                K.tt("dve", mo[:, :], hs[0][:, :], grs[:, :], ALU.mult, [hs[0].key, "grs"], ["mixo"])
                K.dma("sp", mixT0[cc * 128:(cc + 1) * 128, :], mo[:, :], "Smixo", reads=["mixo"], accw=["mixT0"])
            K.barrier()

    if "lru" not in P.dbg.get("skip", ()):
        phase_lru()
    if stop_after == "lru":
        K.barrier(); pes.close(); P.es.close(); return P

    def phase_attn():
        with ExitStack() as es:
            def alloc(name, shape, dt):
                P.uid += 1
                return es.enter_context(nc.sbuf_tensor(f"{name}_{P.uid}", list(shape), dt))
            kres = alloc("kres", [128, 4, T], BF16)
            vres = alloc("vres", [128, NT, 512], BF16)
            onesb = alloc("onesb", [128, 128], BF16)
            onesf = alloc("onesf", [128, 128], F32)
            lrow = alloc("lamrow", [1, 260], F32)
            lamc = alloc("lamc", [128, 2], F32)
            subc = alloc("subc", [128, 2], F32)
            subrow = alloc("subrow", [1, 128], F32)
            qb_ = Ring(alloc, "qblk", 2, [128, 4, 512], BF16)
            gd_ = Ring(alloc, "gdblk", 2, [128, 512], BF16)
            E_ = Ring(alloc, "Eb", 6, [128, 512], BF16)
            f_ = Ring(alloc, "af", 6, [128, 512], F32)
            ob_ = Ring(alloc, "aob", 4, [128, 512], BF16)
            K.op("dve", lambda e: e.memset(onesb[:], 1.0), writes=["onesb"])
            K.op("dve", lambda e: e.memset(onesf[:], 1.0), writes=["onesf"])
            K.dma("sp", lrow[:, 0:256], da_lam[:, :], "Llam", writes=["lamrow"])
            K.dma("sp", subrow[:, :], da_sub[:, :], "Lsub", writes=["subrow"])
            K.tt("dve", lrow[:, 0:64], lrow[:, 0:64], lrow[:, 64:128], ALU.mult, ["lamrow"], ["lamrow"])
            K.tt("dve", lrow[:, 128:192], lrow[:, 128:192], lrow[:, 192:256], ALU.mult, ["lamrow"], ["lamrow"])
            K.op("dve", lambda e: e.reduce_sum(lrow[:, 256:257], lrow[:, 0:64], AX.X), ["lamrow"], ["lamrow"])
            K.op("dve", lambda e: e.reduce_sum(lrow[:, 257:258], lrow[:, 128:192], AX.X), ["lamrow"], ["lamrow"])
            K.act(lrow[:, 256:258], lrow[:, 256:258], AF.Exp, ["lamrow"], ["lamrow"])
            K.tt("dve", lrow[:, 258:259], lrow[:, 256:257], lrow[:, 257:258], ALU.subtract, ["lamrow"], ["lamrow"])
            K.ts("dve", lrow[:, 258:259], lrow[:, 258:259], -1.0, -LAMBDA_INIT0, ALU.mult, ALU.add, ["lamrow"], ["lamrow"])
            K.mm(psum[0][:, 0:1], onesf[0:1, :], lrow[0:1, 258:259], True, True, ["onesf", "lamrow"], ["ps#0"], True)
            K.cp("dve", lamc[:, 0:1], psum[0][:, 0:1], ["ps#0"], ["lamc"])
            K.op("pe", lambda e: e.transpose(psum[1][:, 0:1], subrow[0:1, :], idf[0:1, 0:1]),
                 reads=["subrow", "idf"], writes=["ps#1"])
            K.ts("dve", subc[:, 0:1], psum[1][:, 0:1], 1.0 - LAMBDA_INIT0, None, ALU.mult, None, ["ps#1"], ["subc"])
            for h in range(4):
                K.dma("sp", kres[:, h, :], kT[h * 128:(h + 1) * 128, :], "Lkres", accw=["kres"])
            for n0 in range(0, NT, 2):
                K.dma("sp", vres[:, n0:n0 + 2, :], vtok[n0 * 128:(n0 + 2) * 128, :].rearrange("(n p) e -> p n e", p=128),
                      "Lvres", accw=["vres"])
            qblocks = [(0, CTX, 0, 2)] + [(CTX + 512 * g, 512, 0, NT) for g in range(SEQ // 512)]
            nqb = P.dbg.get("nqb", len(qblocks))
            sti = 0
            acc_ = Ring(alloc, "dacc", 4, [128, 512], F32)
            for (q0, nq, kt0, kt1) in qblocks[:nqb]:
                qb = qb_.next()
                for h in range(4):
                    K.dma("sp", qb[:, h, 0:nq], qT[h * 128:(h + 1) * 128, q0:q0 + nq], "L" + qb.key, accw=[qb.key])
                for h in range(4):
                    gd = gd_.next()
                    K.dma("sp", gd[:, 0:nq], gdT[h * 128:(h + 1) * 128, q0:q0 + nq], "L" + gd.key, writes=[gd.key])
                    accs = [acc_.next(), acc_.next()]
                    kts = list(range(kt0, kt1))
                    pkmap = {}

                    def emit_qk(kt):
                        nonlocal sti
                        for c in range(2):
                            pk = sti % 4; sti += 1
                            pkmap[(kt, c)] = pk
                            K.mm(psum[pk][:, 0:nq], kres[c * 64:(c + 1) * 64, h, kt * 128:(kt + 1) * 128],
                                 qb[c * 64:(c + 1) * 64, h, 0:nq], True, True, ["kres", qb.key], [f"ps#{pk}"], True)

                    def emit_rest(kt):
                        for c in range(2):
                            pk = pkmap[(kt, c)]
                            E = E_.next()
                            K.act(E[:, 0:nq], psum[pk][:, 0:nq], AF.Exp, [f"ps#{pk}"], [E.key], scale=0.125)
                            K.mm(psum[4 + c][:, 0:nq], vres[:, kt, h * 128:(h + 1) * 128], E[:, 0:nq],
                                 kt == kt0, kt == kt1 - 1, ["vres", E.key], [f"ps#{4 + c}"], kt == kt1 - 1)
                            if c == 1:
                                K.mm(psum[7][:, 0:nq], onesb[:, :], E[:, 0:nq], kt == kt0, kt == kt1 - 1,
                                     ["onesb", E.key], ["ps#7"], kt == kt1 - 1)
                            elif kt == kt0:
                                K.cp("dve", accs[c][:, 0:nq], E[:, 0:nq], [E.key], [accs[c].key])
                            else:
                                K.tt("dve", accs[c][:, 0:nq], accs[c][:, 0:nq], E[:, 0:nq], ALU.add,
                                     [E.key, accs[c].key], [accs[c].key])

                    emit_qk(kts[0])
                    for i, kt in enumerate(kts):
                        if i + 1 < len(kts):
                            emit_qk(kts[i + 1])
                        emit_rest(kt)
                    r0 = f_.next(); r1 = f_.next(); t0 = f_.next(); t1 = f_.next()
                    K.cp("act", t0[:, 0:nq], psum[4][:, 0:nq], ["ps#4"], [t0.key])
                    K.cp("act", t1[:, 0:nq], psum[5][:, 0:nq], ["ps#5"], [t1.key])
                    ab = ob_.next()
                    K.cp("pool", ab[:, 0:nq], accs[0][:, 0:nq], [accs[0].key], [ab.key])
                    K.mm(psum[6][:, 0:nq], onesb[:, :], ab[:, 0:nq], True, True, ["onesb", ab.key], ["ps#6"], True)
                    K.cp("act", r0[:, 0:nq], psum[6][:, 0:nq], ["ps#6"], [r0.key])
                    K.tt("dve", t0[:, 0:nq], t0[:, 0:nq], psum[7][:, 0:nq], ALU.mult, [t0.key, "ps#7"], [t0.key])
                    K.tt("dve", t1[:, 0:nq], t1[:, 0:nq], r0[:, 0:nq], ALU.mult, [t1.key, r0.key], [t1.key])
                    K.stt("dve", t0[:, 0:nq], t1[:, 0:nq], lamc[:, 0:1], t0[:, 0:nq], ALU.mult, ALU.add,
                          [t0.key, t1.key, "lamc"], [t0.key])
                    K.tt("dve", r0[:, 0:nq], r0[:, 0:nq], psum[7][:, 0:nq], ALU.mult, [r0.key, "ps#7"], [r0.key])
                    K.stt("dve", r1[:, 0:nq], r0[:, 0:nq], EPS, r0[:, 0:nq], ALU.mult, ALU.mult, [r0.key], [r1.key])
                    osq = ob_.next()
                    K.tt("pool", osq[:, 0:nq], t0[:, 0:nq], t0[:, 0:nq], ALU.mult, [t0.key], [osq.key])
                    K.mm(psum[6][:, 0:nq], onesb[:, :], osq[:, 0:nq], True, True, ["onesb", osq.key], ["ps#6"], True)
                    K.stt("dve", r1[:, 0:nq], psum[6][:, 0:nq], 1.0 / 128, r1[:, 0:nq], ALU.mult, ALU.add,
                          ["ps#6", r1.key], [r1.key])
                    K.act(r1[:, 0:nq], r1[:, 0:nq], AF.Ln, [r1.key], [r1.key])
                    K.act(r1[:, 0:nq], r1[:, 0:nq], AF.Exp, [r1.key], [r1.key], scale=-0.5)
                    K.tt("dve", t0[:, 0:nq], t0[:, 0:nq], r1[:, 0:nq], ALU.mult, [t0.key, r1.key], [t0.key])
                    mo = ob_.next()
                    K.stt("dve", mo[:, 0:nq], t0[:, 0:nq], subc[:, 0:1], gd[:, 0:nq], ALU.mult, ALU.mult,
                          [t0.key, "subc", gd.key], [mo.key])
                    K.dma("sp", mixT0[(4 + h) * 128:(5 + h) * 128, q0:q0 + nq], mo[:, 0:nq], "S" + mo.key,
                          reads=[mo.key], accw=["mixT0"])
            K.barrier()

    if "attn" not in P.dbg.get("skip", ()):
        phase_attn()
    if stop_after == "attn":
        K.barrier(); pes.close(); P.es.close(); return P

    def phase_out(layer, mixT, KC, Wd, res_src, dst, tiles):
        with ExitStack() as es:
            def alloc(name, shape, dt):
                P.uid += 1
                return es.enter_context(nc.sbuf_tensor(f"{name}_{P.uid}", list(shape), dt))
            Wb = alloc("Wo", [128, KC, D], BF16)
            wst = Ring(alloc, "wost", 2, [128, KC, 256], F32)
            mx_ = Ring(alloc, "mxin", 2, [128, KC, 512], BF16)
            rs_ = Ring(alloc, "resin", 3, [128, D], F32)
            tm_ = Ring(alloc, "otmp", 2, [128, D], F32)
            st_ = Ring(alloc, "ostat", 4, [128, 4], F32)
            junk = alloc("ojunk", [128, 512], BF16)
            for pc in range(4):
                wb = wst.next()
                K.dma("sp", wb[:], Wd[:, pc * 256:(pc + 1) * 256].rearrange("(j p) n -> p j n", p=128), "L" + wb.key,
                      writes=[wb.key])
                K.cp(["dve", "pool"][pc % 2], Wb[:, :, pc * 256:(pc + 1) * 256], wb[:], [wb.key], accw=["Wo"])
            gi = 0
            cur = None
            for (tok0, drow, tmod) in tiles:
                g0 = (tok0 // 512) * 512 if tok0 >= CTX else 0
                if tok0 >= CTX:
                    g0 = CTX + ((tok0 - CTX) // 512) * 512
                gn = CTX if tok0 < CTX else 512
                if cur is None or cur[0] != g0:
                    mx = mx_.next()
                    K.dma("sp", mx[:, :, 0:gn], mixT[:, g0:g0 + gn].rearrange("(j p) t -> p j t", p=128), "L" + mx.key,
                          writes=[mx.key])
                    cur = (g0, mx)
                mx = cur[1]; lo = tok0 - g0
                rs = rs_.next(); tm = tm_.next(); st = st_.next()
                K.dma("sp", rs[:, :], res_src[tok0:tok0 + 128, :], "L" + rs.key, writes=[rs.key])
                pks = [(2 * gi) % 6, (2 * gi + 1) % 6]; gi += 1
                for nb in range(2):
                    for j in range(KC):
                        K.mm(psum[pks[nb]][:, :], mx[:, j, lo:lo + 128], Wb[:, j, nb * 512:(nb + 1) * 512],
                             j == 0, j == KC - 1, [mx.key, "Wo"], [f"ps#{pks[nb]}"], j == KC - 1)
                for nb in range(2):
                    K.act(junk[:, :], psum[pks[nb]][:, :], AF.Square, [f"ps#{pks[nb]}"], ["ojunk", st.key] if nb == 0 else ["ojunk"],
                          accw=() if nb == 0 else [st.key], accum_out=st[:, nb:nb + 1])
                K.tt("dve", st[:, 2:3], st[:, 0:1], st[:, 1:2], ALU.add, [st.key], [st.key])
                K.act(st[:, 3:4], st[:, 2:3], AF.Sqrt, [st.key, "epsb"], [st.key], scale=1.0 / D, bias=epsb[:, 0:1])
                K.op("dve", lambda e, st=st: e.reciprocal(st[:, 2:3], st[:, 3:4]), [st.key], [st.key])
                for nb in range(2):
                    K.stt("dve", tm[:, nb * 512:(nb + 1) * 512], psum[pks[nb]][:, :], st[:, 2:3],
                          ggbc[:, layer, tmod, nb * 512:(nb + 1) * 512], ALU.mult, ALU.mult,
                          [f"ps#{pks[nb]}", st.key, "ggbc"], accw=[tm.key])
                K.tt("pool", tm[:, :], tm[:, :], rs[:, :], ALU.add, [tm.key, rs.key], [tm.key])
                K.dma("sp", dst[drow:drow + 128, :], tm[:, :], "S" + tm.key, reads=[tm.key], accw=[dst.name])
            K.barrier()

    nt0 = P.dbg.get("out0_tiles", NT)
    tiles0 = [(i * 128, i * 128, 1 if i < 2 else 0) for i in range(nt0)]
    phase_out(0, mixT0, 8, e_w_out, src0, h1, tiles0)
    if stop_after == "out0":
        K.barrier(); pes.close(); P.es.close(); return P


    def l1_alloc(alloc):
        c = {}
        c["sb"] = Ring(alloc, "sb1", 4, [128, 512], BF16)
        c["sf"] = Ring(alloc, "sf1", 2, [128, 64], F32)
        c["n"] = 0
        return c

    def epi_xbc(c, pks, bi, tok0, ntok):
        pk = pks[0]; b = c["sb"].next()
        c["n"] += 1
        K.cp("act" if c["n"] % 2 else "dve", b[:, 0:ntok], psum[pk][:, 0:ntok], [f"ps#{pk}"], [b.key])
        K.dma("sp", xbcT[bi * 128:(bi + 1) * 128, tok0:tok0 + ntok], b[:, 0:ntok], "S" + b.key,
              reads=[b.key], accw=["xbcT"])

    def epi_z(zc):
        def f(c, pk, tok0):
            b = c["sb"].next()
            K.act(b[:, :], psum[pk][:, :], AF.Silu, [f"ps#{pk}"], [b.key])
            K.dma("sp", zs[tok0:tok0 + 128, zc * 512:(zc + 1) * 512], b[:, :], "S" + b.key, reads=[b.key], accw=["zs"])
        return f

    def epi_dt(c, pk, tok0):
        b = c["sf"].next()
        K.cp("dve", b[:, :], psum[pk][:, 0:64], [f"ps#{pk}"], [b.key])
        K.dma("sp", dtraw[tok0:tok0 + 128, :], b[:, :], "S" + b.key, reads=[b.key], accw=["dtraw"])

    l1_f = [("xbc", [[2048 + f * 128] for f in range(24)], epi_xbc)]
    l1_t = [(zc * 512, 512, epi_z(zc)) for zc in range(4)] + [(5120, 64, epi_dt)]
    if "l1" not in P.dbg.get("skip", ()):
        phase_proj(1, h1, o_w_in, 5184, l1_f, l1_t, l1_alloc, None)
    if stop_after == "proj1":
        K.barrier(); pes.close(); P.es.close(); return P

    def phase_conv():
        with ExitStack() as es:
            def alloc(name, shape, dt):
                P.uid += 1
                return es.enter_context(nc.sbuf_tensor(f"{name}_{P.uid}", list(shape), dt))
            rows = alloc("cvrows", [120, 128], F32)
            cv = alloc("cv", [128, 5, 24], F32)
            dg = alloc("dg", [128, 4, 24, 128], BF16)
            xin_ = Ring(alloc, "cxin", 2, [128, 24, 516], BF16)
            sb_ = Ring(alloc, "csb", 3, [128, 512], BF16)
            rb_ = Ring(alloc, "crow", 8, [128, 2560], BF16)
            K.dma("sp", rows[:, :], ssd_cv.rearrange("v (f p) -> (v f) p", p=128), "Lcvrows", writes=["cvrows"])
            K.op("pe", lambda e: e.transpose(psum[0][:, 0:120], rows[:, :], idf[0:120, 0:120]),
                 reads=["cvrows", "idf"], writes=["ps#0"])
            K.cp("dve", cv[:].rearrange("p v f -> p (v f)"), psum[0][:, 0:120], ["ps#0"], ["cv"])
            for j in range(4):
                for f in range(24):
                    K.ts("dve" if (f % 2) else "pool", dg[:, j, f, :], idf[:, :], cv[:, j, f:f + 1], None, ALU.mult, None,
                         ["idf", "cv"], accw=["dg"])
            blocks = [(0, CTX, 0, CTX)] + [(CTX + 512 * g, 512, CTX, T) for g in range(SEQ // 512)]
            cpi = 0
            for (b0, bn, sa, sb_end) in blocks[:P.dbg.get("nconv", 99)]:
                xin = xin_.next()
                l0 = max(sa, b0 - 2); l1 = min(sb_end, b0 + bn + 1)
                K.dma("sp", xin[:, :, l0 - (b0 - 2):l1 - (b0 - 2)],
                      xbcT[:, l0:l1].rearrange("(f p) t -> p f t", p=128), "L" + xin.key, writes=[xin.key])
                nt_ = bn // 128
                rbs = [rb_.next() for _ in range(nt_)]
                for ft in range(24):
                    pk = 4 + (cpi % 2); cpi += 1
                    order = [2, 0, 1, 3]
                    for oi, j in enumerate(order):
                        sh = j - 2
                        lo = max(b0, sa - sh); hi = min(b0 + bn, sb_end - sh)
                        K.mm(psum[pk][:, lo - b0:hi - b0], dg[:, j, ft, :],
                             xin[:, ft, lo + sh - (b0 - 2):hi + sh - (b0 - 2)], oi == 0, oi == 3,
                             ["dg", xin.key], [f"ps#{pk}"], oi == 3)
                    sb = sb_.next()
                    K.act(sb[:, 0:bn], psum[pk][:, 0:bn], AF.Silu, [f"ps#{pk}", "cv"], [sb.key], bias=cv[:, 4, ft:ft + 1])
                    if ft >= 16:
                        K.dma("sp", bcT[(ft - 16) * 128:(ft - 15) * 128, b0:b0 + bn], sb[:, 0:bn], "S" + sb.key,
                              reads=[sb.key], accw=["bcT"])
                    if ft < 20:
                        for i in range(nt_):
                            tb = psum[i][:].bitcast(BF16)
                            K.op("pe", lambda e, tb=tb, sb=sb, i=i, ft=ft: e.transpose(
                                tb[:, (ft % 8) * 128:(ft % 8 + 1) * 128], sb[:, i * 128:(i + 1) * 128], idb[:]),
                                reads=[sb.key, "idb"], writes=[f"ps#{i}"], inc=True)
                        if ft % 8 == 7 or ft == 19:
                            ncol = (ft % 8 + 1) * 128
                            c0 = (ft // 8) * 1024
                            for i in range(nt_):
                                tb = psum[i][:].bitcast(BF16)
                                K.cp("dve" if i % 2 else "act", rbs[i][:, c0:c0 + ncol], tb[:, 0:ncol], [f"ps#{i}"],
                                     accw=[rbs[i].key])
                for i in range(nt_):
                    K.dma("sp", xsB[b0 + i * 128:b0 + (i + 1) * 128, :], rbs[i][:, :], "S" + rbs[i].key,
                          reads=[rbs[i].key], accw=["xsB"])
            K.barrier()

    if "conv" not in P.dbg.get("skip", ()):
        phase_conv()
    if stop_after == "conv":
        K.barrier(); pes.close(); P.es.close(); return P


    def phase_ssd():
        with ExitStack() as es:
            def alloc(name, shape, dt):
                P.uid += 1
                return es.enter_context(nc.sbuf_tensor(f"{name}_{P.uid}", list(shape), dt))
            vb = alloc("vb", [128, 160], F32)
            aneg = alloc("aneg", [128, 64], F32)
            nwbc = alloc("nwbc", [128, 2048], F32)
            mk = alloc("mk", [128, 2, 128], F32)
            self_ = alloc("self", [64, 1024], F32)
            selb = alloc("selb", [64, 32, 128], BF16)
            onesf = alloc("onesf2", [128, 128], F32)
            ones1 = alloc("ones1b", [128, 1], F32)
            state = alloc("state", [128, 2048], F32)
            stbf_ = Ring(alloc, "stbf", 2, [128, 2048], BF16)
            S_ = Ring(alloc, "Ssb", 2, [128, 2048], F32)
            ST = {}
            xb_ = Ring(alloc, "xb", 5, [128, 2560], BF16)
            bc_ = Ring(alloc, "bc", 5, [128, 8, 128], BF16)
            dr_ = Ring(alloc, "dr", 3, [128, 32], F32)
            sm_ = Ring(alloc, "sm", 5, [128, 8, 32], F32)
            cs4_ = Ring(alloc, "cs4", 3, [128, 64], F32)
            cst_ = Ring(alloc, "cst", 3, [64, 128], F32)
            hl_ = Ring(alloc, "hl", 3, [64, 3, 128], BF16)
            X_ = Ring(alloc, "Xd", 3, [128, 2048], BF16)
            Xc_ = Ring(alloc, "Xc", 3, [128, 2048], BF16)
            xd_ = Ring(alloc, "xdk", 1, [128, 2048], F32)
            cbm_ = Ring(alloc, "cbm", 2, [128, 4, 128], F32)
            E_ = Ring(alloc, "sE", 2, [128, 512], F32)
            MT_ = Ring(alloc, "sMT", 4, [128, 4, 128], BF16)
            to_ = Ring(alloc, "sto", 2, [128, 512], F32)
            ysb_ = Ring(alloc, "ysb", 2, [128, 2048], F32)
            yf_ = Ring(alloc, "yfl", 1, [128, 2048], F32)
            zt_ = Ring(alloc, "ztl", 1, [128, 2048], BF16)
            ynb_ = Ring(alloc, "ynb", 1, [128, 2048], BF16)
            yT_ = Ring(alloc, "yTs", 1, [128, 16, 128], BF16)
            junk = alloc("sjunk", [128, 512], BF16)
            K.op("dve", lambda e: e.memset(onesf[:], 1.0), writes=["onesf2"])
            K.op("dve", lambda e: e.memset(ones1[:], 1.0), writes=["ones1b"])
            negb = alloc("negb", [128, 2, 512], BF16)
            for dd in range(2):
                K.dma("sp", to_.bufs[dd][:, :], negm[dd, :, :], "Lnegm%d" % dd, writes=[to_.bufs[dd].key])
                K.cp("dve", negb[:, dd, :], to_.bufs[dd][:, :], [to_.bufs[dd].key], accw=["negb"])
            K.dma("sp", vb[:, :], ssd_vec[0, :].partition_broadcast(128), "Lvb", writes=["vb"])
            K.dma("sp", nwbc[:, :], ssd_norm.partition_broadcast(128), "Lnwbc", writes=["nwbc"])
            K.dma("sp", mk[:], maskT.rearrange("d s l -> s d l"), "Lmk", writes=["mk"])
            for q4 in range(4):
                K.dma("sp", self_[:, :], selc[:, q4 * 1024:(q4 + 1) * 1024], "Lself", writes=["self"])
                K.cp("dve", selb[:, q4 * 8:(q4 + 1) * 8, :].rearrange("k h l -> k (h l)"), self_[:, :], ["self"], accw=["selb"])
            K.act(aneg[:, :], vb[:, 0:64], AF.Exp, ["vb"], ["aneg"])
            K.ts("dve", aneg[:, :], aneg[:, :], -1.0, None, ALU.mult, None, ["aneg"], ["aneg"])
            lat_chunks = list(range(2, NT))
            ncl = P.dbg.get("nchunk", len(lat_chunks))
            lat_chunks = lat_chunks[:ncl]
            PB = {"yd": 0, "seg": 0, "so": 0}

            def stage_a(d, c):
                tok0 = c * 128
                lat = c >= 2
                xb = xb_.next(); bc = bc_.next(); dr = dr_.next(); sm = sm_.next()
                ctx_ = {"xb": xb, "bc": bc, "sm": sm, "lat": lat, "tok0": tok0}
                K.dma("sp", xb[:, :], xsB[tok0:tok0 + 128, :], "L" + xb.key, writes=[xb.key])
                K.dma("sp", bc[:], bcT[:, tok0:tok0 + 128].rearrange("(f p) t -> p f t", p=128), "L" + bc.key,
                      writes=[bc.key])
                K.dma("sp", dr[:, :], dtraw[tok0:tok0 + 128, d * 32:(d + 1) * 32], "L" + dr.key, writes=[dr.key])
                dt = sm[:, 0, :]; adt = sm[:, 1, :]; cs = sm[:, 2, :]; ecs = sm[:, 3, :]
                etot = sm[:, 4, :]; w2 = sm[:, 5, :]; tmp = sm[:, 6, :]
                k_ = sm.key
                K.tt("dve", tmp, dr[:, :], vb[:, 64 + d * 32:96 + d * 32], ALU.add, [dr.key, "vb"], [k_])
                K.act(tmp, tmp, AF.Exp, [k_], [k_])
                K.act(dt, tmp, AF.Ln, [k_, "ones1b"], [k_], bias=ones1[:, 0:1])
                K.tt("dve", adt, dt, aneg[:, d * 32:(d + 1) * 32], ALU.mult, [k_, "aneg"], [k_])
                return ctx_

            def stage_s2(d, ctx_):
                xb, sm, lat = ctx_["xb"], ctx_["sm"], ctx_["lat"]
                dt = sm[:, 0, :]; adt = sm[:, 1, :]; cs = sm[:, 2, :]; ecs = sm[:, 3, :]
                etot = sm[:, 4, :]; w2 = sm[:, 5, :]; tmp = sm[:, 6, :]
                k_ = sm.key
                K.mm(psum[4][:, 0:32], mk[:, d, :], adt, True, True, ["mk", k_], ["ps#4a"], True)
                K.mm(psum[4][:, 32:64], onesf[:, :], adt, True, True, ["onesf2", k_], ["ps#4a"], True)
                K.cp("dve", cs, psum[4][:, 0:32], ["ps#4a"], [k_])
                K.act(etot, psum[4][:, 32:64], AF.Exp, ["ps#4a"], [k_])
                K.tt("dve", tmp, psum[4][:, 32:64], cs, ALU.subtract, ["ps#4a", k_], [k_])
                K.act(w2, tmp, AF.Exp, [k_], [k_])
                K.tt("dve", w2, w2, dt, ALU.mult, [k_], [k_])
                xs3 = xb[:, 0:2048].rearrange("p (h q) -> p h q", q=64)
                Xc = Xc_.next()
                ctx_["Xc"] = Xc
                K.tt("pool", Xc[:, :].rearrange("p (h q) -> p h q", q=64), xs3,
                     w2.unsqueeze(2).to_broadcast([128, 32, 64]), ALU.mult, [xb.key, k_], [Xc.key])
                if not lat:
                    return ctx_
                K.act(ecs, cs, AF.Exp, [k_], [k_])
                X = X_.next()
                K.tt("pool", X[:, :].rearrange("p (h q) -> p h q", q=64), xs3,
                     dt.unsqueeze(2).to_broadcast([128, 32, 64]), ALU.mult, [xb.key, k_], [X.key])
                cs4 = cs4_.next(); cst = cst_.next(); hl = hl_.next()
                K.cp("dve", cs4[:, 0:32], cs, [k_], accw=[cs4.key])
                K.cp("dve", cs4[:, 32:64], cs, [k_], accw=[cs4.key])
                ctx_["X"] = X; ctx_["cs4"] = cs4; ctx_["cst"] = cst; ctx_["hl"] = hl
                return ctx_

            def stage_s3(d, ctx_):
                if not ctx_["lat"]:
                    return ctx_
                cs4, cst, hl, X = ctx_["cs4"], ctx_["cst"], ctx_["hl"], ctx_["X"]
                K.op("pe", lambda e, cs4=cs4: e.transpose(psum[4][0:64, 128:256], cs4[:, :], idf[:, :]),
                     reads=[cs4.key, "idf"], writes=["ps#4b"])
                K.cp("dve", cst[:, :], psum[4][0:64, 128:256], ["ps#4b"], [cst.key])
                K.cp("dve", hl[0:32, 0, :], cst[0:32, :], [cst.key], accw=[hl.key])
                K.cp("dve", hl[32:64, 2, :], cst[32:64, :], [cst.key], accw=[hl.key])
                K.tt("dve", hl[32:64, 0, :], cst[32:64, :], hl[32:64, 2, :], ALU.subtract, [cst.key, hl.key], accw=[hl.key])
                K.ts("dve", hl[:, 1, :], hl[:, 0, :], -1.0, None, ALU.mult, None, [hl.key], accw=[hl.key])
                ctx_["X"] = X; ctx_["hl"] = hl
                return ctx_

            def stage_a1(d, cx, bgen=None):
                if cx["lat"]:
                    stage_a1_lat(d, cx, bgen)
                xb, Xc = cx["xb"], cx["Xc"]
                Ssb = S_.next()
                cx["Ssb"] = Ssb
                for g in range(4):
                    pk = 6 + PB["so"] % 2; PB["so"] += 1
                    K.mm(psum[pk][:, :], xb[:, 2048 + g * 128:2048 + (g + 1) * 128], Xc[:, g * 512:(g + 1) * 512],
                         True, True, [xb.key, Xc.key], [f"ps#{pk}"], True)
                    K.cp("act", Ssb[:, g * 512:(g + 1) * 512], psum[pk][:, :], [f"ps#{pk}"], accw=[Ssb.key])

            def stage_a1_lat(d, cx, bgen=None):
                xb, bc, sm, tok0, X, hl = cx["xb"], cx["bc"], cx["sm"], cx["tok0"], cx["X"], cx["hl"]
                ctx_ = cx
                for g in range(4):
                    K.mm(psum[5][:, g * 128:(g + 1) * 128], bc[:, g, :], bc[:, 4 + g, :], True, True,
                         [bc.key], ["ps#5"], g == 3)
                cbm = cbm_.next()
                K.tt("dve", cbm[:], psum[5][:, :].rearrange("p (g l) -> p g l", l=128),
                     mk[:, d:d + 1, :].to_broadcast([128, 4, 128]), ALU.mult, ["ps#5", "mk"], [cbm.key])
                ysb = ysb_.next()
                ctx_["ysb"] = ysb
                if d == 1:
                    yf = yf_.next()
                    K.dma("sp", yf[:, :], yfw[tok0:tok0 + 128, :], "L" + yf.key, reads=["yfw"], writes=[yf.key])
                segbank = {}

                def emit_seg(hq):
                    pk = 2 + PB["seg"] % 2; PB["seg"] += 1
                    segbank[hq] = pk
                    K.mm(psum[pk][:, :], idb[:, :], negb[:, d, :], True, False, ["idb", "negb"], [f"ps#{pk}"], False)
                    for j in range(4):
                        h = hq * 4 + j
                        K.mm(psum[pk][:, j * 128:(j + 1) * 128], selb[:, h, :], hl[:, 0, :], False, False,
                             ["selb", hl.key], [f"ps#{pk}"], False)
                        K.mm(psum[pk][:, j * 128:(j + 1) * 128], hl[:, 1, :], selb[:, h, :], False, True,
                             ["selb", hl.key], [f"ps#{pk}"], j == 3)

                pyd = 0
                emit_seg(0)
                for hq in range(8):
                    g = hq // 2
                    if hq + 1 < 8:
                        emit_seg(hq + 1)
                    pk = segbank[hq]
                    if hq % 2 == 0:
                        pyd = PB["yd"] % 2; PB["yd"] += 1
                    E = E_.next(); MT = MT_.next()
                    K.act(E[:, :], psum[pk][:, :], AF.Exp, [f"ps#{pk}"], [E.key])
                    K.stt("dve", MT[:], E[:, :].rearrange("p (j l) -> p j l", l=128), 1e30,
                          cbm[:, g:g + 1, :].to_broadcast([128, 4, 128]), ALU.min, ALU.mult,
                          [E.key, cbm.key], [MT.key])
                    for j in range(4):
                        h = hq * 4 + j
                        K.mm(psum[pyd][:, (h % 8) * 64:(h % 8 + 1) * 64], MT[:, j, :], X[:, h * 64:(h + 1) * 64],
                             True, True, [MT.key, X.key], [f"ps#{pyd}"], (h % 8 == 7))
                    if hq % 2 == 1:
                        if d == 0:
                            K.cp("act", ysb[:, g * 512:(g + 1) * 512], psum[pyd][:, :], [f"ps#{pyd}"], accw=[ysb.key])
                        else:
                            K.tt("dve", ysb[:, g * 512:(g + 1) * 512], psum[pyd][:, :], yf[:, g * 512:(g + 1) * 512],
                                 ALU.add, [f"ps#{pyd}", yf.key], accw=[ysb.key])
                    if bgen is not None:
                        next(bgen, None)
                return ctx_

            def stage_b(d, cx):
                xb, bc, sm, lat, tok0, Xc = cx["xb"], cx["bc"], cx["sm"], cx["lat"], cx["tok0"], cx["Xc"]
                k_ = sm.key
                ecs = sm[:, 3, :]; etot = sm[:, 4, :]
                xs3 = xb[:, 0:2048].rearrange("p (h q) -> p h q", q=64)
                Ssb = cx["Ssb"]
                stprev = ST["cur"]
                stnew = stbf_.next()
                ST["cur"] = stnew
                K.tt("dve", state[:, :].rearrange("p (h q) -> p h q", q=64), state[:, :].rearrange("p (h q) -> p h q", q=64),
                     etot.unsqueeze(2).to_broadcast([128, 32, 64]), ALU.mult, ["state", k_], ["state"])
                yield
                K.tt("dve", state[:, :], state[:, :], Ssb[:, :], ALU.add, ["state", Ssb.key], ["state"])
                K.cp("act", stnew[:, :], state[:, :], ["state"], [stnew.key])
                yield
                if lat:
                    ysb = cx["ysb"]
                    for g in range(4):
                        pk = 6 + PB["so"] % 2; PB["so"] += 1
                        K.mm(psum[pk][:, :], bc[:, 4 + g, :], stprev[:, g * 512:(g + 1) * 512], True, True,
                             [bc.key, stprev.key], [f"ps#{pk}"], True)
                        to = to_.next()
                        K.tt("dve", to[:, :].rearrange("p (h q) -> p h q", q=64),
                             psum[pk][:, :].rearrange("p (h q) -> p h q", q=64),
                             ecs[:, g * 8:(g + 1) * 8].unsqueeze(2).to_broadcast([128, 8, 64]), ALU.mult,
                             [f"ps#{pk}", k_], [to.key])
                        K.tt("dve", ysb[:, g * 512:(g + 1) * 512], ysb[:, g * 512:(g + 1) * 512], to[:, :], ALU.add,
                             [ysb.key, to.key], [ysb.key])
                        yield
                if not lat:
                    return
                if d == 0:
                    K.dma("sp", yfw[tok0:tok0 + 128, :], ysb[:, :], "S" + ysb.key, reads=[ysb.key], accw=["yfw"])
                    return
                zt = zt_.next(); ynb = ynb_.next(); yT = yT_.next(); xd = xd_.next()
                K.dma("sp", zt[:, :], zs[tok0:tok0 + 128, :], "L" + zt.key, writes=[zt.key])
                K.tt("pool", xd[:, :].rearrange("p (h q) -> p h q", q=64), xs3,
                     vb[:, 128:160].unsqueeze(2).to_broadcast([128, 32, 64]), ALU.mult, [xb.key, "vb"], [xd.key])
                K.tt("dve", ysb[:, :], ysb[:, :], xd[:, :], ALU.add, [ysb.key, xd.key], [ysb.key])
                K.tt("dve", ysb[:, :], ysb[:, :], zt[:, :], ALU.mult, [ysb.key, zt.key], [ysb.key])
                yield
                for g in range(4):
                    K.act(junk[:, :], ysb[:, g * 512:(g + 1) * 512], AF.Square, [ysb.key], ["sjunk"], accw=[k_],
                          accum_out=sm[:, 7, g:g + 1])
                K.act(sm[:, 7, 4:8], sm[:, 7, 0:4], AF.Sqrt, [k_, "epsb"], [k_], scale=1.0 / 512, bias=epsb[:, 0:1])
                K.op("dve", lambda e, sm=sm: e.reciprocal(sm[:, 7, 8:12], sm[:, 7, 4:8]), [k_], [k_])
                for g in range(4):
                    K.stt("dve", ynb[:, g * 512:(g + 1) * 512], ysb[:, g * 512:(g + 1) * 512], sm[:, 7, 8 + g:9 + g],
                          nwbc[:, g * 512:(g + 1) * 512], ALU.mult, ALU.mult, [ysb.key, k_, "nwbc"], accw=[ynb.key])
                yield
                for half in range(2):
                    pk = half
                    tb = psum[pk][:].bitcast(BF16)
                    for jj in range(8):
                        j = half * 8 + jj
                        K.op("pe", lambda e, tb=tb, jj=jj, j=j, ynb=ynb: e.transpose(
                            tb[:, jj * 128:(jj + 1) * 128], ynb[:, j * 128:(j + 1) * 128], idb[:]),
                            reads=[ynb.key, "idb"], writes=[f"ps#{pk}"], inc=(jj == 7))
                    K.cp("act", yT[:, half * 8:(half + 1) * 8, :].rearrange("p j t -> p (j t)"), tb[:, :], [f"ps#{pk}"],
                         accw=[yT.key])
                K.dma("sp", mixT1[:, tok0:tok0 + 128].rearrange("(j p) t -> p j t", p=128), yT[:], "S" + yT.key,
                      reads=[yT.key], accw=["mixT1"])

            for d in range(2):
                order = [0, 1] + lat_chunks if d == 0 else [1, 0] + lat_chunks[::-1]
                K.op("pool", lambda e: e.memset(state[:], 0.0), writes=["state"])
                st0 = stbf_.next()
                ST["cur"] = st0
                K.op("pool", lambda e, st0=st0: e.memset(st0[:], 0.0), writes=[st0.key])
                n_ = len(order)
                cxs = {}
                def run(stage, idx):
                    if 0 <= idx < n_:
                        if stage == 1:
                            cxs[idx] = stage_a(d, order[idx])
                        elif stage == 2:
                            stage_s2(d, cxs[idx])
                        elif stage == 3:
                            stage_s3(d, cxs[idx])
                        elif stage == 4:
                            stage_a1(d, cxs[idx])
                        else:
                            stage_b(d, cxs.pop(idx))
                for it in range(-3, n_):
                    run(3, it + 1)
                    bgen = stage_b(d, cxs.pop(it)) if 0 <= it < n_ else None
                    if 0 <= it + 1 < n_:
                        stage_a1(d, cxs[it + 1], bgen)
                    if bgen is not None:
                        for _ in bgen:
                            pass
                    run(1, it + 3); run(2, it + 2)
            K.barrier()

    if "ssd" not in P.dbg.get("skip", ()):
        sq_prev = K.store_q
        K.store_q = os.environ.get("KSSDQ", sq_prev) or None
        phase_ssd()
        K.store_q = sq_prev
    if stop_after == "ssd":
        K.barrier(); pes.close(); P.es.close(); return P

    nt1 = P.dbg.get("out1_tiles", SEQ // 128)
    tiles1 = [(CTX + i * 128, i * 128, 0) for i in range(nt1)]
    phase_out(1, mixT1, 16, o_w_out, h1, out_h, tiles1)

    K.barrier()
    pes.close()
    P.es.close()
    return P


def _rope_tables():
    n_freq = 16
    inv = (10000.0 ** (-np.arange(n_freq, dtype=np.float32) / np.float32(n_freq))).astype(np.float32)
    t = np.arange(SEQ)
    row = (t // 64).astype(np.float32)
    col = (t % 64).astype(np.float32)
    ang = np.concatenate([row[:, None] * inv, col[:, None] * inv], axis=-1).astype(np.float32)
    cos, sin = np.cos(ang).astype(np.float32), np.sin(ang).astype(np.float32)
    tab = np.zeros((2, 128, T), np.float32)
    tab[0, :, :CTX] = 1.0
    for p in range(128):
        d = p % 64
        fi = (d % 16) + 16 * (d // 32)
        sgn = -1.0 if (d % 32) < 16 else 1.0
        tab[0, p, CTX:] = cos[:, fi]
        tab[1, p, CTX:] = sgn * sin[:, fi]
    return tab


def _rope_perm():
    perm = np.zeros(512, np.int64)
    for f in range(512):
        d = f % 64
        e = d % 32
        e2 = e + 16 if e < 16 else e - 16
        perm[f] = f - d + (d // 32) * 32 + e2
    return perm


def make_in_maps(inp):
    B = inp["x"].shape[0]
    perm = _rope_perm()
    w = np.asarray(inp["e_w_in"][0], np.float32)
    w_aug = np.ascontiguousarray(np.concatenate([w, w[:, 1024 + perm], w[:, 1536 + perm]], axis=1))
    tab = _rope_tables()
    ident = np.eye(128, dtype=np.float32)
    wbd = np.zeros((2, 2, 4, 128, 128), np.float32)
    for d in range(2):
        for g, nm in enumerate(("lru_w_r", "lru_w_i")):
            wsrc = np.asarray(inp[nm][0][d], np.float32)
            for cc in range(4):
                wbd[d, g, cc, 0:64, 0:64] = wsrc[2 * cc]
                wbd[d, g, cc, 64:128, 64:128] = wsrc[2 * cc + 1]
    wbd = np.ascontiguousarray(wbd.reshape(16, 128, 128))
    lvec = np.ascontiguousarray(np.concatenate([
        np.asarray(inp["lru_conv_w"][0], np.float32), np.asarray(inp["lru_conv_b"], np.float32).reshape(1, 512),
        np.asarray(inp["lru_b_r"][0], np.float32), np.asarray(inp["lru_b_i"][0], np.float32),
        np.asarray(inp["lru_lambda"][0], np.float32)], axis=0))
    ssd_cv = np.ascontiguousarray(np.concatenate([np.asarray(inp["ssd_conv_w"][0], np.float32),
                                                  np.asarray(inp["ssd_conv_b"], np.float32).reshape(1, 3072)], 0))
    ssd_vec = np.ascontiguousarray(np.concatenate([np.asarray(inp["ssd_a_log"][0], np.float32).reshape(-1),
                                                   np.asarray(inp["ssd_dt_bias"][0], np.float32).reshape(-1),
                                                   np.asarray(inp["ssd_d"][0], np.float32).reshape(-1)]).reshape(1, 160))
    ii = np.arange(128)
    maskT = np.stack([(ii[None, :] >= ii[:, None]), (ii[None, :] <= ii[:, None])], 0).astype(np.float32)
    negm = np.ascontiguousarray(np.tile((maskT - 1.0) * 30000.0, (1, 1, 4)).astype(np.float32))
    selc = np.zeros((64, 32, 128), np.float32)
    for hh in range(32):
        selc[hh, hh, :] = 1.0
        selc[32 + hh, hh, :] = 1.0
    selc = np.ascontiguousarray(selc.reshape(64, 32 * 128))
    maps = []
    for b in range(B):
        m = {
            "src0": np.ascontiguousarray(np.concatenate([inp["ctx"][b], inp["x"][b]], axis=0), dtype=np.float32),
            "cvec": np.ascontiguousarray(np.stack([inp["c"][b], inp["c_ctx"]], 0), dtype=np.float32),
            "w_mod": np.asarray(inp["w_mod"], np.float32),
            "b_mod": np.asarray(inp["b_mod"], np.float32),
            "g_pre": np.asarray(inp["g_pre"], np.float32),
            "g_post": np.asarray(inp["g_post"], np.float32),
            "e_w_in_aug": w_aug,
            "e_w_out": np.asarray(inp["e_w_out"][0], np.float32),
            "ident": ident,
            "ropetab": tab,
            "lru_wbd": wbd,
            "lru_vec": lvec,
            "da_lam": np.ascontiguousarray(np.asarray(inp["da_lambda"][0], np.float32).reshape(1, 256)),
            "da_sub": np.ascontiguousarray(np.asarray(inp["da_subln"][0], np.float32).reshape(1, 128)),
            "o_w_in": np.asarray(inp["o_w_in"][0], np.float32),
            "o_w_out": np.asarray(inp["o_w_out"][0], np.float32),
            "ssd_cv": ssd_cv,
            "ssd_vec": ssd_vec,
            "ssd_norm": np.asarray(inp["ssd_norm"][0], np.float32),
            "maskT": maskT,
            "selc": selc,
            "negm": negm,
        }
        maps.append(m)
    return maps


def kernel(**inp):
    P = build_program()
    maps = make_in_maps(inp)
    res = run_bass_kernel_spmd(P.nc, maps, core_ids=list(range(8)))
    return np.stack([np.asarray(r["out"], np.float32) for r in res.results], 0)
```

```python
import os
from contextlib import ExitStack
import numpy as np
import concourse.bass as bass
import concourse.mybir as mybir
from concourse.bass_utils import run_bass_kernel_spmd

F32, BF16 = mybir.dt.float32, mybir.dt.bfloat16
AF = mybir.ActivationFunctionType
ALU = mybir.AluOpType
AX = mybir.AxisListType

D = 1024
SEQ = 4096
CTX = 256
T = SEQ + CTX
NT = T // 128
EPS = 1e-6


class Sched:
    ENG = ("pe", "dve", "act", "pool", "sp")

    def __init__(self, nc, es):
        self.nc, self.es = nc, es
        self.e = {"pe": nc.tensor, "dve": nc.vector, "act": nc.scalar, "pool": nc.gpsimd, "sp": nc.sync}
        self.sem, self.cnt = {}, {}
        for n in self.ENG:
            self.sem[n] = es.enter_context(nc.semaphore("s_" + n))
            self.cnt[n] = 0
        self.seen = {n: {} for n in self.ENG}
        self.W, self.Rd = {}, {}
        self.pend = {n: [] for n in self.ENG}
        self.nwait = 0
        self.nins = 0
        self.store_q = os.environ.get('KSTOREQ', 'pool') or None

    def _wait(self, eng, need):
        for s, v in need.items():
            if s == "pe" and eng == "pe":
                continue
            if self.seen[eng].get(s, 0) >= v:
                continue
            self.e[eng].wait_ge(self.sem[s], v)
            self.seen[eng][s] = v
            self.nwait += 1

    def _deps(self, eng, reads, writes, accw):
        need = {}

        def add(d):
            for s, v in d.items():
                if need.get(s, 0) < v:
                    need[s] = v
        for r in reads:
            add(self.W.get(r, {}))
        for w in writes:
            add(self.W.get(w, {}))
            add(self.Rd.get(w, {}))
        for w in accw:
            add(self.Rd.get(w, {}))
        self._wait(eng, need)

    def _register(self, ev, reads, writes, accw):
        s, v = ev
        for r in reads:
            d = self.Rd.setdefault(r, {})
            d[s] = max(d.get(s, 0), v)
        for w in writes:
            self.W[w] = {s: v}
            self.Rd[w] = {}
        for w in accw:
            d = self.W.setdefault(w, {})
            d[s] = max(d.get(s, 0), v)

    def op(self, eng, fn, reads=(), writes=(), accw=(), inc=True):
        self._deps(eng, reads, writes, accw)
        ins = fn(self.e[eng])
        self.nins += 1
        if inc:
            self.cnt[eng] += 1
            ins.then_inc(self.sem[eng], 1)
            ev = (eng, self.cnt[eng])
            for (r, w, a) in self.pend[eng]:
                self._register(ev, r, w, a)
            self.pend[eng] = []
            self._register(ev, reads, writes, accw)
        else:
            self.pend[eng].append((tuple(reads), tuple(writes), tuple(accw)))

    def dma(self, q, out, in_, semkey, reads=(), writes=(), accw=(), **kw):
        if self.store_q and semkey.startswith("S"):
            q = self.store_q
        if semkey not in self.sem:
            self.sem[semkey] = self.es.enter_context(self.nc.semaphore("d_" + semkey.replace("#", "_")))
            self.cnt[semkey] = 0
        self._deps(q, reads, writes, accw)
        ins = self.e[q].dma_start(out=out, in_=in_, **kw)
        ins.then_inc(self.sem[semkey], 16)
        self.cnt[semkey] += 16
        self.nins += 1
        self._register((semkey, self.cnt[semkey]), reads, writes, accw)

    def tt(self, eng, out, a, b, op, reads, writes=(), accw=()):
        self.op(eng, lambda e: e.tensor_tensor(out, a, b, op), reads, writes, accw)

    def ts(self, eng, out, a, s1, s2, op0, op1=None, reads=(), writes=(), accw=()):
        if op1 is None:
            self.op(eng, lambda e: e.tensor_scalar(out, a, s1, None, op0), reads, writes, accw)
        else:
            self.op(eng, lambda e: e.tensor_scalar(out, a, s1, s2, op0, op1), reads, writes, accw)

    def stt(self, eng, out, a, sc, b, op0, op1, reads, writes=(), accw=()):
        self.op(eng, lambda e: e.scalar_tensor_tensor(out, a, sc, b, op0, op1), reads, writes, accw)

    def act(self, out, in_, func, reads, writes=(), accw=(), **kw):
        self.op("act", lambda e: e.activation(out=out, in_=in_, func=func, **kw), reads, writes, accw)

    def cp(self, eng, out, in_, reads, writes=(), accw=()):
        if eng == "act":
            self.op("act", lambda e: e.copy(out, in_), reads, writes, accw)
        else:
            self.op(eng, lambda e: e.tensor_copy(out, in_), reads, writes, accw)

    def mm(self, out, lhsT, rhs, start, stop, reads, writes, inc):
        self.op("pe", lambda e: e.matmul(out, lhsT, rhs, start=start, stop=stop), reads, writes, inc=inc)

    def barrier(self):
        for n in self.ENG:
            assert not self.pend[n]
        allev = {s: c for s, c in self.cnt.items() if c > 0}
        for n in self.ENG:
            self._wait(n, allev)
        self.W, self.Rd = {}, {}


class Buf:
    def __init__(self, t, key):
        self.t, self.key = t, key

    def __getitem__(self, k):
        return self.t[k]


class Ring:
    def __init__(self, alloc, name, n, shape, dtype):
        self.bufs = [Buf(alloc(f"{name}{i}", shape, dtype), f"{name}#{i}") for i in range(n)]
        self.i = 0

    def next(self):
        b = self.bufs[self.i % len(self.bufs)]
        self.i += 1
        return b


class Prog:
    def __init__(self, dbg=None):
        self.dbg = dbg or {}
        self.nc = nc = bass.Bass("TRN2", target_bir_lowering=False)
        self.es = ExitStack()
        self.K = Sched(nc, self.es)
        self.dram = {}
        self.uid = 0

    def din(self, name, shape, dt=F32):
        self.dram[name] = self.nc.dram_tensor(name, list(shape), dt, kind="ExternalInput").ap()
        return self.dram[name]

    def dout(self, name, shape, dt=F32):
        self.dram[name] = self.nc.dram_tensor(name, list(shape), dt, kind="ExternalOutput").ap()
        return self.dram[name]

    def dscr(self, name, shape, dt):
        if name in self.dbg.get("dump", ()):
            return self.dout(name, shape, dt)
        self.dram[name] = self.nc.dram_tensor(name, list(shape), dt).ap()
        return self.dram[name]


def build_program(dbg=None):
    P = Prog(dbg)
    nc, K = P.nc, P.K
    stop_after = P.dbg.get("stop_after", "all")

    src0 = P.din("src0", [T, D])
    cvec = P.din("cvec", [2, D])
    w_mod = P.din("w_mod", [2, D, 3 * D])
    b_mod = P.din("b_mod", [2, 3 * D])
    g_pre = P.din("g_pre", [2, D])
    g_post = P.din("g_post", [2, D])
    e_w_in = P.din("e_w_in_aug", [D, 4096])
    e_w_out = P.din("e_w_out", [D, D])
    ident = P.din("ident", [128, 128])
    ropetab = P.din("ropetab", [2, 128, T])
    out_h = P.dout("out", [SEQ, D])
    lru_wbd = P.din("lru_wbd", [16, 128, 128])
    lru_vec = P.din("lru_vec", [11, 512])
    da_lam = P.din("da_lam", [1, 256])
    da_sub = P.din("da_sub", [1, 128])
    mixT0 = P.dscr("mixT0", [D, T], BF16)
    o_w_in = P.din("o_w_in", [D, 5184])
    o_w_out = P.din("o_w_out", [2048, D])
    ssd_cv = P.din("ssd_cv", [5, 3072])
    ssd_vec = P.din("ssd_vec", [1, 160])
    ssd_norm = P.din("ssd_norm", [2048])
    maskT = P.din("maskT", [2, 128, 128])
    selc = P.din("selc", [64, 32 * 128])
    negm = P.din("negm", [2, 128, 512])
    xbcT = P.dscr("xbcT", [3072, T], BF16)
    zs = P.dscr("zs", [T, 2048], BF16)
    dtraw = P.dscr("dtraw", [T, 64], F32)
    xsB = P.dscr("xsB", [T, 2560], BF16)
    bcT = P.dscr("bcT", [1024, T], BF16)
    yfw = P.dscr("yfw", [T, 2048], F32)
    mixT1 = P.dscr("mixT1", [2048, T], BF16)
    h1 = P.dscr("h1", [T, D], F32)

    xrT = P.dscr("xrT", [512, T], F32)
    grT = P.dscr("grT", [512, T], BF16)
    gdT = P.dscr("gdT", [512, T], BF16)
    qT = P.dscr("qT", [512, T], BF16)
    kT = P.dscr("kT", [512, T], BF16)
    vtok = P.dscr("vtok", [T, 512], BF16)

    pes = ExitStack()
    def palloc(name, shape, dt):
        return pes.enter_context(nc.sbuf_tensor(name, list(shape), dt))
    psum = [pes.enter_context(nc.psum_tensor(f"psb{i}", [128, 512], F32)) for i in range(8)]
    idf = palloc("idf", [128, 128], F32)
    idb = palloc("idb", [128, 128], BF16)
    epsb = palloc("epsb", [128, 1], F32)
    modA = palloc("modA", [128, 2, 8, 2], F32)
    modS = palloc("modS", [128, 2, 8, 2], F32)
    ggbc = palloc("ggbc", [128, 2, 2, D], F32)

    K.dma("sp", idf[:], ident[:, :], "Lidf", writes=["idf"])
    K.op("dve", lambda e: e.tensor_copy(idb[:], idf[:]), reads=["idf"], writes=["idb"])
    K.op("dve", lambda e: e.memset(epsb[:], EPS), writes=["epsb"])

    def phase_mod():
        with ExitStack() as es:
            def alloc(name, shape, dt):
                P.uid += 1
                return es.enter_context(nc.sbuf_tensor(f"{name}_{P.uid}", list(shape), dt))
            cT = alloc("cT", [128, 2, 8], F32)
            sig = alloc("sig", [128, 2, 8], F32)
            srep = alloc("srep", [128, 2, 8, 128], F32)
            bT = alloc("bT", [128, 2, 24], F32)
            gpT = alloc("gpT", [128, 2, 8], F32)
            bgbc = alloc("bgbc", [128, 2, D], F32)
            gpbc = alloc("gpbc", [128, 2, D], F32)
            wst = Ring(alloc, "wst", 2, [128, 8, 512], F32)
            tmp = alloc("mtmp", [128, 16, 2], F32)

            rows = alloc("rows", [80, 128], F32)
            K.dma("sp", rows[0:16, :], cvec.rearrange("t (j p) -> (t j) p", p=128), "Lrows", accw=["rows"])
            K.dma("sp", rows[16:64, :], b_mod.rearrange("l (f p) -> (l f) p", p=128), "Lrows", accw=["rows"])
            K.dma("sp", rows[64:80, :], g_pre.rearrange("l (j p) -> (l j) p", p=128), "Lrows", accw=["rows"])
            K.op("pe", lambda e: e.transpose(psum[4][:, 0:80], rows[:, :], idf[0:80, 0:80]),
                 reads=["rows", "idf"], writes=["ps#4"])
            K.op("dve", lambda e: e.tensor_copy(cT[:].rearrange("p t j -> p (t j)"), psum[4][:, 0:16]),
                 reads=["ps#4"], writes=["cT"])
            K.op("dve", lambda e: e.tensor_copy(bT[:].rearrange("p l f -> p (l f)"), psum[4][:, 16:64]),
                 reads=["ps#4"], writes=["bT"])
            K.op("dve", lambda e: e.tensor_copy(gpT[:].rearrange("p l j -> p (l j)"), psum[4][:, 64:80]),
                 reads=["ps#4"], writes=["gpT"])
            for l in range(2):
                K.dma("sp", bgbc[:, l, :], b_mod[l, 2 * D:3 * D].partition_broadcast(128), "Lbgbc", accw=["bgbc"])
                K.dma("sp", gpbc[:, l, :], g_post[l, :].partition_broadcast(128), "Lgpbc", accw=["gpbc"])
            K.op("act", lambda e: e.activation(out=sig[:], in_=cT[:], func=AF.Sigmoid), reads=["cT"], writes=["sig"])
            K.op("dve", lambda e: e.tensor_tensor(cT[:], cT[:], sig[:], ALU.mult), reads=["sig", "cT"], writes=["cT"])
            for t in range(2):
                K.op("dve", lambda e, t=t: e.tensor_copy(
                    srep[:, t, :, :], cT[:, t, :].unsqueeze(2).to_broadcast([128, 8, 128])),
                    reads=["cT"], accw=["srep"])
            for l in range(2):
                for pc in range(6):
                    wb = wst.next()
                    K.dma("sp", wb[:], w_mod[l, :, pc * 512:(pc + 1) * 512].rearrange("(j p) n -> p j n", p=128),
                          "L" + wb.key, writes=[wb.key])
                    if pc < 4:
                        pst = psum[pc % 2]
                        for f in range(4):
                            for j in range(8):
                                K.op("pe", lambda e, f=f, j=j, pst=pst, wb=wb: e.matmul(
                                    pst[:, 2 * f:2 * f + 2], wb[:, j, f * 128:(f + 1) * 128], cT[:, :, j],
                                    start=(j == 0), stop=(j == 7)),
                                    reads=[wb.key, "cT"], writes=[f"ps#{pc % 2}"], inc=(j == 7 and f == 3))
                        K.op("dve", lambda e, pst=pst, pc=pc: e.tensor_copy(
                            tmp[:, pc * 4:(pc + 1) * 4, :], pst[:, 0:8].rearrange("p (f t) -> p f t", t=2)),
                            reads=[f"ps#{pc % 2}"], accw=["mtmp"])
                    else:
                        for t in range(2):
                            pst = psum[2 + t]
                            for j in range(8):
                                K.op("pe", lambda e, j=j, t=t, pst=pst, wb=wb: e.matmul(
                                    pst[:, :], srep[:, t, j, :], wb[:, j, :], start=(j == 0), stop=(j == 7)),
                                    reads=[wb.key, "srep"], writes=[f"ps#{2 + t}"], inc=(j == 7))
                            c0 = (pc - 4) * 512
                            K.op("dve", lambda e, t=t, l=l, c0=c0, pst=pst: e.tensor_tensor(
                                ggbc[:, l, t, c0:c0 + 512], pst[:, :], bgbc[:, l, c0:c0 + 512], ALU.add),
                                reads=[f"ps#{2 + t}", "bgbc"], accw=["ggbc"])
                for t in range(2):
                    K.op("dve", lambda e, t=t, l=l: e.tensor_tensor(
                        modS[:, l, :, t], tmp[:, 0:8, t], bT[:, l, 0:8], ALU.add),
                        reads=["mtmp", "bT"], accw=["modS"])
                    K.op("dve", lambda e, t=t, l=l: e.scalar_tensor_tensor(
                        modA[:, l, :, t], tmp[:, 8:16, t], 1.0, bT[:, l, 8:16], ALU.add, ALU.add),
                        reads=["mtmp", "bT"], accw=["modA"])
                    K.op("dve", lambda e, t=t, l=l: e.tensor_tensor(
                        modA[:, l, :, t], modA[:, l, :, t], gpT[:, l, :], ALU.mult),
                        reads=["modA", "gpT"], writes=["modA"])
                    K.op("dve", lambda e, t=t, l=l: e.tensor_tensor(
                        ggbc[:, l, t, :], ggbc[:, l, t, :], gpbc[:, l, :], ALU.mult),
                        reads=["ggbc", "gpbc"], writes=["ggbc"])
            K.barrier()

    phase_mod()
    if "mod" in P.dbg.get("dump", ()):
        dA = P.dout("dbg_modA", [128, 32]); dS = P.dout("dbg_modS", [128, 32]); dG = P.dout("dbg_gg", [128, 4 * D])
        K.dma("sp", dA[:, :], modA[:].rearrange("p l j t -> p (l j t)"), "Sdbg", reads=["modA"])
        K.dma("sp", dS[:, :], modS[:].rearrange("p l j t -> p (l j t)"), "Sdbg", reads=["modS"])
        K.dma("sp", dG[:, :], ggbc[:].rearrange("p l t f -> p (l t f)"), "Sdbg", reads=["ggbc"])
    if stop_after == "mod":
        K.barrier(); pes.close(); P.es.close(); return P

    def phase_proj(layer, src, Wd, ncols, fspecs, tspecs, extra_alloc=None, per_group=None):
        sq_prev = K.store_q
        K.store_q = os.environ.get("KPROJQ", "act")
        _phase_proj(layer, src, Wd, ncols, fspecs, tspecs, extra_alloc, per_group)
        K.store_q = sq_prev

    def _phase_proj(layer, src, Wd, ncols, fspecs, tspecs, extra_alloc=None, per_group=None):
        with ExitStack() as es:
            def alloc(name, shape, dt):
                P.uid += 1
                return es.enter_context(nc.sbuf_tensor(f"{name}_{P.uid}", list(shape), dt))
            Wb = alloc("Wb", [128, 8, ncols], BF16)
            wst = Ring(alloc, "wst", 2, [128, 8, 256], F32)
            xr_ = Ring(alloc, "xin", 4, [128, D], F32)
            xn_ = Ring(alloc, "xn", 4, [128, D], BF16)
            uT_ = Ring(alloc, "uT", 2, [128, 8, 512], BF16)
            junk = alloc("junk", [128, D], BF16)
            stat = Ring(alloc, "stat", 4, [128, 4], F32)
            ctxo = extra_alloc(alloc) if extra_alloc else None
            ceng = ["dve", "pool", "act"]
            for pc in range(ncols // 256 + (1 if ncols % 256 else 0)):
                c0 = pc * 256
                cw = min(256, ncols - c0)
                wb = wst.next()
                K.dma("sp", wb[:, :, 0:cw], Wd[:, c0:c0 + cw].rearrange("(j p) n -> p j n", p=128),
                      "L" + wb.key, writes=[wb.key])
                en = ceng[pc % 3]
                if en == "act":
                    K.op("act", lambda e, wb=wb, c0=c0, cw=cw: e.copy(Wb[:, :, c0:c0 + cw], wb[:, :, 0:cw]),
                         reads=[wb.key], accw=["Wb"])
                else:
                    K.op(en, lambda e, wb=wb, c0=c0, cw=cw: e.tensor_copy(Wb[:, :, c0:c0 + cw], wb[:, :, 0:cw]),
                         reads=[wb.key], accw=["Wb"])
            groups = [(0, CTX, 1)] + [(CTX + 512 * g, 512, 0) for g in range(SEQ // 512)]
            pst_i = [0]
            pso_i = [0]

            def front_parts(gi):
                tok0, ntok, tmod = groups[gi]
                uT = uT_.next()
                p1s, p2s = [], []
                for ti in range(ntok // 128):
                    def part1(ti=ti):
                        box = {}
                        xt = xr_.next(); xn = xn_.next(); st = stat.next()
                        box["xn"] = xn
                        K.dma("sp", xt[:], src[tok0 + ti * 128: tok0 + (ti + 1) * 128, :], "L" + xt.key, writes=[xt.key])
                        K.op("act", lambda e: e.activation(out=junk[:], in_=xt[:], func=AF.Square, accum_out=st[:, 0:1]),
                             reads=[xt.key], writes=["junk", st.key])
                        K.op("act", lambda e: e.activation(out=st[:, 1:2], in_=st[:, 0:1], func=AF.Sqrt,
                                                           scale=1.0 / D, bias=epsb[:, 0:1]),
                             reads=[st.key, "epsb"], writes=[st.key])
                        K.op("dve", lambda e: e.reciprocal(st[:, 2:3], st[:, 1:2]), reads=[st.key], writes=[st.key])
                        K.op("dve", lambda e: e.tensor_scalar(xn[:], xt[:], st[:, 2:3], None, ALU.mult),
                             reads=[xt.key, st.key], writes=[xn.key])
                        return box

                    def part2(box, ti=ti):
                        xn = box["xn"]
                        pk = 6 + (pst_i[0] % 2); pst_i[0] += 1
                        pst = psum[pk][:].bitcast(BF16)
                        for j in range(8):
                            K.op("pe", lambda e, j=j: e.transpose(
                                pst[:, j * 128:(j + 1) * 128], xn[:, j * 128:(j + 1) * 128], idb[:]),
                                reads=[xn.key, "idb"], writes=[f"ps#{pk}"], inc=(j == 7))
                        for j in range(8):
                            K.op("dve", lambda e, j=j: e.tensor_scalar(
                                uT[:, j, ti * 128:(ti + 1) * 128], pst[:, j * 128:(j + 1) * 128],
                                modA[:, layer, j, tmod:tmod + 1], modS[:, layer, j, tmod:tmod + 1], ALU.mult, ALU.add),
                                reads=[f"ps#{pk}", "modA", "modS"], accw=[uT.key])
                    p1s.append(part1); p2s.append(part2)
                return uT, p1s, p2s

            def mm(gi, uT, hooks):
                tok0, ntok, tmod = groups[gi]
                if per_group:
                    per_group(ctxo, gi, tok0, ntok)
                nb_tot = sum(len(b) for (_, b, _) in fspecs)
                nhk = max(1, len(hooks))
                step = max(1, nb_tot // nhk)
                bcount = 0
                hooks = list(hooks)
                for (name, bundles, epi) in fspecs:
                    for bi, cols in enumerate(bundles):
                        if hooks and bcount % step == 0:
                            hooks.pop(0)()
                        bcount += 1
                        pks = []
                        for c0 in cols:
                            pk = pso_i[0] % 6; pso_i[0] += 1
                            pks.append(pk)
                            for j in range(8):
                                K.op("pe", lambda e, j=j, pk=pk, c0=c0, uT=uT, ntok=ntok: e.matmul(
                                    psum[pk][:, 0:ntok], Wb[:, j, c0:c0 + 128], uT[:, j, 0:ntok],
                                    start=(j == 0), stop=(j == 7)),
                                    reads=["Wb", uT.key], writes=[f"ps#{pk}"], inc=(j == 7))
                        epi(ctxo, pks, bi, tok0, ntok)
                for (c0, cw, epi) in tspecs:
                    for ti in range(ntok // 128):
                        pk = pso_i[0] % 6; pso_i[0] += 1
                        for j in range(8):
                            K.op("pe", lambda e, j=j, pk=pk, uT=uT, ti=ti: e.matmul(
                                psum[pk][:, 0:cw], uT[:, j, ti * 128:(ti + 1) * 128], Wb[:, j, c0:c0 + cw],
                                start=(j == 0), stop=(j == 7)),
                                reads=["Wb", uT.key], writes=[f"ps#{pk}"], inc=(j == 7))
                        epi(ctxo, pk, tok0 + ti * 128)
                while hooks:
                    hooks.pop(0)()

            ng = P.dbg.get("ngroups", len(groups))
            uT0, p1s, p2s = front_parts(0)
            for p1, p2 in zip(p1s, p2s):
                p2(p1())
            cur = uT0
            for gi in range(ng):
                hooks = []
                nxt = None
                if gi + 1 < ng:
                    nxt, p1s, p2s = front_parts(gi + 1)
                    boxes = {}
                    def mk(k, p1s=p1s, p2s=p2s, boxes=boxes):
                        def h():
                            if k == 0:
                                for kk in range(min(2, len(p1s))):
                                    boxes[kk] = p1s[kk]()
                                return
                            if 0 <= k - 1 < len(p1s):
                                p2s[k - 1](boxes[k - 1])
                            if k + 1 < len(p1s):
                                boxes[k + 1] = p1s[k + 1]()
                        return h
                    hooks = [mk(k) for k in range(len(p1s) + 1)]
                mm(gi, cur, hooks)
                cur = nxt
            K.barrier()

    def l0_alloc(alloc):
        c = {}
        c["sf"] = Ring(alloc, "sf", 3, [128, 512], F32)
        c["sb"] = Ring(alloc, "sb", 4, [128, 512], BF16)
        c["t1"] = Ring(alloc, "t1", 2, [128, 512], F32)
        c["t2"] = Ring(alloc, "t2", 2, [128, 512], F32)
        c["tab"] = Ring(alloc, "tab", 2, [128, 2, 512], F32)
        return c

    def l0_group(c, gi, tok0, ntok):
        tb = c["tab"].next()
        c["curtab"] = tb
        K.dma("sp", tb[:, :, 0:ntok], ropetab[:, :, tok0:tok0 + ntok].rearrange("c p t -> p c t"),
              "L" + tb.key, writes=[tb.key])

    def epi_copy_f32(dst):
        def f(c, pks, bi, tok0, ntok):
            pk = pks[0]; b = c["sf"].next()
            K.op("act", lambda e: e.copy(b[:, 0:ntok], psum[pk][:, 0:ntok]), reads=[f"ps#{pk}"], writes=[b.key])
            K.dma("sp", dst[bi * 128:(bi + 1) * 128, tok0:tok0 + ntok], b[:, 0:ntok], "S" + b.key,
                  reads=[b.key], accw=[dst.name])
        return f

    def epi_silu_bf(dst):
        def f(c, pks, bi, tok0, ntok):
            pk = pks[0]; b = c["sb"].next()
            K.op("act", lambda e: e.activation(out=b[:, 0:ntok], in_=psum[pk][:, 0:ntok], func=AF.Silu),
                 reads=[f"ps#{pk}"], writes=[b.key])
            K.dma("sp", dst[bi * 128:(bi + 1) * 128, tok0:tok0 + ntok], b[:, 0:ntok], "S" + b.key,
                  reads=[b.key], accw=[dst.name])
        return f

    def epi_rope(dst):
        def f(c, pks, bi, tok0, ntok):
            pa, pb = pks; t1 = c["t1"].next(); t2 = c["t2"].next(); b = c["sb"].next(); tb = c["curtab"]
            K.op("dve", lambda e: e.tensor_tensor(t1[:, 0:ntok], psum[pa][:, 0:ntok], tb[:, 0, 0:ntok], ALU.mult),
                 reads=[f"ps#{pa}", tb.key], writes=[t1.key])
            K.op("dve", lambda e: e.tensor_tensor(t2[:, 0:ntok], psum[pb][:, 0:ntok], tb[:, 1, 0:ntok], ALU.mult),
                 reads=[f"ps#{pb}", tb.key], writes=[t2.key])
            K.op("pool", lambda e: e.tensor_tensor(b[:, 0:ntok], t1[:, 0:ntok], t2[:, 0:ntok], ALU.add),
                 reads=[t1.key, t2.key], writes=[b.key])
            K.dma("sp", dst[bi * 128:(bi + 1) * 128, tok0:tok0 + ntok], b[:, 0:ntok], "S" + b.key,
                  reads=[b.key], accw=[dst.name])
        return f

    def epi_v(c, pk, tok0):
        b = c["sb"].next()
        K.op("act", lambda e: e.copy(b[:, :], psum[pk][:, :]), reads=[f"ps#{pk}"], writes=[b.key])
        K.dma("sp", vtok[tok0:tok0 + 128, :], b[:, :], "S" + b.key, reads=[b.key], accw=["vtok"])

    l0_f = [
        ("xr", [[f * 128] for f in range(0, 4)], epi_copy_f32(xrT)),
        ("gr", [[f * 128] for f in range(4, 8)], epi_silu_bf(grT)),
        ("q", [[1024 + f * 128, 3072 + f * 128] for f in range(4)], epi_rope(qT)),
        ("k", [[1536 + f * 128, 3584 + f * 128] for f in range(4)], epi_rope(kT)),
        ("gd", [[f * 128] for f in range(20, 24)], epi_silu_bf(gdT)),
    ]
    l0_t = [(2048, 512, epi_v)]
    phase_proj(0, src0, e_w_in, 4096, l0_f, l0_t, l0_alloc, l0_group)
    if stop_after == "proj0":
        K.barrier(); pes.close(); P.es.close(); return P


    LAMBDA_INIT0 = 0.8 - 0.6 * 1.0

    def phase_lru():
        with ExitStack() as es:
            def alloc(name, shape, dt):
                P.uid += 1
                return es.enter_context(nc.sbuf_tensor(f"{name}_{P.uid}", list(shape), dt))
            rows = alloc("lrows", [44, 128], F32)
            pv = alloc("lpv", [128, 11, 4], F32)
            coef = alloc("lcoef", [128, 2, 4], F32)
            ones1 = alloc("ones1", [128, 1], F32)
            wbf = alloc("wbf", [128, 16, 128], F32)
            wbb = alloc("wbb", [128, 16, 128], BF16)
            big = Ring(alloc, "big", 7, [128, T], F32)
            xcb = alloc("xcb", [128, T], BF16)
            grs = alloc("grs", [128, T], BF16)
            mo = alloc("mixo", [128, T], BF16)
            K.op("dve", lambda e: e.memset(ones1[:], 1.0), writes=["ones1"])
            K.dma("sp", rows[:, :], lru_vec.rearrange("v (c p) -> (v c) p", p=128), "Lrows", writes=["lrows"])
            K.op("pe", lambda e: e.transpose(psum[0][:, 0:44], rows[:, :], idf[0:44, 0:44]),
                 reads=["lrows", "idf"], writes=["ps#0"])
            K.cp("dve", pv[:].rearrange("p v c -> p (v c)"), psum[0][:, 0:44], ["ps#0"], ["lpv"])
            K.act(coef[:].rearrange("p d c -> p (d c)"), pv[:, 9:11, :].rearrange("p d c -> p (d c)"), AF.Exp,
                  ["lpv"], ["lcoef"], scale=-1.0)
            K.act(coef[:].rearrange("p d c -> p (d c)"), coef[:].rearrange("p d c -> p (d c)"), AF.Ln,
                  ["lcoef", "ones1"], ["lcoef"], bias=ones1[:, 0:1])
            K.ts("dve", coef[:].rearrange("p d c -> p (d c)"), coef[:].rearrange("p d c -> p (d c)"), -8.0, None,
                 ALU.mult, None, ["lcoef"], ["lcoef"])
            K.dma("sp", wbf[:], lru_wbd.rearrange("n k m -> k n m"), "Lwbf", writes=["wbf"])
            K.cp("dve", wbb[:], wbf[:], ["wbf"], ["wbb"])
            segs = [(0, CTX), (CTX, T)]
            blocks = [(b0, min(512, T - b0)) for b0 in range(0, T, 512)]
            for cc in range(4):
                x = big.next(); xc = big.next()
                K.dma("sp", x[:, :], xrT[cc * 128:(cc + 1) * 128, :], "L" + x.key, writes=[x.key])
                K.dma("sp", grs[:, :], grT[cc * 128:(cc + 1) * 128, :], "Lgrs", writes=["grs"])
                K.ts("dve", xc[:, :], x[:, :], pv[:, 2, cc:cc + 1], pv[:, 4, cc:cc + 1], ALU.mult, ALU.add,
                     [x.key, "lpv"], [xc.key])
                for (a, b) in segs:
                    for tap, sh in ((0, -2), (1, -1), (3, 1)):
                        lo = max(a, a - sh); hi = min(b, b - sh)
                        K.stt("dve", xc[:, lo:hi], x[:, lo + sh:hi + sh], pv[:, tap, cc:cc + 1],
                              xc[:, lo:hi], ALU.mult, ALU.add, [x.key, xc.key, "lpv"], [xc.key])
                K.cp("act", xcb[:, :], xc[:, :], [xc.key], ["xcb"])
                hs = []
                for d in range(2):
                    rb = big.next(); ib = big.next()
                    for (b0, bn) in blocks:
                        for g, dstb, brow in ((0, rb, 5 + d), (1, ib, 7 + d)):
                            pk = (2 * (b0 // 512) + g) % 6
                            K.mm(psum[pk][:, 0:bn], wbb[:, (d * 2 + g) * 4 + cc, :], xcb[:, b0:b0 + bn], True, True,
                                 ["wbb", "xcb"], [f"ps#{pk}"], True)
                            K.act(dstb[:, b0:b0 + bn], psum[pk][:, 0:bn], AF.Sigmoid, [f"ps#{pk}", "lpv"], accw=[dstb.key],
                                  bias=pv[:, brow, cc:cc + 1])
                    K.ts("dve", rb[:, :], rb[:, :], coef[:, d, cc:cc + 1], None, ALU.mult, None, [rb.key, "lcoef"], [rb.key])
                    K.act(rb[:, :], rb[:, :], AF.Exp, [rb.key], [rb.key])
                    sq = big.next()
                    K.act(sq[:, :], rb[:, :], AF.Square, [rb.key], [sq.key])
                    K.act(sq[:, :], sq[:, :], AF.Sqrt, [sq.key, "ones1"], [sq.key], scale=-1.0, bias=ones1[:, 0:1])
                    K.tt("pool", ib[:, :], ib[:, :], xc[:, :], ALU.mult, [ib.key, xc.key], [ib.key])
                    K.tt("dve", ib[:, :], ib[:, :], sq[:, :], ALU.mult, [ib.key, sq.key], [ib.key])
                    h = sq
                    if d == 0:
                        K.op("dve", lambda e, h=h, rb=rb, ib=ib: e.tensor_tensor_scan(
                            h[:, :], rb[:, :], ib[:, :], 0.0, ALU.mult, ALU.add), [rb.key, ib.key, h.key], [h.key])
                    else:
                        K.op("dve", lambda e, h=h, rb=rb, ib=ib: e.tensor_tensor_scan(
                            h[:, CTX - 1::-1] if False else h[:, 0:CTX][:, ::-1], rb[:, 0:CTX][:, ::-1], ib[:, 0:CTX][:, ::-1],
                            0.0, ALU.mult, ALU.add), [rb.key, ib.key, h.key], [h.key])
                        K.op("dve", lambda e, h=h, rb=rb, ib=ib: e.tensor_tensor_scan(
                            h[:, CTX:T][:, ::-1], rb[:, CTX:T][:, ::-1], ib[:, CTX:T][:, ::-1],
                            h[:, 0:1], ALU.mult, ALU.add), [rb.key, ib.key, h.key], [h.key])
                    hs.append(h)
                K.tt("pool", hs[0][:, :], hs[0][:, :], hs[1][:, :], ALU.add, [hs[0].key, hs[1].key], [hs[0].key])
                K.tt("dve", mo[:, :], hs[0][:, :], grs[:, :], ALU.mult, [hs[0].key, "grs"], ["mixo"])
                K.dma("sp", mixT0[cc * 128:(cc + 1) * 128, :], mo[:, :], "Smixo", reads=["mixo"], accw=["mixT0"])
            K.barrier()

    if "lru" not in P.dbg.get("skip", ()):
        phase_lru()
    if stop_after == "lru":
        K.barrier(); pes.close(); P.es.close(); return P

    def phase_attn():
        with ExitStack() as es:
            def alloc(name, shape, dt):
                P.uid += 1
                return es.enter_context(nc.sbuf_tensor(f"{name}_{P.uid}", list(shape), dt))
            kres = alloc("kres", [128, 4, T], BF16)
            vres = alloc("vres", [128, NT, 512], BF16)
            onesb = alloc("onesb", [128, 128], BF16)
            onesf = alloc("onesf", [128, 128], F32)
            lrow = alloc("lamrow", [1, 260], F32)
            lamc = alloc("lamc", [128, 2], F32)
            subc = alloc("subc", [128, 2], F32)
            subrow = alloc("subrow", [1, 128], F32)
            qb_ = Ring(alloc, "qblk", 2, [128, 4, 512], BF16)
            gd_ = Ring(alloc, "gdblk", 2, [128, 512], BF16)
            E_ = Ring(alloc, "Eb", 6, [128, 512], BF16)
            f_ = Ring(alloc, "af", 6, [128, 512], F32)
            ob_ = Ring(alloc, "aob", 4, [128, 512], BF16)
            K.op("dve", lambda e: e.memset(onesb[:], 1.0), writes=["onesb"])
            K.op("dve", lambda e: e.memset(onesf[:], 1.0), writes=["onesf"])
            K.dma("sp", lrow[:, 0:256], da_lam[:, :], "Llam", writes=["lamrow"])
            K.dma("sp", subrow[:, :], da_sub[:, :], "Lsub", writes=["subrow"])
            K.tt("dve", lrow[:, 0:64], lrow[:, 0:64], lrow[:, 64:128], ALU.mult, ["lamrow"], ["lamrow"])
            K.tt("dve", lrow[:, 128:192], lrow[:, 128:192], lrow[:, 192:256], ALU.mult, ["lamrow"], ["lamrow"])
            K.op("dve", lambda e: e.reduce_sum(lrow[:, 256:257], lrow[:, 0:64], AX.X), ["lamrow"], ["lamrow"])
            K.op("dve", lambda e: e.reduce_sum(lrow[:, 257:258], lrow[:, 128:192], AX.X), ["lamrow"], ["lamrow"])
            K.act(lrow[:, 256:258], lrow[:, 256:258], AF.Exp, ["lamrow"], ["lamrow"])
            K.tt("dve", lrow[:, 258:259], lrow[:, 256:257], lrow[:, 257:258], ALU.subtract, ["lamrow"], ["lamrow"])
            K.ts("dve", lrow[:, 258:259], lrow[:, 258:259], -1.0, -LAMBDA_INIT0, ALU.mult, ALU.add, ["lamrow"], ["lamrow"])
            K.mm(psum[0][:, 0:1], onesf[0:1, :], lrow[0:1, 258:259], True, True, ["onesf", "lamrow"], ["ps#0"], True)
            K.cp("dve", lamc[:, 0:1], psum[0][:, 0:1], ["ps#0"], ["lamc"])
            K.op("pe", lambda e: e.transpose(psum[1][:, 0:1], subrow[0:1, :], idf[0:1, 0:1]),
                 reads=["subrow", "idf"], writes=["ps#1"])
            K.ts("dve", subc[:, 0:1], psum[1][:, 0:1], 1.0 - LAMBDA_INIT0, None, ALU.mult, None, ["ps#1"], ["subc"])
            for h in range(4):
                K.dma("sp", kres[:, h, :], kT[h * 128:(h + 1) * 128, :], "Lkres", accw=["kres"])
            for n0 in range(0, NT, 2):
                K.dma("sp", vres[:, n0:n0 + 2, :], vtok[n0 * 128:(n0 + 2) * 128, :].rearrange("(n p) e -> p n e", p=128),
                      "Lvres", accw=["vres"])
            qblocks = [(0, CTX, 0, 2)] + [(CTX + 512 * g, 512, 0, NT) for g in range(SEQ // 512)]
            nqb = P.dbg.get("nqb", len(qblocks))
            sti = 0
            acc_ = Ring(alloc, "dacc", 4, [128, 512], F32)
            for (q0, nq, kt0, kt1) in qblocks[:nqb]:
                qb = qb_.next()
                for h in range(4):
                    K.dma("sp", qb[:, h, 0:nq], qT[h * 128:(h + 1) * 128, q0:q0 + nq], "L" + qb.key, accw=[qb.key])
                for h in range(4):
                    gd = gd_.next()
                    K.dma("sp", gd[:, 0:nq], gdT[h * 128:(h + 1) * 128, q0:q0 + nq], "L" + gd.key, writes=[gd.key])
                    accs = [acc_.next(), acc_.next()]
                    kts = list(range(kt0, kt1))
                    pkmap = {}

                    def emit_qk(kt):
                        nonlocal sti
                        for c in range(2):
                            pk = sti % 4; sti += 1
                            pkmap[(kt, c)] = pk
                            K.mm(psum[pk][:, 0:nq], kres[c * 64:(c + 1) * 64, h, kt * 128:(kt + 1) * 128],
                                 qb[c * 64:(c + 1) * 64, h, 0:nq], True, True, ["kres", qb.key], [f"ps#{pk}"], True)

                    def emit_rest(kt):
                        for c in range(2):
                            pk = pkmap[(kt, c)]
                            E = E_.next()
                            K.act(E[:, 0:nq], psum[pk][:, 0:nq], AF.Exp, [f"ps#{pk}"], [E.key], scale=0.125)
                            K.mm(psum[4 + c][:, 0:nq], vres[:, kt, h * 128:(h + 1) * 128], E[:, 0:nq],
                                 kt == kt0, kt == kt1 - 1, ["vres", E.key], [f"ps#{4 + c}"], kt == kt1 - 1)
                            if c == 1:
                                K.mm(psum[7][:, 0:nq], onesb[:, :], E[:, 0:nq], kt == kt0, kt == kt1 - 1,
                                     ["onesb", E.key], ["ps#7"], kt == kt1 - 1)
                            elif kt == kt0:
                                K.cp("dve", accs[c][:, 0:nq], E[:, 0:nq], [E.key], [accs[c].key])
                            else:
                                K.tt("dve", accs[c][:, 0:nq], accs[c][:, 0:nq], E[:, 0:nq], ALU.add,
                                     [E.key, accs[c].key], [accs[c].key])

                    emit_qk(kts[0])
                    for i, kt in enumerate(kts):
                        if i + 1 < len(kts):
                            emit_qk(kts[i + 1])
                        emit_rest(kt)
                    r0 = f_.next(); r1 = f_.next(); t0 = f_.next(); t1 = f_.next()
                    K.cp("act", t0[:, 0:nq], psum[4][:, 0:nq], ["ps#4"], [t0.key])
                    K.cp("act", t1[:, 0:nq], psum[5][:, 0:nq], ["ps#5"], [t1.key])
                    ab = ob_.next()
                    K.cp("pool", ab[:, 0:nq], accs[0][:, 0:nq], [accs[0].key], [ab.key])
                    K.mm(psum[6][:, 0:nq], onesb[:, :], ab[:, 0:nq], True, True, ["onesb", ab.key], ["ps#6"], True)
                    K.cp("act", r0[:, 0:nq], psum[6][:, 0:nq], ["ps#6"], [r0.key])
                    K.tt("dve", t0[:, 0:nq], t0[:, 0:nq], psum[7][:, 0:nq], ALU.mult, [t0.key, "ps#7"], [t0.key])
                    K.tt("dve", t1[:, 0:nq], t1[:, 0:nq], r0[:, 0:nq], ALU.mult, [t1.key, r0.key], [t1.key])
                    K.stt("dve", t0[:, 0:nq], t1[:, 0:nq], lamc[:, 0:1], t0[:, 0:nq], ALU.mult, ALU.add,
                          [t0.key, t1.key, "lamc"], [t0.key])
                    K.tt("dve", r0[:, 0:nq], r0[:, 0:nq], psum[7][:, 0:nq], ALU.mult, [r0.key, "ps#7"], [r0.key])
                    K.stt("dve", r1[:, 0:nq], r0[:, 0:nq], EPS, r0[:, 0:nq], ALU.mult, ALU.mult, [r0.key], [r1.key])
                    osq = ob_.next()
                    K.tt("pool", osq[:, 0:nq], t0[:, 0:nq], t0[:, 0:nq], ALU.mult, [t0.key], [osq.key])
                    K.mm(psum[6][:, 0:nq], onesb[:, :], osq[:, 0:nq], True, True, ["onesb", osq.key], ["ps#6"], True)
                    K.stt("dve", r1[:, 0:nq], psum[6][:, 0:nq], 1.0 / 128, r1[:, 0:nq], ALU.mult, ALU.add,
                          ["ps#6", r1.key], [r1.key])
                    K.act(r1[:, 0:nq], r1[:, 0:nq], AF.Ln, [r1.key], [r1.key])
                    K.act(r1[:, 0:nq], r1[:, 0:nq], AF.Exp, [r1.key], [r1.key], scale=-0.5)
                    K.tt("dve", t0[:, 0:nq], t0[:, 0:nq], r1[:, 0:nq], ALU.mult, [t0.key, r1.key], [t0.key])
                    mo = ob_.next()
                    K.stt("dve", mo[:, 0:nq], t0[:, 0:nq], subc[:, 0:1], gd[:, 0:nq], ALU.mult, ALU.mult,
                          [t0.key, "subc", gd.key], [mo.key])
                    K.dma("sp", mixT0[(4 + h) * 128:(5 + h) * 128, q0:q0 + nq], mo[:, 0:nq], "S" + mo.key,
                          reads=[mo.key], accw=["mixT0"])
            K.barrier()

    if "attn" not in P.dbg.get("skip", ()):
        phase_attn()
    if stop_after == "attn":
        K.barrier(); pes.close(); P.es.close(); return P

    def phase_out(layer, mixT, KC, Wd, res_src, dst, tiles):
        with ExitStack() as es:
            def alloc(name, shape, dt):
                P.uid += 1
                return es.enter_context(nc.sbuf_tensor(f"{name}_{P.uid}", list(shape), dt))
            Wb = alloc("Wo", [128, KC, D], BF16)
            wst = Ring(alloc, "wost", 2, [128, KC, 256], F32)
            mx_ = Ring(alloc, "mxin", 2, [128, KC, 512], BF16)
            rs_ = Ring(alloc, "resin", 3, [128, D], F32)
            tm_ = Ring(alloc, "otmp", 2, [128, D], F32)
            st_ = Ring(alloc, "ostat", 4, [128, 4], F32)
            junk = alloc("ojunk", [128, 512], BF16)
            for pc in range(4):
                wb = wst.next()
                K.dma("sp", wb[:], Wd[:, pc * 256:(pc + 1) * 256].rearrange("(j p) n -> p j n", p=128), "L" + wb.key,
                      writes=[wb.key])
                K.cp(["dve", "pool"][pc % 2], Wb[:, :, pc * 256:(pc + 1) * 256], wb[:], [wb.key], accw=["Wo"])
            gi = 0
            cur = None
            for (tok0, drow, tmod) in tiles:
                g0 = (tok0 // 512) * 512 if tok0 >= CTX else 0
                if tok0 >= CTX:
                    g0 = CTX + ((tok0 - CTX) // 512) * 512
                gn = CTX if tok0 < CTX else 512
                if cur is None or cur[0] != g0:
                    mx = mx_.next()
                    K.dma("sp", mx[:, :, 0:gn], mixT[:, g0:g0 + gn].rearrange("(j p) t -> p j t", p=128), "L" + mx.key,
                          writes=[mx.key])
                    cur = (g0, mx)
                mx = cur[1]; lo = tok0 - g0
                rs = rs_.next(); tm = tm_.next(); st = st_.next()
                K.dma("sp", rs[:, :], res_src[tok0:tok0 + 128, :], "L" + rs.key, writes=[rs.key])
                pks = [(2 * gi) % 6, (2 * gi + 1) % 6]; gi += 1
                for nb in range(2):
                    for j in range(KC):
                        K.mm(psum[pks[nb]][:, :], mx[:, j, lo:lo + 128], Wb[:, j, nb * 512:(nb + 1) * 512],
                             j == 0, j == KC - 1, [mx.key, "Wo"], [f"ps#{pks[nb]}"], j == KC - 1)
                for nb in range(2):
                    K.act(junk[:, :], psum[pks[nb]][:, :], AF.Square, [f"ps#{pks[nb]}"], ["ojunk", st.key] if nb == 0 else ["ojunk"],
                          accw=() if nb == 0 else [st.key], accum_out=st[:, nb:nb + 1])
                K.tt("dve", st[:, 2:3], st[:, 0:1], st[:, 1:2], ALU.add, [st.key], [st.key])
                K.act(st[:, 3:4], st[:, 2:3], AF.Sqrt, [st.key, "epsb"], [st.key], scale=1.0 / D, bias=epsb[:, 0:1])
                K.op("dve", lambda e, st=st: e.reciprocal(st[:, 2:3], st[:, 3:4]), [st.key], [st.key])
                for nb in range(2):
                    K.stt("dve", tm[:, nb * 512:(nb + 1) * 512], psum[pks[nb]][:, :], st[:, 2:3],
                          ggbc[:, layer, tmod, nb * 512:(nb + 1) * 512], ALU.mult, ALU.mult,
                          [f"ps#{pks[nb]}", st.key, "ggbc"], accw=[tm.key])
                K.tt("pool", tm[:, :], tm[:, :], rs[:, :], ALU.add, [tm.key, rs.key], [tm.key])
                K.dma("sp", dst[drow:drow + 128, :], tm[:, :], "S" + tm.key, reads=[tm.key], accw=[dst.name])
            K.barrier()

    nt0 = P.dbg.get("out0_tiles", NT)
    tiles0 = [(i * 128, i * 128, 1 if i < 2 else 0) for i in range(nt0)]
    phase_out(0, mixT0, 8, e_w_out, src0, h1, tiles0)
    if stop_after == "out0":
        K.barrier(); pes.close(); P.es.close(); return P


    def l1_alloc(alloc):
        c = {}
        c["sb"] = Ring(alloc, "sb1", 4, [128, 512], BF16)
        c["sf"] = Ring(alloc, "sf1", 2, [128, 64], F32)
        c["n"] = 0
        return c

    def epi_xbc(c, pks, bi, tok0, ntok):
        pk = pks[0]; b = c["sb"].next()
        c["n"] += 1
        K.cp("act" if c["n"] % 2 else "dve", b[:, 0:ntok], psum[pk][:, 0:ntok], [f"ps#{pk}"], [b.key])
        K.dma("sp", xbcT[bi * 128:(bi + 1) * 128, tok0:tok0 + ntok], b[:, 0:ntok], "S" + b.key,
              reads=[b.key], accw=["xbcT"])

    def epi_z(zc):
        def f(c, pk, tok0):
            b = c["sb"].next()
            K.act(b[:, :], psum[pk][:, :], AF.Silu, [f"ps#{pk}"], [b.key])
            K.dma("sp", zs[tok0:tok0 + 128, zc * 512:(zc + 1) * 512], b[:, :], "S" + b.key, reads=[b.key], accw=["zs"])
        return f

    def epi_dt(c, pk, tok0):
        b = c["sf"].next()
        K.cp("dve", b[:, :], psum[pk][:, 0:64], [f"ps#{pk}"], [b.key])
        K.dma("sp", dtraw[tok0:tok0 + 128, :], b[:, :], "S" + b.key, reads=[b.key], accw=["dtraw"])

    l1_f = [("xbc", [[2048 + f * 128] for f in range(24)], epi_xbc)]
    l1_t = [(zc * 512, 512, epi_z(zc)) for zc in range(4)] + [(5120, 64, epi_dt)]
    if "l1" not in P.dbg.get("skip", ()):
        phase_proj(1, h1, o_w_in, 5184, l1_f, l1_t, l1_alloc, None)
    if stop_after == "proj1":
        K.barrier(); pes.close(); P.es.close(); return P

    def phase_conv():
        with ExitStack() as es:
            def alloc(name, shape, dt):
                P.uid += 1
                return es.enter_context(nc.sbuf_tensor(f"{name}_{P.uid}", list(shape), dt))
            rows = alloc("cvrows", [120, 128], F32)
            cv = alloc("cv", [128, 5, 24], F32)
            dg = alloc("dg", [128, 4, 24, 128], BF16)
            xin_ = Ring(alloc, "cxin", 2, [128, 24, 516], BF16)
            sb_ = Ring(alloc, "csb", 3, [128, 512], BF16)
            rb_ = Ring(alloc, "crow", 8, [128, 2560], BF16)
            K.dma("sp", rows[:, :], ssd_cv.rearrange("v (f p) -> (v f) p", p=128), "Lcvrows", writes=["cvrows"])
            K.op("pe", lambda e: e.transpose(psum[0][:, 0:120], rows[:, :], idf[0:120, 0:120]),
                 reads=["cvrows", "idf"], writes=["ps#0"])
            K.cp("dve", cv[:].rearrange("p v f -> p (v f)"), psum[0][:, 0:120], ["ps#0"], ["cv"])
            for j in range(4):
                for f in range(24):
                    K.ts("dve" if (f % 2) else "pool", dg[:, j, f, :], idf[:, :], cv[:, j, f:f + 1], None, ALU.mult, None,
                         ["idf", "cv"], accw=["dg"])
            blocks = [(0, CTX, 0, CTX)] + [(CTX + 512 * g, 512, CTX, T) for g in range(SEQ // 512)]
            cpi = 0
            for (b0, bn, sa, sb_end) in blocks[:P.dbg.get("nconv", 99)]:
                xin = xin_.next()
                l0 = max(sa, b0 - 2); l1 = min(sb_end, b0 + bn + 1)
                K.dma("sp", xin[:, :, l0 - (b0 - 2):l1 - (b0 - 2)],
                      xbcT[:, l0:l1].rearrange("(f p) t -> p f t", p=128), "L" + xin.key, writes=[xin.key])
                nt_ = bn // 128
                rbs = [rb_.next() for _ in range(nt_)]
                for ft in range(24):
                    pk = 4 + (cpi % 2); cpi += 1
                    order = [2, 0, 1, 3]
                    for oi, j in enumerate(order):
                        sh = j - 2
                        lo = max(b0, sa - sh); hi = min(b0 + bn, sb_end - sh)
                        K.mm(psum[pk][:, lo - b0:hi - b0], dg[:, j, ft, :],
                             xin[:, ft, lo + sh - (b0 - 2):hi + sh - (b0 - 2)], oi == 0, oi == 3,
                             ["dg", xin.key], [f"ps#{pk}"], oi == 3)
                    sb = sb_.next()
                    K.act(sb[:, 0:bn], psum[pk][:, 0:bn], AF.Silu, [f"ps#{pk}", "cv"], [sb.key], bias=cv[:, 4, ft:ft + 1])
                    if ft >= 16:
                        K.dma("sp", bcT[(ft - 16) * 128:(ft - 15) * 128, b0:b0 + bn], sb[:, 0:bn], "S" + sb.key,
                              reads=[sb.key], accw=["bcT"])
                    if ft < 20:
                        for i in range(nt_):
                            tb = psum[i][:].bitcast(BF16)
                            K.op("pe", lambda e, tb=tb, sb=sb, i=i, ft=ft: e.transpose(
                                tb[:, (ft % 8) * 128:(ft % 8 + 1) * 128], sb[:, i * 128:(i + 1) * 128], idb[:]),
                                reads=[sb.key, "idb"], writes=[f"ps#{i}"], inc=True)
                        if ft % 8 == 7 or ft == 19:
                            ncol = (ft % 8 + 1) * 128
                            c0 = (ft // 8) * 1024
                            for i in range(nt_):
                                tb = psum[i][:].bitcast(BF16)
                                K.cp("dve" if i % 2 else "act", rbs[i][:, c0:c0 + ncol], tb[:, 0:ncol], [f"ps#{i}"],
                                     accw=[rbs[i].key])
                for i in range(nt_):
                    K.dma("sp", xsB[b0 + i * 128:b0 + (i + 1) * 128, :], rbs[i][:, :], "S" + rbs[i].key,
                          reads=[rbs[i].key], accw=["xsB"])
            K.barrier()

    if "conv" not in P.dbg.get("skip", ()):
        phase_conv()
    if stop_after == "conv":
        K.barrier(); pes.close(); P.es.close(); return P


    def phase_ssd():
        with ExitStack() as es:
            def alloc(name, shape, dt):
                P.uid += 1
                return es.enter_context(nc.sbuf_tensor(f"{name}_{P.uid}", list(shape), dt))
            vb = alloc("vb", [128, 160], F32)
            aneg = alloc("aneg", [128, 64], F32)
            nwbc = alloc("nwbc", [128, 2048], F32)
            mk = alloc("mk", [128, 2, 128], F32)
            self_ = alloc("self", [64, 1024], F32)
            selb = alloc("selb", [64, 32, 128], BF16)
            onesf = alloc("onesf2", [128, 128], F32)
            ones1 = alloc("ones1b", [128, 1], F32)
            state = alloc("state", [128, 2048], F32)
            stbf_ = Ring(alloc, "stbf", 2, [128, 2048], BF16)
            S_ = Ring(alloc, "Ssb", 2, [128, 2048], F32)
            ST = {}
            xb_ = Ring(alloc, "xb", 5, [128, 2560], BF16)
            bc_ = Ring(alloc, "bc", 5, [128, 8, 128], BF16)
            dr_ = Ring(alloc, "dr", 3, [128, 32], F32)
            sm_ = Ring(alloc, "sm", 5, [128, 8, 32], F32)
            cs4_ = Ring(alloc, "cs4", 3, [128, 64], F32)
            cst_ = Ring(alloc, "cst", 3, [64, 128], F32)
            hl_ = Ring(alloc, "hl", 3, [64, 3, 128], BF16)
            X_ = Ring(alloc, "Xd", 3, [128, 2048], BF16)
            Xc_ = Ring(alloc, "Xc", 3, [128, 2048], BF16)
            xd_ = Ring(alloc, "xdk", 1, [128, 2048], F32)
            cbm_ = Ring(alloc, "cbm", 2, [128, 4, 128], F32)
            E_ = Ring(alloc, "sE", 2, [128, 512], F32)
            MT_ = Ring(alloc, "sMT", 4, [128, 4, 128], BF16)
            to_ = Ring(alloc, "sto", 2, [128, 512], F32)
            ysb_ = Ring(alloc, "ysb", 2, [128, 2048], F32)
            yf_ = Ring(alloc, "yfl", 1, [128, 2048], F32)
            zt_ = Ring(alloc, "ztl", 1, [128, 2048], BF16)
            ynb_ = Ring(alloc, "ynb", 1, [128, 2048], BF16)
            yT_ = Ring(alloc, "yTs", 1, [128, 16, 128], BF16)
            junk = alloc("sjunk", [128, 512], BF16)
            K.op("dve", lambda e: e.memset(onesf[:], 1.0), writes=["onesf2"])
            K.op("dve", lambda e: e.memset(ones1[:], 1.0), writes=["ones1b"])
            negb = alloc("negb", [128, 2, 512], BF16)
            for dd in range(2):
                K.dma("sp", to_.bufs[dd][:, :], negm[dd, :, :], "Lnegm%d" % dd, writes=[to_.bufs[dd].key])
                K.cp("dve", negb[:, dd, :], to_.bufs[dd][:, :], [to_.bufs[dd].key], accw=["negb"])
            K.dma("sp", vb[:, :], ssd_vec[0, :].partition_broadcast(128), "Lvb", writes=["vb"])
            K.dma("sp", nwbc[:, :], ssd_norm.partition_broadcast(128), "Lnwbc", writes=["nwbc"])
            K.dma("sp", mk[:], maskT.rearrange("d s l -> s d l"), "Lmk", writes=["mk"])
            for q4 in range(4):
                K.dma("sp", self_[:, :], selc[:, q4 * 1024:(q4 + 1) * 1024], "Lself", writes=["self"])
                K.cp("dve", selb[:, q4 * 8:(q4 + 1) * 8, :].rearrange("k h l -> k (h l)"), self_[:, :], ["self"], accw=["selb"])
            K.act(aneg[:, :], vb[:, 0:64], AF.Exp, ["vb"], ["aneg"])
            K.ts("dve", aneg[:, :], aneg[:, :], -1.0, None, ALU.mult, None, ["aneg"], ["aneg"])
            lat_chunks = list(range(2, NT))
            ncl = P.dbg.get("nchunk", len(lat_chunks))
            lat_chunks = lat_chunks[:ncl]
            PB = {"yd": 0, "seg": 0, "so": 0}

            def stage_a(d, c):
                tok0 = c * 128
                lat = c >= 2
                xb = xb_.next(); bc = bc_.next(); dr = dr_.next(); sm = sm_.next()
                ctx_ = {"xb": xb, "bc": bc, "sm": sm, "lat": lat, "tok0": tok0}
                K.dma("sp", xb[:, :], xsB[tok0:tok0 + 128, :], "L" + xb.key, writes=[xb.key])
                K.dma("sp", bc[:], bcT[:, tok0:tok0 + 128].rearrange("(f p) t -> p f t", p=128), "L" + bc.key,
                      writes=[bc.key])
                K.dma("sp", dr[:, :], dtraw[tok0:tok0 + 128, d * 32:(d + 1) * 32], "L" + dr.key, writes=[dr.key])
                dt = sm[:, 0, :]; adt = sm[:, 1, :]; cs = sm[:, 2, :]; ecs = sm[:, 3, :]
                etot = sm[:, 4, :]; w2 = sm[:, 5, :]; tmp = sm[:, 6, :]
                k_ = sm.key
                K.tt("dve", tmp, dr[:, :], vb[:, 64 + d * 32:96 + d * 32], ALU.add, [dr.key, "vb"], [k_])
                K.act(tmp, tmp, AF.Exp, [k_], [k_])
                K.act(dt, tmp, AF.Ln, [k_, "ones1b"], [k_], bias=ones1[:, 0:1])
                K.tt("dve", adt, dt, aneg[:, d * 32:(d + 1) * 32], ALU.mult, [k_, "aneg"], [k_])
                return ctx_

            def stage_s2(d, ctx_):
                xb, sm, lat = ctx_["xb"], ctx_["sm"], ctx_["lat"]
                dt = sm[:, 0, :]; adt = sm[:, 1, :]; cs = sm[:, 2, :]; ecs = sm[:, 3, :]
                etot = sm[:, 4, :]; w2 = sm[:, 5, :]; tmp = sm[:, 6, :]
                k_ = sm.key
                K.mm(psum[4][:, 0:32], mk[:, d, :], adt, True, True, ["mk", k_], ["ps#4a"], True)
                K.mm(psum[4][:, 32:64], onesf[:, :], adt, True, True, ["onesf2", k_], ["ps#4a"], True)
                K.cp("dve", cs, psum[4][:, 0:32], ["ps#4a"], [k_])
                K.act(etot, psum[4][:, 32:64], AF.Exp, ["ps#4a"], [k_])
                K.tt("dve", tmp, psum[4][:, 32:64], cs, ALU.subtract, ["ps#4a", k_], [k_])
                K.act(w2, tmp, AF.Exp, [k_], [k_])
                K.tt("dve", w2, w2, dt, ALU.mult, [k_], [k_])
                xs3 = xb[:, 0:2048].rearrange("p (h q) -> p h q", q=64)
                Xc = Xc_.next()
                ctx_["Xc"] = Xc
                K.tt("pool", Xc[:, :].rearrange("p (h q) -> p h q", q=64), xs3,
                     w2.unsqueeze(2).to_broadcast([128, 32, 64]), ALU.mult, [xb.key, k_], [Xc.key])
                if not lat:
                    return ctx_
                K.act(ecs, cs, AF.Exp, [k_], [k_])
                X = X_.next()
                K.tt("pool", X[:, :].rearrange("p (h q) -> p h q", q=64), xs3,
                     dt.unsqueeze(2).to_broadcast([128, 32, 64]), ALU.mult, [xb.key, k_], [X.key])
                cs4 = cs4_.next(); cst = cst_.next(); hl = hl_.next()
                K.cp("dve", cs4[:, 0:32], cs, [k_], accw=[cs4.key])
                K.cp("dve", cs4[:, 32:64], cs, [k_], accw=[cs4.key])
                ctx_["X"] = X; ctx_["cs4"] = cs4; ctx_["cst"] = cst; ctx_["hl"] = hl
                return ctx_

            def stage_s3(d, ctx_):
                if not ctx_["lat"]:
                    return ctx_
                cs4, cst, hl, X = ctx_["cs4"], ctx_["cst"], ctx_["hl"], ctx_["X"]
                K.op("pe", lambda e, cs4=cs4: e.transpose(psum[4][0:64, 128:256], cs4[:, :], idf[:, :]),
                     reads=[cs4.key, "idf"], writes=["ps#4b"])
                K.cp("dve", cst[:, :], psum[4][0:64, 128:256], ["ps#4b"], [cst.key])
                K.cp("dve", hl[0:32, 0, :], cst[0:32, :], [cst.key], accw=[hl.key])
                K.cp("dve", hl[32:64, 2, :], cst[32:64, :], [cst.key], accw=[hl.key])
                K.tt("dve", hl[32:64, 0, :], cst[32:64, :], hl[32:64, 2, :], ALU.subtract, [cst.key, hl.key], accw=[hl.key])
                K.ts("dve", hl[:, 1, :], hl[:, 0, :], -1.0, None, ALU.mult, None, [hl.key], accw=[hl.key])
                ctx_["X"] = X; ctx_["hl"] = hl
                return ctx_

            def stage_a1(d, cx, bgen=None):
                if cx["lat"]:
                    stage_a1_lat(d, cx, bgen)
                xb, Xc = cx["xb"], cx["Xc"]
                Ssb = S_.next()
                cx["Ssb"] = Ssb
                for g in range(4):
                    pk = 6 + PB["so"] % 2; PB["so"] += 1
                    K.mm(psum[pk][:, :], xb[:, 2048 + g * 128:2048 + (g + 1) * 128], Xc[:, g * 512:(g + 1) * 512],
                         True, True, [xb.key, Xc.key], [f"ps#{pk}"], True)
                    K.cp("act", Ssb[:, g * 512:(g + 1) * 512], psum[pk][:, :], [f"ps#{pk}"], accw=[Ssb.key])

            def stage_a1_lat(d, cx, bgen=None):
                xb, bc, sm, tok0, X, hl = cx["xb"], cx["bc"], cx["sm"], cx["tok0"], cx["X"], cx["hl"]
                ctx_ = cx
                for g in range(4):
                    K.mm(psum[5][:, g * 128:(g + 1) * 128], bc[:, g, :], bc[:, 4 + g, :], True, True,
                         [bc.key], ["ps#5"], g == 3)
                cbm = cbm_.next()
                K.tt("dve", cbm[:], psum[5][:, :].rearrange("p (g l) -> p g l", l=128),
                     mk[:, d:d + 1, :].to_broadcast([128, 4, 128]), ALU.mult, ["ps#5", "mk"], [cbm.key])
                ysb = ysb_.next()
                ctx_["ysb"] = ysb
                if d == 1:
                    yf = yf_.next()
                    K.dma("sp", yf[:, :], yfw[tok0:tok0 + 128, :], "L" + yf.key, reads=["yfw"], writes=[yf.key])
                segbank = {}

                def emit_seg(hq):
                    pk = 2 + PB["seg"] % 2; PB["seg"] += 1
                    segbank[hq] = pk
                    K.mm(psum[pk][:, :], idb[:, :], negb[:, d, :], True, False, ["idb", "negb"], [f"ps#{pk}"], False)
                    for j in range(4):
                        h = hq * 4 + j
                        K.mm(psum[pk][:, j * 128:(j + 1) * 128], selb[:, h, :], hl[:, 0, :], False, False,
                             ["selb", hl.key], [f"ps#{pk}"], False)
                        K.mm(psum[pk][:, j * 128:(j + 1) * 128], hl[:, 1, :], selb[:, h, :], False, j == 3,
                             ["selb", hl.key], [f"ps#{pk}"], j == 3)

                pyd = 0
                emit_seg(0)
                for hq in range(8):
                    g = hq // 2
                    if hq + 1 < 8:
                        emit_seg(hq + 1)
                    pk = segbank[hq]
                    if hq % 2 == 0:
                        pyd = PB["yd"] % 2; PB["yd"] += 1
                    E = E_.next(); MT = MT_.next()
                    K.act(E[:, :], psum[pk][:, :], AF.Exp, [f"ps#{pk}"], [E.key])
                    K.stt("dve", MT[:], E[:, :].rearrange("p (j l) -> p j l", l=128), 1e30,
                          cbm[:, g:g + 1, :].to_broadcast([128, 4, 128]), ALU.min, ALU.mult,
                          [E.key, cbm.key], [MT.key])
                    for j in range(4):
                        h = hq * 4 + j
                        K.mm(psum[pyd][:, (h % 8) * 64:(h % 8 + 1) * 64], MT[:, j, :], X[:, h * 64:(h + 1) * 64],
                             True, True, [MT.key, X.key], [f"ps#{pyd}"], (h % 8 == 7))
                    if hq % 2 == 1:
                        if d == 0:
                            K.cp("act", ysb[:, g * 512:(g + 1) * 512], psum[pyd][:, :], [f"ps#{pyd}"], accw=[ysb.key])
                        else:
                            K.tt("dve", ysb[:, g * 512:(g + 1) * 512], psum[pyd][:, :], yf[:, g * 512:(g + 1) * 512],
                                 ALU.add, [f"ps#{pyd}", yf.key], accw=[ysb.key])
                    if bgen is not None:
                        next(bgen, None)
                return ctx_

            def stage_b(d, cx):
                xb, bc, sm, lat, tok0, Xc = cx["xb"], cx["bc"], cx["sm"], cx["lat"], cx["tok0"], cx["Xc"]
                k_ = sm.key
                ecs = sm[:, 3, :]; etot = sm[:, 4, :]
                xs3 = xb[:, 0:2048].rearrange("p (h q) -> p h q", q=64)
                Ssb = cx["Ssb"]
                stprev = ST["cur"]
                stnew = stbf_.next()
                ST["cur"] = stnew
                K.tt("dve", state[:, :].rearrange("p (h q) -> p h q", q=64), state[:, :].rearrange("p (h q) -> p h q", q=64),
                     etot.unsqueeze(2).to_broadcast([128, 32, 64]), ALU.mult, ["state", k_], ["state"])
                yield
                K.tt("dve", state[:, :], state[:, :], Ssb[:, :], ALU.add, ["state", Ssb.key], ["state"])
                K.cp("act", stnew[:, :], state[:, :], ["state"], [stnew.key])
                yield
                if lat:
                    ysb = cx["ysb"]
                    for g in range(4):
                        pk = 6 + PB["so"] % 2; PB["so"] += 1
                        K.mm(psum[pk][:, :], bc[:, 4 + g, :], stprev[:, g * 512:(g + 1) * 512], True, True,
                             [bc.key, stprev.key], [f"ps#{pk}"], True)
                        to = to_.next()
                        K.tt("dve", to[:, :].rearrange("p (h q) -> p h q", q=64),
                             psum[pk][:, :].rearrange("p (h q) -> p h q", q=64),
                             ecs[:, g * 8:(g + 1) * 8].unsqueeze(2).to_broadcast([128, 8, 64]), ALU.mult,
                             [f"ps#{pk}", k_], [to.key])
                        K.tt("dve", ysb[:, g * 512:(g + 1) * 512], ysb[:, g * 512:(g + 1) * 512], to[:, :], ALU.add,
                             [ysb.key, to.key], [ysb.key])
                        yield
                if not lat:
                    return
                if d == 0:
                    K.dma("sp", yfw[tok0:tok0 + 128, :], ysb[:, :], "S" + ysb.key, reads=[ysb.key], accw=["yfw"])
                    return
                zt = zt_.next(); ynb = ynb_.next(); yT = yT_.next(); xd = xd_.next()
                K.dma("sp", zt[:, :], zs[tok0:tok0 + 128, :], "L" + zt.key, writes=[zt.key])
                K.tt("pool", xd[:, :].rearrange("p (h q) -> p h q", q=64), xs3,
                     vb[:, 128:160].unsqueeze(2).to_broadcast([128, 32, 64]), ALU.mult, [xb.key, "vb"], [xd.key])
                K.tt("dve", ysb[:, :], ysb[:, :], xd[:, :], ALU.add, [ysb.key, xd.key], [ysb.key])
                K.tt("dve", ysb[:, :], ysb[:, :], zt[:, :], ALU.mult, [ysb.key, zt.key], [ysb.key])
                yield
                for g in range(4):
                    K.act(junk[:, :], ysb[:, g * 512:(g + 1) * 512], AF.Square, [ysb.key], ["sjunk"], accw=[k_],
                          accum_out=sm[:, 7, g:g + 1])
                K.act(sm[:, 7, 4:8], sm[:, 7, 0:4], AF.Sqrt, [k_, "epsb"], [k_], scale=1.0 / 512, bias=epsb[:, 0:1])
                K.op("dve", lambda e, sm=sm: e.reciprocal(sm[:, 7, 8:12], sm[:, 7, 4:8]), [k_], [k_])
                for g in range(4):
                    K.stt("dve", ynb[:, g * 512:(g + 1) * 512], ysb[:, g * 512:(g + 1) * 512], sm[:, 7, 8 + g:9 + g],
                          nwbc[:, g * 512:(g + 1) * 512], ALU.mult, ALU.mult, [ysb.key, k_, "nwbc"], accw=[ynb.key])
                yield
                for half in range(2):
                    pk = half
                    tb = psum[pk][:].bitcast(BF16)
                    for jj in range(8):
                        j = half * 8 + jj
                        K.op("pe", lambda e, tb=tb, jj=jj, j=j, ynb=ynb: e.transpose(
                            tb[:, jj * 128:(jj + 1) * 128], ynb[:, j * 128:(j + 1) * 128], idb[:]),
                            reads=[ynb.key, "idb"], writes=[f"ps#{pk}"], inc=(jj == 7))
                    K.cp("act", yT[:, half * 8:(half + 1) * 8, :].rearrange("p j t -> p (j t)"), tb[:, :], [f"ps#{pk}"],
                         accw=[yT.key])
                K.dma("sp", mixT1[:, tok0:tok0 + 128].rearrange("(j p) t -> p j t", p=128), yT[:], "S" + yT.key,
                      reads=[yT.key], accw=["mixT1"])

            for d in range(2):
                order = [0, 1] + lat_chunks if d == 0 else [1, 0] + lat_chunks[::-1]
                K.op("pool", lambda e: e.memset(state[:], 0.0), writes=["state"])
                st0 = stbf_.next()
                ST["cur"] = st0
                K.op("pool", lambda e, st0=st0: e.memset(st0[:], 0.0), writes=[st0.key])
                n_ = len(order)
                cxs = {}
                def run(stage, idx):
                    if 0 <= idx < n_:
                        if stage == 1:
                            cxs[idx] = stage_a(d, order[idx])
                        elif stage == 2:
                            stage_s2(d, cxs[idx])
                        elif stage == 3:
                            stage_s3(d, cxs[idx])
                        elif stage == 4:
                            stage_a1(d, cxs[idx])
                        else:
                            stage_b(d, cxs.pop(idx))
                for it in range(-3, n_):
                    run(3, it + 1)
                    bgen = stage_b(d, cxs.pop(it)) if 0 <= it < n_ else None
                    if 0 <= it + 1 < n_:
                        stage_a1(d, cxs[it + 1], bgen)
                    if bgen is not None:
                        for _ in bgen:
                            pass
                    run(1, it + 3); run(2, it + 2)
            K.barrier()

    if "ssd" not in P.dbg.get("skip", ()):
        sq_prev = K.store_q
        K.store_q = os.environ.get("KSSDQ", sq_prev) or None
        phase_ssd()
        K.store_q = sq_prev
    if stop_after == "ssd":
        K.barrier(); pes.close(); P.es.close(); return P

    nt1 = P.dbg.get("out1_tiles", SEQ // 128)
    tiles1 = [(CTX + i * 128, i * 128, 0) for i in range(nt1)]
    phase_out(1, mixT1, 16, o_w_out, h1, out_h, tiles1)

    K.barrier()
    pes.close()
    P.es.close()
    return P


def _rope_tables():
    n_freq = 16
    inv = (10000.0 ** (-np.arange(n_freq, dtype=np.float32) / np.float32(n_freq))).astype(np.float32)
    t = np.arange(SEQ)
    row = (t // 64).astype(np.float32)
    col = (t % 64).astype(np.float32)
    ang = np.concatenate([row[:, None] * inv, col[:, None] * inv], axis=-1).astype(np.float32)
    cos, sin = np.cos(ang).astype(np.float32), np.sin(ang).astype(np.float32)
    tab = np.zeros((2, 128, T), np.float32)
    tab[0, :, :CTX] = 1.0
    for p in range(128):
        d = p % 64
        fi = (d % 16) + 16 * (d // 32)
        sgn = -1.0 if (d % 32) < 16 else 1.0
        tab[0, p, CTX:] = cos[:, fi]
        tab[1, p, CTX:] = sgn * sin[:, fi]
    return tab


def _rope_perm():
    perm = np.zeros(512, np.int64)
    for f in range(512):
        d = f % 64
        e = d % 32
        e2 = e + 16 if e < 16 else e - 16
        perm[f] = f - d + (d // 32) * 32 + e2
    return perm


def make_in_maps(inp):
    B = inp["x"].shape[0]
    perm = _rope_perm()
    w = np.asarray(inp["e_w_in"][0], np.float32)
    w_aug = np.ascontiguousarray(np.concatenate([w, w[:, 1024 + perm], w[:, 1536 + perm]], axis=1))
    tab = _rope_tables()
    ident = np.eye(128, dtype=np.float32)
    wbd = np.zeros((2, 2, 4, 128, 128), np.float32)
    for d in range(2):
        for g, nm in enumerate(("lru_w_r", "lru_w_i")):
            wsrc = np.asarray(inp[nm][0][d], np.float32)
            for cc in range(4):
                wbd[d, g, cc, 0:64, 0:64] = wsrc[2 * cc]
                wbd[d, g, cc, 64:128, 64:128] = wsrc[2 * cc + 1]
    wbd = np.ascontiguousarray(wbd.reshape(16, 128, 128))
    lvec = np.ascontiguousarray(np.concatenate([
        np.asarray(inp["lru_conv_w"][0], np.float32), np.asarray(inp["lru_conv_b"], np.float32).reshape(1, 512),
        np.asarray(inp["lru_b_r"][0], np.float32), np.asarray(inp["lru_b_i"][0], np.float32),
        np.asarray(inp["lru_lambda"][0], np.float32)], axis=0))
    ssd_cv = np.ascontiguousarray(np.concatenate([np.asarray(inp["ssd_conv_w"][0], np.float32),
                                                  np.asarray(inp["ssd_conv_b"], np.float32).reshape(1, 3072)], 0))
    ssd_vec = np.ascontiguousarray(np.concatenate([np.asarray(inp["ssd_a_log"][0], np.float32).reshape(-1),
                                                   np.asarray(inp["ssd_dt_bias"][0], np.float32).reshape(-1),
                                                   np.asarray(inp["ssd_d"][0], np.float32).reshape(-1)]).reshape(1, 160))
    ii = np.arange(128)
    maskT = np.stack([(ii[None, :] >= ii[:, None]), (ii[None, :] <= ii[:, None])], 0).astype(np.float32)
    negm = np.ascontiguousarray(np.tile((maskT - 1.0) * 30000.0, (1, 1, 4)).astype(np.float32))
    selc = np.zeros((64, 32, 128), np.float32)
    for hh in range(32):
        selc[hh, hh, :] = 1.0
        selc[32 + hh, hh, :] = 1.0
    selc = np.ascontiguousarray(selc.reshape(64, 32 * 128))
    maps = []
    for b in range(B):
        m = {
            "src0": np.ascontiguousarray(np.concatenate([inp["ctx"][b], inp["x"][b]], axis=0), dtype=np.float32),
            "cvec": np.ascontiguousarray(np.stack([inp["c"][b], inp["c_ctx"]], 0), dtype=np.float32),
            "w_mod": np.asarray(inp["w_mod"], np.float32),
            "b_mod": np.asarray(inp["b_mod"], np.float32),
            "g_pre": np.asarray(inp["g_pre"], np.float32),
            "g_post": np.asarray(inp["g_post"], np.float32),
            "e_w_in_aug": w_aug,
            "e_w_out": np.asarray(inp["e_w_out"][0], np.float32),
            "ident": ident,
            "ropetab": tab,
            "lru_wbd": wbd,
            "lru_vec": lvec,
            "da_lam": np.ascontiguousarray(np.asarray(inp["da_lambda"][0], np.float32).reshape(1, 256)),
            "da_sub": np.ascontiguousarray(np.asarray(inp["da_subln"][0], np.float32).reshape(1, 128)),
            "o_w_in": np.asarray(inp["o_w_in"][0], np.float32),
            "o_w_out": np.asarray(inp["o_w_out"][0], np.float32),
            "ssd_cv": ssd_cv,
            "ssd_vec": ssd_vec,
            "ssd_norm": np.asarray(inp["ssd_norm"][0], np.float32),
            "maskT": maskT,
            "selc": selc,
            "negm": negm,
        }
        maps.append(m)
    return maps


def kernel(**inp):
    P = build_program()
    maps = make_in_maps(inp)
    res = run_bass_kernel_spmd(P.nc, maps, core_ids=list(range(8)))
    return np.stack([np.asarray(r["out"], np.float32) for r in res.results], 0)
```
